# Optimizing a Trainium2 kernel written in Bass

```python
import jax
import jax.numpy as jnp
from jax import lax
import numpy as np

D_MODEL = 1024
BATCH = 8
SEQ = 2048
DEPTH = 4

GRID_W = 64
N_MIXERS = 4
GROUP_W = D_MODEL // N_MIXERS
D_MIX = N_MIXERS * GROUP_W
HEAD_DIM = 64
D_FF = 4 * D_MODEL
NORM_EPS = 1e-5

RWKV_HEADS = GROUP_W // HEAD_DIM
RWKV_W_RANK = 64
RWKV_A_RANK = 64
RWKV_G_RANK = 128
RWKV_DECAY_SCALE = 0.6065306597126334
RWKV_GN_EPS = 64e-5
RWKV_COLS = 3 * GROUP_W + 2 * RWKV_W_RANK + 2 * RWKV_A_RANK + RWKV_G_RANK

SSD_HEADS = GROUP_W // HEAD_DIM
SSD_BC_GROUPS = 2
SSD_D_STATE = 128
SSD_CONV = 4
SSD_CHUNK = 64
SSD_XBC = GROUP_W + 2 * SSD_BC_GROUPS * SSD_D_STATE
SSD_COLS = GROUP_W + SSD_XBC + 2 * SSD_HEADS

LRU_BLOCKS = 4
LRU_BLOCK_W = GROUP_W // LRU_BLOCKS
LRU_CONV = 4
LRU_C = 8.0
LRU_COLS = 2 * GROUP_W

NA_HEADS = GROUP_W // HEAD_DIM
NA_WIN_ROWS = 8
NA_WIN_COLS = 16
NA_QCOL_BLOCK = 16
NA_KCOL_BLOCK = NA_QCOL_BLOCK + NA_WIN_COLS
NA_COLS = 3 * GROUP_W

IN_COLS = RWKV_COLS + SSD_COLS + LRU_COLS + NA_COLS

kernel_name = 'hybrid_bidir_rwkv7_ssd_rglru_natten_encoder'


def split_cols(t, sizes):
    offs = np.cumsum(sizes)[:-1].tolist()
    return jnp.split(t, offs, axis=-1)


def rms_norm(x, w):
    xf = x.astype(jnp.float32)
    y = xf * lax.rsqrt(jnp.mean(xf * xf, axis=-1, keepdims=True) + NORM_EPS)
    return (y * w.astype(jnp.float32)).astype(x.dtype)


def centred_dwconv(u, w, b):
    k = w.shape[0]
    out = lax.conv_general_dilated(u, w[:, None, :].astype(u.dtype), window_strides=(1,),
                                   padding=[(k // 2, k - 1 - k // 2)],
                                   dimension_numbers=('NWC', 'WIO', 'NWC'),
                                   feature_group_count=u.shape[-1])
    return out + b.astype(u.dtype)


def bidir_token_shift(u, mu):
    prev = jnp.pad(u[:, :-1], ((0, 0), (1, 0), (0, 0)))
    nxt = jnp.pad(u[:, 1:], ((0, 0), (0, 1), (0, 0)))
    return u + mu[0] * (prev - u) + mu[1] * (nxt - u)


def time_major(t):
    t = jnp.stack([t[0], t[1][:, ::-1]])
    return jnp.moveaxis(t, 2, 0)


def rwkv7_mixer(u, shift_mu, w0, w_up, a0, a_up, g_up, k_k, k_a, r_k, gn_w, gn_b):
    f32 = jnp.float32
    bsz, s, _ = u.shape
    u = bidir_token_shift(u, shift_mu)
    r, k, v, wd, ad, gd = split_cols(u, [GROUP_W, GROUP_W, GROUP_W, 2 * RWKV_W_RANK,
                                         2 * RWKV_A_RANK, RWKV_G_RANK])
    wd = jnp.tanh(wd).reshape(bsz, s, 2, RWKV_W_RANK)
    ad = ad.reshape(bsz, s, 2, RWKV_A_RANK)
    z_w = w0[:, None, None, :] + jnp.einsum('bsdr,drc->dbsc', wd, w_up)
    decay = jnp.exp(-RWKV_DECAY_SCALE * jax.nn.sigmoid(z_w.astype(f32)))
    alpha = jax.nn.sigmoid((a0[:, None, None, :] + jnp.einsum('bsdr,drc->dbsc', ad, a_up)).astype(f32))
    g = jax.nn.sigmoid(gd) @ g_up
    heads = lambda t: t.reshape(t.shape[:-1] + (RWKV_HEADS, HEAD_DIM))
    rh = heads(r.astype(f32))
    vh = heads(v.astype(f32))
    kk = heads((k * k_k).astype(f32))
    kk = kk * lax.rsqrt(jnp.sum(kk * kk, axis=-1, keepdims=True) + 1e-12)
    kd = heads(k.astype(f32)[None] * (1.0 + (alpha - 1.0) * k_a))
    two = lambda t: jnp.broadcast_to(t[None], (2,) + t.shape)
    seq_in = (time_major(two(rh)), time_major(heads(decay)), time_major(kd),
              time_major(two(vh)), time_major(two(kk)), time_major(heads(alpha) * kk[None]))

    def step(state, inp):
        r_t, w_t, k_t, v_t, kk_t, b_t = inp
        s_kk = jnp.einsum('dbhij,dbhj->dbhi', state, kk_t)
        state = (state * w_t[..., None, :] - s_kk[..., :, None] * b_t[..., None, :]
                 + v_t[..., :, None] * k_t[..., None, :])
        return state, jnp.einsum('dbhij,dbhj->dbhi', state, r_t)

    state0 = jnp.zeros((2, bsz, RWKV_HEADS, HEAD_DIM, HEAD_DIM), f32)
    _, ys = lax.scan(step, state0, seq_in)
    ys = jnp.moveaxis(ys, 0, 2)
    y = ys[0] + ys[1][:, ::-1]
    mean = jnp.mean(y, axis=-1, keepdims=True)
    var = jnp.mean(jnp.square(y - mean), axis=-1, keepdims=True)
    y = ((y - mean) * lax.rsqrt(var + RWKV_GN_EPS)).reshape(bsz, s, GROUP_W) * gn_w + gn_b
    bonus = jnp.sum(rh[None] * kd * r_k, axis=(0, -1))[..., None] * vh
    y = (y + bonus.reshape(bsz, s, GROUP_W)) * g
    return y.astype(u.dtype)


def segsum(x):
    t = x.shape[-1]
    cs = jnp.cumsum(x, axis=-1)
    seg = cs[..., :, None] - cs[..., None, :]
    return jnp.where(jnp.tril(jnp.ones((t, t), dtype=bool)), seg, -jnp.inf)


def ssd_chunked(x, a, bm, cm):
    nb, s, h, p = x.shape
    n = bm.shape[-1]
    c, l = s // SSD_CHUNK, SSD_CHUNK
    x = x.reshape(nb, c, l, h, p)
    bm = bm.reshape(nb, c, l, h, n)
    cm = cm.reshape(nb, c, l, h, n)
    a = a.reshape(nb, c, l, h).transpose(0, 3, 1, 2)
    a_cs = jnp.cumsum(a, axis=-1)
    scores = jnp.einsum('bclhn,bcshn->bhcls', cm, bm) * jnp.exp(segsum(a))
    y_diag = jnp.einsum('bhcls,bcshp->bclhp', scores, x)
    decay_to_end = jnp.exp(a_cs[..., -1:] - a_cs)
    states = jnp.einsum('bclhn,bhcl,bclhp->bchpn', bm, decay_to_end, x)
    states = jnp.concatenate([jnp.zeros_like(states[:, :1]), states], axis=1)
    chunk_decay = jnp.exp(segsum(jnp.pad(a_cs[..., -1], ((0, 0), (0, 0), (1, 0)))))
    states = jnp.einsum('bhzc,bchpn->bzhpn', chunk_decay, states)[:, :-1]
    y_off = jnp.einsum('bclhn,bchpn,bhcl->bclhp', cm, states, jnp.exp(a_cs))
    return (y_diag + y_off).reshape(nb, s, h, p)


def ssd_mixer(u, conv_w, conv_b, dt_bias, a_log, d_skip, norm_w):
    f32 = jnp.float32
    bsz, s, _ = u.shape
    z, xbc, dt_raw = split_cols(u, [GROUP_W, SSD_XBC, 2 * SSD_HEADS])
    xbc = jax.nn.silu(centred_dwconv(xbc, conv_w, conv_b))
    xs, bm, cm = split_cols(xbc, [GROUP_W, SSD_BC_GROUPS * SSD_D_STATE, SSD_BC_GROUPS * SSD_D_STATE])
    rep = SSD_HEADS // SSD_BC_GROUPS
    xs = xs.reshape(bsz, s, SSD_HEADS, HEAD_DIM).astype(f32)
    bm = jnp.repeat(bm.reshape(bsz, s, SSD_BC_GROUPS, SSD_D_STATE), rep, axis=2).astype(f32)
    cm = jnp.repeat(cm.reshape(bsz, s, SSD_BC_GROUPS, SSD_D_STATE), rep, axis=2).astype(f32)
    dt = jax.nn.softplus(dt_raw.reshape(bsz, s, 2, SSD_HEADS).astype(f32) + dt_bias)
    a = -jnp.exp(a_log.astype(f32))
    dirs = lambda tf, tb: jnp.concatenate([tf, tb[:, ::-1]], axis=0)
    x_dt = dirs(xs * dt[:, :, 0, :, None], xs * dt[:, :, 1, :, None])
    a_dt = dirs(dt[:, :, 0] * a[0], dt[:, :, 1] * a[1])
    y = ssd_chunked(x_dt, a_dt, dirs(bm, bm), dirs(cm, cm))
    y = y[:bsz] + y[bsz:, ::-1] + d_skip[:, None] * xs
    y = y.reshape(bsz, s, GROUP_W) * jax.nn.silu(z.astype(f32))
    return rms_norm(y, norm_w).astype(u.dtype)


def rglru_mixer(u, conv_w, conv_b, gate_a_w, gate_a_b, gate_x_w, gate_x_b, lam):
    f32 = jnp.float32
    bsz, s, _ = u.shape
    gate_in, x_in = split_cols(u, [GROUP_W, GROUP_W])
    gate = jax.nn.gelu(gate_in.astype(f32), approximate=True)
    xf = centred_dwconv(x_in, conv_w, conv_b).astype(f32)
    xb = xf.reshape(bsz, s, LRU_BLOCKS, LRU_BLOCK_W)

    def block_gate(w, bias):
        z = jnp.einsum('bski,dkij->dbskj', xb, w.astype(f32)).reshape(2, bsz, s, GROUP_W)
        return jax.nn.sigmoid(z + bias[:, None, None, :])

    rec = block_gate(gate_a_w, gate_a_b)
    inp = block_gate(gate_x_w, gate_x_b)
    log_a = -LRU_C * rec * jax.nn.softplus(-lam.astype(f32))[:, None, None, :]
    a = jnp.exp(log_a)
    bterm = jnp.sqrt(-jnp.expm1(2.0 * log_a)) * inp * xf[None]
    a = jnp.stack([a[0], a[1][:, ::-1]])
    bterm = jnp.stack([bterm[0], bterm[1][:, ::-1]])

    def combine(lhs, rhs):
        return (lhs[0] * rhs[0], rhs[0] * lhs[1] + rhs[1])

    _, h = lax.associative_scan(combine, (a, bterm), axis=2)
    h = h[0] + h[1][:, ::-1]
    return (gate * h).astype(u.dtype)


def neighbourhood_attention(q, k, v, rel_bias):
    bsz, s, _ = q.shape
    rows = s // GRID_W
    kr = min(NA_WIN_ROWS, rows)
    ncb = GRID_W // NA_QCOL_BLOCK
    grid = lambda t: t.reshape(bsz, rows, GRID_W, NA_HEADS, HEAD_DIM).transpose(0, 3, 1, 2, 4)
    qg, kg, vg = grid(q), grid(k), grid(v)
    r_idx = jnp.arange(rows)
    key_rows = jnp.clip(r_idx - kr // 2, 0, rows - kr)[:, None] + jnp.arange(kr)[None, :]
    cb = jnp.arange(ncb)
    key_cols = (jnp.clip(cb * NA_QCOL_BLOCK - NA_WIN_COLS // 2, 0, GRID_W - NA_KCOL_BLOCK)[:, None]
                + jnp.arange(NA_KCOL_BLOCK)[None, :])
    ri = key_rows[:, None, :, None]
    ci = key_cols[None, :, None, :]
    k_blk = kg[:, :, ri, ci]
    v_blk = vg[:, :, ri, ci]
    q_blk = qg.reshape(bsz, NA_HEADS, rows, ncb, NA_QCOL_BLOCK, HEAD_DIM)
    scores = jnp.einsum('bhrcqd,bhrckwd->bhrcqkw', q_blk, k_blk).astype(jnp.float32) * (HEAD_DIM ** -0.5)
    q_cols = cb[:, None] * NA_QCOL_BLOCK + jnp.arange(NA_QCOL_BLOCK)[None, :]
    win_start = jnp.clip(q_cols - NA_WIN_COLS // 2, 0, GRID_W - NA_WIN_COLS)
    kc = key_cols[:, None, :]
    in_win = (kc >= win_start[..., None]) & (kc < win_start[..., None] + NA_WIN_COLS)
    row_i = key_rows - r_idx[:, None] + NA_WIN_ROWS - 1
    col_i = jnp.clip(kc - q_cols[..., None] + NA_WIN_COLS - 1, 0, 2 * NA_WIN_COLS - 2)
    bias = rel_bias[:, row_i[:, None, None, :, None], col_i[None, :, :, None, :]]
    scores = jnp.where(in_win[:, :, None, :], scores + bias.astype(jnp.float32)[None], -jnp.inf)
    shp = scores.shape
    p = jax.nn.softmax(scores.reshape(shp[:-2] + (kr * NA_KCOL_BLOCK,)), axis=-1).reshape(shp)
    out = jnp.einsum('bhrcqkw,bhrckwd->bhrcqd', p.astype(v.dtype), v_blk)
    return out.reshape(bsz, NA_HEADS, rows, GRID_W, HEAD_DIM).transpose(0, 2, 3, 1, 4).reshape(bsz, s, GROUP_W)


def setup_inputs(seed: int = 0) -> dict:
    key = jax.random.key(seed)
    ks = iter(jax.random.split(key, 40))
    f32 = jnp.float32
    nrm = lambda shape, scale: scale * jax.random.normal(next(ks), shape, f32)
    uni = lambda shape, lo, hi: jax.random.uniform(next(ks), shape, f32, lo, hi)
    L = DEPTH
    x = nrm((BATCH, SEQ, D_MODEL), 1.0)
    norm1_w = 1.0 + nrm((L, D_MODEL), 0.05)
    w_in = nrm((L, D_MODEL, IN_COLS), D_MODEL ** -0.5)
    rwkv_shift_mu = uni((L, 2, RWKV_COLS), 0.0, 0.5)
    rwkv_w0 = uni((L, 2, GROUP_W), -6.0, 1.0)
    rwkv_w_up = nrm((L, 2, RWKV_W_RANK, GROUP_W), 0.1)
    rwkv_a0 = nrm((L, 2, GROUP_W), 0.5)
    rwkv_a_up = nrm((L, 2, RWKV_A_RANK, GROUP_W), 0.1)
    rwkv_g_up = nrm((L, RWKV_G_RANK, GROUP_W), RWKV_G_RANK ** -0.5)
    rwkv_k_k = 0.85 + nrm((L, GROUP_W), 0.05)
    rwkv_k_a = 1.0 + nrm((L, GROUP_W), 0.05)
    rwkv_r_k = nrm((L, RWKV_HEADS, HEAD_DIM), 0.1)
    rwkv_gn_w = 1.0 + nrm((L, GROUP_W), 0.05)
    rwkv_gn_b = nrm((L, GROUP_W), 0.02)
    ssd_conv_w = nrm((L, SSD_CONV, SSD_XBC), 0.5)
    ssd_conv_b = nrm((L, SSD_XBC), 0.02)
    dt0 = jnp.exp(uni((L, 2, SSD_HEADS), float(np.log(1e-3)), float(np.log(1e-1))))
    ssd_dt_bias = dt0 + jnp.log(-jnp.expm1(-dt0))
    ssd_a_log = jnp.log(uni((L, 2, SSD_HEADS), 1.0, 16.0))
    ssd_d = 1.0 + nrm((L, SSD_HEADS), 0.1)
    ssd_norm_w = 1.0 + nrm((L, GROUP_W), 0.05)
    lru_conv_w = nrm((L, LRU_CONV, GROUP_W), 0.5)
    lru_conv_b = nrm((L, GROUP_W), 0.02)
    lru_gate_a_w = nrm((L, 2, LRU_BLOCKS, LRU_BLOCK_W, LRU_BLOCK_W), LRU_BLOCK_W ** -0.5)
    lru_gate_a_b = nrm((L, 2, GROUP_W), 0.1)
    lru_gate_x_w = nrm((L, 2, LRU_BLOCKS, LRU_BLOCK_W, LRU_BLOCK_W), LRU_BLOCK_W ** -0.5)
    lru_gate_x_b = nrm((L, 2, GROUP_W), 0.1)
    a_c = uni((L, 2, GROUP_W), 0.9, 0.999) ** (1.0 / LRU_C)
    lru_lambda = jnp.log(a_c) - jnp.log1p(-a_c)
    na_rel_bias = nrm((L, NA_HEADS, 2 * NA_WIN_ROWS - 1, 2 * NA_WIN_COLS - 1), 0.2)
    w_out = nrm((L, D_MIX, D_MODEL), D_MIX ** -0.5)
    norm2_w = 1.0 + nrm((L, D_MODEL), 0.05)
    w_mlp1 = nrm((L, D_MODEL, D_FF), D_MODEL ** -0.5)
    w_mlp2 = nrm((L, D_FF, D_MODEL), D_FF ** -0.5)
    final_norm_w = 1.0 + nrm((D_MODEL,), 0.05)
    return {'x': x, 'norm1_w': norm1_w, 'w_in': w_in,
            'rwkv_shift_mu': rwkv_shift_mu, 'rwkv_w0': rwkv_w0, 'rwkv_w_up': rwkv_w_up,
            'rwkv_a0': rwkv_a0, 'rwkv_a_up': rwkv_a_up, 'rwkv_g_up': rwkv_g_up,
            'rwkv_k_k': rwkv_k_k, 'rwkv_k_a': rwkv_k_a, 'rwkv_r_k': rwkv_r_k,
            'rwkv_gn_w': rwkv_gn_w, 'rwkv_gn_b': rwkv_gn_b,
            'ssd_conv_w': ssd_conv_w, 'ssd_conv_b': ssd_conv_b, 'ssd_dt_bias': ssd_dt_bias,
            'ssd_a_log': ssd_a_log, 'ssd_d': ssd_d, 'ssd_norm_w': ssd_norm_w,
            'lru_conv_w': lru_conv_w, 'lru_conv_b': lru_conv_b,
            'lru_gate_a_w': lru_gate_a_w, 'lru_gate_a_b': lru_gate_a_b,
            'lru_gate_x_w': lru_gate_x_w, 'lru_gate_x_b': lru_gate_x_b, 'lru_lambda': lru_lambda,
            'na_rel_bias': na_rel_bias, 'w_out': w_out, 'norm2_w': norm2_w,
            'w_mlp1': w_mlp1, 'w_mlp2': w_mlp2, 'final_norm_w': final_norm_w}


def reference(x, norm1_w, w_in, rwkv_shift_mu, rwkv_w0, rwkv_w_up, rwkv_a0, rwkv_a_up, rwkv_g_up,
              rwkv_k_k, rwkv_k_a, rwkv_r_k, rwkv_gn_w, rwkv_gn_b, ssd_conv_w, ssd_conv_b, ssd_dt_bias,
              ssd_a_log, ssd_d, ssd_norm_w, lru_conv_w, lru_conv_b, lru_gate_a_w, lru_gate_a_b,
              lru_gate_x_w, lru_gate_x_b, lru_lambda, na_rel_bias, w_out, norm2_w, w_mlp1, w_mlp2,
              final_norm_w):
    h = x
    for layer in range(DEPTH):
        u = rms_norm(h, norm1_w[layer]) @ w_in[layer]
        u_a, u_b, u_c, u_d = split_cols(u, [RWKV_COLS, SSD_COLS, LRU_COLS, NA_COLS])
        y_a = rwkv7_mixer(u_a, rwkv_shift_mu[layer], rwkv_w0[layer], rwkv_w_up[layer], rwkv_a0[layer],
                          rwkv_a_up[layer], rwkv_g_up[layer], rwkv_k_k[layer], rwkv_k_a[layer],
                          rwkv_r_k[layer], rwkv_gn_w[layer], rwkv_gn_b[layer])
        y_b = ssd_mixer(u_b, ssd_conv_w[layer], ssd_conv_b[layer], ssd_dt_bias[layer], ssd_a_log[layer],
                        ssd_d[layer], ssd_norm_w[layer])
        y_c = rglru_mixer(u_c, lru_conv_w[layer], lru_conv_b[layer], lru_gate_a_w[layer], lru_gate_a_b[layer],
                          lru_gate_x_w[layer], lru_gate_x_b[layer], lru_lambda[layer])
        q, k, v = split_cols(u_d, [GROUP_W, GROUP_W, GROUP_W])
        y_d = neighbourhood_attention(q, k, v, na_rel_bias[layer])
        h = h + jnp.concatenate([y_a, y_b, y_c, y_d], axis=-1) @ w_out[layer]
        m = rms_norm(h, norm2_w[layer]) @ w_mlp1[layer]
        h = h + jnp.square(jax.nn.relu(m)) @ w_mlp2[layer]
    return rms_norm(h, final_norm_w)
```

```python
import numpy as np
from contextlib import ExitStack
import concourse.bass as bass
import concourse.mybir as mybir
from concourse.bass_utils import run_bass_kernel_spmd

F32 = mybir.dt.float32
BF16 = mybir.dt.bfloat16
AF = mybir.ActivationFunctionType
ALU = mybir.AluOpType
AX = mybir.AxisListType
DSZ = {F32: 4, BF16: 2}

L_ALL = 4
D = 1024
S = 2048
NT = 16
IN_COLS = 3464
RW0, SSD0, LRU0, NA0 = 0, 1152, 2184, 2696
CELL = 32
NSLOT = 8
ARENA_F32 = 53000
DECAY = 0.6065306597126334
NEG = -30000.0

PP_SPEC = [("n1", 8), ("n2", 8), ("mu0", 9), ("mu1", 9), ("w0", 4), ("a0", 4), ("kk", 2), ("ka", 2),
           ("rk", 2), ("scw", 24), ("scb", 6), ("lcw", 8), ("lcb", 2), ("lab", 4), ("lxb", 4), ("lam", 4)]
PP_OFF = {}
_o = 0
for _n, _k in PP_SPEC:
    PP_OFF[_n] = (_o, _k)
    _o += _k
PP_L = _o
PP_TOT = PP_L * L_ALL + 8
RB_SPEC = [("gnw", 256), ("gnb", 256), ("dtb", 8), ("alog", 8), ("dsk", 4), ("snw", 256)]
RB_OFF = {}
_o = 0
for _n, _k in RB_SPEC:
    RB_OFF[_n] = (_o, _k)
    _o += _k
RB_L = _o


def _colvec(a):
    a = np.asarray(a, np.float32)
    n = a.shape[-1] // 128
    b = a.reshape(a.shape[:-1] + (n, 128))
    return np.moveaxis(b, -1, 0)


def build_host_params(inp):
    pp = np.zeros((128, PP_TOT), np.float32)
    rb = np.zeros((L_ALL, 128, RB_L), np.float32)
    for l in range(L_ALL):
        def put(name, arr):
            o, k = PP_OFF[name]
            pp[:, l * PP_L + o: l * PP_L + o + k] = arr.reshape(128, k)
        put("n1", _colvec(inp["norm1_w"][l]))
        put("n2", _colvec(inp["norm2_w"][l]))
        put("mu0", _colvec(inp["rwkv_shift_mu"][l, 0]))
        put("mu1", _colvec(inp["rwkv_shift_mu"][l, 1]))
        put("w0", _colvec(inp["rwkv_w0"][l]))
        put("a0", _colvec(inp["rwkv_a0"][l]))
        put("kk", _colvec(inp["rwkv_k_k"][l]))
        put("ka", _colvec(inp["rwkv_k_a"][l]))
        put("rk", _colvec(inp["rwkv_r_k"][l].reshape(256)))
        put("scw", _colvec(inp["ssd_conv_w"][l]))
        put("scb", _colvec(inp["ssd_conv_b"][l]))
        put("lcw", _colvec(inp["lru_conv_w"][l]))
        put("lcb", _colvec(inp["lru_conv_b"][l]))
        put("lab", _colvec(inp["lru_gate_a_b"][l]))
        put("lxb", _colvec(inp["lru_gate_x_b"][l]))
        put("lam", _colvec(inp["lru_lambda"][l]))

        def putr(name, vec):
            o, k = RB_OFF[name]
            rb[l, :, o:o + k] = np.broadcast_to(np.asarray(vec, np.float32).reshape(1, k), (128, k))
        putr("gnw", inp["rwkv_gn_w"][l])
        putr("gnb", inp["rwkv_gn_b"][l])
        putr("dtb", inp["ssd_dt_bias"][l].reshape(8))
        putr("alog", inp["ssd_a_log"][l].reshape(8))
        putr("dsk", inp["ssd_d"][l])
        putr("snw", inp["ssd_norm_w"][l])
    pp[:, PP_L * L_ALL:] = _colvec(inp["final_norm_w"])
    lbd = np.zeros((L_ALL, 2, 2, 2, 128, 128), np.float32)
    for wi, nm in enumerate(("lru_gate_a_w", "lru_gate_x_w")):
        w = np.asarray(inp[nm], np.float32)
        for pt in range(2):
            for j in range(2):
                lbd[:, wi, :, pt, j * 64:(j + 1) * 64, j * 64:(j + 1) * 64] = w[:, :, 2 * pt + j]
    rbias = np.asarray(inp["na_rel_bias"], np.float32)
    qc = np.arange(64)
    kc = np.arange(64)
    ws = np.clip(qc - 8, 0, 48)
    inwin = (kc[:, None] >= ws[None, :]) & (kc[:, None] < ws[None, :] + 16)
    ci = np.clip(kc[:, None] - qc[None, :] + 15, 0, 30)
    t2 = np.full((L_ALL, 128, 14, 4, 64), NEG, np.float32)
    for kh in range(2):
        for dr in range(14):
            g = rbias[:, :, dr + kh, :][:, :, ci]
            g = np.where(inwin[None, None], g, np.float32(NEG))
            t2[:, kh * 64:(kh + 1) * 64, dr, :, :] = np.transpose(g, (0, 2, 1, 3))
    return pp, rb, lbd, t2


def build_level_masks():
    t = np.arange(128)
    out = np.zeros((128, 14, 128), np.float32)
    for k in range(7):
        ll = (((t[:, None] >> (k + 1)) == (t[None, :] >> (k + 1))) & (((t[:, None] >> k) & 1) == 1)
              & (((t[None, :] >> k) & 1) == 0))
        out[:, k, :] = ll
        out[:, 7 + k, :] = ll.T
    return out.reshape(128, 14 * 128)


class Sched:
    ISSUERS = ["pe", "act", "dve", "pool", "sp"]

    def __init__(self, nc, arena_name, psum_name):
        self.nc = nc
        self.semnames = ["pe", "act", "dve", "pool"]
        for q in ("sp", "act", "pool"):
            for i in range(NSLOT):
                self.semnames.append(f"{q}q{i}")
        self.sidx = {n: i for i, n in enumerate(self.semnames)}
        ns = len(self.semnames)
        self.ns = ns
        self.spaces = {arena_name: 0, psum_name: 1}
        ncell = [ARENA_F32 * 4 // CELL + 2, 16384 // CELL + 2]
        self.W = [np.zeros((4, c, ns), np.int64) for c in ncell]
        self.R = [np.zeros((4, c, ns), np.int64) for c in ncell]
        self.cnt = {e: 0 for e in ["pe", "act", "dve", "pool"]}
        self.dman = {"sp": 0, "act": 0, "pool": 0}
        self.seen = {e: np.zeros(ns, np.int64) for e in self.ISSUERS}
        self.q = {e: [] for e in self.ISSUERS}
        self.nops = 0
        self.limit = None
        self.trace = []

    def region(self, ap):
        sp = self.spaces.get(ap.tensor.name)
        if sp is None:
            return None
        a = ap.ap
        ds = DSZ[ap.dtype]
        rowlen = a[0][0]
        pcount = a[0][1]
        p0 = ap.offset // rowlen
        c0 = ap.offset % rowlen
        lo = hi = c0
        for step, cnt in a[1:]:
            ext = step * (cnt - 1)
            if ext < 0:
                lo += ext
            else:
                hi += ext
        blo, bhi = lo * ds, (hi + 1) * ds
        if sp == 1:
            blo = blo // 2048 * 2048
            bhi = (bhi + 2047) // 2048 * 2048
        return (sp, p0 // 32, (p0 + pcount + 31) // 32, blo // CELL, (bhi + CELL - 1) // CELL)

    def _need(self, r, w):
        need = np.zeros(self.ns, np.int64)
        rr = [x for x in (self.region(a) for a in r) if x is not None]
        wr = [x for x in (self.region(a) for a in w) if x is not None]
        for (sp, p0, p1, c0, c1) in rr:
            need = np.maximum(need, self.W[sp][p0:p1, c0:c1].max(axis=(0, 1)))
        for (sp, p0, p1, c0, c1) in wr:
            need = np.maximum(need, self.W[sp][p0:p1, c0:c1].max(axis=(0, 1)))
            need = np.maximum(need, self.R[sp][p0:p1, c0:c1].max(axis=(0, 1)))
        return need, rr, wr

    def _mark(self, rr, wr, si, val):
        for (sp, p0, p1, c0, c1) in wr:
            self.W[sp][p0:p1, c0:c1, si] = val
        for (sp, p0, p1, c0, c1) in rr:
            self.R[sp][p0:p1, c0:c1, si] = val

    def op(self, eng, fn, r=(), w=()):
        if self.limit is not None and self.nops >= self.limit:
            return
        need, rr, wr = self._need(r, w)
        if eng == "pe":
            need[self.sidx["pe"]] = 0
        seen = self.seen[eng]
        waits = [(i, int(need[i])) for i in np.nonzero(need > seen)[0]]
        np.maximum(seen, need, out=seen)
        self.cnt[eng] += 1
        if self.limit is not None:
            import sys as _s
            f = _s._getframe(1)
            self.trace.append((self.nops, eng, f.f_code.co_name, [(a.tensor.name, a.offset, a.ap) for a in list(w) + list(r)]))
        self.q[eng].append((waits, fn, self.sidx[eng], 1))
        self._mark(rr, wr, self.sidx[eng], self.cnt[eng])
        self.nops += 1

    def dma(self, queue, out, in_):
        if self.limit is not None and self.nops >= self.limit:
            return
        need, rr, wr = self._need([in_], [out])
        n = self.dman[queue]
        self.dman[queue] = n + 1
        slot, rnd = n % NSLOT, n // NSLOT
        si = self.sidx[f"{queue}q{slot}"]
        need[si] = max(need[si], 16 * rnd)
        seen = self.seen[queue]
        waits = [(i, int(need[i])) for i in np.nonzero(need > seen)[0]]
        np.maximum(seen, need, out=seen)
        fn = (lambda e, out=out, in_=in_: e.dma_start(out=out, in_=in_))
        self.q[queue].append((waits, fn, si, 16))
        self._mark(rr, wr, si, 16 * (rnd + 1))
        self.nops += 1

    def finish(self):
        final = np.zeros(self.ns, np.int64)
        for e in ["pe", "act", "dve", "pool"]:
            final[self.sidx[e]] = self.cnt[e]
        for qn in ("sp", "act", "pool"):
            n = self.dman[qn]
            for s in range(NSLOT):
                k = (n - s + NSLOT - 1) // NSLOT if n > s else 0
                final[self.sidx[f"{qn}q{s}"]] = 16 * k
        self.final = final

    def emit(self, stack):
        nc = self.nc
        sems = [stack.enter_context(nc.semaphore(f"s_{n}")) for n in self.semnames]
        block = stack.enter_context(nc.Block())

        def replay(name, e):
            for waits, fn, si, inc in self.q[name]:
                for (s, v) in waits:
                    e.wait_ge(sems[s], v)
                fn(e).then_inc(sems[si], inc)
            if name == "sp":
                for s in range(self.ns):
                    if self.final[s] > 0:
                        e.wait_ge(sems[s], int(self.final[s]))

        @block.tensor
        def _(e):
            replay("pe", e)

        @block.scalar
        def _(e):
            replay("act", e)

        @block.vector
        def _(e):
            replay("dve", e)

        @block.gpsimd
        def _(e):
            replay("pool", e)

        @block.sync
        def _(e):
            replay("sp", e)


class Arena:
    def __init__(self, base_ap_f32, ncols_f32, start=0):
        self.base = base_ap_f32
        self.lo = start * 4
        self.top = self.lo
        self.limit = (start + ncols_f32) * 4
        self.peak = self.top

    def alloc(self, cols, dtype, parts=128):
        nbytes = cols * DSZ[dtype]
        nbytes = (nbytes + CELL - 1) // CELL * CELL
        off = self.top
        self.top += nbytes
        self.peak = max(self.peak, self.top)
        assert self.top <= self.limit, f"arena overflow {self.top} > {self.limit}"
        ap = self.base[0:parts, off // 4:(off + nbytes) // 4]
        if dtype != F32:
            ap = ap.bitcast(dtype)
        return ap[:, 0:cols]

    def mark(self):
        return self.top

    def release(self, m):
        self.top = m


def build_program(n_layers=L_ALL, dbg=None):
    nc = bass.Bass("TRN2", target_bir_lowering=False)
    dt = nc.dram_tensor
    x_d = dt("x", [S, D], F32, kind="ExternalInput").ap()
    w_in_d = dt("w_in", [L_ALL, D, IN_COLS], F32, kind="ExternalInput").ap()
    w_out_d = dt("w_out", [L_ALL, D, D], F32, kind="ExternalInput").ap()
    w1_d = dt("w_mlp1", [L_ALL, D, 4 * D], F32, kind="ExternalInput").ap()
    w2_d = dt("w_mlp2", [L_ALL, 4 * D, D], F32, kind="ExternalInput").ap()
    wup_d = dt("rwkv_w_up", [L_ALL, 2, 64, 256], F32, kind="ExternalInput").ap()
    aup_d = dt("rwkv_a_up", [L_ALL, 2, 64, 256], F32, kind="ExternalInput").ap()
    gup_d = dt("rwkv_g_up", [L_ALL, 128, 256], F32, kind="ExternalInput").ap()
    pp_d = dt("pp", [128, PP_TOT], F32, kind="ExternalInput").ap()
    rb_d = dt("rb", [L_ALL, 128, RB_L], F32, kind="ExternalInput").ap()
    lbd_d = dt("lbd", [L_ALL, 2, 2, 2, 128, 128], F32, kind="ExternalInput").ap()
    t2_d = dt("t2", [L_ALL, 128, 14 * 256], F32, kind="ExternalInput").ap()
    lvl_d = dt("lvl", [128, 14 * 128], F32, kind="ExternalInput").ap()
    out_d = dt("out", [S, D], F32, kind="ExternalOutput").ap()
    dbg_d = None
    if dbg:
        dbg_d = dt("dbg", [D, S], F32, kind="ExternalOutput").ap()

    stack = ExitStack()
    arena_t = stack.enter_context(nc.sbuf_tensor("arena", [128, ARENA_F32], F32))
    ps_t = stack.enter_context(nc.psum_tensor("ps", [128, 4096], F32))
    arena_ap = arena_t[:, :]
    ps_ap = ps_t[:, :]
    Sc = Sched(nc, "arena", "ps")
    if dbg and dbg.get("limit"):
        Sc.limit = dbg["limit"]
    A = Arena(arena_ap, ARENA_F32)

    def bank(b):
        return ps_ap[:, b * 512:(b + 1) * 512]

    rot = [0]

    def gbank():
        b = rot[0] % 4
        rot[0] += 1
        return bank(b)

    def mm(out, lhsT, rhs, start=True, stop=True, skip=False):
        Sc.op("pe", lambda e: e.matmul(out, lhsT=lhsT, rhs=rhs, start=start, stop=stop, skip_group_check=skip),
              r=[lhsT, rhs], w=[out])

    def tr(out, in_, ident):
        Sc.op("pe", lambda e: e.transpose(out, in_, ident), r=[in_, ident], w=[out])

    def act(out, in_, func, scale=None, bias=None):
        kw = {}
        rd = [in_]
        if scale is not None:
            kw["scale"] = scale
            if not isinstance(scale, (int, float)):
                rd.append(scale)
        if bias is not None:
            kw["bias"] = bias
            if not isinstance(bias, (int, float)):
                rd.append(bias)
        Sc.op("act", lambda e: e.activation(out=out, in_=in_, func=func, **kw), r=rd, w=[out])

    def tt(out, in0, in1, op, eng="dve"):
        Sc.op(eng, lambda e: e.tensor_tensor(out=out, in0=in0, in1=in1, op=op), r=[in0, in1], w=[out])

    def ts(out, in0, s1, op0, s2=None, op1=None, eng="dve"):
        rd = [in0] + [s for s in (s1, s2) if s is not None and not isinstance(s, (int, float))]
        if op1 is None:
            Sc.op(eng, lambda e: e.tensor_scalar(out=out, in0=in0, scalar1=s1, scalar2=None, op0=op0), r=rd, w=[out])
        else:
            Sc.op(eng, lambda e: e.tensor_scalar(out=out, in0=in0, scalar1=s1, scalar2=s2, op0=op0, op1=op1),
                  r=rd, w=[out])

    def stt(out, in0, scalar, in1, op0, op1):
        rd = [in0, in1] + ([] if isinstance(scalar, (int, float)) else [scalar])
        Sc.op("dve", lambda e: e.scalar_tensor_tensor(out=out, in0=in0, scalar=scalar, in1=in1, op0=op0, op1=op1),
              r=rd, w=[out])

    def cp(out, in_, eng="dve"):
        if eng == "act":
            Sc.op("act", lambda e: e.activation(out=out, in_=in_, func=AF.Copy), r=[in_], w=[out])
        else:
            Sc.op(eng, lambda e: e.tensor_copy(out=out, in_=in_), r=[in_], w=[out])

    def memset(out, val, eng="pool"):
        Sc.op(eng, lambda e: e.memset(out, val), r=[], w=[out])

    def recip(out, in_):
        Sc.op("dve", lambda e: e.reciprocal(out=out, in_=in_), r=[in_], w=[out])

    def scan(out, d0, d1, init=0.0):
        Sc.op("dve", lambda e: e.tensor_tensor_scan(out=out, data0=d0, data1=d1, initial=init, op0=ALU.mult,
                                                    op1=ALU.add), r=[d0, d1], w=[out])

    def reduce_sum(out, in_):
        Sc.op("dve", lambda e: e.tensor_reduce(out=out, in_=in_, axis=AX.X, op=ALU.add), r=[in_], w=[out])

    def affsel(out, in_, pattern, cmp, fill, base, cm):
        Sc.op("pool", lambda e: e.affine_select(out=out, in_=in_, pattern=pattern, compare_op=cmp, fill=fill,
                                                base=base, channel_multiplier=cm), r=[in_], w=[out])

    def dma(q, out, in_):
        Sc.dma(q, out, in_)

    hF = A.alloc(8 * S, F32).rearrange("p (f t) -> p f t", f=8)
    xnT = A.alloc(8 * S, BF16).rearrange("p (f t) -> p f t", f=8)
    xn_region_start_f32 = (A.top - 8 * S * 2) // 4
    pp = A.alloc(PP_TOT, F32)
    ppd = A.alloc(L_ALL * 24, F32)
    rbt = A.alloc(RB_L, F32)
    identb = A.alloc(128, BF16)
    identf = A.alloc(128, F32)
    onesb = A.alloc(128, BF16)
    onesf = A.alloc(128, F32)
    blkb = A.alloc(128, BF16)
    headsel = A.alloc(2, BF16)
    mU = A.alloc(128, F32)
    mL = A.alloc(128, F32)
    mUI = A.alloc(128, F32)
    mLI = A.alloc(128, F32)
    nmU = A.alloc(128, F32)
    nmL = A.alloc(128, F32)
    nmUI = A.alloc(128, F32)
    nmLI = A.alloc(128, F32)
    negU4 = A.alloc(512, BF16)
    negL4 = A.alloc(512, BF16)
    lvl = A.alloc(14 * 128, BF16).rearrange("p (k c) -> p k c", k=14)
    ident4 = A.alloc(512, BF16)
    wb_region_start_f32 = A.top // 4
    wbufs = [A.alloc(8 * 512, BF16).rearrange("p (k c) -> p k c", k=8) for _ in range(2)]
    wb_i = [0]

    def next_wbuf():
        b = wbufs[wb_i[0] % 2]
        wb_i[0] += 1
        return b

    memset(identf, 0.0)
    affsel(identf, identf, [[-1, 128]], ALU.not_equal, 1.0, 0, 1)
    cp(identb, identf, "pool")
    memset(onesf, 1.0)
    memset(onesb, 1.0)
    for h in range(4):
        cp(ident4[:, h * 128:(h + 1) * 128], identf, "pool")
    dma("pool", lvl, lvl_d.rearrange("p (k c) -> p k c", k=14))
    affsel(mU, onesf, [[1, 128]], ALU.is_gt, 0.0, 0, -1)
    affsel(mUI, onesf, [[1, 128]], ALU.is_ge, 0.0, 0, -1)
    affsel(mL, onesf, [[-1, 128]], ALU.is_gt, 0.0, 0, 1)
    affsel(mLI, onesf, [[-1, 128]], ALU.is_ge, 0.0, 0, 1)
    for src, dst in ((mU, nmU), (mL, nmL), (mUI, nmUI), (mLI, nmLI)):
        ts(dst, src, -1.0, ALU.mult, eng="pool")
    for h in range(4):
        ts(negU4[:, h * 128:(h + 1) * 128], mU, NEG, ALU.mult, eng="pool")
        ts(negL4[:, h * 128:(h + 1) * 128], mL, NEG, ALU.mult, eng="pool")
    memset(blkb, 0.0)
    memset(blkb[0:64, 0:64], 1.0)
    memset(blkb[64:128, 64:128], 1.0)
    memset(headsel, 0.0)
    memset(headsel[0:64, 0:1], 1.0)
    memset(headsel[64:128, 1:2], 1.0)
    dma("sp", pp, pp_d)
    for l in range(n_layers):
        b0 = l * PP_L
        d0 = l * 24
        o_mu0, o_mu1 = PP_OFF["mu0"][0] + b0, PP_OFF["mu1"][0] + b0
        tt(ppd[:, d0:d0 + 9], pp[:, o_mu0:o_mu0 + 9], pp[:, o_mu1:o_mu1 + 9], ALU.add)
        ts(ppd[:, d0:d0 + 9], ppd[:, d0:d0 + 9], -1.0, ALU.mult, 1.0, ALU.add)
        o_ka = PP_OFF["ka"][0] + b0
        ts(ppd[:, d0 + 9:d0 + 11], pp[:, o_ka:o_ka + 2], -1.0, ALU.mult, 1.0, ALU.add)
        o_lam = PP_OFF["lam"][0] + b0
        act(ppd[:, d0 + 11:d0 + 15], pp[:, o_lam:o_lam + 4], AF.Exp, scale=-1.0)
        act(ppd[:, d0 + 11:d0 + 15], ppd[:, d0 + 11:d0 + 15], AF.Ln, bias=1.0)
        ts(ppd[:, d0 + 15:d0 + 19], ppd[:, d0 + 11:d0 + 15], -16.0, ALU.mult)
        ts(ppd[:, d0 + 11:d0 + 15], ppd[:, d0 + 11:d0 + 15], -8.0, ALU.mult)

    def P_(l, name, j=0, n=1):
        o, k = PP_OFF[name]
        return pp[:, l * PP_L + o + j: l * PP_L + o + j + n]

    m0 = A.mark()
    xst = [A.alloc(D, F32) for _ in range(2)]
    for tt_i in range(NT):
        xs_ = xst[tt_i % 2]
        dma("sp", xs_, x_d[tt_i * 128:(tt_i + 1) * 128, :])
        for half in range(2):
            pb = gbank()
            for j in range(4):
                f = half * 4 + j
                tr(pb[:, j * 128:(j + 1) * 128], xs_[:, f * 128:(f + 1) * 128], identf)
            cp(hF[:, half * 4:half * 4 + 4, tt_i * 128:(tt_i + 1) * 128],
               pb.rearrange("p (j t) -> p j t", j=4), "act" if half else "dve")
    A.release(m0)

    def rmsnorm_to_xnT(nw_of_f):
        m = A.mark()
        sq = A.alloc(8 * 512, BF16).rearrange("p (f t) -> p f t", f=8)
        rstd = A.alloc(512, F32)
        for tc in range(4):
            tsl = slice(tc * 512, (tc + 1) * 512)
            for f in range(8):
                act(sq[:, f, :], hF[:, f, tsl], AF.Square)
            pb = gbank()
            for f in range(8):
                mm(pb, onesb, sq[:, f, :], start=(f == 0), stop=(f == 7))
            act(rstd, pb, AF.Sqrt, scale=1.0 / D, bias=1e-5)
            recip(rstd, rstd)
            for f in range(8):
                stt(xnT[:, f, tsl], hF[:, f, tsl], nw_of_f(f), rstd, ALU.mult, ALU.mult)
        A.release(m)

    def load_w_cols(src2d, c0, ncols, q="pool"):
        wb = next_wbuf()
        dma(q, wb[:, :, 0:ncols], src2d.rearrange("(k p) c -> p k c", p=128)[:, :, c0:c0 + ncols])
        return wb

    def proj_fm(wb, cw, ncw, evac):
        for tc in range(4):
            pb = gbank()
            for k in range(8):
                mm(pb[0:ncw, :], wb[:, k, cw:cw + ncw], xnT[:, k, tc * 512:(tc + 1) * 512], start=(k == 0),
                   stop=(k == 7))
            evac(tc, pb[0:ncw, :])

    def proj_tm(wb, cw, ncw, tile_i, tok0, out_psum):
        for k in range(8):
            mm(out_psum, xnT[:, k, tok0:tok0 + 128], wb[:, k, cw:cw + ncw], start=(k == 0), stop=(k == 7))

    def outproj_acc(l, g, ymT):
        m = A.mark()
        wo = A.alloc(2 * 1024, BF16).rearrange("p (k c) -> p k c", k=2)
        dma("pool", wo, w_out_d[l, g * 256:(g + 1) * 256, :].rearrange("(k p) c -> p k c", p=128))
        for ot in range(8):
            for tc in range(4):
                tsl = slice(tc * 512, (tc + 1) * 512)
                pb = gbank()
                for k in range(2):
                    mm(pb, wo[:, k, ot * 128:(ot + 1) * 128], ymT[:, k, tsl], start=(k == 0), stop=(k == 1))
                tt(hF[:, ot, tsl], pb, hF[:, ot, tsl], ALU.add)
        A.release(m)

    def dump_dbg(g, ymT):
        if dbg_d is None:
            return
        m = A.mark()
        st = A.alloc(S, F32)
        for k in range(2):
            cp(st, ymT[:, k, :], "pool")
            dma("sp", dbg_d[g * 256 + k * 128: g * 256 + (k + 1) * 128, :], st)
        A.release(m)

    def conv4(stage, wcol, bias_ap, out, func=None, out2=None):
        m = A.mark()
        acc = A.alloc(S, F32)
        ts(acc, stage[:, 0:S], wcol(0), ALU.mult)
        for k in range(1, 4):
            stt(acc, stage[:, k:k + S], wcol(k), acc, ALU.mult, ALU.add)
        if func is None:
            ts(out, acc, bias_ap, ALU.add)
            if out2 is not None:
                cp(out2, out, "pool")
        else:
            act(out, acc, func, bias=bias_ap)
        A.release(m)

    def ssd_mixer(l):
        m_ssd = A.mark()
        xs_tok = A.alloc(NT * 256, BF16).rearrange("p (i c) -> p i c", i=NT)
        B_tok = A.alloc(NT * 256, BF16).rearrange("p (i c) -> p i c", i=NT)
        BT = A.alloc(2 * S, BF16).rearrange("p (k t) -> p k t", k=2)
        CT = A.alloc(2 * S, BF16).rearrange("p (k t) -> p k t", k=2)
        o_alog, o_dtb = RB_OFF["alog"][0], RB_OFF["dtb"][0]
        m1 = A.mark()
        stage = A.alloc(S + 3, F32)
        xsT = A.alloc(2 * S, BF16).rearrange("p (k t) -> p k t", k=2)
        memset(stage[:, 0:2], 0.0)
        memset(stage[:, S + 2:S + 3], 0.0)
        for half in range(2):
            wb = load_w_cols(w_in_d[l], SSD0 + 256 + half * 384, 384)
            for j in range(3):
                c = half * 3 + j
                proj_fm(wb, j * 128, 128, lambda tc, pb: cp(stage[:, 2 + tc * 512: 2 + (tc + 1) * 512], pb, "act"))
                dest = (xsT[:, c, :] if c < 2 else (BT[:, c - 2, :] if c < 4 else CT[:, c - 4, :]))
                conv4(stage, lambda k, c=c: P_(l, "scw", k * 6 + c), P_(l, "scb", c), dest, func=AF.Silu)
        for i in range(NT):
            pbb = gbank().bitcast(BF16)
            for k in range(2):
                tr(pbb[:, k * 128:(k + 1) * 128], xsT[:, k, i * 128:(i + 1) * 128], identb)
                tr(pbb[:, 256 + k * 128:256 + (k + 1) * 128], BT[:, k, i * 128:(i + 1) * 128], identb)
            cp(xs_tok[:, i, :], pbb[:, 0:256], "act")
            cp(B_tok[:, i, :], pbb[:, 256:512], "act")
        A.release(m1)
        ymT = A.alloc(2 * S, BF16).rearrange("p (k t) -> p k t", k=2)
        z_tok = A.alloc(NT * 256, BF16).rearrange("p (i c) -> p i c", i=NT)
        y_tok = A.alloc(NT * 256, F32).rearrange("p (i c) -> p i c", i=NT)
        dt_tok = A.alloc(NT * 8, F32).rearrange("p (i c) -> p i c", i=NT)
        adt = A.alloc(NT * 8, F32).rearrange("p (i c) -> p i c", i=NT)
        Abc = A.alloc(8, F32)
        act(Abc, rbt[:, o_alog:o_alog + 8], AF.Exp)
        ts(Abc, Abc, -1.0, ALU.mult)
        wz = load_w_cols(w_in_d[l], SSD0, 256)
        for i in range(NT):
            pb = gbank()
            proj_tm(wz, 0, 256, i, i * 128, pb[:, 0:256])
            act(z_tok[:, i, :], pb[:, 0:256], AF.Silu)
        wdt = load_w_cols(w_in_d[l], SSD0 + 1024, 8)
        for i in range(NT):
            pb = gbank()
            proj_tm(wdt, 0, 8, i, i * 128, pb[:, 0:8])
            tt(dt_tok[:, i, :], pb[:, 0:8], rbt[:, o_dtb:o_dtb + 8], ALU.add)
        dtf = dt_tok.rearrange("p i c -> p (i c)")
        act(dtf, dtf, AF.Exp)
        act(dtf, dtf, AF.Ln, bias=1.0)
        tt(adt, dt_tok, Abc.unsqueeze(1).to_broadcast([128, NT, 8]), ALU.mult)
        m2 = A.mark()
        Hs = [A.alloc(256, F32) for _ in range(2)]
        Hb = [A.alloc(256, BF16) for _ in range(2)]
        for d in range(2):
            memset(Hs[d], 0.0)
            memset(Hb[d], 0.0)
        rhs_seg = A.alloc(512, F32)
        dec = A.alloc(512, BF16)
        scT = A.alloc(512, BF16)
        xdt = A.alloc(256, BF16)
        xde = A.alloc(256, BF16)
        ecs = A.alloc(8, F32)
        dte = A.alloc(8, F32)
        etot = A.alloc(8, F32)
        ydg = A.alloc(256, F32)
        for st_i in range(NT):
            for d in range(2):
                c = st_i if d == 0 else NT - 1 - st_i
                csl = slice(c * 128, (c + 1) * 128)
                a4 = adt[:, c, d * 4:(d + 1) * 4]
                pG = gbank()
                for g in range(2):
                    mm(pG[:, g * 128:(g + 1) * 128], BT[:, g, csl], CT[:, g, csl])
                pS = gbank()
                mm(pS[:, 0:4], (mUI if d == 0 else mLI), a4)
                mm(pS[:, 8:12], onesf, a4)
                act(ecs[:, 0:4], pS[:, 0:4], AF.Exp)
                cp(dte[:, 4:8], pS[:, 0:4], "dve")
                tt(dte[:, 0:4], pS[:, 8:12], dte[:, 4:8], ALU.subtract)
                act(dte[:, 0:4], dte[:, 0:4], AF.Exp)
                act(etot[:, 0:4], pS[:, 8:12], AF.Exp)
                msk = mUI if d == 0 else mLI
                tt(rhs_seg.rearrange("p (h l) -> p h l", h=4),
                   msk.unsqueeze(1).to_broadcast([128, 4, 128]),
                   a4.unsqueeze(2).to_broadcast([128, 4, 128]), ALU.mult)
                pD = gbank()
                mm(pD, (mL if d == 0 else mU), rhs_seg, start=True, stop=False)
                mm(pD, identb, (negL4 if d == 0 else negU4), start=False, stop=True)
                act(dec, pD, AF.Exp)
                tt(scT.rearrange("p (g r l) -> p g r l", g=2, r=2),
                   dec.rearrange("p (g r l) -> p g r l", g=2, r=2),
                   pG[:, 0:256].rearrange("p (g l) -> p g l", g=2).unsqueeze(2).to_broadcast([128, 2, 2, 128]),
                   ALU.mult)
                dt4 = dt_tok[:, c, d * 4:(d + 1) * 4]
                tt(xdt.rearrange("p (h q) -> p h q", h=4), xs_tok[:, c, :].rearrange("p (h q) -> p h q", h=4),
                   dt4.unsqueeze(2).to_broadcast([128, 4, 64]), ALU.mult)
                tt(xde.rearrange("p (h q) -> p h q", h=4), xdt.rearrange("p (h q) -> p h q", h=4),
                   dte[:, 0:4].unsqueeze(2).to_broadcast([128, 4, 64]), ALU.mult)
                pY = gbank()
                for h in range(4):
                    mm(pY[:, h * 64:(h + 1) * 64], scT[:, h * 128:(h + 1) * 128], xdt[:, h * 64:(h + 1) * 64])
                    mm(pY[:, 256 + h * 64:256 + (h + 1) * 64], CT[:, h // 2, csl], Hb[d][:, h * 64:(h + 1) * 64])
                first = st_i < NT // 2
                if first:
                    cp(y_tok[:, c, :], pY[:, 0:256], "act")
                else:
                    tt(y_tok[:, c, :], pY[:, 0:256], y_tok[:, c, :], ALU.add)
                for h in range(4):
                    stt(y_tok[:, c, h * 64:(h + 1) * 64], pY[:, 256 + h * 64:256 + (h + 1) * 64], ecs[:, h:h + 1],
                        y_tok[:, c, h * 64:(h + 1) * 64], ALU.mult, ALU.add)
                pH = gbank()
                for h in range(4):
                    mm(pH[:, h * 64:(h + 1) * 64], B_tok[:, c, (h // 2) * 128:(h // 2 + 1) * 128],
                       xde[:, h * 64:(h + 1) * 64])
                for h in range(4):
                    stt(Hs[d][:, h * 64:(h + 1) * 64], Hs[d][:, h * 64:(h + 1) * 64], etot[:, h:h + 1],
                        pH[:, h * 64:(h + 1) * 64], ALU.mult, ALU.add)
                cp(Hb[d], Hs[d], "act")
        A.release(m2)
        m3 = A.mark()
        o_dsk, o_snw = RB_OFF["dsk"][0], RB_OFF["snw"][0]
        tmp = A.alloc(256, F32)
        tmpb = A.alloc(256, BF16)
        ss = A.alloc(2, F32)
        for i in range(NT):
            tt(tmp.rearrange("p (h q) -> p h q", h=4), xs_tok[:, i, :].rearrange("p (h q) -> p h q", h=4),
               rbt[:, o_dsk:o_dsk + 4].unsqueeze(2).to_broadcast([128, 4, 64]), ALU.mult)
            tt(tmp, tmp, y_tok[:, i, :], ALU.add)
            tt(tmp, tmp, z_tok[:, i, :], ALU.mult)
            tt(y_tok[:, i, :], tmp, tmp, ALU.mult)
            reduce_sum(ss[:, 0:1], y_tok[:, i, :])
            act(ss[:, 1:2], ss[:, 0:1], AF.Sqrt, scale=1.0 / 256, bias=1e-5)
            recip(ss[:, 1:2], ss[:, 1:2])
            stt(tmpb, tmp, ss[:, 1:2], rbt[:, o_snw:o_snw + 256], ALU.mult, ALU.mult)
            pbb = gbank().bitcast(BF16)
            for k in range(2):
                tr(pbb[:, k * 128:(k + 1) * 128], tmpb[:, k * 128:(k + 1) * 128], identb)
            cp(ymT[:, :, i * 128:(i + 1) * 128], pbb[:, 0:256].rearrange("p (k t) -> p k t", k=2), "act")
        A.release(m3)
        dump_dbg(1, ymT)
        outproj_acc(l, 1, ymT)
        A.release(m_ssd)

    def lru_mixer(l):
        m_l = A.mark()
        ymT = A.alloc(2 * S, BF16).rearrange("p (k t) -> p k t", k=2)
        gate = A.alloc(2 * S, BF16).rearrange("p (k t) -> p k t", k=2)
        xf = A.alloc(2 * S, F32).rearrange("p (k t) -> p k t", k=2)
        xfb = A.alloc(2 * S, BF16).rearrange("p (k t) -> p k t", k=2)
        m_st = A.mark()
        stage = A.alloc(S + 3, F32)
        memset(stage[:, 0:2], 0.0)
        memset(stage[:, S + 2:S + 3], 0.0)
        wb = load_w_cols(w_in_d[l], LRU0, 512)
        for c in range(2):
            proj_fm(wb, c * 128, 128,
                    lambda tc, pb, c=c: act(gate[:, c, tc * 512:(tc + 1) * 512], pb, AF.Gelu_apprx_tanh))
        for c in range(2):
            proj_fm(wb, 256 + c * 128, 128,
                    lambda tc, pb: cp(stage[:, 2 + tc * 512: 2 + (tc + 1) * 512], pb, "act"))
            conv4(stage, lambda k, c=c: P_(l, "lcw", k * 2 + c), P_(l, "lcb", c), xf[:, c, :], out2=xfb[:, c, :])
        A.release(m_st)
        m_tmp = A.mark()
        wg = A.alloc(8 * 128, BF16).rearrange("p (a c) -> p a c", a=8)
        dma("pool", wg, lbd_d[l].rearrange("w d t p c -> p (w d t) c"))
        rec = A.alloc(S, F32)
        inp = A.alloc(S, F32)
        aa = A.alloc(S, F32)
        hsum = A.alloc(S, F32)
        for pt in range(2):
            for d in range(2):
                for which, dst, bname in ((0, rec, "lab"), (1, inp, "lxb")):
                    for tc in range(4):
                        tsl = slice(tc * 512, (tc + 1) * 512)
                        pb = gbank()
                        mm(pb, wg[:, which * 4 + d * 2 + pt, :], xfb[:, pt, tsl])
                        act(dst[:, tsl], pb, AF.Sigmoid, bias=P_(l, bname, d * 2 + pt))
                lc = ppd[:, l * 24 + 11 + d * 2 + pt: l * 24 + 12 + d * 2 + pt]
                lc2 = ppd[:, l * 24 + 15 + d * 2 + pt: l * 24 + 16 + d * 2 + pt]
                act(aa, rec, AF.Exp, scale=lc)
                act(rec, rec, AF.Exp, scale=lc2)
                act(rec, rec, AF.Sqrt, scale=-1.0, bias=1.0)
                tt(inp, inp, rec, ALU.mult)
                tt(inp, inp, xf[:, pt, :], ALU.mult)
                if d == 0:
                    scan(hsum, aa, inp)
                else:
                    scan(rec[:, ::-1], aa[:, ::-1], inp[:, ::-1])
                    tt(hsum, hsum, rec, ALU.add)
                    tt(ymT[:, pt, :], hsum, gate[:, pt, :], ALU.mult)
        A.release(m_tmp)
        dump_dbg(2, ymT)
        outproj_acc(l, 2, ymT)
        A.release(m_l)

    def na_mixer(l):
        m_n = A.mark()
        ymT = A.alloc(2 * S, BF16).rearrange("p (k t) -> p k t", k=2)
        qm = [A.alloc(S, BF16) for _ in range(4)]
        for h in range(4):
            z0 = 64 - (h % 2) * 64
            memset(qm[h][z0:z0 + 64, :], 0.0)
        kT = A.alloc(2 * S, BF16).rearrange("p (k t) -> p k t", k=2)
        va = [A.alloc(NT * 260, BF16).rearrange("p (i h q) -> p i h q", i=NT, h=4) for _ in range(2)]
        t2 = A.alloc(14 * 256, F32).rearrange("p (r x) -> p r x", r=14)
        dma("sp", t2, t2_d[l].rearrange("p (r x) -> p r x", r=14))
        wb = load_w_cols(w_in_d[l], NA0, 512)
        def q_evac(tc, pb, c):
            tsl = slice(tc * 512, (tc + 1) * 512)
            act(qm[2 * c][0:64, tsl], pb[0:64, :], AF.Copy, scale=0.125)
            act(qm[2 * c + 1][64:128, tsl], pb[64:128, :], AF.Copy, scale=0.125)
        for c in range(2):
            proj_fm(wb, c * 128, 128, lambda tc, pb, c=c: q_evac(tc, pb, c))
        for c in range(2):
            proj_fm(wb, 256 + c * 128, 128, lambda tc, pb, c=c: cp(kT[:, c, tc * 512:(tc + 1) * 512], pb, "act"))
        wv = load_w_cols(w_in_d[l], NA0 + 512, 256)
        for al in range(2):
            memset(va[al][:, :, :, 64:65], 1.0)
            for i in range(NT - al):
                pb = gbank()
                proj_tm(wv, 0, 256, i, i * 128 + al * 64, pb[:, 0:256])
                cp(va[al][:, i, :, 0:64], pb[:, 0:256].rearrange("p (h q) -> p h q", h=4), "act" if i % 2 else "dve")
        ein = A.alloc(1024, F32)
        pT = A.alloc(1024, BF16)
        rc = A.alloc(4, F32)
        ytok = A.alloc(256, BF16, parts=64)
        for r in range(32):
            kr0 = min(max(r - 4, 0), 24)
            dr0 = kr0 - r + 7
            al = kr0 % 2
            ti0 = kr0 // 2
            q0 = r * 64
            pS = [gbank(), gbank()]
            for kt in range(4):
                k0 = kr0 * 64 + kt * 128
                for h in range(4):
                    col = (kt * 4 + h) * 64
                    mm(pS[col // 512][:, col % 512: col % 512 + 64], kT[:, h // 2, k0:k0 + 128],
                       qm[h][:, q0:q0 + 64])
            for half in range(2):
                tt(ein[:, half * 512:(half + 1) * 512].rearrange("p (a x) -> p a x", a=2),
                   pS[half].rearrange("p (a x) -> p a x", a=2),
                   t2[:, dr0 + 4 * half: dr0 + 4 * half + 3: 2, :], ALU.add)
            act(pT, ein, AF.Exp)
            pO = gbank()
            for h in range(4):
                for kt in range(4):
                    col = (kt * 4 + h) * 64
                    mm(pO[0:64, h * 65:(h + 1) * 65], pT[:, col:col + 64], va[al][:, ti0 + kt, h, :],
                       start=(kt == 0), stop=(kt == 3))
            pO3 = pO[0:64, 0:260].rearrange("p (h q) -> p h q", h=4)
            recip(rc[0:64, :], pO3[:, :, 64])
            tt(ytok.rearrange("p (h q) -> p h q", h=4), pO3[:, :, 0:64],
               rc[0:64, :].unsqueeze(2).to_broadcast([64, 4, 64]), ALU.mult)
            pbb = gbank().bitcast(BF16)
            for k in range(2):
                tr(pbb[:, k * 64:(k + 1) * 64], ytok[:, k * 128:(k + 1) * 128], identb[0:64, 0:64])
            cp(ymT[:, :, q0:q0 + 64], pbb[:, 0:128].rearrange("p (k t) -> p k t", k=2), "act")
        dump_dbg(3, ymT)
        outproj_acc(l, 3, ymT)
        A.release(m_n)

    def rwkv_mixer(l):
        m_r = A.mark()
        rT = A.alloc(2 * S, BF16).rearrange("p (k t) -> p k t", k=2)
        kT = A.alloc(2 * S, BF16).rearrange("p (k t) -> p k t", k=2)
        wdm = [A.alloc(S, BF16) for _ in range(2)]
        adm = [A.alloc(S, BF16) for _ in range(2)]
        for d_ in range(2):
            z0 = 64 - d_ * 64
            memset(wdm[d_][z0:z0 + 64, :], 0.0)
            memset(adm[d_][z0:z0 + 64, :], 0.0)
        gdT = A.alloc(S, BF16)
        v_tok = A.alloc(NT * 256, BF16).rearrange("p (i c) -> p i c", i=NT)
        wup = A.alloc(256, BF16)
        aup = A.alloc(256, BF16)
        gup = A.alloc(256, BF16)
        dma("pool", wup, wup_d[l].rearrange("d r c -> (d r) c"))
        dma("pool", aup, aup_d[l].rearrange("d r c -> (d r) c"))
        dma("pool", gup, gup_d[l])
        m1 = A.mark()
        stage = A.alloc(S + 2, F32)
        t1 = A.alloc(S, F32)
        vT = A.alloc(2 * S, BF16).rearrange("p (k t) -> p k t", k=2)
        memset(stage[:, 0:1], 0.0)
        memset(stage[:, S + 1:S + 2], 0.0)
        dests = [rT[:, 0, :], rT[:, 1, :], kT[:, 0, :], kT[:, 1, :], vT[:, 0, :], vT[:, 1, :], None, None, gdT]
        for grp in range(3):
            wb = load_w_cols(w_in_d[l], RW0 + grp * 384, 384)
            for j in range(3):
                c = grp * 3 + j
                proj_fm(wb, j * 128, 128, lambda tc, pb: cp(stage[:, 1 + tc * 512: 1 + (tc + 1) * 512], pb, "act"))
                c0 = ppd[:, l * 24 + c: l * 24 + c + 1]
                ts(t1, stage[:, 1:S + 1], c0, ALU.mult)
                stt(t1, stage[:, 0:S], P_(l, "mu0", c), t1, ALU.mult, ALU.add)
                if c == 6:
                    stt(t1, stage[:, 2:S + 2], P_(l, "mu1", c), t1, ALU.mult, ALU.add)
                    act(wdm[0][0:64, :], t1[0:64, :], AF.Tanh)
                    act(wdm[1][64:128, :], t1[64:128, :], AF.Tanh)
                elif c == 7:
                    stt(t1, stage[:, 2:S + 2], P_(l, "mu1", c), t1, ALU.mult, ALU.add)
                    cp(adm[0][0:64, :], t1[0:64, :], "act")
                    cp(adm[1][64:128, :], t1[64:128, :], "act")
                elif c == 8:
                    stt(t1, stage[:, 2:S + 2], P_(l, "mu1", c), t1, ALU.mult, ALU.add)
                    act(dests[c], t1, AF.Sigmoid)
                else:
                    stt(dests[c], stage[:, 2:S + 2], P_(l, "mu1", c), t1, ALU.mult, ALU.add)
        for i in range(NT):
            pbb = gbank().bitcast(BF16)
            for k in range(2):
                tr(pbb[:, k * 128:(k + 1) * 128], vT[:, k, i * 128:(i + 1) * 128], identb)
            cp(v_tok[:, i, :], pbb[:, 0:256], "act")
        A.release(m1)
        ymT = A.alloc(2 * S, BF16).rearrange("p (k t) -> p k t", k=2)
        y_tok = A.alloc(NT * 256, BF16).rearrange("p (i c) -> p i c", i=NT)
        bonus = A.alloc(NT * 4, F32).rearrange("p (i c) -> p i c", i=NT)

        XA_ = Arena(arena_ap, 8 * S // 2, start=xn_region_start_f32)

        XB_ = Arena(arena_ap, 2 * 8 * 512 // 2, start=wb_region_start_f32)

        class _XA:
            @staticmethod
            def alloc(cols, dtype):
                for ar in (XA_, XB_):
                    if ar.top + (cols * DSZ[dtype] + CELL - 1) // CELL * CELL <= ar.limit:
                        return ar.alloc(cols, dtype)
                return A.alloc(cols, dtype)
        XA = _XA

        class DirBuf:
            pass
        dbs = []
        for d in range(2):
            b = DirBuf()
            b.ST = XA.alloc(2 * 64, F32).rearrange("p (k i) -> p k i", k=2)
            b.STb = XA.alloc(2 * 64, BF16).rearrange("p (k i) -> p k i", k=2)
            memset(b.ST, 0.0)
            memset(b.STb, 0.0)
            _t = XA.alloc(128, F32)
            b.sig = [_t, _t]
            b.ein = [XA.alloc(129, F32) for _ in range(2)]
            _t = XA.alloc(128, F32)
            b.eni = [_t, _t]
            _t = XA.alloc(128, F32)
            b.eh = [_t, _t]
            _t = XA.alloc(128, F32)
            b.al = [_t, _t]
            _t = XA.alloc(128, F32)
            b.kd = [_t, _t]
            _t = XA.alloc(128, F32)
            b.bb = [_t, _t]
            _t = XA.alloc(128, F32)
            b.kks = [_t, _t]
            _t = XA.alloc(128, F32)
            b.kk = [_t, _t]
            b.sq = [XA.alloc(128, BF16) for _ in range(2)]
            b.Rt = [XA.alloc(128, BF16) for _ in range(2)]
            b.Qk = [XA.alloc(128, BF16) for _ in range(2)]
            b.Bt = [XA.alloc(128, BF16) for _ in range(2)]
            b.Kt = [XA.alloc(128, BF16) for _ in range(2)]
            b.Btm = [[XA.alloc(128, BF16) for _ in range(2)] for _ in range(2)]
            b.Qkm = [[XA.alloc(128, BF16) for _ in range(2)] for _ in range(2)]
            b.Ktm = [[XA.alloc(128, BF16) for _ in range(2)] for _ in range(2)]
            b.Rtm = [[XA.alloc(128, BF16) for _ in range(2)] for _ in range(2)]
            for grp in (b.Btm, b.Qkm, b.Ktm, b.Rtm):
                for hp_ in range(2):
                    for h2 in range(2):
                        z0 = 64 - h2 * 64
                        memset(grp[hp_][h2][z0:z0 + 64, :], 0.0)
            b.Bh = [XA.alloc(128, BF16) for _ in range(2)]
            b.Kh = [XA.alloc(128, BF16) for _ in range(2)]
            b.rkd = [XA.alloc(128, BF16) for _ in range(2)]
            b.BhT = XA.alloc(256, BF16)
            b.KhT = XA.alloc(256, BF16)
            b.Mf = XA.alloc(512, BF16)
            b.T = [XA.alloc(512, BF16) for _ in range(2)]
            b.Z = [XA.alloc(512, BF16) for _ in range(2)]
            b.W = XA.alloc(512, BF16)
            b.Akk = XA.alloc(512, BF16)
            b.Arb = XA.alloc(512, BF16)
            b.Ark = XA.alloc(512, BF16)
            b.Xb = XA.alloc(256, BF16)
            b.Ub = XA.alloc(256, BF16)
            b.pX = bank(4 + d)
            b.pA = bank(6 + d)
            dbs.append(b)
        gtmp = XA.alloc(256, F32)
        gt2 = XA.alloc(256, F32)
        gst = XA.alloc(16, F32)
        gout = XA.alloc(256, BF16)

        def finalize_chunk(c):
            y4 = gtmp.rearrange("p (h q) -> p h q", h=4)
            cp(gtmp, y_tok[:, c, :], "pool")
            reduce_sum(gst[:, 0:4], y4)
            tt(gt2, gtmp, gtmp, ALU.mult)
            reduce_sum(gst[:, 4:8], gt2.rearrange("p (h q) -> p h q", h=4))
            ts(gst[:, 0:4], gst[:, 0:4], 1.0 / 64, ALU.mult)
            tt(gst[:, 8:12], gst[:, 0:4], gst[:, 0:4], ALU.mult)
            stt(gst[:, 4:8], gst[:, 4:8], 1.0 / 64, gst[:, 8:12], ALU.mult, ALU.subtract)
            act(gst[:, 4:8], gst[:, 4:8], AF.Sqrt, bias=64e-5)
            recip(gst[:, 4:8], gst[:, 4:8])
            tt(y4, y4, gst[:, 0:4].unsqueeze(2).to_broadcast([128, 4, 64]), ALU.subtract)
            tt(y4, y4, gst[:, 4:8].unsqueeze(2).to_broadcast([128, 4, 64]), ALU.mult)
            o_w, o_b = RB_OFF["gnw"][0], RB_OFF["gnb"][0]
            tt(gtmp, gtmp, rbt[:, o_w:o_w + 256], ALU.mult)
            tt(gtmp, gtmp, rbt[:, o_b:o_b + 256], ALU.add)
            tt(gt2.rearrange("p (h q) -> p h q", h=4), v_tok[:, c, :].rearrange("p (h q) -> p h q", h=4),
               bonus[:, c, :].unsqueeze(2).to_broadcast([128, 4, 64]), ALU.mult)
            tt(gtmp, gtmp, gt2, ALU.add)
            pg = gbank()
            mm(pg[:, 0:256], gdT[:, c * 128:(c + 1) * 128], gup)
            tt(gout, gtmp, pg[:, 0:256], ALU.mult)
            pbb = gbank().bitcast(BF16)
            for k in range(2):
                tr(pbb[:, k * 128:(k + 1) * 128], gout[:, k * 128:(k + 1) * 128], identb)
            cp(ymT[:, :, c * 128:(c + 1) * 128], pbb[:, 0:256].rearrange("p (k t) -> p k t", k=2), "act")

        def step(d, c, first):
            b = dbs[d]
            csl = slice(c * 128, (c + 1) * 128)
            fwd = (d == 0)
            m_strict_neg = nmU if fwd else nmL
            m_strict = mU if fwd else mL
            m_strictT_neg = nmL if fwd else nmU
            m_incl = mUI if fwd else mLI
            m_incl_neg = nmUI if fwd else nmLI
            for hp in range(2):
                pz = gbank()
                mm(pz[:, 0:128], wup[:, hp * 128:(hp + 1) * 128], wdm[d][:, csl])
                mm(pz[:, 128:256], aup[:, hp * 128:(hp + 1) * 128], adm[d][:, csl])
                act(b.sig[hp], pz[:, 0:128], AF.Sigmoid, bias=P_(l, "w0", d * 2 + hp))
                act(b.al[hp], pz[:, 128:256], AF.Sigmoid, bias=P_(l, "a0", d * 2 + hp))
                if fwd:
                    cs = b.ein[hp][:, 1:129]
                    scan(cs, onesf, b.sig[hp])
                    pad = b.ein[hp][:, 0:1]
                    eex = b.ein[hp][:, 0:128]
                    etot = b.ein[hp][:, 128:129]
                else:
                    cs = b.ein[hp][:, 0:128]
                    scan(cs[:, ::-1], onesf, b.sig[hp][:, ::-1])
                    pad = b.ein[hp][:, 128:129]
                    eex = b.ein[hp][:, 1:129]
                    etot = b.ein[hp][:, 0:1]
                act(b.eni[hp], cs, AF.Exp, scale=DECAY)
                act(cs, cs, AF.Exp, scale=-DECAY)
                memset(pad, 1.0, "pool")
                ts(b.eh[hp], b.eni[hp], etot, ALU.mult)
                ts(b.kks[hp], kT[:, hp, csl], P_(l, "kk", hp), ALU.mult)
                act(b.sq[hp], b.kks[hp], AF.Square)
                pn = gbank()
                mm(pn[:, 0:128], blkb, b.sq[hp])
                act(b.kk[hp], pn[:, 0:128], AF.Sqrt, bias=1e-12)
                recip(b.kk[hp], b.kk[hp])
                tt(b.kk[hp], b.kk[hp], b.kks[hp], ALU.mult)
                omka = ppd[:, l * 24 + 9 + hp: l * 24 + 10 + hp]
                ts(b.kd[hp], b.al[hp], P_(l, "ka", hp), ALU.mult, omka, ALU.add)
                tt(b.kd[hp], b.kd[hp], kT[:, hp, csl], ALU.mult)
                tt(b.bb[hp], b.al[hp], b.kk[hp], ALU.mult)
                tt(b.Rt[hp], rT[:, hp, csl], cs, ALU.mult)
                tt(b.Qk[hp], b.kk[hp], eex, ALU.mult)
                tt(b.Bt[hp], b.bb[hp], b.eni[hp], ALU.mult)
                tt(b.Kt[hp], b.kd[hp], b.eni[hp], ALU.mult)
                stt(b.Bh[hp], b.bb[hp], -1.0, b.eh[hp], ALU.mult, ALU.mult)
                tt(b.Kh[hp], b.kd[hp], b.eh[hp], ALU.mult)
                stt(b.rkd[hp], rT[:, hp, csl], P_(l, "rk", hp), b.kd[hp], ALU.mult, ALU.mult)
                for h2 in range(2):
                    pr = slice(h2 * 64, h2 * 64 + 64)
                    for src_, dst_ in ((b.Bt, b.Btm), (b.Qk, b.Qkm), (b.Kt, b.Ktm), (b.Rt, b.Rtm)):
                        cp(dst_[hp][h2][pr, :], src_[hp][pr, :], "pool")
            yield
            pbn = gbank()
            for hp in range(2):
                mm(pbn[:, hp * 2:hp * 2 + 2], b.rkd[hp], headsel)
            if first:
                cp(bonus[:, c, :], pbn[:, 0:4], "dve")
            else:
                tt(bonus[:, c, :], pbn[:, 0:4], bonus[:, c, :], ALU.add)
            pbb = gbank().bitcast(BF16)
            for hp in range(2):
                tr(pbb[:, hp * 128:(hp + 1) * 128], b.Bh[hp], identb)
                tr(pbb[:, 256 + hp * 128:256 + (hp + 1) * 128], b.Kh[hp], identb)
            cp(b.BhT, pbb[:, 0:256], "act")
            cp(b.KhT, pbb[:, 256:512], "act")
            def heads(fn):
                for h in range(4):
                    fn(h, h // 2, slice((h % 2) * 64, (h % 2) * 64 + 64), slice(h * 128, (h + 1) * 128))
            pa = gbank()
            heads(lambda h, hp, jr, hs: mm(pa[:, hs], b.Btm[hp][h % 2], b.Qk[hp]))
            tt(b.Mf.rearrange("p (h t) -> p h t", h=4), pa.rearrange("p (h t) -> p h t", h=4),
               m_strict_neg.unsqueeze(1).to_broadcast([128, 4, 128]), ALU.mult)
            pa = gbank()
            heads(lambda h, hp, jr, hs: mm(pa[:, hs], b.Ktm[hp][h % 2], b.Qk[hp]))
            tt(b.Akk.rearrange("p (h t) -> p h t", h=4), pa.rearrange("p (h t) -> p h t", h=4),
               m_strict.unsqueeze(1).to_broadcast([128, 4, 128]), ALU.mult)
            pa = gbank()
            heads(lambda h, hp, jr, hs: mm(pa[:, hs], b.Btm[hp][h % 2], b.Rt[hp]))
            tt(b.Arb.rearrange("p (h t) -> p h t", h=4), pa.rearrange("p (h t) -> p h t", h=4),
               m_incl_neg.unsqueeze(1).to_broadcast([128, 4, 128]), ALU.mult)
            pa = gbank()
            heads(lambda h, hp, jr, hs: mm(pa[:, hs], b.Ktm[hp][h % 2], b.Rt[hp]))
            tt(b.Ark.rearrange("p (h t) -> p h t", h=4), pa.rearrange("p (h t) -> p h t", h=4),
               m_incl.unsqueeze(1).to_broadcast([128, 4, 128]), ALU.mult)
            yield
            for k in range(7):
                cur, nxt = k % 2, (k + 1) % 2
                Tc = ident4 if k == 0 else b.T[cur]
                Zc = ident4 if k == 0 else b.Z[cur]
                heads(lambda h, hp, jr, hs: mm(b.pA[:, hs], b.Mf[:, hs], Tc[:, hs]))
                mk = lvl[:, (k if fwd else 7 + k), :]
                tt(b.W.rearrange("p (h t) -> p h t", h=4), b.pA.rearrange("p (h t) -> p h t", h=4),
                   mk.unsqueeze(1).to_broadcast([128, 4, 128]), ALU.mult)
                if k < 6:
                    mm(b.pX, identb, Tc, start=True, stop=False, skip=True)
                    heads(lambda h, hp, jr, hs: mm(b.pX[:, hs], Zc[:, hs], b.W[:, hs], start=False, stop=False,
                                                   skip=True))
                    cp(b.T[nxt], b.pX, "act")
                pz = gbank()
                mm(pz, identb, Zc, start=True, stop=False, skip=True)
                heads(lambda h, hp, jr, hs: mm(pz[:, hs], b.W[:, hs], Zc[:, hs], start=False, stop=False, skip=True))
                cp(b.Z[nxt], pz, "dve")
                yield
            Z7 = b.Z[1]
            px = gbank()

            def xinit(h, hp, jr, hs):
                xs_ = slice(h * 64, (h + 1) * 64)
                mm(px[:, xs_], b.Qkm[hp][h % 2], b.STb[:, hp, :], start=True, stop=False)
                mm(px[:, xs_], b.Akk[:, hs], v_tok[:, c, xs_], start=False, stop=True)
            heads(xinit)
            cp(b.Xb, px[:, 0:256], "act")
            pu = gbank()
            heads(lambda h, hp, jr, hs: mm(pu[:, h * 64:(h + 1) * 64], Z7[:, hs], b.Xb[:, h * 64:(h + 1) * 64]))
            cp(b.Ub, pu[:, 0:256], "act")
            py = gbank()

            def yfn(h, hp, jr, hs):
                xs_ = slice(h * 64, (h + 1) * 64)
                mm(py[:, xs_], b.Rtm[hp][h % 2], b.STb[:, hp, :], start=True, stop=False)
                mm(py[:, xs_], b.Arb[:, hs], b.Ub[:, xs_], start=False, stop=False)
                mm(py[:, xs_], b.Ark[:, hs], v_tok[:, c, xs_], start=False, stop=True)
            heads(yfn)
            if first:
                cp(y_tok[:, c, :], py[:, 0:256], "act")
            else:
                tt(y_tok[:, c, :], py[:, 0:256], y_tok[:, c, :], ALU.add)
            ph = gbank()
            for h in range(4):
                hp = h // 2
                xs_ = slice(h * 64, (h + 1) * 64)
                mm(ph[:, xs_], b.BhT[:, hp * 128:(hp + 1) * 128], b.Ub[:, xs_], start=True, stop=False)
                mm(ph[:, xs_], b.KhT[:, hp * 128:(hp + 1) * 128], v_tok[:, c, xs_], start=False, stop=True)
            for h in range(4):
                hp = h // 2
                jr = slice((h % 2) * 64, (h % 2) * 64 + 64)
                etot = (b.ein[hp][jr, 128:129] if fwd else b.ein[hp][jr, 0:1])
                stt(b.ST[jr, hp, :], b.ST[jr, hp, :], etot, ph[jr, h * 64:(h + 1) * 64], ALU.mult, ALU.add)
            cp(b.STb, b.ST, "act")
            yield

        for st_i in range(NT):
            first = st_i < NT // 2
            gens = [step(0, st_i, first), step(1, NT - 1 - st_i, first)]
            alive = True
            while alive:
                alive = False
                for g in gens:
                    try:
                        next(g)
                        alive = True
                    except StopIteration:
                        pass
            if not first:
                finalize_chunk(st_i)
                finalize_chunk(NT - 1 - st_i)
        dump_dbg(0, ymT)
        outproj_acc(l, 0, ymT)
        A.release(m_r)

    def mlp(l):
        m_m = A.mark()
        mT = A.alloc(8 * S, BF16).rearrange("p (k t) -> p k t", k=8)
        w2b = [A.alloc(8 * 512, BF16).rearrange("p (k c) -> p k c", k=8) for _ in range(2)]
        rl = A.alloc(512, BF16)
        for hg in range(4):
            for half in range(2):
                wb = load_w_cols(w1_d[l], hg * 1024 + half * 512, 512)
                for j in range(4):
                    hpt = half * 4 + j
                    for tc in range(4):
                        tsl = slice(tc * 512, (tc + 1) * 512)
                        pb = gbank()
                        for k in range(8):
                            mm(pb, wb[:, k, j * 128:(j + 1) * 128], xnT[:, k, tsl], start=(k == 0), stop=(k == 7))
                        act(rl, pb, AF.Relu)
                        tt(mT[:, hpt, tsl], rl, rl, ALU.mult, eng="pool")
            for half in range(2):
                dma("pool", w2b[half],
                    w2_d[l, hg * 1024:(hg + 1) * 1024, half * 512:(half + 1) * 512].rearrange("(k p) c -> p k c", p=128))
            for ot in range(8):
                for tc in range(4):
                    tsl = slice(tc * 512, (tc + 1) * 512)
                    pb = gbank()
                    for k in range(8):
                        mm(pb, w2b[ot // 4][:, k, (ot % 4) * 128:(ot % 4 + 1) * 128], mT[:, k, tsl], start=(k == 0),
                           stop=(k == 7))
                    tt(hF[:, ot, tsl], pb, hF[:, ot, tsl], ALU.add)
        A.release(m_m)

    mixers = (dbg or {}).get("mixers", "srna") if dbg else "srna"
    for l in range(n_layers):
        dma("sp", rbt, rb_d[l])
        rmsnorm_to_xnT(lambda f, l=l: P_(l, "n1", f))
        if "s" in mixers:
            ssd_mixer(l)
        if "r" in mixers:
            lru_mixer(l)
        if "n" in mixers:
            na_mixer(l)
        if "a" in mixers:
            rwkv_mixer(l)
        if not (dbg and dbg.get("no_mlp")):
            rmsnorm_to_xnT(lambda f, l=l: P_(l, "n2", f))
            mlp(l)

    m0 = A.mark()
    sq = A.alloc(8 * 512, BF16).rearrange("p (f t) -> p f t", f=8)
    rstd = A.alloc(512, F32)
    yn = A.alloc(8 * 512, F32).rearrange("p (f t) -> p f t", f=8)
    ost = [A.alloc(D, F32) for _ in range(2)]
    fo = PP_L * L_ALL
    for tc in range(4):
        tsl = slice(tc * 512, (tc + 1) * 512)
        for f in range(8):
            act(sq[:, f, :], hF[:, f, tsl], AF.Square)
        pb = gbank()
        for f in range(8):
            mm(pb, onesb, sq[:, f, :], start=(f == 0), stop=(f == 7))
        act(rstd, pb, AF.Sqrt, scale=1.0 / D, bias=1e-5)
        recip(rstd, rstd)
        for f in range(8):
            stt(yn[:, f, :], hF[:, f, tsl], pp[:, fo + f:fo + f + 1], rstd, ALU.mult, ALU.mult)
        for ti in range(4):
            o = ost[ti % 2]
            for half in range(2):
                pb2 = gbank()
                for j in range(4):
                    f = half * 4 + j
                    tr(pb2[:, j * 128:(j + 1) * 128], yn[:, f, ti * 128:(ti + 1) * 128], identf)
                cp(o[:, half * 512:(half + 1) * 512], pb2, "act" if half else "dve")
            tok0 = tc * 512 + ti * 128
            dma("sp", out_d[tok0:tok0 + 128, :], o)
    A.release(m0)

    Sc.finish()
    Sc.emit(stack)
    stack.close()
    return nc, Sc, A


_CACHE = {}


def kernel(**inputs):
    inp = {k: np.asarray(v) for k, v in inputs.items()}
    pp, rb, lbd, t2 = build_host_params(inp)
    if "nc" not in _CACHE:
        _CACHE["nc"] = build_program(L_ALL)[0]
    nc = _CACHE["nc"]
    shared = {
        "w_in": np.ascontiguousarray(inp["w_in"], np.float32),
        "w_out": np.ascontiguousarray(inp["w_out"], np.float32),
        "w_mlp1": np.ascontiguousarray(inp["w_mlp1"], np.float32),
        "w_mlp2": np.ascontiguousarray(inp["w_mlp2"], np.float32),
        "rwkv_w_up": np.ascontiguousarray(inp["rwkv_w_up"], np.float32),
        "rwkv_a_up": np.ascontiguousarray(inp["rwkv_a_up"], np.float32),
        "rwkv_g_up": np.ascontiguousarray(inp["rwkv_g_up"], np.float32),
        "pp": pp, "rb": rb, "lbd": lbd, "t2": t2.reshape(L_ALL, 128, 14 * 256), "lvl": build_level_masks(),
    }
    x = np.ascontiguousarray(inp["x"], np.float32)
    in_maps = [dict(shared, x=x[b]) for b in range(8)]
    res = run_bass_kernel_spmd(nc, in_maps, core_ids=list(range(8)))
    return np.stack([np.asarray(r["out"], np.float32) for r in res.results], axis=0)
```

```python
import numpy as np
from contextlib import ExitStack
import concourse.bass as bass
import concourse.mybir as mybir
from concourse.bass_utils import run_bass_kernel_spmd

F32 = mybir.dt.float32
BF16 = mybir.dt.bfloat16
AF = mybir.ActivationFunctionType
ALU = mybir.AluOpType
AX = mybir.AxisListType
DSZ = {F32: 4, BF16: 2}

L_ALL = 4
D = 1024
S = 2048
NT = 16
IN_COLS = 3464
RW0, SSD0, LRU0, NA0 = 0, 1152, 2184, 2696
CELL = 32
NSLOT = 8
ARENA_F32 = 53000
DECAY = 0.6065306597126334
NEG = -30000.0

PP_SPEC = [("n1", 8), ("n2", 8), ("mu0", 9), ("mu1", 9), ("w0", 4), ("a0", 4), ("kk", 2), ("ka", 2),
           ("rk", 2), ("scw", 24), ("scb", 6), ("lcw", 8), ("lcb", 2), ("lab", 4), ("lxb", 4), ("lam", 4)]
PP_OFF = {}
_o = 0
for _n, _k in PP_SPEC:
    PP_OFF[_n] = (_o, _k)
    _o += _k
PP_L = _o
PP_TOT = PP_L * L_ALL + 8
RB_SPEC = [("gnw", 256), ("gnb", 256), ("dtb", 8), ("alog", 8), ("dsk", 4), ("snw", 256)]
RB_OFF = {}
_o = 0
for _n, _k in RB_SPEC:
    RB_OFF[_n] = (_o, _k)
    _o += _k
RB_L = _o


def _colvec(a):
    a = np.asarray(a, np.float32)
    n = a.shape[-1] // 128
    b = a.reshape(a.shape[:-1] + (n, 128))
    return np.moveaxis(b, -1, 0)


def build_host_params(inp):
    pp = np.zeros((128, PP_TOT), np.float32)
    rb = np.zeros((L_ALL, 128, RB_L), np.float32)
    for l in range(L_ALL):
        def put(name, arr):
            o, k = PP_OFF[name]
            pp[:, l * PP_L + o: l * PP_L + o + k] = arr.reshape(128, k)
        put("n1", _colvec(inp["norm1_w"][l]))
        put("n2", _colvec(inp["norm2_w"][l]))
        put("mu0", _colvec(inp["rwkv_shift_mu"][l, 0]))
        put("mu1", _colvec(inp["rwkv_shift_mu"][l, 1]))
        put("w0", _colvec(inp["rwkv_w0"][l]))
        put("a0", _colvec(inp["rwkv_a0"][l]))
        put("kk", _colvec(inp["rwkv_k_k"][l]))
        put("ka", _colvec(inp["rwkv_k_a"][l]))
        put("rk", _colvec(inp["rwkv_r_k"][l].reshape(256)))
        put("scw", _colvec(inp["ssd_conv_w"][l]))
        put("scb", _colvec(inp["ssd_conv_b"][l]))
        put("lcw", _colvec(inp["lru_conv_w"][l]))
        put("lcb", _colvec(inp["lru_conv_b"][l]))
        put("lab", _colvec(inp["lru_gate_a_b"][l]))
        put("lxb", _colvec(inp["lru_gate_x_b"][l]))
        put("lam", _colvec(inp["lru_lambda"][l]))

        def putr(name, vec):
            o, k = RB_OFF[name]
            rb[l, :, o:o + k] = np.broadcast_to(np.asarray(vec, np.float32).reshape(1, k), (128, k))
        putr("gnw", inp["rwkv_gn_w"][l])
        putr("gnb", inp["rwkv_gn_b"][l])
        putr("dtb", inp["ssd_dt_bias"][l].reshape(8))
        putr("alog", inp["ssd_a_log"][l].reshape(8))
        putr("dsk", inp["ssd_d"][l])
        putr("snw", inp["ssd_norm_w"][l])
    pp[:, PP_L * L_ALL:] = _colvec(inp["final_norm_w"])
    lbd = np.zeros((L_ALL, 2, 2, 2, 128, 128), np.float32)
    for wi, nm in enumerate(("lru_gate_a_w", "lru_gate_x_w")):
        w = np.asarray(inp[nm], np.float32)
        for pt in range(2):
            for j in range(2):
                lbd[:, wi, :, pt, j * 64:(j + 1) * 64, j * 64:(j + 1) * 64] = w[:, :, 2 * pt + j]
    rbias = np.asarray(inp["na_rel_bias"], np.float32)
    qc = np.arange(64)
    kc = np.arange(64)
    ws = np.clip(qc - 8, 0, 48)
    inwin = (kc[:, None] >= ws[None, :]) & (kc[:, None] < ws[None, :] + 16)
    ci = np.clip(kc[:, None] - qc[None, :] + 15, 0, 30)
    t2 = np.full((L_ALL, 128, 14, 4, 64), NEG, np.float32)
    for kh in range(2):
        for dr in range(14):
            g = rbias[:, :, dr + kh, :][:, :, ci]
            g = np.where(inwin[None, None], g, np.float32(NEG))
            t2[:, kh * 64:(kh + 1) * 64, dr, :, :] = np.transpose(g, (0, 2, 1, 3))
    return pp, rb, lbd, t2


def build_level_masks():
    t = np.arange(128)
    out = np.zeros((128, 14, 128), np.float32)
    for k in range(7):
        ll = (((t[:, None] >> (k + 1)) == (t[None, :] >> (k + 1))) & (((t[:, None] >> k) & 1) == 1)
              & (((t[None, :] >> k) & 1) == 0))
        out[:, k, :] = ll
        out[:, 7 + k, :] = ll.T
    return out.reshape(128, 14 * 128)


class Sched:
    ISSUERS = ["pe", "act", "dve", "pool", "sp"]

    def __init__(self, nc, arena_name, psum_name):
        self.nc = nc
        self.semnames = ["pe", "act", "dve", "pool"]
        for q in ("sp", "act", "pool"):
            for i in range(NSLOT):
                self.semnames.append(f"{q}q{i}")
        self.sidx = {n: i for i, n in enumerate(self.semnames)}
        ns = len(self.semnames)
        self.ns = ns
        self.spaces = {arena_name: 0, psum_name: 1}
        ncell = [ARENA_F32 * 4 // CELL + 2, 16384 // CELL + 2]
        self.W = [np.zeros((4, c, ns), np.int64) for c in ncell]
        self.R = [np.zeros((4, c, ns), np.int64) for c in ncell]
        self.cnt = {e: 0 for e in ["pe", "act", "dve", "pool"]}
        self.dman = {"sp": 0, "act": 0, "pool": 0}
        self.seen = {e: np.zeros(ns, np.int64) for e in self.ISSUERS}
        self.q = {e: [] for e in self.ISSUERS}
        self.nops = 0
        self.limit = None
        self.trace = []

    def region(self, ap):
        sp = self.spaces.get(ap.tensor.name)
        if sp is None:
            return None
        a = ap.ap
        ds = DSZ[ap.dtype]
        rowlen = a[0][0]
        pcount = a[0][1]
        p0 = ap.offset // rowlen
        c0 = ap.offset % rowlen
        lo = hi = c0
        for step, cnt in a[1:]:
            ext = step * (cnt - 1)
            if ext < 0:
                lo += ext
            else:
                hi += ext
        blo, bhi = lo * ds, (hi + 1) * ds
        if sp == 1:
            blo = blo // 2048 * 2048
            bhi = (bhi + 2047) // 2048 * 2048
        return (sp, p0 // 32, (p0 + pcount + 31) // 32, blo // CELL, (bhi + CELL - 1) // CELL)

    def _need(self, r, w):
        need = np.zeros(self.ns, np.int64)
        rr = [x for x in (self.region(a) for a in r) if x is not None]
        wr = [x for x in (self.region(a) for a in w) if x is not None]
        for (sp, p0, p1, c0, c1) in rr:
            need = np.maximum(need, self.W[sp][p0:p1, c0:c1].max(axis=(0, 1)))
        for (sp, p0, p1, c0, c1) in wr:
            need = np.maximum(need, self.W[sp][p0:p1, c0:c1].max(axis=(0, 1)))
            need = np.maximum(need, self.R[sp][p0:p1, c0:c1].max(axis=(0, 1)))
        return need, rr, wr

    def _mark(self, rr, wr, si, val):
        for (sp, p0, p1, c0, c1) in wr:
            self.W[sp][p0:p1, c0:c1, si] = val
        for (sp, p0, p1, c0, c1) in rr:
            self.R[sp][p0:p1, c0:c1, si] = val

    def op(self, eng, fn, r=(), w=()):
        if self.limit is not None and self.nops >= self.limit:
            return
        need, rr, wr = self._need(r, w)
        if eng == "pe":
            need[self.sidx["pe"]] = 0
        seen = self.seen[eng]
        waits = [(i, int(need[i])) for i in np.nonzero(need > seen)[0]]
        np.maximum(seen, need, out=seen)
        self.cnt[eng] += 1
        if self.limit is not None:
            import sys as _s
            f = _s._getframe(1)
            self.trace.append((self.nops, eng, f.f_code.co_name, [(a.tensor.name, a.offset, a.ap) for a in list(w) + list(r)]))
        self.q[eng].append((waits, fn, self.sidx[eng], 1))
        self._mark(rr, wr, self.sidx[eng], self.cnt[eng])
        self.nops += 1

    def dma(self, queue, out, in_):
        if self.limit is not None and self.nops >= self.limit:
            return
        need, rr, wr = self._need([in_], [out])
        n = self.dman[queue]
        self.dman[queue] = n + 1
        slot, rnd = n % NSLOT, n // NSLOT
        si = self.sidx[f"{queue}q{slot}"]
        need[si] = max(need[si], 16 * rnd)
        seen = self.seen[queue]
        waits = [(i, int(need[i])) for i in np.nonzero(need > seen)[0]]
        np.maximum(seen, need, out=seen)
        fn = (lambda e, out=out, in_=in_: e.dma_start(out=out, in_=in_))
        self.q[queue].append((waits, fn, si, 16))
        self._mark(rr, wr, si, 16 * (rnd + 1))
        self.nops += 1

    def finish(self):
        final = np.zeros(self.ns, np.int64)
        for e in ["pe", "act", "dve", "pool"]:
            final[self.sidx[e]] = self.cnt[e]
        for qn in ("sp", "act", "pool"):
            n = self.dman[qn]
            for s in range(NSLOT):
                k = (n - s + NSLOT - 1) // NSLOT if n > s else 0
                final[self.sidx[f"{qn}q{s}"]] = 16 * k
        self.final = final

    def emit(self, stack):
        nc = self.nc
        sems = [stack.enter_context(nc.semaphore(f"s_{n}")) for n in self.semnames]
        block = stack.enter_context(nc.Block())

        def replay(name, e):
            for waits, fn, si, inc in self.q[name]:
                for (s, v) in waits:
                    e.wait_ge(sems[s], v)
                fn(e).then_inc(sems[si], inc)
            if name == "sp":
                for s in range(self.ns):
                    if self.final[s] > 0:
                        e.wait_ge(sems[s], int(self.final[s]))

        @block.tensor
        def _(e):
            replay("pe", e)

        @block.scalar
        def _(e):
            replay("act", e)

        @block.vector
        def _(e):
            replay("dve", e)

        @block.gpsimd
        def _(e):
            replay("pool", e)

        @block.sync
        def _(e):
            replay("sp", e)


class Arena:
    def __init__(self, base_ap_f32, ncols_f32, start=0):
        self.base = base_ap_f32
        self.lo = start * 4
        self.top = self.lo
        self.limit = (start + ncols_f32) * 4
        self.peak = self.top

    def alloc(self, cols, dtype, parts=128):
        nbytes = cols * DSZ[dtype]
        nbytes = (nbytes + CELL - 1) // CELL * CELL
        off = self.top
        self.top += nbytes
        self.peak = max(self.peak, self.top)
        assert self.top <= self.limit, f"arena overflow {self.top} > {self.limit}"
        ap = self.base[0:parts, off // 4:(off + nbytes) // 4]
        if dtype != F32:
            ap = ap.bitcast(dtype)
        return ap[:, 0:cols]

    def mark(self):
        return self.top

    def release(self, m):
        self.top = m


def build_program(n_layers=L_ALL, dbg=None):
    nc = bass.Bass("TRN2", target_bir_lowering=False)
    dt = nc.dram_tensor
    x_d = dt("x", [S, D], F32, kind="ExternalInput").ap()
    w_in_d = dt("w_in", [L_ALL, D, IN_COLS], F32, kind="ExternalInput").ap()
    w_out_d = dt("w_out", [L_ALL, D, D], F32, kind="ExternalInput").ap()
    w1_d = dt("w_mlp1", [L_ALL, D, 4 * D], F32, kind="ExternalInput").ap()
    w2_d = dt("w_mlp2", [L_ALL, 4 * D, D], F32, kind="ExternalInput").ap()
    wup_d = dt("rwkv_w_up", [L_ALL, 2, 64, 256], F32, kind="ExternalInput").ap()
    aup_d = dt("rwkv_a_up", [L_ALL, 2, 64, 256], F32, kind="ExternalInput").ap()
    gup_d = dt("rwkv_g_up", [L_ALL, 128, 256], F32, kind="ExternalInput").ap()
    pp_d = dt("pp", [128, PP_TOT], F32, kind="ExternalInput").ap()
    rb_d = dt("rb", [L_ALL, 128, RB_L], F32, kind="ExternalInput").ap()
    lbd_d = dt("lbd", [L_ALL, 2, 2, 2, 128, 128], F32, kind="ExternalInput").ap()
    t2_d = dt("t2", [L_ALL, 128, 14 * 256], F32, kind="ExternalInput").ap()
    lvl_d = dt("lvl", [128, 14 * 128], F32, kind="ExternalInput").ap()
    out_d = dt("out", [S, D], F32, kind="ExternalOutput").ap()
    dbg_d = None
    if dbg:
        dbg_d = dt("dbg", [D, S], F32, kind="ExternalOutput").ap()

    stack = ExitStack()
    arena_t = stack.enter_context(nc.sbuf_tensor("arena", [128, ARENA_F32], F32))
    ps_t = stack.enter_context(nc.psum_tensor("ps", [128, 4096], F32))
    arena_ap = arena_t[:, :]
    ps_ap = ps_t[:, :]
    Sc = Sched(nc, "arena", "ps")
    if dbg and dbg.get("limit"):
        Sc.limit = dbg["limit"]
    A = Arena(arena_ap, ARENA_F32)

    def bank(b):
        return ps_ap[:, b * 512:(b + 1) * 512]

    rot = [0]

    def gbank():
        b = rot[0] % 8
        rot[0] += 1
        return bank(b)

    def mm(out, lhsT, rhs, start=True, stop=True, skip=False):
        Sc.op("pe", lambda e: e.matmul(out, lhsT=lhsT, rhs=rhs, start=start, stop=stop, skip_group_check=skip),
              r=[lhsT, rhs], w=[out])

    def tr(out, in_, ident):
        Sc.op("pe", lambda e: e.transpose(out, in_, ident), r=[in_, ident], w=[out])

    def act(out, in_, func, scale=None, bias=None):
        kw = {}
        rd = [in_]
        if scale is not None:
            kw["scale"] = scale
            if not isinstance(scale, (int, float)):
                rd.append(scale)
        if bias is not None:
            kw["bias"] = bias
            if not isinstance(bias, (int, float)):
                rd.append(bias)
        Sc.op("act", lambda e: e.activation(out=out, in_=in_, func=func, **kw), r=rd, w=[out])

    def tt(out, in0, in1, op, eng="dve"):
        Sc.op(eng, lambda e: e.tensor_tensor(out=out, in0=in0, in1=in1, op=op), r=[in0, in1], w=[out])

    def ts(out, in0, s1, op0, s2=None, op1=None, eng="dve"):
        rd = [in0] + [s for s in (s1, s2) if s is not None and not isinstance(s, (int, float))]
        if op1 is None:
            Sc.op(eng, lambda e: e.tensor_scalar(out=out, in0=in0, scalar1=s1, scalar2=None, op0=op0), r=rd, w=[out])
        else:
            Sc.op(eng, lambda e: e.tensor_scalar(out=out, in0=in0, scalar1=s1, scalar2=s2, op0=op0, op1=op1),
                  r=rd, w=[out])

    def stt(out, in0, scalar, in1, op0, op1):
        rd = [in0, in1] + ([] if isinstance(scalar, (int, float)) else [scalar])
        Sc.op("dve", lambda e: e.scalar_tensor_tensor(out=out, in0=in0, scalar=scalar, in1=in1, op0=op0, op1=op1),
              r=rd, w=[out])

    def cp(out, in_, eng="dve"):
        if eng == "act":
            Sc.op("act", lambda e: e.activation(out=out, in_=in_, func=AF.Copy), r=[in_], w=[out])
        else:
            Sc.op(eng, lambda e: e.tensor_copy(out=out, in_=in_), r=[in_], w=[out])

    def memset(out, val, eng="pool"):
        Sc.op(eng, lambda e: e.memset(out, val), r=[], w=[out])

    def recip(out, in_):
        Sc.op("dve", lambda e: e.reciprocal(out=out, in_=in_), r=[in_], w=[out])

    def scan(out, d0, d1, init=0.0):
        Sc.op("dve", lambda e: e.tensor_tensor_scan(out=out, data0=d0, data1=d1, initial=init, op0=ALU.mult,
                                                    op1=ALU.add), r=[d0, d1], w=[out])

    def reduce_sum(out, in_):
        Sc.op("dve", lambda e: e.tensor_reduce(out=out, in_=in_, axis=AX.X, op=ALU.add), r=[in_], w=[out])

    def affsel(out, in_, pattern, cmp, fill, base, cm):
        Sc.op("pool", lambda e: e.affine_select(out=out, in_=in_, pattern=pattern, compare_op=cmp, fill=fill,
                                                base=base, channel_multiplier=cm), r=[in_], w=[out])

    def dma(q, out, in_):
        Sc.dma(q, out, in_)

    hF = A.alloc(8 * S, F32).rearrange("p (f t) -> p f t", f=8)
    xnT = A.alloc(8 * S, BF16).rearrange("p (f t) -> p f t", f=8)
    xn_region_start_f32 = (A.top - 8 * S * 2) // 4
    pp = A.alloc(PP_TOT, F32)
    ppd = A.alloc(L_ALL * 32, F32)
    rbt = A.alloc(RB_L, F32)
    identb = A.alloc(128, BF16)
    identf = A.alloc(128, F32)
    onesb = A.alloc(128, BF16)
    onesf = A.alloc(128, F32)
    blkb = A.alloc(128, BF16)
    headsel = A.alloc(2, BF16)
    mU = A.alloc(128, F32)
    mL = A.alloc(128, F32)
    mUI = A.alloc(128, F32)
    mLI = A.alloc(128, F32)
    nmU = A.alloc(128, F32)
    nmL = A.alloc(128, F32)
    nmUI = A.alloc(128, F32)
    nmLI = A.alloc(128, F32)
    negU4 = A.alloc(512, BF16)
    negL4 = A.alloc(512, BF16)
    lvl = A.alloc(14 * 128, BF16).rearrange("p (k c) -> p k c", k=14)
    ident4 = A.alloc(512, BF16)
    wb_region_start_f32 = A.top // 4
    wbufs = [A.alloc(8 * 512, BF16).rearrange("p (k c) -> p k c", k=8) for _ in range(2)]
    wb_i = [0]

    def next_wbuf():
        b = wbufs[wb_i[0] % 2]
        wb_i[0] += 1
        return b

    memset(identf, 0.0)
    affsel(identf, identf, [[-1, 128]], ALU.not_equal, 1.0, 0, 1)
    cp(identb, identf, "pool")
    memset(onesf, 1.0)
    memset(onesb, 1.0)
    for h in range(4):
        cp(ident4[:, h * 128:(h + 1) * 128], identf, "pool")
    dma("pool", lvl, lvl_d.rearrange("p (k c) -> p k c", k=14))
    affsel(mU, onesf, [[1, 128]], ALU.is_gt, 0.0, 0, -1)
    affsel(mUI, onesf, [[1, 128]], ALU.is_ge, 0.0, 0, -1)
    affsel(mL, onesf, [[-1, 128]], ALU.is_gt, 0.0, 0, 1)
    affsel(mLI, onesf, [[-1, 128]], ALU.is_ge, 0.0, 0, 1)
    for src, dst in ((mU, nmU), (mL, nmL), (mUI, nmUI), (mLI, nmLI)):
        ts(dst, src, -1.0, ALU.mult, eng="pool")
    for h in range(4):
        ts(negU4[:, h * 128:(h + 1) * 128], mU, NEG, ALU.mult, eng="pool")
        ts(negL4[:, h * 128:(h + 1) * 128], mL, NEG, ALU.mult, eng="pool")
    memset(blkb, 0.0)
    memset(blkb[0:64, 0:64], 1.0)
    memset(blkb[64:128, 64:128], 1.0)
    memset(headsel, 0.0)
    memset(headsel[0:64, 0:1], 1.0)
    memset(headsel[64:128, 1:2], 1.0)
    dma("sp", pp, pp_d)
    for l in range(n_layers):
        b0 = l * PP_L
        d0 = l * 32
        o_mu0, o_mu1 = PP_OFF["mu0"][0] + b0, PP_OFF["mu1"][0] + b0
        tt(ppd[:, d0:d0 + 9], pp[:, o_mu0:o_mu0 + 9], pp[:, o_mu1:o_mu1 + 9], ALU.add)
        ts(ppd[:, d0:d0 + 9], ppd[:, d0:d0 + 9], -1.0, ALU.mult, 1.0, ALU.add)
        o_ka = PP_OFF["ka"][0] + b0
        ts(ppd[:, d0 + 9:d0 + 11], pp[:, o_ka:o_ka + 2], -1.0, ALU.mult, 1.0, ALU.add)
        o_lam = PP_OFF["lam"][0] + b0
        act(ppd[:, d0 + 11:d0 + 15], pp[:, o_lam:o_lam + 4], AF.Exp, scale=-1.0)
        act(ppd[:, d0 + 11:d0 + 15], ppd[:, d0 + 11:d0 + 15], AF.Ln, bias=1.0)
        ts(ppd[:, d0 + 15:d0 + 19], ppd[:, d0 + 11:d0 + 15], -16.0, ALU.mult)
        ts(ppd[:, d0 + 11:d0 + 15], ppd[:, d0 + 11:d0 + 15], -8.0, ALU.mult)
        o_w0, o_a0 = PP_OFF["w0"][0] + b0, PP_OFF["a0"][0] + b0
        ts(ppd[:, d0 + 19:d0 + 23], pp[:, o_w0:o_w0 + 4], -1.0, ALU.mult)
        ts(ppd[:, d0 + 23:d0 + 27], pp[:, o_a0:o_a0 + 4], -1.0, ALU.mult)

    def P_(l, name, j=0, n=1):
        o, k = PP_OFF[name]
        return pp[:, l * PP_L + o + j: l * PP_L + o + j + n]

    m0 = A.mark()
    xst = [A.alloc(D, F32) for _ in range(2)]
    for tt_i in range(NT):
        xs_ = xst[tt_i % 2]
        dma("sp", xs_, x_d[tt_i * 128:(tt_i + 1) * 128, :])
        for half in range(2):
            pb = gbank()
            for j in range(4):
                f = half * 4 + j
                tr(pb[:, j * 128:(j + 1) * 128], xs_[:, f * 128:(f + 1) * 128], identf)
            cp(hF[:, half * 4:half * 4 + 4, tt_i * 128:(tt_i + 1) * 128],
               pb.rearrange("p (j t) -> p j t", j=4), "act" if half else "dve")
    A.release(m0)

    def rmsnorm_to_xnT(nw_of_f):
        m = A.mark()
        sq = A.alloc(8 * 512, BF16).rearrange("p (f t) -> p f t", f=8)
        rstd = A.alloc(512, F32)
        for tc in range(4):
            tsl = slice(tc * 512, (tc + 1) * 512)
            for f in range(8):
                act(sq[:, f, :], hF[:, f, tsl], AF.Square)
            pb = gbank()
            for f in range(8):
                mm(pb, onesb, sq[:, f, :], start=(f == 0), stop=(f == 7))
            act(rstd, pb, AF.Sqrt, scale=1.0 / D, bias=1e-5)
            recip(rstd, rstd)
            for f in range(8):
                stt(xnT[:, f, tsl], hF[:, f, tsl], nw_of_f(f), rstd, ALU.mult, ALU.mult)
        A.release(m)

    def load_w_cols(src2d, c0, ncols, q="pool"):
        wb = next_wbuf()
        dma(q, wb[:, :, 0:ncols], src2d.rearrange("(k p) c -> p k c", p=128)[:, :, c0:c0 + ncols])
        return wb

    def proj_fm(wb, cw, ncw, evac):
        for tc in range(4):
            pb = gbank()
            for k in range(8):
                mm(pb[0:ncw, :], wb[:, k, cw:cw + ncw], xnT[:, k, tc * 512:(tc + 1) * 512], start=(k == 0),
                   stop=(k == 7))
            evac(tc, pb[0:ncw, :])

    def proj_tm(wb, cw, ncw, tile_i, tok0, out_psum):
        for k in range(8):
            mm(out_psum, xnT[:, k, tok0:tok0 + 128], wb[:, k, cw:cw + ncw], start=(k == 0), stop=(k == 7))

    def outproj_acc(l, g, ymT):
        m = A.mark()
        wo = A.alloc(2 * 1024, BF16).rearrange("p (k c) -> p k c", k=2)
        dma("pool", wo, w_out_d[l, g * 256:(g + 1) * 256, :].rearrange("(k p) c -> p k c", p=128))
        for ot in range(8):
            for tc in range(4):
                tsl = slice(tc * 512, (tc + 1) * 512)
                pb = gbank()
                for k in range(2):
                    mm(pb, wo[:, k, ot * 128:(ot + 1) * 128], ymT[:, k, tsl], start=(k == 0), stop=(k == 1))
                tt(hF[:, ot, tsl], pb, hF[:, ot, tsl], ALU.add)
        A.release(m)

    def dump_dbg(g, ymT):
        if dbg_d is None or dbg.get('nodump'):
            return
        m = A.mark()
        st = A.alloc(S, F32)
        for k in range(2):
            cp(st, ymT[:, k, :], "pool")
            dma("sp", dbg_d[g * 256 + k * 128: g * 256 + (k + 1) * 128, :], st)
        A.release(m)

    def conv4(stage, wcol, bias_ap, out, func=None, out2=None):
        m = A.mark()
        acc = A.alloc(S, F32)
        ts(acc, stage[:, 0:S], wcol(0), ALU.mult)
        for k in range(1, 4):
            stt(acc, stage[:, k:k + S], wcol(k), acc, ALU.mult, ALU.add)
        if func is None:
            ts(out, acc, bias_ap, ALU.add)
            if out2 is not None:
                cp(out2, out, "pool")
        else:
            act(out, acc, func, bias=bias_ap)
        A.release(m)

    def ssd_mixer(l):
        m_ssd = A.mark()
        xs_tok = A.alloc(NT * 256, BF16).rearrange("p (i c) -> p i c", i=NT)
        B_tok = A.alloc(NT * 256, BF16).rearrange("p (i c) -> p i c", i=NT)
        BT = A.alloc(2 * S, BF16).rearrange("p (k t) -> p k t", k=2)
        CT = A.alloc(2 * S, BF16).rearrange("p (k t) -> p k t", k=2)
        o_alog, o_dtb = RB_OFF["alog"][0], RB_OFF["dtb"][0]
        m1 = A.mark()
        stage = A.alloc(S + 3, F32)
        xsT = A.alloc(2 * S, BF16).rearrange("p (k t) -> p k t", k=2)
        memset(stage[:, 0:2], 0.0)
        memset(stage[:, S + 2:S + 3], 0.0)
        for half in range(2):
            wb = load_w_cols(w_in_d[l], SSD0 + 256 + half * 384, 384)
            for j in range(3):
                c = half * 3 + j
                proj_fm(wb, j * 128, 128, lambda tc, pb: cp(stage[:, 2 + tc * 512: 2 + (tc + 1) * 512], pb, "act"))
                dest = (xsT[:, c, :] if c < 2 else (BT[:, c - 2, :] if c < 4 else CT[:, c - 4, :]))
                conv4(stage, lambda k, c=c: P_(l, "scw", k * 6 + c), P_(l, "scb", c), dest, func=AF.Silu)
        for i in range(NT):
            pbb = gbank().bitcast(BF16)
            for k in range(2):
                tr(pbb[:, k * 128:(k + 1) * 128], xsT[:, k, i * 128:(i + 1) * 128], identb)
                tr(pbb[:, 256 + k * 128:256 + (k + 1) * 128], BT[:, k, i * 128:(i + 1) * 128], identb)
            cp(xs_tok[:, i, :], pbb[:, 0:256], "act")
            cp(B_tok[:, i, :], pbb[:, 256:512], "act")
        A.release(m1)
        ymT = A.alloc(2 * S, BF16).rearrange("p (k t) -> p k t", k=2)
        z_tok = A.alloc(NT * 256, BF16).rearrange("p (i c) -> p i c", i=NT)
        y_tok = A.alloc(NT * 256, F32).rearrange("p (i c) -> p i c", i=NT)
        dt_tok = A.alloc(NT * 8, F32).rearrange("p (i c) -> p i c", i=NT)
        adt = A.alloc(NT * 8, F32).rearrange("p (i c) -> p i c", i=NT)
        Abc = A.alloc(8, F32)
        act(Abc, rbt[:, o_alog:o_alog + 8], AF.Exp)
        ts(Abc, Abc, -1.0, ALU.mult)
        wz = load_w_cols(w_in_d[l], SSD0, 256)
        for i in range(NT):
            pb = gbank()
            proj_tm(wz, 0, 256, i, i * 128, pb[:, 0:256])
            act(z_tok[:, i, :], pb[:, 0:256], AF.Silu)
        wdt = load_w_cols(w_in_d[l], SSD0 + 1024, 8)
        for i in range(NT):
            pb = gbank()
            proj_tm(wdt, 0, 8, i, i * 128, pb[:, 0:8])
            tt(dt_tok[:, i, :], pb[:, 0:8], rbt[:, o_dtb:o_dtb + 8], ALU.add)
        dtf = dt_tok.rearrange("p i c -> p (i c)")
        act(dtf, dtf, AF.Exp)
        act(dtf, dtf, AF.Ln, bias=1.0)
        tt(adt, dt_tok, Abc.unsqueeze(1).to_broadcast([128, NT, 8]), ALU.mult)
        m2 = A.mark()
        Hs = [A.alloc(256, F32) for _ in range(2)]
        Hb = [A.alloc(256, BF16) for _ in range(2)]
        for d in range(2):
            memset(Hs[d], 0.0)
            memset(Hb[d], 0.0)
        rhs_seg = A.alloc(512, F32)
        dec = A.alloc(512, BF16)
        scT = A.alloc(512, BF16)
        xdt = A.alloc(256, BF16)
        xde = A.alloc(256, BF16)
        ecs = A.alloc(8, F32)
        dte = A.alloc(8, F32)
        etot = A.alloc(8, F32)
        ydg = A.alloc(256, F32)
        for st_i in range(NT):
            for d in range(2):
                c = st_i if d == 0 else NT - 1 - st_i
                csl = slice(c * 128, (c + 1) * 128)
                a4 = adt[:, c, d * 4:(d + 1) * 4]
                pG = gbank()
                for g in range(2):
                    mm(pG[:, g * 128:(g + 1) * 128], BT[:, g, csl], CT[:, g, csl])
                pS = gbank()
                mm(pS[:, 0:4], (mUI if d == 0 else mLI), a4)
                mm(pS[:, 8:12], onesf, a4)
                act(ecs[:, 0:4], pS[:, 0:4], AF.Exp)
                cp(dte[:, 4:8], pS[:, 0:4], "dve")
                tt(dte[:, 0:4], pS[:, 8:12], dte[:, 4:8], ALU.subtract)
                act(dte[:, 0:4], dte[:, 0:4], AF.Exp)
                act(etot[:, 0:4], pS[:, 8:12], AF.Exp)
                msk = mUI if d == 0 else mLI
                tt(rhs_seg.rearrange("p (h l) -> p h l", h=4),
                   msk.unsqueeze(1).to_broadcast([128, 4, 128]),
                   a4.unsqueeze(2).to_broadcast([128, 4, 128]), ALU.mult)
                pD = gbank()
                mm(pD, (mL if d == 0 else mU), rhs_seg, start=True, stop=False)
                mm(pD, identb, (negL4 if d == 0 else negU4), start=False, stop=True)
                act(dec, pD, AF.Exp)
                tt(scT.rearrange("p (g r l) -> p g r l", g=2, r=2),
                   dec.rearrange("p (g r l) -> p g r l", g=2, r=2),
                   pG[:, 0:256].rearrange("p (g l) -> p g l", g=2).unsqueeze(2).to_broadcast([128, 2, 2, 128]),
                   ALU.mult)
                dt4 = dt_tok[:, c, d * 4:(d + 1) * 4]
                tt(xdt.rearrange("p (h q) -> p h q", h=4), xs_tok[:, c, :].rearrange("p (h q) -> p h q", h=4),
                   dt4.unsqueeze(2).to_broadcast([128, 4, 64]), ALU.mult)
                tt(xde.rearrange("p (h q) -> p h q", h=4), xdt.rearrange("p (h q) -> p h q", h=4),
                   dte[:, 0:4].unsqueeze(2).to_broadcast([128, 4, 64]), ALU.mult)
                pY = gbank()
                for h in range(4):
                    mm(pY[:, h * 64:(h + 1) * 64], scT[:, h * 128:(h + 1) * 128], xdt[:, h * 64:(h + 1) * 64])
                    mm(pY[:, 256 + h * 64:256 + (h + 1) * 64], CT[:, h // 2, csl], Hb[d][:, h * 64:(h + 1) * 64])
                first = st_i < NT // 2
                if first:
                    cp(y_tok[:, c, :], pY[:, 0:256], "act")
                else:
                    tt(y_tok[:, c, :], pY[:, 0:256], y_tok[:, c, :], ALU.add)
                for h in range(4):
                    stt(y_tok[:, c, h * 64:(h + 1) * 64], pY[:, 256 + h * 64:256 + (h + 1) * 64], ecs[:, h:h + 1],
                        y_tok[:, c, h * 64:(h + 1) * 64], ALU.mult, ALU.add)
                pH = gbank()
                for h in range(4):
                    mm(pH[:, h * 64:(h + 1) * 64], B_tok[:, c, (h // 2) * 128:(h // 2 + 1) * 128],
                       xde[:, h * 64:(h + 1) * 64])
                for h in range(4):
                    stt(Hs[d][:, h * 64:(h + 1) * 64], Hs[d][:, h * 64:(h + 1) * 64], etot[:, h:h + 1],
                        pH[:, h * 64:(h + 1) * 64], ALU.mult, ALU.add)
                cp(Hb[d], Hs[d], "act")
        A.release(m2)
        m3 = A.mark()
        o_dsk, o_snw = RB_OFF["dsk"][0], RB_OFF["snw"][0]
        tmp = A.alloc(256, F32)
        tmpb = A.alloc(256, BF16)
        ss = A.alloc(2, F32)
        for i in range(NT):
            tt(tmp.rearrange("p (h q) -> p h q", h=4), xs_tok[:, i, :].rearrange("p (h q) -> p h q", h=4),
               rbt[:, o_dsk:o_dsk + 4].unsqueeze(2).to_broadcast([128, 4, 64]), ALU.mult)
            tt(tmp, tmp, y_tok[:, i, :], ALU.add)
            tt(tmp, tmp, z_tok[:, i, :], ALU.mult)
            tt(y_tok[:, i, :], tmp, tmp, ALU.mult)
            reduce_sum(ss[:, 0:1], y_tok[:, i, :])
            act(ss[:, 1:2], ss[:, 0:1], AF.Sqrt, scale=1.0 / 256, bias=1e-5)
            recip(ss[:, 1:2], ss[:, 1:2])
            stt(tmpb, tmp, ss[:, 1:2], rbt[:, o_snw:o_snw + 256], ALU.mult, ALU.mult)
            pbb = gbank().bitcast(BF16)
            for k in range(2):
                tr(pbb[:, k * 128:(k + 1) * 128], tmpb[:, k * 128:(k + 1) * 128], identb)
            cp(ymT[:, :, i * 128:(i + 1) * 128], pbb[:, 0:256].rearrange("p (k t) -> p k t", k=2), "act")
        A.release(m3)
        dump_dbg(1, ymT)
        outproj_acc(l, 1, ymT)
        A.release(m_ssd)

    def lru_mixer(l):
        m_l = A.mark()
        ymT = A.alloc(2 * S, BF16).rearrange("p (k t) -> p k t", k=2)
        gate = A.alloc(2 * S, BF16).rearrange("p (k t) -> p k t", k=2)
        xf = A.alloc(2 * S, F32).rearrange("p (k t) -> p k t", k=2)
        xfb = A.alloc(2 * S, BF16).rearrange("p (k t) -> p k t", k=2)
        m_st = A.mark()
        stage = A.alloc(S + 3, F32)
        memset(stage[:, 0:2], 0.0)
        memset(stage[:, S + 2:S + 3], 0.0)
        wb = load_w_cols(w_in_d[l], LRU0, 512)
        for c in range(2):
            proj_fm(wb, c * 128, 128,
                    lambda tc, pb, c=c: act(gate[:, c, tc * 512:(tc + 1) * 512], pb, AF.Gelu_apprx_tanh))
        for c in range(2):
            proj_fm(wb, 256 + c * 128, 128,
                    lambda tc, pb: cp(stage[:, 2 + tc * 512: 2 + (tc + 1) * 512], pb, "act"))
            conv4(stage, lambda k, c=c: P_(l, "lcw", k * 2 + c), P_(l, "lcb", c), xf[:, c, :], out2=xfb[:, c, :])
        A.release(m_st)
        m_tmp = A.mark()
        wg = A.alloc(8 * 128, BF16).rearrange("p (a c) -> p a c", a=8)
        dma("pool", wg, lbd_d[l].rearrange("w d t p c -> p (w d t) c"))
        rec = A.alloc(S, F32)
        inp = A.alloc(S, F32)
        aa = A.alloc(S, F32)
        hsum = A.alloc(S, F32)
        for pt in range(2):
            for d in range(2):
                for which, dst, bname in ((0, rec, "lab"), (1, inp, "lxb")):
                    for tc in range(4):
                        tsl = slice(tc * 512, (tc + 1) * 512)
                        pb = gbank()
                        mm(pb, wg[:, which * 4 + d * 2 + pt, :], xfb[:, pt, tsl])
                        act(dst[:, tsl], pb, AF.Sigmoid, bias=P_(l, bname, d * 2 + pt))
                lc = ppd[:, l * 32 + 11 + d * 2 + pt: l * 32 + 12 + d * 2 + pt]
                lc2 = ppd[:, l * 32 + 15 + d * 2 + pt: l * 32 + 16 + d * 2 + pt]
                act(aa, rec, AF.Exp, scale=lc)
                act(rec, rec, AF.Exp, scale=lc2)
                act(rec, rec, AF.Sqrt, scale=-1.0, bias=1.0)
                tt(inp, inp, rec, ALU.mult)
                tt(inp, inp, xf[:, pt, :], ALU.mult)
                if d == 0:
                    scan(hsum, aa, inp)
                else:
                    scan(rec[:, ::-1], aa[:, ::-1], inp[:, ::-1])
                    tt(hsum, hsum, rec, ALU.add)
                    tt(ymT[:, pt, :], hsum, gate[:, pt, :], ALU.mult)
        A.release(m_tmp)
        dump_dbg(2, ymT)
        outproj_acc(l, 2, ymT)
        A.release(m_l)

    def na_mixer(l):
        m_n = A.mark()
        ymT = A.alloc(2 * S, BF16).rearrange("p (k t) -> p k t", k=2)
        qm = [A.alloc(S, BF16) for _ in range(4)]
        for h in range(4):
            z0 = 64 - (h % 2) * 64
            memset(qm[h][z0:z0 + 64, :], 0.0)
        kT = A.alloc(2 * S, BF16).rearrange("p (k t) -> p k t", k=2)
        va = [A.alloc(NT * 260, BF16).rearrange("p (i h q) -> p i h q", i=NT, h=4) for _ in range(2)]
        t2 = A.alloc(14 * 256, F32).rearrange("p (r x) -> p r x", r=14)
        dma("sp", t2, t2_d[l].rearrange("p (r x) -> p r x", r=14))
        wb = load_w_cols(w_in_d[l], NA0, 512)
        def q_evac(tc, pb, c):
            tsl = slice(tc * 512, (tc + 1) * 512)
            act(qm[2 * c][0:64, tsl], pb[0:64, :], AF.Copy, scale=0.125)
            act(qm[2 * c + 1][64:128, tsl], pb[64:128, :], AF.Copy, scale=0.125)
        for c in range(2):
            proj_fm(wb, c * 128, 128, lambda tc, pb, c=c: q_evac(tc, pb, c))
        for c in range(2):
            proj_fm(wb, 256 + c * 128, 128, lambda tc, pb, c=c: cp(kT[:, c, tc * 512:(tc + 1) * 512], pb, "act"))
        wv = load_w_cols(w_in_d[l], NA0 + 512, 256)
        for al in range(2):
            memset(va[al][:, :, :, 64:65], 1.0)
            for i in range(NT - al):
                pb = gbank()
                proj_tm(wv, 0, 256, i, i * 128 + al * 64, pb[:, 0:256])
                cp(va[al][:, i, :, 0:64], pb[:, 0:256].rearrange("p (h q) -> p h q", h=4), "act" if i % 2 else "dve")
        ein = A.alloc(1024, F32)
        pT = A.alloc(1024, BF16)
        rc = A.alloc(4, F32)
        ytok = A.alloc(256, BF16, parts=64)
        for r in range(32):
            kr0 = min(max(r - 4, 0), 24)
            dr0 = kr0 - r + 7
            al = kr0 % 2
            ti0 = kr0 // 2
            q0 = r * 64
            pS = [gbank(), gbank()]
            for kt in range(4):
                k0 = kr0 * 64 + kt * 128
                for h in range(4):
                    col = (kt * 4 + h) * 64
                    mm(pS[col // 512][:, col % 512: col % 512 + 64], kT[:, h // 2, k0:k0 + 128],
                       qm[h][:, q0:q0 + 64])
            for half in range(2):
                tt(ein[:, half * 512:(half + 1) * 512].rearrange("p (a x) -> p a x", a=2),
                   pS[half].rearrange("p (a x) -> p a x", a=2),
                   t2[:, dr0 + 4 * half: dr0 + 4 * half + 3: 2, :], ALU.add)
            act(pT, ein, AF.Exp)
            pO = gbank()
            for h in range(4):
                for kt in range(4):
                    col = (kt * 4 + h) * 64
                    mm(pO[0:64, h * 65:(h + 1) * 65], pT[:, col:col + 64], va[al][:, ti0 + kt, h, :],
                       start=(kt == 0), stop=(kt == 3))
            pO3 = pO[0:64, 0:260].rearrange("p (h q) -> p h q", h=4)
            recip(rc[0:64, :], pO3[:, :, 64])
            tt(ytok.rearrange("p (h q) -> p h q", h=4), pO3[:, :, 0:64],
               rc[0:64, :].unsqueeze(2).to_broadcast([64, 4, 64]), ALU.mult)
            pbb = gbank().bitcast(BF16)
            for k in range(2):
                tr(pbb[:, k * 64:(k + 1) * 64], ytok[:, k * 128:(k + 1) * 128], identb[0:64, 0:64])
            cp(ymT[:, :, q0:q0 + 64], pbb[:, 0:128].rearrange("p (k t) -> p k t", k=2), "act")
        dump_dbg(3, ymT)
        outproj_acc(l, 3, ymT)
        A.release(m_n)

    def rwkv_mixer(l):
        m_r = A.mark()
        rT = A.alloc(2 * S, BF16).rearrange("p (k t) -> p k t", k=2)
        kT = A.alloc(2 * S, BF16).rearrange("p (k t) -> p k t", k=2)
        wdT = A.alloc(S, BF16)
        adT = A.alloc(S, BF16)
        gdT = A.alloc(S, BF16)
        v_tok = A.alloc(NT * 256, BF16).rearrange("p (i c) -> p i c", i=NT)
        wup = A.alloc(256, BF16)
        aup = A.alloc(256, BF16)
        gup = A.alloc(256, BF16)
        dma("pool", wup, wup_d[l].rearrange("d r c -> (d r) c"))
        dma("pool", aup, aup_d[l].rearrange("d r c -> (d r) c"))
        dma("pool", gup, gup_d[l])
        m1 = A.mark()
        stage = A.alloc(S + 2, F32)
        t1 = A.alloc(S, F32)
        vT = A.alloc(2 * S, BF16).rearrange("p (k t) -> p k t", k=2)
        memset(stage[:, 0:1], 0.0)
        memset(stage[:, S + 1:S + 2], 0.0)
        dests = [rT[:, 0, :], rT[:, 1, :], kT[:, 0, :], kT[:, 1, :], vT[:, 0, :], vT[:, 1, :], None, None, gdT]
        for grp in range(3):
            wb = load_w_cols(w_in_d[l], RW0 + grp * 384, 384)
            for j in range(3):
                c = grp * 3 + j
                proj_fm(wb, j * 128, 128, lambda tc, pb: cp(stage[:, 1 + tc * 512: 1 + (tc + 1) * 512], pb, "act"))
                c0 = ppd[:, l * 32 + c: l * 32 + c + 1]
                ts(t1, stage[:, 1:S + 1], c0, ALU.mult)
                stt(t1, stage[:, 0:S], P_(l, "mu0", c), t1, ALU.mult, ALU.add)
                if c == 6:
                    stt(t1, stage[:, 2:S + 2], P_(l, "mu1", c), t1, ALU.mult, ALU.add)
                    act(wdT, t1, AF.Tanh)
                elif c == 7:
                    stt(t1, stage[:, 2:S + 2], P_(l, "mu1", c), t1, ALU.mult, ALU.add)
                    cp(adT, t1, "act")
                elif c == 8:
                    stt(t1, stage[:, 2:S + 2], P_(l, "mu1", c), t1, ALU.mult, ALU.add)
                    act(dests[c], t1, AF.Sigmoid)
                else:
                    stt(dests[c], stage[:, 2:S + 2], P_(l, "mu1", c), t1, ALU.mult, ALU.add)
        for i in range(NT):
            pbb = gbank().bitcast(BF16)
            for k in range(2):
                tr(pbb[:, k * 128:(k + 1) * 128], vT[:, k, i * 128:(i + 1) * 128], identb)
            cp(v_tok[:, i, :], pbb[:, 0:256], "act")
        A.release(m1)
        ymT = A.alloc(2 * S, BF16).rearrange("p (k t) -> p k t", k=2)
        y_tok = A.alloc(NT * 256, BF16).rearrange("p (i c) -> p i c", i=NT)
        bonus = A.alloc(NT * 4, F32).rearrange("p (i c) -> p i c", i=NT)

        XA_ = Arena(arena_ap, 8 * S // 2, start=xn_region_start_f32)

        XB_ = Arena(arena_ap, 2 * 8 * 512 // 2, start=wb_region_start_f32)

        class _XA:
            @staticmethod
            def alloc(cols, dtype):
                for ar in (XA_, XB_):
                    if ar.top + (cols * DSZ[dtype] + CELL - 1) // CELL * CELL <= ar.limit:
                        return ar.alloc(cols, dtype)
                return A.alloc(cols, dtype)
        XA = _XA

        class DirBuf:
            pass
        G_SETS = 3
        STs, STbs = [], []
        for d in range(2):
            st_ = XA.alloc(2 * 64, F32).rearrange("p (k i) -> p k i", k=2)
            stb_ = XA.alloc(2 * 64, BF16).rearrange("p (k i) -> p k i", k=2)
            memset(st_, 0.0)
            memset(stb_, 0.0)
            STs.append(st_)
            STbs.append(stb_)
        Xbs = [XA.alloc(256, BF16) for _ in range(2)]
        Ubs = [XA.alloc(256, BF16) for _ in range(2)]
        dbs = []
        for g_ in range(G_SETS):
            b = DirBuf()
            b.sa = XA.alloc(256, F32)
            b.sig = b.sa[:, 0:128]
            b.al = b.sa[:, 128:256]
            b.ein = [XA.alloc(129, F32) for _ in range(2)]
            for nm in ("eni", "eh", "kd", "bb", "kks", "kk"):
                setattr(b, nm, XA.alloc(128, BF16))
            b.sq = b.eh
            b.Bh = [XA.alloc(128, BF16) for _ in range(2)]
            b.Kh = [XA.alloc(128, BF16) for _ in range(2)]
            b.rkd = [XA.alloc(128, BF16) for _ in range(2)]
            b.Rt = [XA.alloc(128, BF16) for _ in range(2)]
            b.Qk = [XA.alloc(128, BF16) for _ in range(2)]
            b.Btm = [[XA.alloc(128, BF16) for _ in range(2)] for _ in range(2)]
            b.Qkm = [[XA.alloc(128, BF16) for _ in range(2)] for _ in range(2)]
            b.Ktm = [[XA.alloc(128, BF16) for _ in range(2)] for _ in range(2)]
            b.Rtm = [[XA.alloc(128, BF16) for _ in range(2)] for _ in range(2)]
            for grp in (b.Btm, b.Qkm, b.Ktm, b.Rtm):
                for hp_ in range(2):
                    for h2 in range(2):
                        z0 = 64 - h2 * 64
                        memset(grp[hp_][h2][z0:z0 + 64, :], 0.0)
            b.wdm = [XA.alloc(128, BF16) for _ in range(2)]
            b.adm = [XA.alloc(128, BF16) for _ in range(2)]
            for d_ in range(2):
                z0 = 64 - d_ * 64
                memset(b.wdm[d_][z0:z0 + 64, :], 0.0)
                memset(b.adm[d_][z0:z0 + 64, :], 0.0)
            b.BhT = XA.alloc(256, BF16)
            b.KhT = XA.alloc(256, BF16)
            b.Mf = XA.alloc(512, BF16)
            b.T = XA.alloc(512, BF16)
            b.Z = XA.alloc(512, BF16)
            b.W = XA.alloc(512, BF16)
            b.Akk = XA.alloc(512, BF16)
            b.Arb = XA.alloc(512, BF16)
            b.Ark = XA.alloc(512, BF16)
            dbs.append(b)
        gtmp = XA.alloc(256, F32)
        gt2 = XA.alloc(256, F32)
        gst = XA.alloc(16, F32)
        gout = gt2.bitcast(BF16)[:, 0:256]

        def finalize_chunk(c):
            y4 = gtmp.rearrange("p (h q) -> p h q", h=4)
            cp(gtmp, y_tok[:, c, :], "pool")
            reduce_sum(gst[:, 0:4], y4)
            tt(gt2, gtmp, gtmp, ALU.mult)
            reduce_sum(gst[:, 4:8], gt2.rearrange("p (h q) -> p h q", h=4))
            ts(gst[:, 0:4], gst[:, 0:4], 1.0 / 64, ALU.mult)
            tt(gst[:, 8:12], gst[:, 0:4], gst[:, 0:4], ALU.mult)
            stt(gst[:, 4:8], gst[:, 4:8], 1.0 / 64, gst[:, 8:12], ALU.mult, ALU.subtract)
            act(gst[:, 4:8], gst[:, 4:8], AF.Sqrt, bias=64e-5)
            recip(gst[:, 4:8], gst[:, 4:8])
            tt(y4, y4, gst[:, 0:4].unsqueeze(2).to_broadcast([128, 4, 64]), ALU.subtract)
            tt(y4, y4, gst[:, 4:8].unsqueeze(2).to_broadcast([128, 4, 64]), ALU.mult)
            o_w, o_b = RB_OFF["gnw"][0], RB_OFF["gnb"][0]
            tt(gtmp, gtmp, rbt[:, o_w:o_w + 256], ALU.mult)
            tt(gtmp, gtmp, rbt[:, o_b:o_b + 256], ALU.add)
            tt(gt2.rearrange("p (h q) -> p h q", h=4), v_tok[:, c, :].rearrange("p (h q) -> p h q", h=4),
               bonus[:, c, :].unsqueeze(2).to_broadcast([128, 4, 64]), ALU.mult)
            tt(gtmp, gtmp, gt2, ALU.add)
            pg = gbank()
            mm(pg[:, 0:256], gdT[:, c * 128:(c + 1) * 128], gup)
            tt(gout, gtmp, pg[:, 0:256], ALU.mult)
            pbb = gbank().bitcast(BF16)
            for k in range(2):
                tr(pbb[:, k * 128:(k + 1) * 128], gout[:, k * 128:(k + 1) * 128], identb)
            cp(ymT[:, :, c * 128:(c + 1) * 128], pbb[:, 0:256].rearrange("p (k t) -> p k t", k=2), "act")

        seen_y = [False] * NT
        seen_bonus = [False] * NT

        def step(d, c, b):
            csl = slice(c * 128, (c + 1) * 128)
            fwd = (d == 0)
            ST, STb, Xb, Ub = STs[d], STbs[d], Xbs[d], Ubs[d]
            m_strict_neg = nmU if fwd else nmL
            m_strict = mU if fwd else mL
            m_incl = mUI if fwd else mLI
            m_incl_neg = nmUI if fwd else nmLI

            def heads(fn):
                for h in range(4):
                    fn(h, h // 2, slice((h % 2) * 64, (h % 2) * 64 + 64), slice(h * 128, (h + 1) * 128))
            dsl = slice(d * 64, d * 64 + 64)
            cp(b.wdm[d][dsl, :], wdT[dsl, csl], "dve")
            cp(b.adm[d][dsl, :], adT[dsl, csl], "dve")
            for hp in range(2):
                pz = gbank()
                mm(pz[:, 0:128], wup[:, hp * 128:(hp + 1) * 128], b.wdm[d])
                mm(pz[:, 128:256], aup[:, hp * 128:(hp + 1) * 128], b.adm[d])
                nw0 = ppd[:, l * 32 + 19 + d * 2 + hp: l * 32 + 20 + d * 2 + hp]
                na0 = ppd[:, l * 32 + 23 + d * 2 + hp: l * 32 + 24 + d * 2 + hp]
                act(b.sig, pz[:, 0:128], AF.Exp, scale=-1.0, bias=nw0)
                act(b.al, pz[:, 128:256], AF.Exp, scale=-1.0, bias=na0)
                act(b.sa, b.sa, AF.Ln, bias=1.0)
                act(b.sa, b.sa, AF.Exp, scale=-1.0)
                if fwd:
                    cs = b.ein[hp][:, 1:129]
                    scan(cs, onesf, b.sig)
                    pad = b.ein[hp][:, 0:1]
                    eex = b.ein[hp][:, 0:128]
                    etot = b.ein[hp][:, 128:129]
                else:
                    cs = b.ein[hp][:, 0:128]
                    scan(cs[:, ::-1], onesf, b.sig[:, ::-1])
                    pad = b.ein[hp][:, 128:129]
                    eex = b.ein[hp][:, 1:129]
                    etot = b.ein[hp][:, 0:1]
                ts(b.kks, kT[:, hp, csl], P_(l, "kk", hp), ALU.mult)
                tt(b.sq, b.kks, b.kks, ALU.mult)
                pn = gbank()
                mm(pn[:, 0:128], blkb, b.sq)
                act(b.eni, cs, AF.Exp, scale=DECAY)
                act(cs, cs, AF.Exp, scale=-DECAY)
                act(b.kk, pn[:, 0:128], AF.Ln, bias=1e-12)
                act(b.kk, b.kk, AF.Exp, scale=-0.5)
                yield
                memset(pad, 1.0, "pool")
                ts(b.eh, b.eni, etot, ALU.mult)
                tt(b.kk, b.kk, b.kks, ALU.mult)
                omka = ppd[:, l * 32 + 9 + hp: l * 32 + 10 + hp]
                ts(b.kd, b.al, P_(l, "ka", hp), ALU.mult, omka, ALU.add)
                tt(b.kd, b.kd, kT[:, hp, csl], ALU.mult)
                tt(b.bb, b.al, b.kk, ALU.mult)
                tt(b.Rt[hp], rT[:, hp, csl], cs, ALU.mult)
                tt(b.Qk[hp], b.kk, eex, ALU.mult)
                for h2 in range(2):
                    pr = slice(h2 * 64, h2 * 64 + 64)
                    tt(b.Btm[hp][h2][pr, :], b.bb[pr, :], b.eni[pr, :], ALU.mult)
                    tt(b.Ktm[hp][h2][pr, :], b.kd[pr, :], b.eni[pr, :], ALU.mult)
                    cp(b.Qkm[hp][h2][pr, :], b.Qk[hp][pr, :], "dve")
                    cp(b.Rtm[hp][h2][pr, :], b.Rt[hp][pr, :], "dve")
                stt(b.Bh[hp], b.bb, -1.0, b.eh, ALU.mult, ALU.mult)
                tt(b.Kh[hp], b.kd, b.eh, ALU.mult)
                stt(b.rkd[hp], rT[:, hp, csl], P_(l, "rk", hp), b.kd, ALU.mult, ALU.mult)
                yield
            pbn = gbank()
            pbb = gbank().bitcast(BF16)
            for hp in range(2):
                mm(pbn[:, hp * 2:hp * 2 + 2], b.rkd[hp], headsel)
                tr(pbb[:, hp * 128:(hp + 1) * 128], b.Bh[hp], identb)
                tr(pbb[:, 256 + hp * 128:256 + (hp + 1) * 128], b.Kh[hp], identb)
            if not seen_bonus[c]:
                seen_bonus[c] = True
                cp(bonus[:, c, :], pbn[:, 0:4], "dve")
            else:
                tt(bonus[:, c, :], pbn[:, 0:4], bonus[:, c, :], ALU.add)
            cp(b.BhT, pbb[:, 0:256], "act")
            cp(b.KhT, pbb[:, 256:512], "act")
            for lhs_, rhs_, dst_, msk_ in ((b.Btm, b.Qk, b.Mf, m_strict_neg), (b.Ktm, b.Qk, b.Akk, m_strict),
                                           (b.Btm, b.Rt, b.Arb, m_incl_neg), (b.Ktm, b.Rt, b.Ark, m_incl)):
                pa = gbank()
                heads(lambda h, hp, jr, hs: mm(pa[:, hs], lhs_[hp][h % 2], rhs_[hp]))
                tt(dst_.rearrange("p (h t) -> p h t", h=4), pa.rearrange("p (h t) -> p h t", h=4),
                   msk_.unsqueeze(1).to_broadcast([128, 4, 128]), ALU.mult)
            yield
            for k in range(7):
                Tc = ident4 if k == 0 else b.T
                Zc = ident4 if k == 0 else b.Z
                pa = gbank()
                heads(lambda h, hp, jr, hs: mm(pa[:, hs], b.Mf[:, hs], Tc[:, hs]))
                mk = lvl[:, (k if fwd else 7 + k), :]
                tt(b.W.rearrange("p (h t) -> p h t", h=4), pa.rearrange("p (h t) -> p h t", h=4),
                   mk.unsqueeze(1).to_broadcast([128, 4, 128]), ALU.mult)
                if k < 6:
                    pt = gbank()
                    mm(pt, identb, Tc, start=True, stop=False, skip=True)
                    heads(lambda h, hp, jr, hs: mm(pt[:, hs], Zc[:, hs], b.W[:, hs], start=False, stop=False,
                                                   skip=True))
                pz = gbank()
                mm(pz, identb, Zc, start=True, stop=False, skip=True)
                heads(lambda h, hp, jr, hs: mm(pz[:, hs], b.W[:, hs], Zc[:, hs], start=False, stop=False, skip=True))
                if k < 6:
                    cp(b.T, pt, "act")
                cp(b.Z, pz, "dve" if k % 2 else "act")
                yield
            yield "CHAIN"
            px = gbank()

            def xinit(h, hp, jr, hs):
                xs_ = slice(h * 64, (h + 1) * 64)
                mm(px[:, xs_], b.Qkm[hp][h % 2], STb[:, hp, :], start=True, stop=False)
                mm(px[:, xs_], b.Akk[:, hs], v_tok[:, c, xs_], start=False, stop=True)
            heads(xinit)
            cp(Xb, px[:, 0:256], "act")
            pu = gbank()
            heads(lambda h, hp, jr, hs: mm(pu[:, h * 64:(h + 1) * 64], b.Z[:, hs], Xb[:, h * 64:(h + 1) * 64]))
            cp(Ub, pu[:, 0:256], "act")
            py = gbank()

            def yfn(h, hp, jr, hs):
                xs_ = slice(h * 64, (h + 1) * 64)
                mm(py[:, xs_], b.Rtm[hp][h % 2], STb[:, hp, :], start=True, stop=False)
                mm(py[:, xs_], b.Arb[:, hs], Ub[:, xs_], start=False, stop=False)
                mm(py[:, xs_], b.Ark[:, hs], v_tok[:, c, xs_], start=False, stop=True)
            heads(yfn)
            if not seen_y[c]:
                seen_y[c] = True
                cp(y_tok[:, c, :], py[:, 0:256], "act")
            else:
                tt(y_tok[:, c, :], py[:, 0:256], y_tok[:, c, :], ALU.add)
            ph = gbank()
            for h in range(4):
                hp = h // 2
                xs_ = slice(h * 64, (h + 1) * 64)
                mm(ph[:, xs_], b.BhT[:, hp * 128:(hp + 1) * 128], Ub[:, xs_], start=True, stop=False)
                mm(ph[:, xs_], b.KhT[:, hp * 128:(hp + 1) * 128], v_tok[:, c, xs_], start=False, stop=True)
            for h in range(4):
                hp = h // 2
                jr = slice((h % 2) * 64, (h % 2) * 64 + 64)
                etot = (b.ein[hp][jr, 128:129] if fwd else b.ein[hp][jr, 0:1])
                stt(ST[jr, hp, :], ST[jr, hp, :], etot, ph[jr, h * 64:(h + 1) * 64], ALU.mult, ALU.add)
            cp(STb, ST, "act")
            yield

        pending = []
        for st_i in range(NT):
            pending.append((0, st_i, st_i))
            pending.append((1, NT - 1 - st_i, st_i))
        live = []
        free_sets = list(range(G_SETS))
        done = [0, 0]
        ndone_chunk = [0] * NT
        while pending or live:
            while pending and free_sets:
                d_, c_, idx_ = pending.pop(0)
                si = free_sets.pop(0)
                live.append({"g": step(d_, c_, dbs[si]), "d": d_, "c": c_, "idx": idx_, "set": si, "wait": False})
            for it in list(live):
                if it["wait"] and done[it["d"]] < it["idx"]:
                    continue
                try:
                    r_ = next(it["g"])
                    if r_ == "CHAIN":
                        it["wait"] = True
                except StopIteration:
                    live.remove(it)
                    free_sets.append(it["set"])
                    done[it["d"]] += 1
                    ndone_chunk[it["c"]] += 1
                    if ndone_chunk[it["c"]] == 2:
                        finalize_chunk(it["c"])
        dump_dbg(0, ymT)
        outproj_acc(l, 0, ymT)
        A.release(m_r)

    def mlp(l):
        m_m = A.mark()
        mT = A.alloc(8 * S, BF16).rearrange("p (k t) -> p k t", k=8)
        w2b = [A.alloc(8 * 512, BF16).rearrange("p (k c) -> p k c", k=8) for _ in range(2)]
        rl = A.alloc(512, BF16)
        for hg in range(4):
            for half in range(2):
                wb = load_w_cols(w1_d[l], hg * 1024 + half * 512, 512)
                for j in range(4):
                    hpt = half * 4 + j
                    for tc in range(4):
                        tsl = slice(tc * 512, (tc + 1) * 512)
                        pb = gbank()
                        for k in range(8):
                            mm(pb, wb[:, k, j * 128:(j + 1) * 128], xnT[:, k, tsl], start=(k == 0), stop=(k == 7))
                        act(rl, pb, AF.Relu)
                        tt(mT[:, hpt, tsl], rl, rl, ALU.mult, eng="pool")
            for half in range(2):
                dma("pool", w2b[half],
                    w2_d[l, hg * 1024:(hg + 1) * 1024, half * 512:(half + 1) * 512].rearrange("(k p) c -> p k c", p=128))
            for ot in range(8):
                for tc in range(4):
                    tsl = slice(tc * 512, (tc + 1) * 512)
                    pb = gbank()
                    for k in range(8):
                        mm(pb, w2b[ot // 4][:, k, (ot % 4) * 128:(ot % 4 + 1) * 128], mT[:, k, tsl], start=(k == 0),
                           stop=(k == 7))
                    tt(hF[:, ot, tsl], pb, hF[:, ot, tsl], ALU.add)
        A.release(m_m)

    mixers = (dbg or {}).get("mixers", "srna") if dbg else "srna"
    for l in range(n_layers):
        dma("sp", rbt, rb_d[l])
        rmsnorm_to_xnT(lambda f, l=l: P_(l, "n1", f))
        if "s" in mixers:
            ssd_mixer(l)
        if "r" in mixers:
            lru_mixer(l)
        if "n" in mixers:
            na_mixer(l)
        if "a" in mixers:
            rwkv_mixer(l)
        if not (dbg and dbg.get("no_mlp")):
            rmsnorm_to_xnT(lambda f, l=l: P_(l, "n2", f))
            mlp(l)

    m0 = A.mark()
    sq = A.alloc(8 * 512, BF16).rearrange("p (f t) -> p f t", f=8)
    rstd = A.alloc(512, F32)
    yn = A.alloc(8 * 512, F32).rearrange("p (f t) -> p f t", f=8)
    ost = [A.alloc(D, F32) for _ in range(2)]
    fo = PP_L * L_ALL
    for tc in range(4):
        tsl = slice(tc * 512, (tc + 1) * 512)
        for f in range(8):
            act(sq[:, f, :], hF[:, f, tsl], AF.Square)
        pb = gbank()
        for f in range(8):
            mm(pb, onesb, sq[:, f, :], start=(f == 0), stop=(f == 7))
        act(rstd, pb, AF.Sqrt, scale=1.0 / D, bias=1e-5)
        recip(rstd, rstd)
        for f in range(8):
            stt(yn[:, f, :], hF[:, f, tsl], pp[:, fo + f:fo + f + 1], rstd, ALU.mult, ALU.mult)
        for ti in range(4):
            o = ost[ti % 2]
            for half in range(2):
                pb2 = gbank()
                for j in range(4):
                    f = half * 4 + j
                    tr(pb2[:, j * 128:(j + 1) * 128], yn[:, f, ti * 128:(ti + 1) * 128], identf)
                cp(o[:, half * 512:(half + 1) * 512], pb2, "act" if half else "dve")
            tok0 = tc * 512 + ti * 128
            dma("sp", out_d[tok0:tok0 + 128, :], o)
    A.release(m0)

    Sc.finish()
    Sc.emit(stack)
    stack.close()
    return nc, Sc, A


_CACHE = {}


def kernel(**inputs):
    inp = {k: np.asarray(v) for k, v in inputs.items()}
    pp, rb, lbd, t2 = build_host_params(inp)
    if "nc" not in _CACHE:
        _CACHE["nc"] = build_program(L_ALL)[0]
    nc = _CACHE["nc"]
    shared = {
        "w_in": np.ascontiguousarray(inp["w_in"], np.float32),
        "w_out": np.ascontiguousarray(inp["w_out"], np.float32),
        "w_mlp1": np.ascontiguousarray(inp["w_mlp1"], np.float32),
        "w_mlp2": np.ascontiguousarray(inp["w_mlp2"], np.float32),
        "rwkv_w_up": np.ascontiguousarray(inp["rwkv_w_up"], np.float32),
        "rwkv_a_up": np.ascontiguousarray(inp["rwkv_a_up"], np.float32),
        "rwkv_g_up": np.ascontiguousarray(inp["rwkv_g_up"], np.float32),
        "pp": pp, "rb": rb, "lbd": lbd, "t2": t2.reshape(L_ALL, 128, 14 * 256), "lvl": build_level_masks(),
    }
    x = np.ascontiguousarray(inp["x"], np.float32)
    in_maps = [dict(shared, x=x[b]) for b in range(8)]
    res = run_bass_kernel_spmd(nc, in_maps, core_ids=list(range(8)))
    return np.stack([np.asarray(r["out"], np.float32) for r in res.results], axis=0)
```

```python
import numpy as np
from contextlib import ExitStack
import concourse.bass as bass
import concourse.mybir as mybir
from concourse.bass_utils import run_bass_kernel_spmd

F32 = mybir.dt.float32
BF16 = mybir.dt.bfloat16
AF = mybir.ActivationFunctionType
ALU = mybir.AluOpType
AX = mybir.AxisListType
DSZ = {F32: 4, BF16: 2}

L_ALL = 4
D = 1024
S = 2048
NT = 16
IN_COLS = 3464
RW0, SSD0, LRU0, NA0 = 0, 1152, 2184, 2696
CELL = 32
NSLOT = 8
ARENA_F32 = 53000
DECAY = 0.6065306597126334
NEG = -30000.0

PP_SPEC = [("n1", 8), ("n2", 8), ("mu0", 9), ("mu1", 9), ("w0", 4), ("a0", 4), ("kk", 2), ("ka", 2),
           ("rk", 2), ("scw", 24), ("scb", 6), ("lcw", 8), ("lcb", 2), ("lab", 4), ("lxb", 4), ("lam", 4)]
PP_OFF = {}
_o = 0
for _n, _k in PP_SPEC:
    PP_OFF[_n] = (_o, _k)
    _o += _k
PP_L = _o
PP_TOT = PP_L * L_ALL + 8
RB_SPEC = [("gnw", 256), ("gnb", 256), ("dtb", 8), ("alog", 8), ("dsk", 4), ("snw", 256)]
RB_OFF = {}
_o = 0
for _n, _k in RB_SPEC:
    RB_OFF[_n] = (_o, _k)
    _o += _k
RB_L = _o


def _colvec(a):
    a = np.asarray(a, np.float32)
    n = a.shape[-1] // 128
    b = a.reshape(a.shape[:-1] + (n, 128))
    return np.moveaxis(b, -1, 0)


def build_host_params(inp):
    pp = np.zeros((128, PP_TOT), np.float32)
    rb = np.zeros((L_ALL, 128, RB_L), np.float32)
    for l in range(L_ALL):
        def put(name, arr):
            o, k = PP_OFF[name]
            pp[:, l * PP_L + o: l * PP_L + o + k] = arr.reshape(128, k)
        put("n1", _colvec(inp["norm1_w"][l]))
        put("n2", _colvec(inp["norm2_w"][l]))
        put("mu0", _colvec(inp["rwkv_shift_mu"][l, 0]))
        put("mu1", _colvec(inp["rwkv_shift_mu"][l, 1]))
        put("w0", _colvec(inp["rwkv_w0"][l]))
        put("a0", _colvec(inp["rwkv_a0"][l]))
        put("kk", _colvec(inp["rwkv_k_k"][l]))
        put("ka", _colvec(inp["rwkv_k_a"][l]))
        put("rk", _colvec(inp["rwkv_r_k"][l].reshape(256)))
        put("scw", _colvec(inp["ssd_conv_w"][l]))
        put("scb", _colvec(inp["ssd_conv_b"][l]))
        put("lcw", _colvec(inp["lru_conv_w"][l]))
        put("lcb", _colvec(inp["lru_conv_b"][l]))
        put("lab", _colvec(inp["lru_gate_a_b"][l]))
        put("lxb", _colvec(inp["lru_gate_x_b"][l]))
        put("lam", _colvec(inp["lru_lambda"][l]))

        def putr(name, vec):
            o, k = RB_OFF[name]
            rb[l, :, o:o + k] = np.broadcast_to(np.asarray(vec, np.float32).reshape(1, k), (128, k))
        putr("gnw", inp["rwkv_gn_w"][l])
        putr("gnb", inp["rwkv_gn_b"][l])
        putr("dtb", inp["ssd_dt_bias"][l].reshape(8))
        putr("alog", inp["ssd_a_log"][l].reshape(8))
        putr("dsk", inp["ssd_d"][l])
        putr("snw", inp["ssd_norm_w"][l])
    pp[:, PP_L * L_ALL:] = _colvec(inp["final_norm_w"])
    lbd = np.zeros((L_ALL, 2, 2, 2, 128, 128), np.float32)
    for wi, nm in enumerate(("lru_gate_a_w", "lru_gate_x_w")):
        w = np.asarray(inp[nm], np.float32)
        for pt in range(2):
            for j in range(2):
                lbd[:, wi, :, pt, j * 64:(j + 1) * 64, j * 64:(j + 1) * 64] = w[:, :, 2 * pt + j]
    rbias = np.asarray(inp["na_rel_bias"], np.float32)
    qc = np.arange(64)
    kc = np.arange(64)
    ws = np.clip(qc - 8, 0, 48)
    inwin = (kc[:, None] >= ws[None, :]) & (kc[:, None] < ws[None, :] + 16)
    ci = np.clip(kc[:, None] - qc[None, :] + 15, 0, 30)
    t2 = np.full((L_ALL, 128, 14, 4, 64), NEG, np.float32)
    for kh in range(2):
        for dr in range(14):
            g = rbias[:, :, dr + kh, :][:, :, ci]
            g = np.where(inwin[None, None], g, np.float32(NEG))
            t2[:, kh * 64:(kh + 1) * 64, dr, :, :] = np.transpose(g, (0, 2, 1, 3))
    return pp, rb, lbd, t2


def build_level_masks():
    t = np.arange(128)
    out = np.zeros((128, 14, 128), np.float32)
    for k in range(7):
        ll = (((t[:, None] >> (k + 1)) == (t[None, :] >> (k + 1))) & (((t[:, None] >> k) & 1) == 1)
              & (((t[None, :] >> k) & 1) == 0))
        out[:, k, :] = ll
        out[:, 7 + k, :] = ll.T
    return out.reshape(128, 14 * 128)


class Sched:
    ISSUERS = ["pe", "act", "dve", "pool", "sp"]

    def __init__(self, nc, arena_name, psum_name):
        self.nc = nc
        self.semnames = ["pe", "act", "dve", "pool"]
        for q in ("sp", "act", "pool"):
            for i in range(NSLOT):
                self.semnames.append(f"{q}q{i}")
        self.sidx = {n: i for i, n in enumerate(self.semnames)}
        ns = len(self.semnames)
        self.ns = ns
        self.spaces = {arena_name: 0, psum_name: 1}
        ncell = [ARENA_F32 * 4 // CELL + 2, 16384 // CELL + 2]
        self.W = [np.zeros((4, c, ns), np.int64) for c in ncell]
        self.R = [np.zeros((4, c, ns), np.int64) for c in ncell]
        self.cnt = {e: 0 for e in ["pe", "act", "dve", "pool"]}
        self.dman = {"sp": 0, "act": 0, "pool": 0}
        self.seen = {e: np.zeros(ns, np.int64) for e in self.ISSUERS}
        self.q = {e: [] for e in self.ISSUERS}
        self.nops = 0
        self.limit = None
        self.trace = []

    def region(self, ap):
        sp = self.spaces.get(ap.tensor.name)
        if sp is None:
            return None
        a = ap.ap
        ds = DSZ[ap.dtype]
        rowlen = a[0][0]
        pcount = a[0][1]
        p0 = ap.offset // rowlen
        c0 = ap.offset % rowlen
        lo = hi = c0
        for step, cnt in a[1:]:
            ext = step * (cnt - 1)
            if ext < 0:
                lo += ext
            else:
                hi += ext
        blo, bhi = lo * ds, (hi + 1) * ds
        if sp == 1:
            blo = blo // 2048 * 2048
            bhi = (bhi + 2047) // 2048 * 2048
        return (sp, p0 // 32, (p0 + pcount + 31) // 32, blo // CELL, (bhi + CELL - 1) // CELL)

    def _need(self, r, w):
        need = np.zeros(self.ns, np.int64)
        rr = [x for x in (self.region(a) for a in r) if x is not None]
        wr = [x for x in (self.region(a) for a in w) if x is not None]
        for (sp, p0, p1, c0, c1) in rr:
            need = np.maximum(need, self.W[sp][p0:p1, c0:c1].max(axis=(0, 1)))
        for (sp, p0, p1, c0, c1) in wr:
            need = np.maximum(need, self.W[sp][p0:p1, c0:c1].max(axis=(0, 1)))
            need = np.maximum(need, self.R[sp][p0:p1, c0:c1].max(axis=(0, 1)))
        return need, rr, wr

    def _mark(self, rr, wr, si, val):
        for (sp, p0, p1, c0, c1) in wr:
            self.W[sp][p0:p1, c0:c1, si] = val
        for (sp, p0, p1, c0, c1) in rr:
            self.R[sp][p0:p1, c0:c1, si] = val

    def op(self, eng, fn, r=(), w=()):
        if self.limit is not None and self.nops >= self.limit:
            return
        need, rr, wr = self._need(r, w)
        if eng == "pe":
            need[self.sidx["pe"]] = 0
        seen = self.seen[eng]
        waits = [(i, int(need[i])) for i in np.nonzero(need > seen)[0]]
        np.maximum(seen, need, out=seen)
        self.cnt[eng] += 1
        if self.limit is not None:
            import sys as _s
            f = _s._getframe(1)
            self.trace.append((self.nops, eng, f.f_code.co_name, [(a.tensor.name, a.offset, a.ap) for a in list(w) + list(r)]))
        self.q[eng].append((waits, fn, self.sidx[eng], 1))
        self._mark(rr, wr, self.sidx[eng], self.cnt[eng])
        self.nops += 1

    def dma(self, queue, out, in_):
        if self.limit is not None and self.nops >= self.limit:
            return
        need, rr, wr = self._need([in_], [out])
        n = self.dman[queue]
        self.dman[queue] = n + 1
        slot, rnd = n % NSLOT, n // NSLOT
        si = self.sidx[f"{queue}q{slot}"]
        need[si] = max(need[si], 16 * rnd)
        seen = self.seen[queue]
        waits = [(i, int(need[i])) for i in np.nonzero(need > seen)[0]]
        np.maximum(seen, need, out=seen)
        fn = (lambda e, out=out, in_=in_: e.dma_start(out=out, in_=in_))
        self.q[queue].append((waits, fn, si, 16))
        self._mark(rr, wr, si, 16 * (rnd + 1))
        self.nops += 1

    def finish(self):
        final = np.zeros(self.ns, np.int64)
        for e in ["pe", "act", "dve", "pool"]:
            final[self.sidx[e]] = self.cnt[e]
        for qn in ("sp", "act", "pool"):
            n = self.dman[qn]
            for s in range(NSLOT):
                k = (n - s + NSLOT - 1) // NSLOT if n > s else 0
                final[self.sidx[f"{qn}q{s}"]] = 16 * k
        self.final = final

    def emit(self, stack):
        nc = self.nc
        sems = [stack.enter_context(nc.semaphore(f"s_{n}")) for n in self.semnames]
        block = stack.enter_context(nc.Block())

        def replay(name, e):
            for waits, fn, si, inc in self.q[name]:
                for (s, v) in waits:
                    e.wait_ge(sems[s], v)
                fn(e).then_inc(sems[si], inc)
            if name == "sp":
                for s in range(self.ns):
                    if self.final[s] > 0:
                        e.wait_ge(sems[s], int(self.final[s]))

        @block.tensor
        def _(e):
            replay("pe", e)

        @block.scalar
        def _(e):
            replay("act", e)

        @block.vector
        def _(e):
            replay("dve", e)

        @block.gpsimd
        def _(e):
            replay("pool", e)

        @block.sync
        def _(e):
            replay("sp", e)


class Arena:
    def __init__(self, base_ap_f32, ncols_f32, start=0):
        self.base = base_ap_f32
        self.lo = start * 4
        self.top = self.lo
        self.limit = (start + ncols_f32) * 4
        self.peak = self.top

    def alloc(self, cols, dtype, parts=128):
        nbytes = cols * DSZ[dtype]
        nbytes = (nbytes + CELL - 1) // CELL * CELL
        off = self.top
        self.top += nbytes
        self.peak = max(self.peak, self.top)
        assert self.top <= self.limit, f"arena overflow {self.top} > {self.limit}"
        ap = self.base[0:parts, off // 4:(off + nbytes) // 4]
        if dtype != F32:
            ap = ap.bitcast(dtype)
        return ap[:, 0:cols]

    def mark(self):
        return self.top

    def release(self, m):
        self.top = m


def build_program(n_layers=L_ALL, dbg=None):
    nc = bass.Bass("TRN2", target_bir_lowering=False)
    dt = nc.dram_tensor
    x_d = dt("x", [S, D], F32, kind="ExternalInput").ap()
    w_in_d = dt("w_in", [L_ALL, D, IN_COLS], F32, kind="ExternalInput").ap()
    w_out_d = dt("w_out", [L_ALL, D, D], F32, kind="ExternalInput").ap()
    w1_d = dt("w_mlp1", [L_ALL, D, 4 * D], F32, kind="ExternalInput").ap()
    w2_d = dt("w_mlp2", [L_ALL, 4 * D, D], F32, kind="ExternalInput").ap()
    wup_d = dt("rwkv_w_up", [L_ALL, 2, 64, 256], F32, kind="ExternalInput").ap()
    aup_d = dt("rwkv_a_up", [L_ALL, 2, 64, 256], F32, kind="ExternalInput").ap()
    gup_d = dt("rwkv_g_up", [L_ALL, 128, 256], F32, kind="ExternalInput").ap()
    pp_d = dt("pp", [128, PP_TOT], F32, kind="ExternalInput").ap()
    rb_d = dt("rb", [L_ALL, 128, RB_L], F32, kind="ExternalInput").ap()
    lbd_d = dt("lbd", [L_ALL, 2, 2, 2, 128, 128], F32, kind="ExternalInput").ap()
    t2_d = dt("t2", [L_ALL, 128, 14 * 256], F32, kind="ExternalInput").ap()
    lvl_d = dt("lvl", [128, 14 * 128], F32, kind="ExternalInput").ap()
    out_d = dt("out", [S, D], F32, kind="ExternalOutput").ap()
    dbg_d = None
    if dbg:
        dbg_d = dt("dbg", [D, S], F32, kind="ExternalOutput").ap()

    stack = ExitStack()
    arena_t = stack.enter_context(nc.sbuf_tensor("arena", [128, ARENA_F32], F32))
    ps_t = stack.enter_context(nc.psum_tensor("ps", [128, 4096], F32))
    arena_ap = arena_t[:, :]
    ps_ap = ps_t[:, :]
    Sc = Sched(nc, "arena", "ps")
    if dbg and dbg.get("limit"):
        Sc.limit = dbg["limit"]
    A = Arena(arena_ap, ARENA_F32)

    def bank(b):
        return ps_ap[:, b * 512:(b + 1) * 512]

    rot = [0]

    def gbank():
        b = rot[0] % 8
        rot[0] += 1
        return bank(b)

    def mm(out, lhsT, rhs, start=True, stop=True, skip=False):
        Sc.op("pe", lambda e: e.matmul(out, lhsT=lhsT, rhs=rhs, start=start, stop=stop, skip_group_check=skip),
              r=[lhsT, rhs], w=[out])

    def tr(out, in_, ident):
        Sc.op("pe", lambda e: e.transpose(out, in_, ident), r=[in_, ident], w=[out])

    def act(out, in_, func, scale=None, bias=None):
        kw = {}
        rd = [in_]
        if scale is not None:
            kw["scale"] = scale
            if not isinstance(scale, (int, float)):
                rd.append(scale)
        if bias is not None:
            kw["bias"] = bias
            if not isinstance(bias, (int, float)):
                rd.append(bias)
        Sc.op("act", lambda e: e.activation(out=out, in_=in_, func=func, **kw), r=rd, w=[out])

    def tt(out, in0, in1, op, eng="dve"):
        Sc.op(eng, lambda e: e.tensor_tensor(out=out, in0=in0, in1=in1, op=op), r=[in0, in1], w=[out])

    def ts(out, in0, s1, op0, s2=None, op1=None, eng="dve"):
        rd = [in0] + [s for s in (s1, s2) if s is not None and not isinstance(s, (int, float))]
        if op1 is None:
            Sc.op(eng, lambda e: e.tensor_scalar(out=out, in0=in0, scalar1=s1, scalar2=None, op0=op0), r=rd, w=[out])
        else:
            Sc.op(eng, lambda e: e.tensor_scalar(out=out, in0=in0, scalar1=s1, scalar2=s2, op0=op0, op1=op1),
                  r=rd, w=[out])

    def stt(out, in0, scalar, in1, op0, op1):
        rd = [in0, in1] + ([] if isinstance(scalar, (int, float)) else [scalar])
        Sc.op("dve", lambda e: e.scalar_tensor_tensor(out=out, in0=in0, scalar=scalar, in1=in1, op0=op0, op1=op1),
              r=rd, w=[out])

    def cp(out, in_, eng="dve"):
        if eng == "act":
            Sc.op("act", lambda e: e.activation(out=out, in_=in_, func=AF.Copy), r=[in_], w=[out])
        else:
            Sc.op(eng, lambda e: e.tensor_copy(out=out, in_=in_), r=[in_], w=[out])

    def memset(out, val, eng="pool"):
        Sc.op(eng, lambda e: e.memset(out, val), r=[], w=[out])

    def recip(out, in_):
        Sc.op("dve", lambda e: e.reciprocal(out=out, in_=in_), r=[in_], w=[out])

    def scan(out, d0, d1, init=0.0):
        Sc.op("dve", lambda e: e.tensor_tensor_scan(out=out, data0=d0, data1=d1, initial=init, op0=ALU.mult,
                                                    op1=ALU.add), r=[d0, d1], w=[out])

    def reduce_sum(out, in_):
        Sc.op("dve", lambda e: e.tensor_reduce(out=out, in_=in_, axis=AX.X, op=ALU.add), r=[in_], w=[out])

    def affsel(out, in_, pattern, cmp, fill, base, cm):
        Sc.op("pool", lambda e: e.affine_select(out=out, in_=in_, pattern=pattern, compare_op=cmp, fill=fill,
                                                base=base, channel_multiplier=cm), r=[in_], w=[out])

    def dma(q, out, in_):
        Sc.dma(q, out, in_)

    hF = A.alloc(8 * S, F32).rearrange("p (f t) -> p f t", f=8)
    xnT = A.alloc(8 * S, BF16).rearrange("p (f t) -> p f t", f=8)
    xn_region_start_f32 = (A.top - 8 * S * 2) // 4
    pp = A.alloc(PP_TOT, F32)
    ppd = A.alloc(L_ALL * 32, F32)
    rbt = A.alloc(RB_L, F32)
    identb = A.alloc(128, BF16)
    identf = A.alloc(128, F32)
    onesb = A.alloc(128, BF16)
    onesf = A.alloc(128, F32)
    blkb = A.alloc(128, BF16)
    headsel = A.alloc(2, BF16)
    mU = A.alloc(128, F32)
    mL = A.alloc(128, F32)
    mUI = A.alloc(128, F32)
    mLI = A.alloc(128, F32)
    nmU = A.alloc(128, F32)
    nmL = A.alloc(128, F32)
    nmUI = A.alloc(128, F32)
    nmLI = A.alloc(128, F32)
    negU4 = A.alloc(512, BF16)
    negL4 = A.alloc(512, BF16)
    lvl = A.alloc(14 * 128, BF16).rearrange("p (k c) -> p k c", k=14)
    ident4 = A.alloc(512, BF16)
    wb_region_start_f32 = A.top // 4
    wbufs = [A.alloc(8 * 512, BF16).rearrange("p (k c) -> p k c", k=8) for _ in range(2)]
    wb_i = [0]

    def next_wbuf():
        b = wbufs[wb_i[0] % 2]
        wb_i[0] += 1
        return b

    memset(identf, 0.0)
    affsel(identf, identf, [[-1, 128]], ALU.not_equal, 1.0, 0, 1)
    cp(identb, identf, "pool")
    memset(onesf, 1.0)
    memset(onesb, 1.0)
    for h in range(4):
        cp(ident4[:, h * 128:(h + 1) * 128], identf, "pool")
    dma("pool", lvl, lvl_d.rearrange("p (k c) -> p k c", k=14))
    affsel(mU, onesf, [[1, 128]], ALU.is_gt, 0.0, 0, -1)
    affsel(mUI, onesf, [[1, 128]], ALU.is_ge, 0.0, 0, -1)
    affsel(mL, onesf, [[-1, 128]], ALU.is_gt, 0.0, 0, 1)
    affsel(mLI, onesf, [[-1, 128]], ALU.is_ge, 0.0, 0, 1)
    for src, dst in ((mU, nmU), (mL, nmL), (mUI, nmUI), (mLI, nmLI)):
        ts(dst, src, -1.0, ALU.mult, eng="pool")
    for h in range(4):
        ts(negU4[:, h * 128:(h + 1) * 128], mU, NEG, ALU.mult, eng="pool")
        ts(negL4[:, h * 128:(h + 1) * 128], mL, NEG, ALU.mult, eng="pool")
    memset(blkb, 0.0)
    memset(blkb[0:64, 0:64], 1.0)
    memset(blkb[64:128, 64:128], 1.0)
    memset(headsel, 0.0)
    memset(headsel[0:64, 0:1], 1.0)
    memset(headsel[64:128, 1:2], 1.0)
    dma("sp", pp, pp_d)
    for l in range(n_layers):
        b0 = l * PP_L
        d0 = l * 32
        o_mu0, o_mu1 = PP_OFF["mu0"][0] + b0, PP_OFF["mu1"][0] + b0
        tt(ppd[:, d0:d0 + 9], pp[:, o_mu0:o_mu0 + 9], pp[:, o_mu1:o_mu1 + 9], ALU.add)
        ts(ppd[:, d0:d0 + 9], ppd[:, d0:d0 + 9], -1.0, ALU.mult, 1.0, ALU.add)
        o_ka = PP_OFF["ka"][0] + b0
        ts(ppd[:, d0 + 9:d0 + 11], pp[:, o_ka:o_ka + 2], -1.0, ALU.mult, 1.0, ALU.add)
        o_lam = PP_OFF["lam"][0] + b0
        act(ppd[:, d0 + 11:d0 + 15], pp[:, o_lam:o_lam + 4], AF.Exp, scale=-1.0)
        act(ppd[:, d0 + 11:d0 + 15], ppd[:, d0 + 11:d0 + 15], AF.Ln, bias=1.0)
        ts(ppd[:, d0 + 15:d0 + 19], ppd[:, d0 + 11:d0 + 15], -16.0, ALU.mult)
        ts(ppd[:, d0 + 11:d0 + 15], ppd[:, d0 + 11:d0 + 15], -8.0, ALU.mult)
        o_w0, o_a0 = PP_OFF["w0"][0] + b0, PP_OFF["a0"][0] + b0
        ts(ppd[:, d0 + 19:d0 + 23], pp[:, o_w0:o_w0 + 4], -1.0, ALU.mult)
        ts(ppd[:, d0 + 23:d0 + 27], pp[:, o_a0:o_a0 + 4], -1.0, ALU.mult)

    def P_(l, name, j=0, n=1):
        o, k = PP_OFF[name]
        return pp[:, l * PP_L + o + j: l * PP_L + o + j + n]

    m0 = A.mark()
    xst = [A.alloc(D, F32) for _ in range(2)]
    for tt_i in range(NT):
        xs_ = xst[tt_i % 2]
        dma("sp", xs_, x_d[tt_i * 128:(tt_i + 1) * 128, :])
        for half in range(2):
            pb = gbank()
            for j in range(4):
                f = half * 4 + j
                tr(pb[:, j * 128:(j + 1) * 128], xs_[:, f * 128:(f + 1) * 128], identf)
            cp(hF[:, half * 4:half * 4 + 4, tt_i * 128:(tt_i + 1) * 128],
               pb.rearrange("p (j t) -> p j t", j=4), "act" if half else "dve")
    A.release(m0)

    def rmsnorm_to_xnT(nw_of_f):
        m = A.mark()
        sq = A.alloc(8 * 512, BF16).rearrange("p (f t) -> p f t", f=8)
        rstd = A.alloc(512, F32)
        for tc in range(4):
            tsl = slice(tc * 512, (tc + 1) * 512)
            for f in range(8):
                act(sq[:, f, :], hF[:, f, tsl], AF.Square)
            pb = gbank()
            for f in range(8):
                mm(pb, onesb, sq[:, f, :], start=(f == 0), stop=(f == 7))
            act(rstd, pb, AF.Sqrt, scale=1.0 / D, bias=1e-5)
            recip(rstd, rstd)
            for f in range(8):
                stt(xnT[:, f, tsl], hF[:, f, tsl], nw_of_f(f), rstd, ALU.mult, ALU.mult)
        A.release(m)

    def load_w_cols(src2d, c0, ncols, q="pool"):
        wb = next_wbuf()
        dma(q, wb[:, :, 0:ncols], src2d.rearrange("(k p) c -> p k c", p=128)[:, :, c0:c0 + ncols])
        return wb

    def proj_fm(wb, cw, ncw, evac):
        for tc in range(4):
            pb = gbank()
            for k in range(8):
                mm(pb[0:ncw, :], wb[:, k, cw:cw + ncw], xnT[:, k, tc * 512:(tc + 1) * 512], start=(k == 0),
                   stop=(k == 7))
            evac(tc, pb[0:ncw, :])

    def proj_tm(wb, cw, ncw, tile_i, tok0, out_psum):
        for k in range(8):
            mm(out_psum, xnT[:, k, tok0:tok0 + 128], wb[:, k, cw:cw + ncw], start=(k == 0), stop=(k == 7))

    def outproj_acc(l, g, ymT):
        m = A.mark()
        wo = A.alloc(2 * 1024, BF16).rearrange("p (k c) -> p k c", k=2)
        dma("pool", wo, w_out_d[l, g * 256:(g + 1) * 256, :].rearrange("(k p) c -> p k c", p=128))
        for ot in range(8):
            for tc in range(4):
                tsl = slice(tc * 512, (tc + 1) * 512)
                pb = gbank()
                for k in range(2):
                    mm(pb, wo[:, k, ot * 128:(ot + 1) * 128], ymT[:, k, tsl], start=(k == 0), stop=(k == 1))
                tt(hF[:, ot, tsl], pb, hF[:, ot, tsl], ALU.add)
        A.release(m)

    def dump_dbg(g, ymT):
        if dbg_d is None or dbg.get('nodump'):
            return
        m = A.mark()
        st = A.alloc(S, F32)
        for k in range(2):
            cp(st, ymT[:, k, :], "pool")
            dma("sp", dbg_d[g * 256 + k * 128: g * 256 + (k + 1) * 128, :], st)
        A.release(m)

    def conv4(stage, wcol, bias_ap, out, func=None, out2=None):
        m = A.mark()
        acc = A.alloc(S, F32)
        ts(acc, stage[:, 0:S], wcol(0), ALU.mult)
        for k in range(1, 4):
            stt(acc, stage[:, k:k + S], wcol(k), acc, ALU.mult, ALU.add)
        if func is None:
            ts(out, acc, bias_ap, ALU.add)
            if out2 is not None:
                cp(out2, out, "pool")
        else:
            act(out, acc, func, bias=bias_ap)
        A.release(m)

    def ssd_mixer(l):
        m_ssd = A.mark()
        xs_tok = A.alloc(NT * 256, BF16).rearrange("p (i c) -> p i c", i=NT)
        B_tok = A.alloc(NT * 256, BF16).rearrange("p (i c) -> p i c", i=NT)
        BT = A.alloc(2 * S, BF16).rearrange("p (k t) -> p k t", k=2)
        CT = A.alloc(2 * S, BF16).rearrange("p (k t) -> p k t", k=2)
        o_alog, o_dtb = RB_OFF["alog"][0], RB_OFF["dtb"][0]
        m1 = A.mark()
        stage = A.alloc(S + 3, F32)
        xsT = A.alloc(2 * S, BF16).rearrange("p (k t) -> p k t", k=2)
        memset(stage[:, 0:2], 0.0)
        memset(stage[:, S + 2:S + 3], 0.0)
        for half in range(2):
            wb = load_w_cols(w_in_d[l], SSD0 + 256 + half * 384, 384)
            for j in range(3):
                c = half * 3 + j
                proj_fm(wb, j * 128, 128, lambda tc, pb: cp(stage[:, 2 + tc * 512: 2 + (tc + 1) * 512], pb, "act"))
                dest = (xsT[:, c, :] if c < 2 else (BT[:, c - 2, :] if c < 4 else CT[:, c - 4, :]))
                conv4(stage, lambda k, c=c: P_(l, "scw", k * 6 + c), P_(l, "scb", c), dest, func=AF.Silu)
        for i in range(NT):
            pbb = gbank().bitcast(BF16)
            for k in range(2):
                tr(pbb[:, k * 128:(k + 1) * 128], xsT[:, k, i * 128:(i + 1) * 128], identb)
                tr(pbb[:, 256 + k * 128:256 + (k + 1) * 128], BT[:, k, i * 128:(i + 1) * 128], identb)
            cp(xs_tok[:, i, :], pbb[:, 0:256], "act")
            cp(B_tok[:, i, :], pbb[:, 256:512], "act")
        A.release(m1)
        z_tok = A.alloc(NT * 256, BF16).rearrange("p (i c) -> p i c", i=NT)
        y_tok = A.alloc(NT * 256, F32).rearrange("p (i c) -> p i c", i=NT)
        dt_tok = A.alloc(NT * 8, F32).rearrange("p (i c) -> p i c", i=NT)
        adt = A.alloc(NT * 8, F32).rearrange("p (i c) -> p i c", i=NT)
        Abc = A.alloc(8, F32)
        act(Abc, rbt[:, o_alog:o_alog + 8], AF.Exp)
        ts(Abc, Abc, -1.0, ALU.mult)
        wz = load_w_cols(w_in_d[l], SSD0, 256)
        for i in range(NT):
            pb = gbank()
            proj_tm(wz, 0, 256, i, i * 128, pb[:, 0:256])
            act(z_tok[:, i, :], pb[:, 0:256], AF.Silu)
        wdt = load_w_cols(w_in_d[l], SSD0 + 1024, 8)
        for i in range(NT):
            pb = gbank()
            proj_tm(wdt, 0, 8, i, i * 128, pb[:, 0:8])
            tt(dt_tok[:, i, :], pb[:, 0:8], rbt[:, o_dtb:o_dtb + 8], ALU.add)
        dtf = dt_tok.rearrange("p i c -> p (i c)")
        act(dtf, dtf, AF.Exp)
        act(dtf, dtf, AF.Ln, bias=1.0)
        tt(adt, dt_tok, Abc.unsqueeze(1).to_broadcast([128, NT, 8]), ALU.mult)
        m2 = A.mark()
        Hs = [A.alloc(256, F32) for _ in range(2)]
        Hb = [A.alloc(256, BF16) for _ in range(2)]
        for d in range(2):
            memset(Hs[d], 0.0)
            memset(Hb[d], 0.0)
        class _T:
            pass
        tmps = []
        for d in range(2):
            T_ = _T()
            T_.rhs_seg = A.alloc(512, F32)
            T_.dec = A.alloc(512, BF16)
            T_.scT = A.alloc(512, BF16)
            T_.xdt = A.alloc(256, BF16)
            T_.xde = A.alloc(256, BF16)
            T_.ecs = A.alloc(8, F32)
            T_.dte = A.alloc(8, F32)
            T_.etot = A.alloc(8, F32)
            tmps.append(T_)

        def ssd_step(d, c, first):
            T_ = tmps[d]
            rhs_seg, dec, scT, xdt, xde, ecs, dte, etot = (T_.rhs_seg, T_.dec, T_.scT, T_.xdt, T_.xde, T_.ecs,
                                                           T_.dte, T_.etot)
            csl = slice(c * 128, (c + 1) * 128)
            a4 = adt[:, c, d * 4:(d + 1) * 4]
            msk = mUI if d == 0 else mLI
            pS = gbank()
            mm(pS[:, 0:4], msk, a4)
            mm(pS[:, 8:12], onesf, a4)
            act(ecs[:, 0:4], pS[:, 0:4], AF.Exp)
            cp(dte[:, 4:8], pS[:, 0:4], "dve")
            tt(dte[:, 0:4], pS[:, 8:12], dte[:, 4:8], ALU.subtract)
            act(dte[:, 0:4], dte[:, 0:4], AF.Exp)
            act(etot[:, 0:4], pS[:, 8:12], AF.Exp)
            tt(rhs_seg.rearrange("p (h l) -> p h l", h=4),
               msk.unsqueeze(1).to_broadcast([128, 4, 128]),
               a4.unsqueeze(2).to_broadcast([128, 4, 128]), ALU.mult)
            dt4 = dt_tok[:, c, d * 4:(d + 1) * 4]
            tt(xdt.rearrange("p (h q) -> p h q", h=4), xs_tok[:, c, :].rearrange("p (h q) -> p h q", h=4),
               dt4.unsqueeze(2).to_broadcast([128, 4, 64]), ALU.mult)
            yield
            pD = gbank()
            mm(pD, (mL if d == 0 else mU), rhs_seg, start=True, stop=False)
            mm(pD, identb, (negL4 if d == 0 else negU4), start=False, stop=True)
            act(dec, pD, AF.Exp)
            tt(xde.rearrange("p (h q) -> p h q", h=4), xdt.rearrange("p (h q) -> p h q", h=4),
               dte[:, 0:4].unsqueeze(2).to_broadcast([128, 4, 64]), ALU.mult)
            yield
            pG = gbank()
            for g in range(2):
                mm(pG[:, g * 128:(g + 1) * 128], BT[:, g, csl], CT[:, g, csl])
            tt(scT.rearrange("p (g r l) -> p g r l", g=2, r=2),
               dec.rearrange("p (g r l) -> p g r l", g=2, r=2),
               pG[:, 0:256].rearrange("p (g l) -> p g l", g=2).unsqueeze(2).to_broadcast([128, 2, 2, 128]),
               ALU.mult)
            yield
            pY = gbank()
            for h in range(4):
                mm(pY[:, h * 64:(h + 1) * 64], scT[:, h * 128:(h + 1) * 128], xdt[:, h * 64:(h + 1) * 64])
                mm(pY[:, 256 + h * 64:256 + (h + 1) * 64], CT[:, h // 2, csl], Hb[d][:, h * 64:(h + 1) * 64])
            pH = gbank()
            for h in range(4):
                mm(pH[:, h * 64:(h + 1) * 64], B_tok[:, c, (h // 2) * 128:(h // 2 + 1) * 128],
                   xde[:, h * 64:(h + 1) * 64])
            if first:
                cp(y_tok[:, c, :], pY[:, 0:256], "act")
            else:
                tt(y_tok[:, c, :], pY[:, 0:256], y_tok[:, c, :], ALU.add)
            for h in range(4):
                stt(y_tok[:, c, h * 64:(h + 1) * 64], pY[:, 256 + h * 64:256 + (h + 1) * 64], ecs[:, h:h + 1],
                    y_tok[:, c, h * 64:(h + 1) * 64], ALU.mult, ALU.add)
            for h in range(4):
                stt(Hs[d][:, h * 64:(h + 1) * 64], Hs[d][:, h * 64:(h + 1) * 64], etot[:, h:h + 1],
                    pH[:, h * 64:(h + 1) * 64], ALU.mult, ALU.add)
            cp(Hb[d], Hs[d], "act")
            yield

        for st_i in range(NT):
            first = st_i < NT // 2
            gens = [ssd_step(0, st_i, first), ssd_step(1, NT - 1 - st_i, first)]
            alive = True
            while alive:
                alive = False
                for g_ in gens:
                    try:
                        next(g_)
                        alive = True
                    except StopIteration:
                        pass
        A.release(m2)
        ymT = A.alloc(2 * S, BF16).rearrange("p (k t) -> p k t", k=2)
        m3 = A.mark()
        o_dsk, o_snw = RB_OFF["dsk"][0], RB_OFF["snw"][0]
        tmp = A.alloc(256, F32)
        tmpb = A.alloc(256, BF16)
        ss = A.alloc(2, F32)
        for i in range(NT):
            tt(tmp.rearrange("p (h q) -> p h q", h=4), xs_tok[:, i, :].rearrange("p (h q) -> p h q", h=4),
               rbt[:, o_dsk:o_dsk + 4].unsqueeze(2).to_broadcast([128, 4, 64]), ALU.mult)
            tt(tmp, tmp, y_tok[:, i, :], ALU.add)
            tt(tmp, tmp, z_tok[:, i, :], ALU.mult)
            tt(y_tok[:, i, :], tmp, tmp, ALU.mult)
            reduce_sum(ss[:, 0:1], y_tok[:, i, :])
            act(ss[:, 1:2], ss[:, 0:1], AF.Sqrt, scale=1.0 / 256, bias=1e-5)
            recip(ss[:, 1:2], ss[:, 1:2])
            stt(tmpb, tmp, ss[:, 1:2], rbt[:, o_snw:o_snw + 256], ALU.mult, ALU.mult)
            pbb = gbank().bitcast(BF16)
            for k in range(2):
                tr(pbb[:, k * 128:(k + 1) * 128], tmpb[:, k * 128:(k + 1) * 128], identb)
            cp(ymT[:, :, i * 128:(i + 1) * 128], pbb[:, 0:256].rearrange("p (k t) -> p k t", k=2), "act")
        A.release(m3)
        dump_dbg(1, ymT)
        outproj_acc(l, 1, ymT)
        A.release(m_ssd)

    def lru_mixer(l):
        m_l = A.mark()
        ymT = A.alloc(2 * S, BF16).rearrange("p (k t) -> p k t", k=2)
        gate = A.alloc(2 * S, BF16).rearrange("p (k t) -> p k t", k=2)
        xf = A.alloc(2 * S, F32).rearrange("p (k t) -> p k t", k=2)
        xfb = A.alloc(2 * S, BF16).rearrange("p (k t) -> p k t", k=2)
        m_st = A.mark()
        stage = A.alloc(S + 3, F32)
        memset(stage[:, 0:2], 0.0)
        memset(stage[:, S + 2:S + 3], 0.0)
        wb = load_w_cols(w_in_d[l], LRU0, 512)
        for c in range(2):
            proj_fm(wb, c * 128, 128,
                    lambda tc, pb, c=c: act(gate[:, c, tc * 512:(tc + 1) * 512], pb, AF.Gelu_apprx_tanh))
        for c in range(2):
            proj_fm(wb, 256 + c * 128, 128,
                    lambda tc, pb: cp(stage[:, 2 + tc * 512: 2 + (tc + 1) * 512], pb, "act"))
            conv4(stage, lambda k, c=c: P_(l, "lcw", k * 2 + c), P_(l, "lcb", c), xf[:, c, :], out2=xfb[:, c, :])
        A.release(m_st)
        m_tmp = A.mark()
        wg = A.alloc(8 * 128, BF16).rearrange("p (a c) -> p a c", a=8)
        dma("pool", wg, lbd_d[l].rearrange("w d t p c -> p (w d t) c"))
        rec = A.alloc(S, F32)
        inp = A.alloc(S, F32)
        aa = A.alloc(S, F32)
        hsum = A.alloc(S, F32)
        for pt in range(2):
            for d in range(2):
                for which, dst, bname in ((0, rec, "lab"), (1, inp, "lxb")):
                    for tc in range(4):
                        tsl = slice(tc * 512, (tc + 1) * 512)
                        pb = gbank()
                        mm(pb, wg[:, which * 4 + d * 2 + pt, :], xfb[:, pt, tsl])
                        act(dst[:, tsl], pb, AF.Sigmoid, bias=P_(l, bname, d * 2 + pt))
                lc = ppd[:, l * 32 + 11 + d * 2 + pt: l * 32 + 12 + d * 2 + pt]
                lc2 = ppd[:, l * 32 + 15 + d * 2 + pt: l * 32 + 16 + d * 2 + pt]
                act(aa, rec, AF.Exp, scale=lc)
                act(rec, rec, AF.Exp, scale=lc2)
                act(rec, rec, AF.Sqrt, scale=-1.0, bias=1.0)
                tt(inp, inp, rec, ALU.mult)
                tt(inp, inp, xf[:, pt, :], ALU.mult)
                if d == 0:
                    scan(hsum, aa, inp)
                else:
                    scan(rec[:, ::-1], aa[:, ::-1], inp[:, ::-1])
                    tt(hsum, hsum, rec, ALU.add)
                    tt(ymT[:, pt, :], hsum, gate[:, pt, :], ALU.mult)
        A.release(m_tmp)
        dump_dbg(2, ymT)
        outproj_acc(l, 2, ymT)
        A.release(m_l)

    def na_mixer(l):
        m_n = A.mark()
        ymT = A.alloc(2 * S, BF16).rearrange("p (k t) -> p k t", k=2)
        m_na_tmp = A.mark()
        qm = [A.alloc(S, BF16) for _ in range(4)]
        for h in range(4):
            z0 = 64 - (h % 2) * 64
            memset(qm[h][z0:z0 + 64, :], 0.0)
        kT = A.alloc(2 * S, BF16).rearrange("p (k t) -> p k t", k=2)
        va = [A.alloc(NT * 260, BF16).rearrange("p (i h q) -> p i h q", i=NT, h=4) for _ in range(2)]
        t2 = A.alloc(14 * 256, F32).rearrange("p (r x) -> p r x", r=14)
        dma("sp", t2, t2_d[l].rearrange("p (r x) -> p r x", r=14))
        wb = load_w_cols(w_in_d[l], NA0, 512)
        def q_evac(tc, pb, c):
            tsl = slice(tc * 512, (tc + 1) * 512)
            act(qm[2 * c][0:64, tsl], pb[0:64, :], AF.Copy, scale=0.125)
            act(qm[2 * c + 1][64:128, tsl], pb[64:128, :], AF.Copy, scale=0.125)
        for c in range(2):
            proj_fm(wb, c * 128, 128, lambda tc, pb, c=c: q_evac(tc, pb, c))
        for c in range(2):
            proj_fm(wb, 256 + c * 128, 128, lambda tc, pb, c=c: cp(kT[:, c, tc * 512:(tc + 1) * 512], pb, "act"))
        wv = load_w_cols(w_in_d[l], NA0 + 512, 256)
        for al in range(2):
            memset(va[al][:, :, :, 64:65], 1.0)
            for i in range(NT - al):
                pb = gbank()
                proj_tm(wv, 0, 256, i, i * 128 + al * 64, pb[:, 0:256])
                cp(va[al][:, i, :, 0:64], pb[:, 0:256].rearrange("p (h q) -> p h q", h=4), "act" if i % 2 else "dve")
        NSLOT_NA = 2
        slots = []
        for _ in range(NSLOT_NA):
            slots.append((A.alloc(1024, F32), A.alloc(1024, BF16), A.alloc(4, F32), A.alloc(256, BF16, parts=64)))

        def na_row(r, slot):
            ein, pT, rc, ytok = slot
            kr0 = min(max(r - 4, 0), 24)
            dr0 = kr0 - r + 7
            al = kr0 % 2
            ti0 = kr0 // 2
            q0 = r * 64
            pS = [gbank(), gbank()]
            for kt in range(4):
                k0 = kr0 * 64 + kt * 128
                for h in range(4):
                    col = (kt * 4 + h) * 64
                    mm(pS[col // 512][:, col % 512: col % 512 + 64], kT[:, h // 2, k0:k0 + 128],
                       qm[h][:, q0:q0 + 64])
            for half in range(2):
                tt(ein[:, half * 512:(half + 1) * 512].rearrange("p (a x) -> p a x", a=2),
                   pS[half].rearrange("p (a x) -> p a x", a=2),
                   t2[:, dr0 + 4 * half: dr0 + 4 * half + 3: 2, :], ALU.add)
            yield
            act(pT, ein, AF.Exp)
            yield
            pO = gbank()
            for h in range(4):
                for kt in range(4):
                    col = (kt * 4 + h) * 64
                    mm(pO[0:64, h * 65:(h + 1) * 65], pT[:, col:col + 64], va[al][:, ti0 + kt, h, :],
                       start=(kt == 0), stop=(kt == 3))
            pO3 = pO[0:64, 0:260].rearrange("p (h q) -> p h q", h=4)
            recip(rc[0:64, :], pO3[:, :, 64])
            tt(ytok.rearrange("p (h q) -> p h q", h=4), pO3[:, :, 0:64],
               rc[0:64, :].unsqueeze(2).to_broadcast([64, 4, 64]), ALU.mult)
            yield
            pbb = gbank().bitcast(BF16)
            for k in range(2):
                tr(pbb[:, k * 64:(k + 1) * 64], ytok[:, k * 128:(k + 1) * 128], identb[0:64, 0:64])
            cp(ymT[:, :, q0:q0 + 64], pbb[:, 0:128].rearrange("p (k t) -> p k t", k=2), "act")
            yield

        for r0 in range(0, 32, NSLOT_NA):
            gens = [na_row(r, slots[i]) for i, r in enumerate(range(r0, min(32, r0 + NSLOT_NA)))]
            alive = True
            while alive:
                alive = False
                for g_ in gens:
                    try:
                        next(g_)
                        alive = True
                    except StopIteration:
                        pass
        A.release(m_na_tmp)
        dump_dbg(3, ymT)
        outproj_acc(l, 3, ymT)
        A.release(m_n)

    def rwkv_mixer(l):
        m_r = A.mark()
        rT = A.alloc(2 * S, BF16).rearrange("p (k t) -> p k t", k=2)
        kT = A.alloc(2 * S, BF16).rearrange("p (k t) -> p k t", k=2)
        wdT = A.alloc(S, BF16)
        adT = A.alloc(S, BF16)
        gdT = A.alloc(S, BF16)
        v_tok = A.alloc(NT * 256, BF16).rearrange("p (i c) -> p i c", i=NT)
        wup = A.alloc(256, BF16)
        aup = A.alloc(256, BF16)
        gup = A.alloc(256, BF16)
        dma("pool", wup, wup_d[l].rearrange("d r c -> (d r) c"))
        dma("pool", aup, aup_d[l].rearrange("d r c -> (d r) c"))
        dma("pool", gup, gup_d[l])
        m1 = A.mark()
        stage = A.alloc(S + 2, F32)
        t1 = A.alloc(S, F32)
        vT = A.alloc(2 * S, BF16).rearrange("p (k t) -> p k t", k=2)
        memset(stage[:, 0:1], 0.0)
        memset(stage[:, S + 1:S + 2], 0.0)
        dests = [rT[:, 0, :], rT[:, 1, :], kT[:, 0, :], kT[:, 1, :], vT[:, 0, :], vT[:, 1, :], None, None, gdT]
        for grp in range(3):
            wb = load_w_cols(w_in_d[l], RW0 + grp * 384, 384)
            for j in range(3):
                c = grp * 3 + j
                proj_fm(wb, j * 128, 128, lambda tc, pb: cp(stage[:, 1 + tc * 512: 1 + (tc + 1) * 512], pb, "act"))
                c0 = ppd[:, l * 32 + c: l * 32 + c + 1]
                ts(t1, stage[:, 1:S + 1], c0, ALU.mult)
                stt(t1, stage[:, 0:S], P_(l, "mu0", c), t1, ALU.mult, ALU.add)
                if c == 6:
                    stt(t1, stage[:, 2:S + 2], P_(l, "mu1", c), t1, ALU.mult, ALU.add)
                    act(wdT, t1, AF.Tanh)
                elif c == 7:
                    stt(t1, stage[:, 2:S + 2], P_(l, "mu1", c), t1, ALU.mult, ALU.add)
                    cp(adT, t1, "act")
                elif c == 8:
                    stt(t1, stage[:, 2:S + 2], P_(l, "mu1", c), t1, ALU.mult, ALU.add)
                    act(dests[c], t1, AF.Sigmoid)
                else:
                    stt(dests[c], stage[:, 2:S + 2], P_(l, "mu1", c), t1, ALU.mult, ALU.add)
        for i in range(NT):
            pbb = gbank().bitcast(BF16)
            for k in range(2):
                tr(pbb[:, k * 128:(k + 1) * 128], vT[:, k, i * 128:(i + 1) * 128], identb)
            cp(v_tok[:, i, :], pbb[:, 0:256], "act")
        A.release(m1)
        ymT = A.alloc(2 * S, BF16).rearrange("p (k t) -> p k t", k=2)
        y_tok = A.alloc(NT * 256, BF16).rearrange("p (i c) -> p i c", i=NT)
        bonus = A.alloc(NT * 4, F32).rearrange("p (i c) -> p i c", i=NT)

        XA_ = Arena(arena_ap, 8 * S // 2, start=xn_region_start_f32)

        XB_ = Arena(arena_ap, 2 * 8 * 512 // 2, start=wb_region_start_f32)

        class _XA:
            @staticmethod
            def alloc(cols, dtype):
                for ar in (XA_, XB_):
                    if ar.top + (cols * DSZ[dtype] + CELL - 1) // CELL * CELL <= ar.limit:
                        return ar.alloc(cols, dtype)
                return A.alloc(cols, dtype)
        XA = _XA

        class DirBuf:
            pass
        G_SETS = 3
        STs, STbs = [], []
        for d in range(2):
            st_ = XA.alloc(2 * 64, F32).rearrange("p (k i) -> p k i", k=2)
            stb_ = XA.alloc(2 * 64, BF16).rearrange("p (k i) -> p k i", k=2)
            memset(st_, 0.0)
            memset(stb_, 0.0)
            STs.append(st_)
            STbs.append(stb_)
        Xbs = [XA.alloc(256, BF16) for _ in range(2)]
        Ubs = [XA.alloc(256, BF16) for _ in range(2)]
        dbs = []
        for g_ in range(G_SETS):
            b = DirBuf()
            b.sa = XA.alloc(256, F32)
            b.sig = b.sa[:, 0:128]
            b.al = b.sa[:, 128:256]
            b.ein = [XA.alloc(129, F32) for _ in range(2)]
            for nm in ("eni", "eh", "kd", "bb", "kks", "kk"):
                setattr(b, nm, XA.alloc(128, BF16))
            b.sq = b.eh
            b.Bh = [XA.alloc(128, BF16) for _ in range(2)]
            b.Kh = [XA.alloc(128, BF16) for _ in range(2)]
            b.rkd = [XA.alloc(128, BF16) for _ in range(2)]
            b.Rt = [XA.alloc(128, BF16) for _ in range(2)]
            b.Qk = [XA.alloc(128, BF16) for _ in range(2)]
            b.Btm = [[XA.alloc(128, BF16) for _ in range(2)] for _ in range(2)]
            b.Qkm = [[XA.alloc(128, BF16) for _ in range(2)] for _ in range(2)]
            b.Ktm = [[XA.alloc(128, BF16) for _ in range(2)] for _ in range(2)]
            b.Rtm = [[XA.alloc(128, BF16) for _ in range(2)] for _ in range(2)]
            for grp in (b.Btm, b.Qkm, b.Ktm, b.Rtm):
                for hp_ in range(2):
                    for h2 in range(2):
                        z0 = 64 - h2 * 64
                        memset(grp[hp_][h2][z0:z0 + 64, :], 0.0)
            b.wdm = [XA.alloc(128, BF16) for _ in range(2)]
            b.adm = [XA.alloc(128, BF16) for _ in range(2)]
            for d_ in range(2):
                z0 = 64 - d_ * 64
                memset(b.wdm[d_][z0:z0 + 64, :], 0.0)
                memset(b.adm[d_][z0:z0 + 64, :], 0.0)
            b.BhT = XA.alloc(256, BF16)
            b.KhT = XA.alloc(256, BF16)
            b.Mf = XA.alloc(512, BF16)
            b.T = XA.alloc(512, BF16)
            b.Z = XA.alloc(512, BF16)
            b.W = XA.alloc(512, BF16)
            b.Akk = XA.alloc(512, BF16)
            b.Arb = XA.alloc(512, BF16)
            b.Ark = XA.alloc(512, BF16)
            dbs.append(b)
        gtmp = XA.alloc(256, F32)
        gt2 = XA.alloc(256, F32)
        gst = XA.alloc(16, F32)
        gout = gt2.bitcast(BF16)[:, 0:256]

        def finalize_chunk(c):
            y4 = gtmp.rearrange("p (h q) -> p h q", h=4)
            cp(gtmp, y_tok[:, c, :], "pool")
            reduce_sum(gst[:, 0:4], y4)
            tt(gt2, gtmp, gtmp, ALU.mult)
            reduce_sum(gst[:, 4:8], gt2.rearrange("p (h q) -> p h q", h=4))
            ts(gst[:, 0:4], gst[:, 0:4], 1.0 / 64, ALU.mult)
            tt(gst[:, 8:12], gst[:, 0:4], gst[:, 0:4], ALU.mult)
            stt(gst[:, 4:8], gst[:, 4:8], 1.0 / 64, gst[:, 8:12], ALU.mult, ALU.subtract)
            act(gst[:, 4:8], gst[:, 4:8], AF.Sqrt, bias=64e-5)
            recip(gst[:, 4:8], gst[:, 4:8])
            tt(y4, y4, gst[:, 0:4].unsqueeze(2).to_broadcast([128, 4, 64]), ALU.subtract)
            tt(y4, y4, gst[:, 4:8].unsqueeze(2).to_broadcast([128, 4, 64]), ALU.mult)
            o_w, o_b = RB_OFF["gnw"][0], RB_OFF["gnb"][0]
            tt(gtmp, gtmp, rbt[:, o_w:o_w + 256], ALU.mult)
            tt(gtmp, gtmp, rbt[:, o_b:o_b + 256], ALU.add)
            tt(gt2.rearrange("p (h q) -> p h q", h=4), v_tok[:, c, :].rearrange("p (h q) -> p h q", h=4),
               bonus[:, c, :].unsqueeze(2).to_broadcast([128, 4, 64]), ALU.mult)
            tt(gtmp, gtmp, gt2, ALU.add)
            pg = gbank()
            mm(pg[:, 0:256], gdT[:, c * 128:(c + 1) * 128], gup)
            tt(gout, gtmp, pg[:, 0:256], ALU.mult)
            pbb = gbank().bitcast(BF16)
            for k in range(2):
                tr(pbb[:, k * 128:(k + 1) * 128], gout[:, k * 128:(k + 1) * 128], identb)
            cp(ymT[:, :, c * 128:(c + 1) * 128], pbb[:, 0:256].rearrange("p (k t) -> p k t", k=2), "act")

        seen_y = [False] * NT
        seen_bonus = [False] * NT

        def step(d, c, b):
            csl = slice(c * 128, (c + 1) * 128)
            fwd = (d == 0)
            ST, STb, Xb, Ub = STs[d], STbs[d], Xbs[d], Ubs[d]
            m_strict_neg = nmU if fwd else nmL
            m_strict = mU if fwd else mL
            m_incl = mUI if fwd else mLI
            m_incl_neg = nmUI if fwd else nmLI

            def heads(fn):
                for h in range(4):
                    fn(h, h // 2, slice((h % 2) * 64, (h % 2) * 64 + 64), slice(h * 128, (h + 1) * 128))
            dsl = slice(d * 64, d * 64 + 64)
            cp(b.wdm[d][dsl, :], wdT[dsl, csl], "dve")
            cp(b.adm[d][dsl, :], adT[dsl, csl], "dve")
            for hp in range(2):
                pz = gbank()
                mm(pz[:, 0:128], wup[:, hp * 128:(hp + 1) * 128], b.wdm[d])
                mm(pz[:, 128:256], aup[:, hp * 128:(hp + 1) * 128], b.adm[d])
                nw0 = ppd[:, l * 32 + 19 + d * 2 + hp: l * 32 + 20 + d * 2 + hp]
                na0 = ppd[:, l * 32 + 23 + d * 2 + hp: l * 32 + 24 + d * 2 + hp]
                act(b.sig, pz[:, 0:128], AF.Exp, scale=-1.0, bias=nw0)
                act(b.al, pz[:, 128:256], AF.Exp, scale=-1.0, bias=na0)
                act(b.sa, b.sa, AF.Ln, bias=1.0)
                act(b.sa, b.sa, AF.Exp, scale=-1.0)
                if fwd:
                    cs = b.ein[hp][:, 1:129]
                    scan(cs, onesf, b.sig)
                    pad = b.ein[hp][:, 0:1]
                    eex = b.ein[hp][:, 0:128]
                    etot = b.ein[hp][:, 128:129]
                else:
                    cs = b.ein[hp][:, 0:128]
                    scan(cs[:, ::-1], onesf, b.sig[:, ::-1])
                    pad = b.ein[hp][:, 128:129]
                    eex = b.ein[hp][:, 1:129]
                    etot = b.ein[hp][:, 0:1]
                ts(b.kks, kT[:, hp, csl], P_(l, "kk", hp), ALU.mult)
                tt(b.sq, b.kks, b.kks, ALU.mult)
                pn = gbank()
                mm(pn[:, 0:128], blkb, b.sq)
                act(b.eni, cs, AF.Exp, scale=DECAY)
                act(cs, cs, AF.Exp, scale=-DECAY)
                act(b.kk, pn[:, 0:128], AF.Ln, bias=1e-12)
                act(b.kk, b.kk, AF.Exp, scale=-0.5)
                yield
                memset(pad, 1.0, "pool")
                ts(b.eh, b.eni, etot, ALU.mult)
                tt(b.kk, b.kk, b.kks, ALU.mult)
                omka = ppd[:, l * 32 + 9 + hp: l * 32 + 10 + hp]
                ts(b.kd, b.al, P_(l, "ka", hp), ALU.mult, omka, ALU.add)
                tt(b.kd, b.kd, kT[:, hp, csl], ALU.mult)
                tt(b.bb, b.al, b.kk, ALU.mult)
                tt(b.Rt[hp], rT[:, hp, csl], cs, ALU.mult)
                tt(b.Qk[hp], b.kk, eex, ALU.mult)
                for h2 in range(2):
                    pr = slice(h2 * 64, h2 * 64 + 64)
                    tt(b.Btm[hp][h2][pr, :], b.bb[pr, :], b.eni[pr, :], ALU.mult)
                    tt(b.Ktm[hp][h2][pr, :], b.kd[pr, :], b.eni[pr, :], ALU.mult)
                    cp(b.Qkm[hp][h2][pr, :], b.Qk[hp][pr, :], "pool")
                    cp(b.Rtm[hp][h2][pr, :], b.Rt[hp][pr, :], "pool")
                stt(b.Bh[hp], b.bb, -1.0, b.eh, ALU.mult, ALU.mult)
                tt(b.Kh[hp], b.kd, b.eh, ALU.mult)
                stt(b.rkd[hp], rT[:, hp, csl], P_(l, "rk", hp), b.kd, ALU.mult, ALU.mult)
                yield
            pbn = gbank()
            pbb = gbank().bitcast(BF16)
            for hp in range(2):
                mm(pbn[:, hp * 2:hp * 2 + 2], b.rkd[hp], headsel)
                tr(pbb[:, hp * 128:(hp + 1) * 128], b.Bh[hp], identb)
                tr(pbb[:, 256 + hp * 128:256 + (hp + 1) * 128], b.Kh[hp], identb)
            if not seen_bonus[c]:
                seen_bonus[c] = True
                cp(bonus[:, c, :], pbn[:, 0:4], "dve")
            else:
                tt(bonus[:, c, :], pbn[:, 0:4], bonus[:, c, :], ALU.add)
            cp(b.BhT, pbb[:, 0:256], "act")
            cp(b.KhT, pbb[:, 256:512], "act")
            for lhs_, rhs_, dst_, msk_ in ((b.Btm, b.Qk, b.Mf, m_strict_neg), (b.Ktm, b.Qk, b.Akk, m_strict),
                                           (b.Btm, b.Rt, b.Arb, m_incl_neg), (b.Ktm, b.Rt, b.Ark, m_incl)):
                pa = gbank()
                heads(lambda h, hp, jr, hs: mm(pa[:, hs], lhs_[hp][h % 2], rhs_[hp]))
                tt(dst_.rearrange("p (h t) -> p h t", h=4), pa.rearrange("p (h t) -> p h t", h=4),
                   msk_.unsqueeze(1).to_broadcast([128, 4, 128]), ALU.mult)
            yield
            for k in range(7):
                Tc = ident4 if k == 0 else b.T
                Zc = ident4 if k == 0 else b.Z
                pa = gbank()
                heads(lambda h, hp, jr, hs: mm(pa[:, hs], b.Mf[:, hs], Tc[:, hs]))
                mk = lvl[:, (k if fwd else 7 + k), :]
                tt(b.W.rearrange("p (h t) -> p h t", h=4), pa.rearrange("p (h t) -> p h t", h=4),
                   mk.unsqueeze(1).to_broadcast([128, 4, 128]), ALU.mult)
                yield
                if k < 6:
                    pt = gbank()
                    mm(pt, identb, Tc, start=True, stop=False, skip=True)
                    heads(lambda h, hp, jr, hs: mm(pt[:, hs], Zc[:, hs], b.W[:, hs], start=False, stop=False,
                                                   skip=True))
                pz = gbank()
                mm(pz, identb, Zc, start=True, stop=False, skip=True)
                heads(lambda h, hp, jr, hs: mm(pz[:, hs], b.W[:, hs], Zc[:, hs], start=False, stop=False, skip=True))
                if k < 6:
                    cp(b.T, pt, "act")
                cp(b.Z, pz, "dve" if k % 2 else "act")
                yield
            yield "CHAIN"
            px = gbank()

            def xinit(h, hp, jr, hs):
                xs_ = slice(h * 64, (h + 1) * 64)
                mm(px[:, xs_], b.Qkm[hp][h % 2], STb[:, hp, :], start=True, stop=False)
                mm(px[:, xs_], b.Akk[:, hs], v_tok[:, c, xs_], start=False, stop=True)
            heads(xinit)
            cp(Xb, px[:, 0:256], "act")
            pu = gbank()
            heads(lambda h, hp, jr, hs: mm(pu[:, h * 64:(h + 1) * 64], b.Z[:, hs], Xb[:, h * 64:(h + 1) * 64]))
            cp(Ub, pu[:, 0:256], "act")
            py = gbank()

            def yfn(h, hp, jr, hs):
                xs_ = slice(h * 64, (h + 1) * 64)
                mm(py[:, xs_], b.Rtm[hp][h % 2], STb[:, hp, :], start=True, stop=False)
                mm(py[:, xs_], b.Arb[:, hs], Ub[:, xs_], start=False, stop=False)
                mm(py[:, xs_], b.Ark[:, hs], v_tok[:, c, xs_], start=False, stop=True)
            heads(yfn)
            if not seen_y[c]:
                seen_y[c] = True
                cp(y_tok[:, c, :], py[:, 0:256], "act")
            else:
                tt(y_tok[:, c, :], py[:, 0:256], y_tok[:, c, :], ALU.add)
            ph = gbank()
            for h in range(4):
                hp = h // 2
                xs_ = slice(h * 64, (h + 1) * 64)
                mm(ph[:, xs_], b.BhT[:, hp * 128:(hp + 1) * 128], Ub[:, xs_], start=True, stop=False)
                mm(ph[:, xs_], b.KhT[:, hp * 128:(hp + 1) * 128], v_tok[:, c, xs_], start=False, stop=True)
            for h in range(4):
                hp = h // 2
                jr = slice((h % 2) * 64, (h % 2) * 64 + 64)
                etot = (b.ein[hp][jr, 128:129] if fwd else b.ein[hp][jr, 0:1])
                stt(ST[jr, hp, :], ST[jr, hp, :], etot, ph[jr, h * 64:(h + 1) * 64], ALU.mult, ALU.add)
            cp(STb, ST, "act")
            yield

        pending = []
        for st_i in range(NT):
            pending.append((0, st_i, st_i))
            pending.append((1, NT - 1 - st_i, st_i))
        live = []
        free_sets = list(range(G_SETS))
        done = [0, 0]
        ndone_chunk = [0] * NT
        while pending or live:
            while pending and free_sets:
                d_, c_, idx_ = pending.pop(0)
                si = free_sets.pop(0)
                live.append({"g": step(d_, c_, dbs[si]), "d": d_, "c": c_, "idx": idx_, "set": si, "wait": False})
            for it in list(live):
                if it["wait"] and done[it["d"]] < it["idx"]:
                    continue
                try:
                    r_ = next(it["g"])
                    if r_ == "CHAIN":
                        it["wait"] = True
                except StopIteration:
                    live.remove(it)
                    free_sets.append(it["set"])
                    done[it["d"]] += 1
                    ndone_chunk[it["c"]] += 1
                    if ndone_chunk[it["c"]] == 2:
                        finalize_chunk(it["c"])
        dump_dbg(0, ymT)
        outproj_acc(l, 0, ymT)
        A.release(m_r)

    def mlp(l):
        m_m = A.mark()
        mT = A.alloc(8 * S, BF16).rearrange("p (k t) -> p k t", k=8)
        w2b = [A.alloc(8 * 512, BF16).rearrange("p (k c) -> p k c", k=8) for _ in range(2)]
        rls = [A.alloc(512, BF16) for _ in range(2)]
        rl_i = [0]
        for hg in range(4):
            for half in range(2):
                wb = load_w_cols(w1_d[l], hg * 1024 + half * 512, 512)
                for j in range(4):
                    hpt = half * 4 + j
                    for tc in range(4):
                        tsl = slice(tc * 512, (tc + 1) * 512)
                        pb = gbank()
                        for k in range(8):
                            mm(pb, wb[:, k, j * 128:(j + 1) * 128], xnT[:, k, tsl], start=(k == 0), stop=(k == 7))
                        rl = rls[rl_i[0] % 2]
                        rl_i[0] += 1
                        act(rl, pb, AF.Relu)
                        tt(mT[:, hpt, tsl], rl, rl, ALU.mult)
            for half in range(2):
                dma("pool", w2b[half],
                    w2_d[l, hg * 1024:(hg + 1) * 1024, half * 512:(half + 1) * 512].rearrange("(k p) c -> p k c", p=128))
            for ot in range(8):
                for tc in range(4):
                    tsl = slice(tc * 512, (tc + 1) * 512)
                    pb = gbank()
                    for k in range(8):
                        mm(pb, w2b[ot // 4][:, k, (ot % 4) * 128:(ot % 4 + 1) * 128], mT[:, k, tsl], start=(k == 0),
                           stop=(k == 7))
                    tt(hF[:, ot, tsl], pb, hF[:, ot, tsl], ALU.add)
        A.release(m_m)

    mixers = (dbg or {}).get("mixers", "srna") if dbg else "srna"
    for l in range(n_layers):
        dma("sp", rbt, rb_d[l])
        rmsnorm_to_xnT(lambda f, l=l: P_(l, "n1", f))
        if "s" in mixers:
            ssd_mixer(l)
        if "r" in mixers:
            lru_mixer(l)
        if "n" in mixers:
            na_mixer(l)
        if "a" in mixers:
            rwkv_mixer(l)
        if not (dbg and dbg.get("no_mlp")):
            rmsnorm_to_xnT(lambda f, l=l: P_(l, "n2", f))
            mlp(l)

    m0 = A.mark()
    sq = A.alloc(8 * 512, BF16).rearrange("p (f t) -> p f t", f=8)
    rstd = A.alloc(512, F32)
    yn = A.alloc(8 * 512, F32).rearrange("p (f t) -> p f t", f=8)
    ost = [A.alloc(D, F32) for _ in range(2)]
    fo = PP_L * L_ALL
    for tc in range(4):
        tsl = slice(tc * 512, (tc + 1) * 512)
        for f in range(8):
            act(sq[:, f, :], hF[:, f, tsl], AF.Square)
        pb = gbank()
        for f in range(8):
            mm(pb, onesb, sq[:, f, :], start=(f == 0), stop=(f == 7))
        act(rstd, pb, AF.Sqrt, scale=1.0 / D, bias=1e-5)
        recip(rstd, rstd)
        for f in range(8):
            stt(yn[:, f, :], hF[:, f, tsl], pp[:, fo + f:fo + f + 1], rstd, ALU.mult, ALU.mult)
        for ti in range(4):
            o = ost[ti % 2]
            for half in range(2):
                pb2 = gbank()
                for j in range(4):
                    f = half * 4 + j
                    tr(pb2[:, j * 128:(j + 1) * 128], yn[:, f, ti * 128:(ti + 1) * 128], identf)
                cp(o[:, half * 512:(half + 1) * 512], pb2, "act" if half else "dve")
            tok0 = tc * 512 + ti * 128
            dma("sp", out_d[tok0:tok0 + 128, :], o)
    A.release(m0)

    Sc.finish()
    Sc.emit(stack)
    stack.close()
    return nc, Sc, A


_CACHE = {}


def kernel(**inputs):
    inp = {k: np.asarray(v) for k, v in inputs.items()}
    pp, rb, lbd, t2 = build_host_params(inp)
    if "nc" not in _CACHE:
        _CACHE["nc"] = build_program(L_ALL)[0]
    nc = _CACHE["nc"]
    shared = {
        "w_in": np.ascontiguousarray(inp["w_in"], np.float32),
        "w_out": np.ascontiguousarray(inp["w_out"], np.float32),
        "w_mlp1": np.ascontiguousarray(inp["w_mlp1"], np.float32),
        "w_mlp2": np.ascontiguousarray(inp["w_mlp2"], np.float32),
        "rwkv_w_up": np.ascontiguousarray(inp["rwkv_w_up"], np.float32),
        "rwkv_a_up": np.ascontiguousarray(inp["rwkv_a_up"], np.float32),
        "rwkv_g_up": np.ascontiguousarray(inp["rwkv_g_up"], np.float32),
        "pp": pp, "rb": rb, "lbd": lbd, "t2": t2.reshape(L_ALL, 128, 14 * 256), "lvl": build_level_masks(),
    }
    x = np.ascontiguousarray(inp["x"], np.float32)
    in_maps = [dict(shared, x=x[b]) for b in range(8)]
    res = run_bass_kernel_spmd(nc, in_maps, core_ids=list(range(8)))
    return np.stack([np.asarray(r["out"], np.float32) for r in res.results], axis=0)
```

```python
import numpy as np
from contextlib import ExitStack
import concourse.bass as bass
import concourse.mybir as mybir
from concourse.bass_utils import run_bass_kernel_spmd

F32 = mybir.dt.float32
BF16 = mybir.dt.bfloat16
AF = mybir.ActivationFunctionType
ALU = mybir.AluOpType
AX = mybir.AxisListType
DSZ = {F32: 4, BF16: 2}

L_ALL = 4
D = 1024
S = 2048
NT = 16
IN_COLS = 3464
RW0, SSD0, LRU0, NA0 = 0, 1152, 2184, 2696
CELL = 32
NSLOT = 8
ARENA_F32 = 53000
DECAY = 0.6065306597126334
NEG = -30000.0

PP_SPEC = [("n1", 8), ("n2", 8), ("mu0", 9), ("mu1", 9), ("w0", 4), ("a0", 4), ("kk", 2), ("ka", 2),
           ("rk", 2), ("scw", 24), ("scb", 6), ("lcw", 8), ("lcb", 2), ("lab", 4), ("lxb", 4), ("lam", 4)]
PP_OFF = {}
_o = 0
for _n, _k in PP_SPEC:
    PP_OFF[_n] = (_o, _k)
    _o += _k
PP_L = _o
PP_TOT = PP_L * L_ALL + 8
RB_SPEC = [("gnw", 256), ("gnb", 256), ("dtb", 8), ("alog", 8), ("dsk", 4), ("snw", 256)]
RB_OFF = {}
_o = 0
for _n, _k in RB_SPEC:
    RB_OFF[_n] = (_o, _k)
    _o += _k
RB_L = _o


def _colvec(a):
    a = np.asarray(a, np.float32)
    n = a.shape[-1] // 128
    b = a.reshape(a.shape[:-1] + (n, 128))
    return np.moveaxis(b, -1, 0)


def build_host_params(inp):
    pp = np.zeros((128, PP_TOT), np.float32)
    rb = np.zeros((L_ALL, 128, RB_L), np.float32)
    for l in range(L_ALL):
        def put(name, arr):
            o, k = PP_OFF[name]
            pp[:, l * PP_L + o: l * PP_L + o + k] = arr.reshape(128, k)
        put("n1", _colvec(inp["norm1_w"][l]))
        put("n2", _colvec(inp["norm2_w"][l]))
        put("mu0", _colvec(inp["rwkv_shift_mu"][l, 0]))
        put("mu1", _colvec(inp["rwkv_shift_mu"][l, 1]))
        put("w0", _colvec(inp["rwkv_w0"][l]))
        put("a0", _colvec(inp["rwkv_a0"][l]))
        put("kk", _colvec(inp["rwkv_k_k"][l]))
        put("ka", _colvec(inp["rwkv_k_a"][l]))
        put("rk", _colvec(inp["rwkv_r_k"][l].reshape(256)))
        put("scw", _colvec(inp["ssd_conv_w"][l]))
        put("scb", _colvec(inp["ssd_conv_b"][l]))
        put("lcw", _colvec(inp["lru_conv_w"][l]))
        put("lcb", _colvec(inp["lru_conv_b"][l]))
        put("lab", _colvec(inp["lru_gate_a_b"][l]))
        put("lxb", _colvec(inp["lru_gate_x_b"][l]))
        put("lam", _colvec(inp["lru_lambda"][l]))

        def putr(name, vec):
            o, k = RB_OFF[name]
            rb[l, :, o:o + k] = np.broadcast_to(np.asarray(vec, np.float32).reshape(1, k), (128, k))
        putr("gnw", inp["rwkv_gn_w"][l])
        putr("gnb", inp["rwkv_gn_b"][l])
        putr("dtb", inp["ssd_dt_bias"][l].reshape(8))
        putr("alog", inp["ssd_a_log"][l].reshape(8))
        putr("dsk", inp["ssd_d"][l])
        putr("snw", inp["ssd_norm_w"][l])
    pp[:, PP_L * L_ALL:] = _colvec(inp["final_norm_w"])
    lbd = np.zeros((L_ALL, 2, 2, 2, 128, 128), np.float32)
    for wi, nm in enumerate(("lru_gate_a_w", "lru_gate_x_w")):
        w = np.asarray(inp[nm], np.float32)
        for pt in range(2):
            for j in range(2):
                lbd[:, wi, :, pt, j * 64:(j + 1) * 64, j * 64:(j + 1) * 64] = w[:, :, 2 * pt + j]
    rbias = np.asarray(inp["na_rel_bias"], np.float32)
    qc = np.arange(64)
    kc = np.arange(64)
    ws = np.clip(qc - 8, 0, 48)
    inwin = (kc[:, None] >= ws[None, :]) & (kc[:, None] < ws[None, :] + 16)
    ci = np.clip(kc[:, None] - qc[None, :] + 15, 0, 30)
    t2 = np.full((L_ALL, 128, 14, 4, 64), NEG, np.float32)
    for kh in range(2):
        for dr in range(14):
            g = rbias[:, :, dr + kh, :][:, :, ci]
            g = np.where(inwin[None, None], g, np.float32(NEG))
            t2[:, kh * 64:(kh + 1) * 64, dr, :, :] = np.transpose(g, (0, 2, 1, 3))
    return pp, rb, lbd, t2


def build_level_masks():
    t = np.arange(128)
    out = np.zeros((128, 14, 128), np.float32)
    for k in range(7):
        ll = (((t[:, None] >> (k + 1)) == (t[None, :] >> (k + 1))) & (((t[:, None] >> k) & 1) == 1)
              & (((t[None, :] >> k) & 1) == 0))
        out[:, k, :] = ll
        out[:, 7 + k, :] = ll.T
    return out.reshape(128, 14 * 128)


class Sched:
    ISSUERS = ["pe", "act", "dve", "pool", "sp"]

    def __init__(self, nc, arena_name, psum_name):
        self.nc = nc
        self.semnames = ["pe", "act", "dve", "pool"]
        for q in ("sp", "act", "pool"):
            for i in range(NSLOT):
                self.semnames.append(f"{q}q{i}")
        self.sidx = {n: i for i, n in enumerate(self.semnames)}
        ns = len(self.semnames)
        self.ns = ns
        self.spaces = {arena_name: 0, psum_name: 1}
        ncell = [ARENA_F32 * 4 // CELL + 2, 16384 // CELL + 2]
        self.W = [np.zeros((4, c, ns), np.int64) for c in ncell]
        self.R = [np.zeros((4, c, ns), np.int64) for c in ncell]
        self.cnt = {e: 0 for e in ["pe", "act", "dve", "pool"]}
        self.dman = {"sp": 0, "act": 0, "pool": 0}
        self.seen = {e: np.zeros(ns, np.int64) for e in self.ISSUERS}
        self.q = {e: [] for e in self.ISSUERS}
        self.nops = 0
        self.limit = None
        self.trace = []

    def region(self, ap):
        sp = self.spaces.get(ap.tensor.name)
        if sp is None:
            return None
        a = ap.ap
        ds = DSZ[ap.dtype]
        rowlen = a[0][0]
        pcount = a[0][1]
        p0 = ap.offset // rowlen
        c0 = ap.offset % rowlen
        lo = hi = c0
        for step, cnt in a[1:]:
            ext = step * (cnt - 1)
            if ext < 0:
                lo += ext
            else:
                hi += ext
        blo, bhi = lo * ds, (hi + 1) * ds
        if sp == 1:
            blo = blo // 2048 * 2048
            bhi = (bhi + 2047) // 2048 * 2048
        return (sp, p0 // 32, (p0 + pcount + 31) // 32, blo // CELL, (bhi + CELL - 1) // CELL)

    def _need(self, r, w):
        need = np.zeros(self.ns, np.int64)
        rr = [x for x in (self.region(a) for a in r) if x is not None]
        wr = [x for x in (self.region(a) for a in w) if x is not None]
        for (sp, p0, p1, c0, c1) in rr:
            need = np.maximum(need, self.W[sp][p0:p1, c0:c1].max(axis=(0, 1)))
        for (sp, p0, p1, c0, c1) in wr:
            need = np.maximum(need, self.W[sp][p0:p1, c0:c1].max(axis=(0, 1)))
            need = np.maximum(need, self.R[sp][p0:p1, c0:c1].max(axis=(0, 1)))
        return need, rr, wr

    def _mark(self, rr, wr, si, val):
        for (sp, p0, p1, c0, c1) in wr:
            self.W[sp][p0:p1, c0:c1, si] = val
        for (sp, p0, p1, c0, c1) in rr:
            self.R[sp][p0:p1, c0:c1, si] = val

    def op(self, eng, fn, r=(), w=()):
        if self.limit is not None and self.nops >= self.limit:
            return
        need, rr, wr = self._need(r, w)
        if eng == "pe":
            need[self.sidx["pe"]] = 0
        seen = self.seen[eng]
        waits = [(i, int(need[i])) for i in np.nonzero(need > seen)[0]]
        np.maximum(seen, need, out=seen)
        self.cnt[eng] += 1
        if self.limit is not None:
            import sys as _s
            f = _s._getframe(1)
            self.trace.append((self.nops, eng, f.f_code.co_name, [(a.tensor.name, a.offset, a.ap) for a in list(w) + list(r)]))
        self.q[eng].append((waits, fn, self.sidx[eng], 1))
        self._mark(rr, wr, self.sidx[eng], self.cnt[eng])
        self.nops += 1

    def dma(self, queue, out, in_):
        if self.limit is not None and self.nops >= self.limit:
            return
        need, rr, wr = self._need([in_], [out])
        n = self.dman[queue]
        self.dman[queue] = n + 1
        slot, rnd = n % NSLOT, n // NSLOT
        si = self.sidx[f"{queue}q{slot}"]
        need[si] = max(need[si], 16 * rnd)
        seen = self.seen[queue]
        waits = [(i, int(need[i])) for i in np.nonzero(need > seen)[0]]
        np.maximum(seen, need, out=seen)
        fn = (lambda e, out=out, in_=in_: e.dma_start(out=out, in_=in_))
        self.q[queue].append((waits, fn, si, 16))
        self._mark(rr, wr, si, 16 * (rnd + 1))
        self.nops += 1

    def finish(self):
        final = np.zeros(self.ns, np.int64)
        for e in ["pe", "act", "dve", "pool"]:
            final[self.sidx[e]] = self.cnt[e]
        for qn in ("sp", "act", "pool"):
            n = self.dman[qn]
            for s in range(NSLOT):
                k = (n - s + NSLOT - 1) // NSLOT if n > s else 0
                final[self.sidx[f"{qn}q{s}"]] = 16 * k
        self.final = final

    def emit(self, stack):
        nc = self.nc
        sems = [stack.enter_context(nc.semaphore(f"s_{n}")) for n in self.semnames]
        block = stack.enter_context(nc.Block())

        def replay(name, e):
            for waits, fn, si, inc in self.q[name]:
                for (s, v) in waits:
                    e.wait_ge(sems[s], v)
                fn(e).then_inc(sems[si], inc)
            if name == "sp":
                for s in range(self.ns):
                    if self.final[s] > 0:
                        e.wait_ge(sems[s], int(self.final[s]))

        @block.tensor
        def _(e):
            replay("pe", e)

        @block.scalar
        def _(e):
            replay("act", e)

        @block.vector
        def _(e):
            replay("dve", e)

        @block.gpsimd
        def _(e):
            replay("pool", e)

        @block.sync
        def _(e):
            replay("sp", e)


class Arena:
    def __init__(self, base_ap_f32, ncols_f32, start=0):
        self.base = base_ap_f32
        self.lo = start * 4
        self.top = self.lo
        self.limit = (start + ncols_f32) * 4
        self.peak = self.top

    def alloc(self, cols, dtype, parts=128):
        nbytes = cols * DSZ[dtype]
        nbytes = (nbytes + CELL - 1) // CELL * CELL
        off = self.top
        self.top += nbytes
        self.peak = max(self.peak, self.top)
        assert self.top <= self.limit, f"arena overflow {self.top} > {self.limit}"
        ap = self.base[0:parts, off // 4:(off + nbytes) // 4]
        if dtype != F32:
            ap = ap.bitcast(dtype)
        return ap[:, 0:cols]

    def mark(self):
        return self.top

    def release(self, m):
        self.top = m


def build_program(n_layers=L_ALL, dbg=None):
    nc = bass.Bass("TRN2", target_bir_lowering=False)
    dt = nc.dram_tensor
    x_d = dt("x", [S, D], F32, kind="ExternalInput").ap()
    w_in_d = dt("w_in", [L_ALL, D, IN_COLS], F32, kind="ExternalInput").ap()
    w_out_d = dt("w_out", [L_ALL, D, D], F32, kind="ExternalInput").ap()
    w1_d = dt("w_mlp1", [L_ALL, D, 4 * D], F32, kind="ExternalInput").ap()
    w2_d = dt("w_mlp2", [L_ALL, 4 * D, D], F32, kind="ExternalInput").ap()
    wup_d = dt("rwkv_w_up", [L_ALL, 2, 64, 256], F32, kind="ExternalInput").ap()
    aup_d = dt("rwkv_a_up", [L_ALL, 2, 64, 256], F32, kind="ExternalInput").ap()
    gup_d = dt("rwkv_g_up", [L_ALL, 128, 256], F32, kind="ExternalInput").ap()
    pp_d = dt("pp", [128, PP_TOT], F32, kind="ExternalInput").ap()
    rb_d = dt("rb", [L_ALL, 128, RB_L], F32, kind="ExternalInput").ap()
    lbd_d = dt("lbd", [L_ALL, 2, 2, 2, 128, 128], F32, kind="ExternalInput").ap()
    t2_d = dt("t2", [L_ALL, 128, 14 * 256], F32, kind="ExternalInput").ap()
    lvl_d = dt("lvl", [128, 14 * 128], F32, kind="ExternalInput").ap()
    out_d = dt("out", [S, D], F32, kind="ExternalOutput").ap()
    dbg_d = None
    if dbg:
        dbg_d = dt("dbg", [D, S], F32, kind="ExternalOutput").ap()

    stack = ExitStack()
    arena_t = stack.enter_context(nc.sbuf_tensor("arena", [128, ARENA_F32], F32))
    ps_t = stack.enter_context(nc.psum_tensor("ps", [128, 4096], F32))
    arena_ap = arena_t[:, :]
    ps_ap = ps_t[:, :]
    Sc = Sched(nc, "arena", "ps")
    if dbg and dbg.get("limit"):
        Sc.limit = dbg["limit"]
    A = Arena(arena_ap, ARENA_F32)

    def bank(b):
        return ps_ap[:, b * 512:(b + 1) * 512]

    rot = [0]

    def gbank():
        b = rot[0] % 8
        rot[0] += 1
        return bank(b)

    def mm(out, lhsT, rhs, start=True, stop=True, skip=False):
        Sc.op("pe", lambda e: e.matmul(out, lhsT=lhsT, rhs=rhs, start=start, stop=stop, skip_group_check=skip),
              r=[lhsT, rhs], w=[out])

    def tr(out, in_, ident):
        Sc.op("pe", lambda e: e.transpose(out, in_, ident), r=[in_, ident], w=[out])

    def act(out, in_, func, scale=None, bias=None):
        kw = {}
        rd = [in_]
        if scale is not None:
            kw["scale"] = scale
            if not isinstance(scale, (int, float)):
                rd.append(scale)
        if bias is not None:
            kw["bias"] = bias
            if not isinstance(bias, (int, float)):
                rd.append(bias)
        Sc.op("act", lambda e: e.activation(out=out, in_=in_, func=func, **kw), r=rd, w=[out])

    def tt(out, in0, in1, op, eng="dve"):
        Sc.op(eng, lambda e: e.tensor_tensor(out=out, in0=in0, in1=in1, op=op), r=[in0, in1], w=[out])

    def ts(out, in0, s1, op0, s2=None, op1=None, eng="dve"):
        rd = [in0] + [s for s in (s1, s2) if s is not None and not isinstance(s, (int, float))]
        if op1 is None:
            Sc.op(eng, lambda e: e.tensor_scalar(out=out, in0=in0, scalar1=s1, scalar2=None, op0=op0), r=rd, w=[out])
        else:
            Sc.op(eng, lambda e: e.tensor_scalar(out=out, in0=in0, scalar1=s1, scalar2=s2, op0=op0, op1=op1),
                  r=rd, w=[out])

    def stt(out, in0, scalar, in1, op0, op1):
        rd = [in0, in1] + ([] if isinstance(scalar, (int, float)) else [scalar])
        Sc.op("dve", lambda e: e.scalar_tensor_tensor(out=out, in0=in0, scalar=scalar, in1=in1, op0=op0, op1=op1),
              r=rd, w=[out])

    def cp(out, in_, eng="dve"):
        if eng == "act":
            Sc.op("act", lambda e: e.activation(out=out, in_=in_, func=AF.Copy), r=[in_], w=[out])
        else:
            Sc.op(eng, lambda e: e.tensor_copy(out=out, in_=in_), r=[in_], w=[out])

    def memset(out, val, eng="pool"):
        Sc.op(eng, lambda e: e.memset(out, val), r=[], w=[out])

    def recip(out, in_):
        Sc.op("dve", lambda e: e.reciprocal(out=out, in_=in_), r=[in_], w=[out])

    def scan(out, d0, d1, init=0.0):
        Sc.op("dve", lambda e: e.tensor_tensor_scan(out=out, data0=d0, data1=d1, initial=init, op0=ALU.mult,
                                                    op1=ALU.add), r=[d0, d1], w=[out])

    def reduce_sum(out, in_):
        Sc.op("dve", lambda e: e.tensor_reduce(out=out, in_=in_, axis=AX.X, op=ALU.add), r=[in_], w=[out])

    def affsel(out, in_, pattern, cmp, fill, base, cm):
        Sc.op("pool", lambda e: e.affine_select(out=out, in_=in_, pattern=pattern, compare_op=cmp, fill=fill,
                                                base=base, channel_multiplier=cm), r=[in_], w=[out])

    def dma(q, out, in_):
        Sc.dma(q, out, in_)

    hF = A.alloc(8 * S, F32).rearrange("p (f t) -> p f t", f=8)
    xnT = A.alloc(8 * S, BF16).rearrange("p (f t) -> p f t", f=8)
    xn_region_start_f32 = (A.top - 8 * S * 2) // 4
    pp = A.alloc(PP_TOT, F32)
    ppd = A.alloc(L_ALL * 32, F32)
    rbt = A.alloc(RB_L, F32)
    identb = A.alloc(128, BF16)
    identf = A.alloc(128, F32)
    onesb = A.alloc(128, BF16)
    onesf = A.alloc(128, F32)
    blkb = A.alloc(128, BF16)
    headsel = A.alloc(2, BF16)
    mU = A.alloc(128, F32)
    mL = A.alloc(128, F32)
    mUI = A.alloc(128, F32)
    mLI = A.alloc(128, F32)
    nmU = A.alloc(128, F32)
    nmL = A.alloc(128, F32)
    nmUI = A.alloc(128, F32)
    nmLI = A.alloc(128, F32)
    negU4 = A.alloc(512, BF16)
    negL4 = A.alloc(512, BF16)
    lvl = A.alloc(14 * 128, BF16).rearrange("p (k c) -> p k c", k=14)
    ident4 = A.alloc(512, BF16)
    wb_region_start_f32 = A.top // 4
    wbufs = [A.alloc(8 * 512, BF16).rearrange("p (k c) -> p k c", k=8) for _ in range(2)]
    wb_i = [0]

    def next_wbuf():
        b = wbufs[wb_i[0] % 2]
        wb_i[0] += 1
        return b

    memset(identf, 0.0)
    affsel(identf, identf, [[-1, 128]], ALU.not_equal, 1.0, 0, 1)
    cp(identb, identf, "pool")
    memset(onesf, 1.0)
    memset(onesb, 1.0)
    for h in range(4):
        cp(ident4[:, h * 128:(h + 1) * 128], identf, "pool")
    dma("pool", lvl, lvl_d.rearrange("p (k c) -> p k c", k=14))
    affsel(mU, onesf, [[1, 128]], ALU.is_gt, 0.0, 0, -1)
    affsel(mUI, onesf, [[1, 128]], ALU.is_ge, 0.0, 0, -1)
    affsel(mL, onesf, [[-1, 128]], ALU.is_gt, 0.0, 0, 1)
    affsel(mLI, onesf, [[-1, 128]], ALU.is_ge, 0.0, 0, 1)
    for src, dst in ((mU, nmU), (mL, nmL), (mUI, nmUI), (mLI, nmLI)):
        ts(dst, src, -1.0, ALU.mult, eng="pool")
    for h in range(4):
        ts(negU4[:, h * 128:(h + 1) * 128], mU, NEG, ALU.mult, eng="pool")
        ts(negL4[:, h * 128:(h + 1) * 128], mL, NEG, ALU.mult, eng="pool")
    memset(blkb, 0.0)
    memset(blkb[0:64, 0:64], 1.0)
    memset(blkb[64:128, 64:128], 1.0)
    memset(headsel, 0.0)
    memset(headsel[0:64, 0:1], 1.0)
    memset(headsel[64:128, 1:2], 1.0)
    dma("sp", pp, pp_d)
    for l in range(n_layers):
        b0 = l * PP_L
        d0 = l * 32
        o_mu0, o_mu1 = PP_OFF["mu0"][0] + b0, PP_OFF["mu1"][0] + b0
        tt(ppd[:, d0:d0 + 9], pp[:, o_mu0:o_mu0 + 9], pp[:, o_mu1:o_mu1 + 9], ALU.add)
        ts(ppd[:, d0:d0 + 9], ppd[:, d0:d0 + 9], -1.0, ALU.mult, 1.0, ALU.add)
        o_ka = PP_OFF["ka"][0] + b0
        ts(ppd[:, d0 + 9:d0 + 11], pp[:, o_ka:o_ka + 2], -1.0, ALU.mult, 1.0, ALU.add)
        o_lam = PP_OFF["lam"][0] + b0
        act(ppd[:, d0 + 11:d0 + 15], pp[:, o_lam:o_lam + 4], AF.Exp, scale=-1.0)
        act(ppd[:, d0 + 11:d0 + 15], ppd[:, d0 + 11:d0 + 15], AF.Ln, bias=1.0)
        ts(ppd[:, d0 + 15:d0 + 19], ppd[:, d0 + 11:d0 + 15], -16.0, ALU.mult)
        ts(ppd[:, d0 + 11:d0 + 15], ppd[:, d0 + 11:d0 + 15], -8.0, ALU.mult)
        o_w0, o_a0 = PP_OFF["w0"][0] + b0, PP_OFF["a0"][0] + b0
        ts(ppd[:, d0 + 19:d0 + 23], pp[:, o_w0:o_w0 + 4], -1.0, ALU.mult)
        ts(ppd[:, d0 + 23:d0 + 27], pp[:, o_a0:o_a0 + 4], -1.0, ALU.mult)

    def P_(l, name, j=0, n=1):
        o, k = PP_OFF[name]
        return pp[:, l * PP_L + o + j: l * PP_L + o + j + n]

    m0 = A.mark()
    xst = [A.alloc(D, F32) for _ in range(2)]
    for tt_i in range(NT):
        xs_ = xst[tt_i % 2]
        dma("sp", xs_, x_d[tt_i * 128:(tt_i + 1) * 128, :])
        for half in range(2):
            pb = gbank()
            for j in range(4):
                f = half * 4 + j
                tr(pb[:, j * 128:(j + 1) * 128], xs_[:, f * 128:(f + 1) * 128], identf)
            cp(hF[:, half * 4:half * 4 + 4, tt_i * 128:(tt_i + 1) * 128],
               pb.rearrange("p (j t) -> p j t", j=4), "act" if half else "dve")
    A.release(m0)

    def rmsnorm_to_xnT(nw_of_f):
        m = A.mark()
        sq = A.alloc(8 * 512, BF16).rearrange("p (f t) -> p f t", f=8)
        rstd = A.alloc(512, F32)
        for tc in range(4):
            tsl = slice(tc * 512, (tc + 1) * 512)
            for f in range(8):
                act(sq[:, f, :], hF[:, f, tsl], AF.Square)
            pb = gbank()
            for f in range(8):
                mm(pb, onesb, sq[:, f, :], start=(f == 0), stop=(f == 7))
            act(rstd, pb, AF.Sqrt, scale=1.0 / D, bias=1e-5)
            recip(rstd, rstd)
            for f in range(8):
                stt(xnT[:, f, tsl], hF[:, f, tsl], nw_of_f(f), rstd, ALU.mult, ALU.mult)
        A.release(m)

    def load_w_cols(src2d, c0, ncols, q="pool"):
        wb = next_wbuf()
        dma(q, wb[:, :, 0:ncols], src2d.rearrange("(k p) c -> p k c", p=128)[:, :, c0:c0 + ncols])
        return wb

    def proj_fm(wb, cw, ncw, evac):
        for tc in range(4):
            pb = gbank()
            for k in range(8):
                mm(pb[0:ncw, :], wb[:, k, cw:cw + ncw], xnT[:, k, tc * 512:(tc + 1) * 512], start=(k == 0),
                   stop=(k == 7))
            evac(tc, pb[0:ncw, :])

    def proj_tm(wb, cw, ncw, tile_i, tok0, out_psum):
        for k in range(8):
            mm(out_psum, xnT[:, k, tok0:tok0 + 128], wb[:, k, cw:cw + ncw], start=(k == 0), stop=(k == 7))

    def outproj_acc(l, g, ymT):
        m = A.mark()
        wo = A.alloc(2 * 1024, BF16).rearrange("p (k c) -> p k c", k=2)
        dma("pool", wo, w_out_d[l, g * 256:(g + 1) * 256, :].rearrange("(k p) c -> p k c", p=128))
        for ot in range(8):
            for tc in range(4):
                tsl = slice(tc * 512, (tc + 1) * 512)
                pb = gbank()
                for k in range(2):
                    mm(pb, wo[:, k, ot * 128:(ot + 1) * 128], ymT[:, k, tsl], start=(k == 0), stop=(k == 1))
                tt(hF[:, ot, tsl], pb, hF[:, ot, tsl], ALU.add)
        A.release(m)

    def dump_dbg(g, ymT):
        if dbg_d is None or dbg.get('nodump'):
            return
        m = A.mark()
        st = A.alloc(S, F32)
        for k in range(2):
            cp(st, ymT[:, k, :], "pool")
            dma("sp", dbg_d[g * 256 + k * 128: g * 256 + (k + 1) * 128, :], st)
        A.release(m)

    def conv4(stage, wcol, bias_ap, out, func=None, out2=None):
        m = A.mark()
        acc = A.alloc(S, F32)
        ts(acc, stage[:, 0:S], wcol(0), ALU.mult)
        for k in range(1, 4):
            stt(acc, stage[:, k:k + S], wcol(k), acc, ALU.mult, ALU.add)
        if func is None:
            ts(out, acc, bias_ap, ALU.add)
            if out2 is not None:
                cp(out2, out, "pool")
        else:
            act(out, acc, func, bias=bias_ap)
        A.release(m)

    def ssd_mixer(l):
        m_ssd = A.mark()
        xs_tok = A.alloc(NT * 256, BF16).rearrange("p (i c) -> p i c", i=NT)
        B_tok = A.alloc(NT * 256, BF16).rearrange("p (i c) -> p i c", i=NT)
        BT = A.alloc(2 * S, BF16).rearrange("p (k t) -> p k t", k=2)
        CT = A.alloc(2 * S, BF16).rearrange("p (k t) -> p k t", k=2)
        o_alog, o_dtb = RB_OFF["alog"][0], RB_OFF["dtb"][0]
        m1 = A.mark()
        stage = A.alloc(S + 3, F32)
        xsT = A.alloc(2 * S, BF16).rearrange("p (k t) -> p k t", k=2)
        memset(stage[:, 0:2], 0.0)
        memset(stage[:, S + 2:S + 3], 0.0)
        for half in range(2):
            wb = load_w_cols(w_in_d[l], SSD0 + 256 + half * 384, 384)
            for j in range(3):
                c = half * 3 + j
                proj_fm(wb, j * 128, 128, lambda tc, pb: cp(stage[:, 2 + tc * 512: 2 + (tc + 1) * 512], pb, "act"))
                dest = (xsT[:, c, :] if c < 2 else (BT[:, c - 2, :] if c < 4 else CT[:, c - 4, :]))
                conv4(stage, lambda k, c=c: P_(l, "scw", k * 6 + c), P_(l, "scb", c), dest, func=AF.Silu)
        for i in range(NT):
            pbb = gbank().bitcast(BF16)
            for k in range(2):
                tr(pbb[:, k * 128:(k + 1) * 128], xsT[:, k, i * 128:(i + 1) * 128], identb)
                tr(pbb[:, 256 + k * 128:256 + (k + 1) * 128], BT[:, k, i * 128:(i + 1) * 128], identb)
            cp(xs_tok[:, i, :], pbb[:, 0:256], "act")
            cp(B_tok[:, i, :], pbb[:, 256:512], "act")
        A.release(m1)
        z_tok = A.alloc(NT * 256, BF16).rearrange("p (i c) -> p i c", i=NT)
        y_tok = A.alloc(NT * 256, F32).rearrange("p (i c) -> p i c", i=NT)
        dt_tok = A.alloc(NT * 8, F32).rearrange("p (i c) -> p i c", i=NT)
        adt = A.alloc(NT * 8, F32).rearrange("p (i c) -> p i c", i=NT)
        Abc = A.alloc(8, F32)
        act(Abc, rbt[:, o_alog:o_alog + 8], AF.Exp)
        ts(Abc, Abc, -1.0, ALU.mult)
        wz = load_w_cols(w_in_d[l], SSD0, 256)
        for i in range(NT):
            pb = gbank()
            proj_tm(wz, 0, 256, i, i * 128, pb[:, 0:256])
            act(z_tok[:, i, :], pb[:, 0:256], AF.Silu)
        wdt = load_w_cols(w_in_d[l], SSD0 + 1024, 8)
        for i in range(NT):
            pb = gbank()
            proj_tm(wdt, 0, 8, i, i * 128, pb[:, 0:8])
            tt(dt_tok[:, i, :], pb[:, 0:8], rbt[:, o_dtb:o_dtb + 8], ALU.add)
        dtf = dt_tok.rearrange("p i c -> p (i c)")
        act(dtf, dtf, AF.Exp)
        act(dtf, dtf, AF.Ln, bias=1.0)
        tt(adt, dt_tok, Abc.unsqueeze(1).to_broadcast([128, NT, 8]), ALU.mult)
        m2 = A.mark()
        Hs = [A.alloc(256, F32) for _ in range(2)]
        Hb = [A.alloc(256, BF16) for _ in range(2)]
        for d in range(2):
            memset(Hs[d], 0.0)
            memset(Hb[d], 0.0)
        class _T:
            pass
        tmps = []
        for d in range(2):
            T_ = _T()
            T_.rhs_seg = A.alloc(512, F32)
            T_.dec = A.alloc(512, BF16)
            T_.scT = A.alloc(512, BF16)
            T_.xdt = A.alloc(256, BF16)
            T_.xde = A.alloc(256, BF16)
            T_.ecs = A.alloc(8, F32)
            T_.dte = A.alloc(8, F32)
            T_.etot = A.alloc(8, F32)
            tmps.append(T_)

        def ssd_step(d, c, first):
            T_ = tmps[d]
            rhs_seg, dec, scT, xdt, xde, ecs, dte, etot = (T_.rhs_seg, T_.dec, T_.scT, T_.xdt, T_.xde, T_.ecs,
                                                           T_.dte, T_.etot)
            csl = slice(c * 128, (c + 1) * 128)
            a4 = adt[:, c, d * 4:(d + 1) * 4]
            msk = mUI if d == 0 else mLI
            pS = gbank()
            mm(pS[:, 0:4], msk, a4)
            mm(pS[:, 8:12], onesf, a4)
            act(ecs[:, 0:4], pS[:, 0:4], AF.Exp)
            cp(dte[:, 4:8], pS[:, 0:4], "dve")
            tt(dte[:, 0:4], pS[:, 8:12], dte[:, 4:8], ALU.subtract)
            act(dte[:, 0:4], dte[:, 0:4], AF.Exp)
            act(etot[:, 0:4], pS[:, 8:12], AF.Exp)
            tt(rhs_seg.rearrange("p (h l) -> p h l", h=4),
               msk.unsqueeze(1).to_broadcast([128, 4, 128]),
               a4.unsqueeze(2).to_broadcast([128, 4, 128]), ALU.mult)
            dt4 = dt_tok[:, c, d * 4:(d + 1) * 4]
            tt(xdt.rearrange("p (h q) -> p h q", h=4), xs_tok[:, c, :].rearrange("p (h q) -> p h q", h=4),
               dt4.unsqueeze(2).to_broadcast([128, 4, 64]), ALU.mult)
            yield
            pD = gbank()
            mm(pD, (mL if d == 0 else mU), rhs_seg, start=True, stop=False)
            mm(pD, identb, (negL4 if d == 0 else negU4), start=False, stop=True)
            act(dec, pD, AF.Exp)
            tt(xde.rearrange("p (h q) -> p h q", h=4), xdt.rearrange("p (h q) -> p h q", h=4),
               dte[:, 0:4].unsqueeze(2).to_broadcast([128, 4, 64]), ALU.mult)
            yield
            pG = gbank()
            for g in range(2):
                mm(pG[:, g * 128:(g + 1) * 128], BT[:, g, csl], CT[:, g, csl])
            tt(scT.rearrange("p (g r l) -> p g r l", g=2, r=2),
               dec.rearrange("p (g r l) -> p g r l", g=2, r=2),
               pG[:, 0:256].rearrange("p (g l) -> p g l", g=2).unsqueeze(2).to_broadcast([128, 2, 2, 128]),
               ALU.mult)
            yield
            pY = gbank()
            for h in range(4):
                mm(pY[:, h * 64:(h + 1) * 64], scT[:, h * 128:(h + 1) * 128], xdt[:, h * 64:(h + 1) * 64])
                mm(pY[:, 256 + h * 64:256 + (h + 1) * 64], CT[:, h // 2, csl], Hb[d][:, h * 64:(h + 1) * 64])
            pH = gbank()
            for h in range(4):
                mm(pH[:, h * 64:(h + 1) * 64], B_tok[:, c, (h // 2) * 128:(h // 2 + 1) * 128],
                   xde[:, h * 64:(h + 1) * 64])
            if first:
                cp(y_tok[:, c, :], pY[:, 0:256], "act")
            else:
                tt(y_tok[:, c, :], pY[:, 0:256], y_tok[:, c, :], ALU.add)
            for h in range(4):
                stt(y_tok[:, c, h * 64:(h + 1) * 64], pY[:, 256 + h * 64:256 + (h + 1) * 64], ecs[:, h:h + 1],
                    y_tok[:, c, h * 64:(h + 1) * 64], ALU.mult, ALU.add)
            for h in range(4):
                stt(Hs[d][:, h * 64:(h + 1) * 64], Hs[d][:, h * 64:(h + 1) * 64], etot[:, h:h + 1],
                    pH[:, h * 64:(h + 1) * 64], ALU.mult, ALU.add)
            cp(Hb[d], Hs[d], "act")
            yield

        for st_i in range(NT):
            first = st_i < NT // 2
            gens = [ssd_step(0, st_i, first), ssd_step(1, NT - 1 - st_i, first)]
            alive = True
            while alive:
                alive = False
                for g_ in gens:
                    try:
                        next(g_)
                        alive = True
                    except StopIteration:
                        pass
        A.release(m2)
        ymT = A.alloc(2 * S, BF16).rearrange("p (k t) -> p k t", k=2)
        m3 = A.mark()
        o_dsk, o_snw = RB_OFF["dsk"][0], RB_OFF["snw"][0]
        tmp = A.alloc(256, F32)
        tmpb = A.alloc(256, BF16)
        ss = A.alloc(2, F32)
        for i in range(NT):
            tt(tmp.rearrange("p (h q) -> p h q", h=4), xs_tok[:, i, :].rearrange("p (h q) -> p h q", h=4),
               rbt[:, o_dsk:o_dsk + 4].unsqueeze(2).to_broadcast([128, 4, 64]), ALU.mult)
            tt(tmp, tmp, y_tok[:, i, :], ALU.add)
            tt(tmp, tmp, z_tok[:, i, :], ALU.mult)
            tt(y_tok[:, i, :], tmp, tmp, ALU.mult)
            reduce_sum(ss[:, 0:1], y_tok[:, i, :])
            act(ss[:, 1:2], ss[:, 0:1], AF.Sqrt, scale=1.0 / 256, bias=1e-5)
            recip(ss[:, 1:2], ss[:, 1:2])
            stt(tmpb, tmp, ss[:, 1:2], rbt[:, o_snw:o_snw + 256], ALU.mult, ALU.mult)
            pbb = gbank().bitcast(BF16)
            for k in range(2):
                tr(pbb[:, k * 128:(k + 1) * 128], tmpb[:, k * 128:(k + 1) * 128], identb)
            cp(ymT[:, :, i * 128:(i + 1) * 128], pbb[:, 0:256].rearrange("p (k t) -> p k t", k=2), "act")
        A.release(m3)
        dump_dbg(1, ymT)
        outproj_acc(l, 1, ymT)
        A.release(m_ssd)

    def lru_mixer(l):
        m_l = A.mark()
        ymT = A.alloc(2 * S, BF16).rearrange("p (k t) -> p k t", k=2)
        gate = A.alloc(2 * S, BF16).rearrange("p (k t) -> p k t", k=2)
        xf = A.alloc(2 * S, F32).rearrange("p (k t) -> p k t", k=2)
        xfb = A.alloc(2 * S, BF16).rearrange("p (k t) -> p k t", k=2)
        m_st = A.mark()
        stage = A.alloc(S + 3, F32)
        memset(stage[:, 0:2], 0.0)
        memset(stage[:, S + 2:S + 3], 0.0)
        wb = load_w_cols(w_in_d[l], LRU0, 512)
        for c in range(2):
            proj_fm(wb, c * 128, 128,
                    lambda tc, pb, c=c: act(gate[:, c, tc * 512:(tc + 1) * 512], pb, AF.Gelu_apprx_tanh))
        for c in range(2):
            proj_fm(wb, 256 + c * 128, 128,
                    lambda tc, pb: cp(stage[:, 2 + tc * 512: 2 + (tc + 1) * 512], pb, "act"))
            conv4(stage, lambda k, c=c: P_(l, "lcw", k * 2 + c), P_(l, "lcb", c), xf[:, c, :], out2=xfb[:, c, :])
        A.release(m_st)
        m_tmp = A.mark()
        wg = A.alloc(8 * 128, BF16).rearrange("p (a c) -> p a c", a=8)
        dma("pool", wg, lbd_d[l].rearrange("w d t p c -> p (w d t) c"))
        rec = A.alloc(S, F32)
        inp = A.alloc(S, F32)
        aa = A.alloc(S, F32)
        hsum = A.alloc(S, F32)
        for pt in range(2):
            for d in range(2):
                for which, dst, bname in ((0, rec, "lab"), (1, inp, "lxb")):
                    for tc in range(4):
                        tsl = slice(tc * 512, (tc + 1) * 512)
                        pb = gbank()
                        mm(pb, wg[:, which * 4 + d * 2 + pt, :], xfb[:, pt, tsl])
                        act(dst[:, tsl], pb, AF.Sigmoid, bias=P_(l, bname, d * 2 + pt))
                lc = ppd[:, l * 32 + 11 + d * 2 + pt: l * 32 + 12 + d * 2 + pt]
                lc2 = ppd[:, l * 32 + 15 + d * 2 + pt: l * 32 + 16 + d * 2 + pt]
                act(aa, rec, AF.Exp, scale=lc)
                act(rec, rec, AF.Exp, scale=lc2)
                act(rec, rec, AF.Sqrt, scale=-1.0, bias=1.0)
                tt(inp, inp, rec, ALU.mult)
                tt(inp, inp, xf[:, pt, :], ALU.mult)
                if d == 0:
                    scan(hsum, aa, inp)
                else:
                    scan(rec[:, ::-1], aa[:, ::-1], inp[:, ::-1])
                    tt(hsum, hsum, rec, ALU.add)
                    tt(ymT[:, pt, :], hsum, gate[:, pt, :], ALU.mult)
        A.release(m_tmp)
        dump_dbg(2, ymT)
        outproj_acc(l, 2, ymT)
        A.release(m_l)

    def na_mixer(l):
        m_n = A.mark()
        ymT = A.alloc(2 * S, BF16).rearrange("p (k t) -> p k t", k=2)
        m_na_tmp = A.mark()
        qm = [A.alloc(S, BF16) for _ in range(4)]
        for h in range(4):
            z0 = 64 - (h % 2) * 64
            memset(qm[h][z0:z0 + 64, :], 0.0)
        kT = A.alloc(2 * S, BF16).rearrange("p (k t) -> p k t", k=2)
        va = [A.alloc(NT * 260, BF16).rearrange("p (i h q) -> p i h q", i=NT, h=4) for _ in range(2)]
        t2 = A.alloc(14 * 256, F32).rearrange("p (r x) -> p r x", r=14)
        dma("sp", t2, t2_d[l].rearrange("p (r x) -> p r x", r=14))
        wb = load_w_cols(w_in_d[l], NA0, 512)
        def q_evac(tc, pb, c):
            tsl = slice(tc * 512, (tc + 1) * 512)
            act(qm[2 * c][0:64, tsl], pb[0:64, :], AF.Copy, scale=0.125)
            act(qm[2 * c + 1][64:128, tsl], pb[64:128, :], AF.Copy, scale=0.125)
        for c in range(2):
            proj_fm(wb, c * 128, 128, lambda tc, pb, c=c: q_evac(tc, pb, c))
        for c in range(2):
            proj_fm(wb, 256 + c * 128, 128, lambda tc, pb, c=c: cp(kT[:, c, tc * 512:(tc + 1) * 512], pb, "act"))
        wv = load_w_cols(w_in_d[l], NA0 + 512, 256)
        for al in range(2):
            memset(va[al][:, :, :, 64:65], 1.0)
            for i in range(NT - al):
                pb = gbank()
                proj_tm(wv, 0, 256, i, i * 128 + al * 64, pb[:, 0:256])
                cp(va[al][:, i, :, 0:64], pb[:, 0:256].rearrange("p (h q) -> p h q", h=4), "act" if i % 2 else "dve")
        NSLOT_NA = 2
        slots = []
        for _ in range(NSLOT_NA):
            slots.append((A.alloc(1024, F32), A.alloc(1024, BF16), A.alloc(4, F32), A.alloc(256, BF16, parts=64)))

        def na_row(r, slot):
            ein, pT, rc, ytok = slot
            kr0 = min(max(r - 4, 0), 24)
            dr0 = kr0 - r + 7
            al = kr0 % 2
            ti0 = kr0 // 2
            q0 = r * 64
            pS = [gbank(), gbank()]
            for kt in range(4):
                k0 = kr0 * 64 + kt * 128
                for h in range(4):
                    col = (kt * 4 + h) * 64
                    mm(pS[col // 512][:, col % 512: col % 512 + 64], kT[:, h // 2, k0:k0 + 128],
                       qm[h][:, q0:q0 + 64])
            for half in range(2):
                tt(ein[:, half * 512:(half + 1) * 512].rearrange("p (a x) -> p a x", a=2),
                   pS[half].rearrange("p (a x) -> p a x", a=2),
                   t2[:, dr0 + 4 * half: dr0 + 4 * half + 3: 2, :], ALU.add)
            yield
            act(pT, ein, AF.Exp)
            yield
            pO = gbank()
            for h in range(4):
                for kt in range(4):
                    col = (kt * 4 + h) * 64
                    mm(pO[0:64, h * 65:(h + 1) * 65], pT[:, col:col + 64], va[al][:, ti0 + kt, h, :],
                       start=(kt == 0), stop=(kt == 3))
            pO3 = pO[0:64, 0:260].rearrange("p (h q) -> p h q", h=4)
            recip(rc[0:64, :], pO3[:, :, 64])
            tt(ytok.rearrange("p (h q) -> p h q", h=4), pO3[:, :, 0:64],
               rc[0:64, :].unsqueeze(2).to_broadcast([64, 4, 64]), ALU.mult)
            yield
            pbb = gbank().bitcast(BF16)
            for k in range(2):
                tr(pbb[:, k * 64:(k + 1) * 64], ytok[:, k * 128:(k + 1) * 128], identb[0:64, 0:64])
            cp(ymT[:, :, q0:q0 + 64], pbb[:, 0:128].rearrange("p (k t) -> p k t", k=2), "act")
            yield

        for r0 in range(0, 32, NSLOT_NA):
            gens = [na_row(r, slots[i]) for i, r in enumerate(range(r0, min(32, r0 + NSLOT_NA)))]
            alive = True
            while alive:
                alive = False
                for g_ in gens:
                    try:
                        next(g_)
                        alive = True
                    except StopIteration:
                        pass
        A.release(m_na_tmp)
        dump_dbg(3, ymT)
        outproj_acc(l, 3, ymT)
        A.release(m_n)

    def rwkv_mixer(l):
        m_r = A.mark()
        rT = A.alloc(2 * S, BF16).rearrange("p (k t) -> p k t", k=2)
        kT = A.alloc(2 * S, BF16).rearrange("p (k t) -> p k t", k=2)
        wdT = A.alloc(S, BF16)
        adT = A.alloc(S, BF16)
        gdT = A.alloc(S, BF16)
        v_tok = A.alloc(NT * 256, BF16).rearrange("p (i c) -> p i c", i=NT)
        wup = A.alloc(256, BF16)
        aup = A.alloc(256, BF16)
        gup = A.alloc(256, BF16)
        dma("pool", wup, wup_d[l].rearrange("d r c -> (d r) c"))
        dma("pool", aup, aup_d[l].rearrange("d r c -> (d r) c"))
        dma("pool", gup, gup_d[l])
        m1 = A.mark()
        stage = A.alloc(S + 2, F32)
        t1 = A.alloc(S, F32)
        vT = A.alloc(2 * S, BF16).rearrange("p (k t) -> p k t", k=2)
        memset(stage[:, 0:1], 0.0)
        memset(stage[:, S + 1:S + 2], 0.0)
        dests = [rT[:, 0, :], rT[:, 1, :], kT[:, 0, :], kT[:, 1, :], vT[:, 0, :], vT[:, 1, :], None, None, gdT]
        for grp in range(3):
            wb = load_w_cols(w_in_d[l], RW0 + grp * 384, 384)
            for j in range(3):
                c = grp * 3 + j
                proj_fm(wb, j * 128, 128, lambda tc, pb: cp(stage[:, 1 + tc * 512: 1 + (tc + 1) * 512], pb, "act"))
                c0 = ppd[:, l * 32 + c: l * 32 + c + 1]
                ts(t1, stage[:, 1:S + 1], c0, ALU.mult)
                stt(t1, stage[:, 0:S], P_(l, "mu0", c), t1, ALU.mult, ALU.add)
                if c == 6:
                    stt(t1, stage[:, 2:S + 2], P_(l, "mu1", c), t1, ALU.mult, ALU.add)
                    act(wdT, t1, AF.Tanh)
                elif c == 7:
                    stt(t1, stage[:, 2:S + 2], P_(l, "mu1", c), t1, ALU.mult, ALU.add)
                    cp(adT, t1, "act")
                elif c == 8:
                    stt(t1, stage[:, 2:S + 2], P_(l, "mu1", c), t1, ALU.mult, ALU.add)
                    act(dests[c], t1, AF.Sigmoid)
                else:
                    stt(dests[c], stage[:, 2:S + 2], P_(l, "mu1", c), t1, ALU.mult, ALU.add)
        for i in range(NT):
            pbb = gbank().bitcast(BF16)
            for k in range(2):
                tr(pbb[:, k * 128:(k + 1) * 128], vT[:, k, i * 128:(i + 1) * 128], identb)
            cp(v_tok[:, i, :], pbb[:, 0:256], "act")
        A.release(m1)
        ymT = A.alloc(2 * S, BF16).rearrange("p (k t) -> p k t", k=2)
        y_tok = A.alloc(NT * 256, BF16).rearrange("p (i c) -> p i c", i=NT)
        bonus = A.alloc(NT * 4, F32).rearrange("p (i c) -> p i c", i=NT)

        XA_ = Arena(arena_ap, 8 * S // 2, start=xn_region_start_f32)

        XB_ = Arena(arena_ap, 2 * 8 * 512 // 2, start=wb_region_start_f32)

        class _XA:
            @staticmethod
            def alloc(cols, dtype):
                for ar in (XA_, XB_):
                    if ar.top + (cols * DSZ[dtype] + CELL - 1) // CELL * CELL <= ar.limit:
                        return ar.alloc(cols, dtype)
                return A.alloc(cols, dtype)
        XA = _XA

        class DirBuf:
            pass
        G_SETS = 3
        STs, STbs = [], []
        for d in range(2):
            st_ = XA.alloc(2 * 64, F32).rearrange("p (k i) -> p k i", k=2)
            stb_ = XA.alloc(2 * 64, BF16).rearrange("p (k i) -> p k i", k=2)
            memset(st_, 0.0)
            memset(stb_, 0.0)
            STs.append(st_)
            STbs.append(stb_)
        Xbs = [XA.alloc(256, BF16) for _ in range(2)]
        Ubs = [XA.alloc(256, BF16) for _ in range(2)]
        dbs = []
        for g_ in range(G_SETS):
            b = DirBuf()
            b.sa = XA.alloc(256, F32)
            b.sig = b.sa[:, 0:128]
            b.al = b.sa[:, 128:256]
            b.ein = [XA.alloc(129, F32) for _ in range(2)]
            for nm in ("eni", "eh", "kd", "bb", "kks", "kk"):
                setattr(b, nm, XA.alloc(128, BF16))
            b.sq = b.eh
            b.Bh = [XA.alloc(128, BF16) for _ in range(2)]
            b.Kh = [XA.alloc(128, BF16) for _ in range(2)]
            b.rkd = [XA.alloc(128, BF16) for _ in range(2)]
            b.Rt = [XA.alloc(128, BF16) for _ in range(2)]
            b.Qk = [XA.alloc(128, BF16) for _ in range(2)]
            b.Btm = [[XA.alloc(128, BF16) for _ in range(2)] for _ in range(2)]
            b.Qkm = [[XA.alloc(128, BF16) for _ in range(2)] for _ in range(2)]
            b.Ktm = [[XA.alloc(128, BF16) for _ in range(2)] for _ in range(2)]
            b.Rtm = [[XA.alloc(128, BF16) for _ in range(2)] for _ in range(2)]
            for grp in (b.Btm, b.Qkm, b.Ktm, b.Rtm):
                for hp_ in range(2):
                    for h2 in range(2):
                        z0 = 64 - h2 * 64
                        memset(grp[hp_][h2][z0:z0 + 64, :], 0.0)
            b.wdm = [XA.alloc(128, BF16) for _ in range(2)]
            b.adm = [XA.alloc(128, BF16) for _ in range(2)]
            for d_ in range(2):
                z0 = 64 - d_ * 64
                memset(b.wdm[d_][z0:z0 + 64, :], 0.0)
                memset(b.adm[d_][z0:z0 + 64, :], 0.0)
            b.BhT = XA.alloc(256, BF16)
            b.KhT = XA.alloc(256, BF16)
            b.Mf = XA.alloc(512, BF16)
            b.T = XA.alloc(512, BF16)
            b.Z = XA.alloc(512, BF16)
            b.W = XA.alloc(512, BF16)
            b.Akk = XA.alloc(512, BF16)
            b.Arb = XA.alloc(512, BF16)
            b.Ark = XA.alloc(512, BF16)
            dbs.append(b)
        gtmp = XA.alloc(256, F32)
        gt2 = XA.alloc(256, F32)
        gst = XA.alloc(16, F32)
        gout = gt2.bitcast(BF16)[:, 0:256]

        def finalize_chunk(c):
            y4 = gtmp.rearrange("p (h q) -> p h q", h=4)
            cp(gtmp, y_tok[:, c, :], "pool")
            reduce_sum(gst[:, 0:4], y4)
            tt(gt2, gtmp, gtmp, ALU.mult)
            reduce_sum(gst[:, 4:8], gt2.rearrange("p (h q) -> p h q", h=4))
            ts(gst[:, 0:4], gst[:, 0:4], 1.0 / 64, ALU.mult)
            tt(gst[:, 8:12], gst[:, 0:4], gst[:, 0:4], ALU.mult)
            stt(gst[:, 4:8], gst[:, 4:8], 1.0 / 64, gst[:, 8:12], ALU.mult, ALU.subtract)
            act(gst[:, 4:8], gst[:, 4:8], AF.Ln, bias=64e-5)
            act(gst[:, 4:8], gst[:, 4:8], AF.Exp, scale=-0.5)
            tt(y4, y4, gst[:, 0:4].unsqueeze(2).to_broadcast([128, 4, 64]), ALU.subtract)
            tt(y4, y4, gst[:, 4:8].unsqueeze(2).to_broadcast([128, 4, 64]), ALU.mult)
            o_w, o_b = RB_OFF["gnw"][0], RB_OFF["gnb"][0]
            tt(gtmp, gtmp, rbt[:, o_w:o_w + 256], ALU.mult)
            tt(gtmp, gtmp, rbt[:, o_b:o_b + 256], ALU.add)
            tt(gt2.rearrange("p (h q) -> p h q", h=4), v_tok[:, c, :].rearrange("p (h q) -> p h q", h=4),
               bonus[:, c, :].unsqueeze(2).to_broadcast([128, 4, 64]), ALU.mult)
            tt(gtmp, gtmp, gt2, ALU.add)
            pg = gbank()
            mm(pg[:, 0:256], gdT[:, c * 128:(c + 1) * 128], gup)
            tt(gout, gtmp, pg[:, 0:256], ALU.mult)
            pbb = gbank().bitcast(BF16)
            for k in range(2):
                tr(pbb[:, k * 128:(k + 1) * 128], gout[:, k * 128:(k + 1) * 128], identb)
            cp(ymT[:, :, c * 128:(c + 1) * 128], pbb[:, 0:256].rearrange("p (k t) -> p k t", k=2), "act")

        seen_y = [False] * NT
        seen_bonus = [False] * NT

        def step(d, c, b):
            csl = slice(c * 128, (c + 1) * 128)
            fwd = (d == 0)
            ST, STb, Xb, Ub = STs[d], STbs[d], Xbs[d], Ubs[d]
            m_strict_neg = nmU if fwd else nmL
            m_strict = mU if fwd else mL
            m_incl = mUI if fwd else mLI
            m_incl_neg = nmUI if fwd else nmLI

            def heads(fn):
                for h in range(4):
                    fn(h, h // 2, slice((h % 2) * 64, (h % 2) * 64 + 64), slice(h * 128, (h + 1) * 128))
            dsl = slice(d * 64, d * 64 + 64)
            cp(b.wdm[d][dsl, :], wdT[dsl, csl], "dve")
            cp(b.adm[d][dsl, :], adT[dsl, csl], "dve")
            for hp in range(2):
                pz = gbank()
                mm(pz[:, 0:128], wup[:, hp * 128:(hp + 1) * 128], b.wdm[d])
                mm(pz[:, 128:256], aup[:, hp * 128:(hp + 1) * 128], b.adm[d])
                nw0 = ppd[:, l * 32 + 19 + d * 2 + hp: l * 32 + 20 + d * 2 + hp]
                na0 = ppd[:, l * 32 + 23 + d * 2 + hp: l * 32 + 24 + d * 2 + hp]
                act(b.sig, pz[:, 0:128], AF.Exp, scale=-1.0, bias=nw0)
                act(b.al, pz[:, 128:256], AF.Exp, scale=-1.0, bias=na0)
                act(b.sa, b.sa, AF.Ln, bias=1.0)
                act(b.sa, b.sa, AF.Exp, scale=-1.0)
                if fwd:
                    cs = b.ein[hp][:, 1:129]
                    scan(cs, onesf, b.sig)
                    pad = b.ein[hp][:, 0:1]
                    eex = b.ein[hp][:, 0:128]
                    etot = b.ein[hp][:, 128:129]
                else:
                    cs = b.ein[hp][:, 0:128]
                    scan(cs[:, ::-1], onesf, b.sig[:, ::-1])
                    pad = b.ein[hp][:, 128:129]
                    eex = b.ein[hp][:, 1:129]
                    etot = b.ein[hp][:, 0:1]
                ts(b.kks, kT[:, hp, csl], P_(l, "kk", hp), ALU.mult)
                tt(b.sq, b.kks, b.kks, ALU.mult)
                pn = gbank()
                mm(pn[:, 0:128], blkb, b.sq)
                act(b.eni, cs, AF.Exp, scale=DECAY)
                act(cs, cs, AF.Exp, scale=-DECAY)
                act(b.kk, pn[:, 0:128], AF.Ln, bias=1e-12)
                act(b.kk, b.kk, AF.Exp, scale=-0.5)
                yield
                memset(pad, 1.0, "pool")
                ts(b.eh, b.eni, etot, ALU.mult)
                tt(b.kk, b.kk, b.kks, ALU.mult)
                omka = ppd[:, l * 32 + 9 + hp: l * 32 + 10 + hp]
                ts(b.kd, b.al, P_(l, "ka", hp), ALU.mult, omka, ALU.add)
                tt(b.kd, b.kd, kT[:, hp, csl], ALU.mult)
                tt(b.bb, b.al, b.kk, ALU.mult)
                tt(b.Rt[hp], rT[:, hp, csl], cs, ALU.mult)
                tt(b.Qk[hp], b.kk, eex, ALU.mult)
                for h2 in range(2):
                    pr = slice(h2 * 64, h2 * 64 + 64)
                    tt(b.Btm[hp][h2][pr, :], b.bb[pr, :], b.eni[pr, :], ALU.mult)
                    tt(b.Ktm[hp][h2][pr, :], b.kd[pr, :], b.eni[pr, :], ALU.mult)
                    cp(b.Qkm[hp][h2][pr, :], b.Qk[hp][pr, :], "pool")
                    cp(b.Rtm[hp][h2][pr, :], b.Rt[hp][pr, :], "pool")
                stt(b.Bh[hp], b.bb, -1.0, b.eh, ALU.mult, ALU.mult)
                tt(b.Kh[hp], b.kd, b.eh, ALU.mult)
                stt(b.rkd[hp], rT[:, hp, csl], P_(l, "rk", hp), b.kd, ALU.mult, ALU.mult)
                yield
            pbn = gbank()
            pbb = gbank().bitcast(BF16)
            for hp in range(2):
                mm(pbn[:, hp * 2:hp * 2 + 2], b.rkd[hp], headsel)
                tr(pbb[:, hp * 128:(hp + 1) * 128], b.Bh[hp], identb)
                tr(pbb[:, 256 + hp * 128:256 + (hp + 1) * 128], b.Kh[hp], identb)
            if not seen_bonus[c]:
                seen_bonus[c] = True
                cp(bonus[:, c, :], pbn[:, 0:4], "dve")
            else:
                tt(bonus[:, c, :], pbn[:, 0:4], bonus[:, c, :], ALU.add)
            cp(b.BhT, pbb[:, 0:256], "act")
            cp(b.KhT, pbb[:, 256:512], "act")
            for lhs_, rhs_, dst_, msk_ in ((b.Btm, b.Qk, b.Mf, m_strict_neg), (b.Ktm, b.Qk, b.Akk, m_strict),
                                           (b.Btm, b.Rt, b.Arb, m_incl_neg), (b.Ktm, b.Rt, b.Ark, m_incl)):
                pa = gbank()
                heads(lambda h, hp, jr, hs: mm(pa[:, hs], lhs_[hp][h % 2], rhs_[hp]))
                tt(dst_.rearrange("p (h t) -> p h t", h=4), pa.rearrange("p (h t) -> p h t", h=4),
                   msk_.unsqueeze(1).to_broadcast([128, 4, 128]), ALU.mult)
            yield
            for k in range(7):
                Tc = ident4 if k == 0 else b.T
                Zc = ident4 if k == 0 else b.Z
                pa = gbank()
                heads(lambda h, hp, jr, hs: mm(pa[:, hs], b.Mf[:, hs], Tc[:, hs]))
                mk = lvl[:, (k if fwd else 7 + k), :]
                tt(b.W.rearrange("p (h t) -> p h t", h=4), pa.rearrange("p (h t) -> p h t", h=4),
                   mk.unsqueeze(1).to_broadcast([128, 4, 128]), ALU.mult)
                yield
                if k < 6:
                    pt = gbank()
                    mm(pt, identb, Tc, start=True, stop=False, skip=True)
                    heads(lambda h, hp, jr, hs: mm(pt[:, hs], Zc[:, hs], b.W[:, hs], start=False, stop=False,
                                                   skip=True))
                pz = gbank()
                mm(pz, identb, Zc, start=True, stop=False, skip=True)
                heads(lambda h, hp, jr, hs: mm(pz[:, hs], b.W[:, hs], Zc[:, hs], start=False, stop=False, skip=True))
                if k < 6:
                    cp(b.T, pt, "act")
                cp(b.Z, pz, "act")
                yield
            yield "CHAIN"
            px = gbank()

            def xinit(h, hp, jr, hs):
                xs_ = slice(h * 64, (h + 1) * 64)
                mm(px[:, xs_], b.Qkm[hp][h % 2], STb[:, hp, :], start=True, stop=False)
                mm(px[:, xs_], b.Akk[:, hs], v_tok[:, c, xs_], start=False, stop=True)
            heads(xinit)
            cp(Xb, px[:, 0:256], "act")
            pu = gbank()
            heads(lambda h, hp, jr, hs: mm(pu[:, h * 64:(h + 1) * 64], b.Z[:, hs], Xb[:, h * 64:(h + 1) * 64]))
            cp(Ub, pu[:, 0:256], "act")
            py = gbank()

            def yfn(h, hp, jr, hs):
                xs_ = slice(h * 64, (h + 1) * 64)
                mm(py[:, xs_], b.Rtm[hp][h % 2], STb[:, hp, :], start=True, stop=False)
                mm(py[:, xs_], b.Arb[:, hs], Ub[:, xs_], start=False, stop=False)
                mm(py[:, xs_], b.Ark[:, hs], v_tok[:, c, xs_], start=False, stop=True)
            heads(yfn)
            if not seen_y[c]:
                seen_y[c] = True
                cp(y_tok[:, c, :], py[:, 0:256], "act")
            else:
                tt(y_tok[:, c, :], py[:, 0:256], y_tok[:, c, :], ALU.add)
            ph = gbank()
            for h in range(4):
                hp = h // 2
                xs_ = slice(h * 64, (h + 1) * 64)
                mm(ph[:, xs_], b.BhT[:, hp * 128:(hp + 1) * 128], Ub[:, xs_], start=True, stop=False)
                mm(ph[:, xs_], b.KhT[:, hp * 128:(hp + 1) * 128], v_tok[:, c, xs_], start=False, stop=True)
            for h in range(4):
                hp = h // 2
                jr = slice((h % 2) * 64, (h % 2) * 64 + 64)
                etot = (b.ein[hp][jr, 128:129] if fwd else b.ein[hp][jr, 0:1])
                stt(ST[jr, hp, :], ST[jr, hp, :], etot, ph[jr, h * 64:(h + 1) * 64], ALU.mult, ALU.add)
            cp(STb, ST, "act")
            yield

        pending = []
        for st_i in range(NT):
            pending.append((0, st_i, st_i))
            pending.append((1, NT - 1 - st_i, st_i))
        live = []
        free_sets = list(range(G_SETS))
        done = [0, 0]
        ndone_chunk = [0] * NT
        while pending or live:
            while pending and free_sets:
                d_, c_, idx_ = pending.pop(0)
                si = free_sets.pop(0)
                live.append({"g": step(d_, c_, dbs[si]), "d": d_, "c": c_, "idx": idx_, "set": si, "wait": False})
            for it in list(live):
                if it["wait"] and done[it["d"]] < it["idx"]:
                    continue
                try:
                    r_ = next(it["g"])
                    if r_ == "CHAIN":
                        it["wait"] = True
                except StopIteration:
                    live.remove(it)
                    free_sets.append(it["set"])
                    done[it["d"]] += 1
                    ndone_chunk[it["c"]] += 1
                    if ndone_chunk[it["c"]] == 2:
                        finalize_chunk(it["c"])
        dump_dbg(0, ymT)
        outproj_acc(l, 0, ymT)
        A.release(m_r)

    def mlp(l):
        m_m = A.mark()
        mT = A.alloc(8 * S, BF16).rearrange("p (k t) -> p k t", k=8)
        w2b = [A.alloc(8 * 512, BF16).rearrange("p (k c) -> p k c", k=8) for _ in range(2)]
        rls = [A.alloc(512, BF16) for _ in range(2)]
        rl_i = [0]
        for hg in range(4):
            for half in range(2):
                wb = load_w_cols(w1_d[l], hg * 1024 + half * 512, 512)
                for j in range(4):
                    hpt = half * 4 + j
                    for tc in range(4):
                        tsl = slice(tc * 512, (tc + 1) * 512)
                        pb = gbank()
                        for k in range(8):
                            mm(pb, wb[:, k, j * 128:(j + 1) * 128], xnT[:, k, tsl], start=(k == 0), stop=(k == 7))
                        rl = rls[rl_i[0] % 2]
                        rl_i[0] += 1
                        act(rl, pb, AF.Relu)
                        tt(mT[:, hpt, tsl], rl, rl, ALU.mult)
            for half in range(2):
                dma("pool", w2b[half],
                    w2_d[l, hg * 1024:(hg + 1) * 1024, half * 512:(half + 1) * 512].rearrange("(k p) c -> p k c", p=128))
            for ot in range(8):
                for tc in range(4):
                    tsl = slice(tc * 512, (tc + 1) * 512)
                    pb = gbank()
                    for k in range(8):
                        mm(pb, w2b[ot // 4][:, k, (ot % 4) * 128:(ot % 4 + 1) * 128], mT[:, k, tsl], start=(k == 0),
                           stop=(k == 7))
                    tt(hF[:, ot, tsl], pb, hF[:, ot, tsl], ALU.add)
        A.release(m_m)

    mixers = (dbg or {}).get("mixers", "srna") if dbg else "srna"
    for l in range(n_layers):
        dma("sp", rbt, rb_d[l])
        rmsnorm_to_xnT(lambda f, l=l: P_(l, "n1", f))
        if "s" in mixers:
            ssd_mixer(l)
        if "r" in mixers:
            lru_mixer(l)
        if "n" in mixers:
            na_mixer(l)
        if "a" in mixers:
            rwkv_mixer(l)
        if not (dbg and dbg.get("no_mlp")):
            rmsnorm_to_xnT(lambda f, l=l: P_(l, "n2", f))
            mlp(l)

    m0 = A.mark()
    sq = A.alloc(8 * 512, BF16).rearrange("p (f t) -> p f t", f=8)
    rstd = A.alloc(512, F32)
    yn = A.alloc(8 * 512, F32).rearrange("p (f t) -> p f t", f=8)
    ost = [A.alloc(D, F32) for _ in range(2)]
    fo = PP_L * L_ALL
    for tc in range(4):
        tsl = slice(tc * 512, (tc + 1) * 512)
        for f in range(8):
            act(sq[:, f, :], hF[:, f, tsl], AF.Square)
        pb = gbank()
        for f in range(8):
            mm(pb, onesb, sq[:, f, :], start=(f == 0), stop=(f == 7))
        act(rstd, pb, AF.Sqrt, scale=1.0 / D, bias=1e-5)
        recip(rstd, rstd)
        for f in range(8):
            stt(yn[:, f, :], hF[:, f, tsl], pp[:, fo + f:fo + f + 1], rstd, ALU.mult, ALU.mult)
        for ti in range(4):
            o = ost[ti % 2]
            for half in range(2):
                pb2 = gbank()
                for j in range(4):
                    f = half * 4 + j
                    tr(pb2[:, j * 128:(j + 1) * 128], yn[:, f, ti * 128:(ti + 1) * 128], identf)
                cp(o[:, half * 512:(half + 1) * 512], pb2, "act" if half else "dve")
            tok0 = tc * 512 + ti * 128
            dma("sp", out_d[tok0:tok0 + 128, :], o)
    A.release(m0)

    Sc.finish()
    Sc.emit(stack)
    stack.close()
    return nc, Sc, A


_CACHE = {}


def kernel(**inputs):
    inp = {k: np.asarray(v) for k, v in inputs.items()}
    pp, rb, lbd, t2 = build_host_params(inp)
    if "nc" not in _CACHE:
        _CACHE["nc"] = build_program(L_ALL)[0]
    nc = _CACHE["nc"]
    shared = {
        "w_in": np.ascontiguousarray(inp["w_in"], np.float32),
        "w_out": np.ascontiguousarray(inp["w_out"], np.float32),
        "w_mlp1": np.ascontiguousarray(inp["w_mlp1"], np.float32),
        "w_mlp2": np.ascontiguousarray(inp["w_mlp2"], np.float32),
        "rwkv_w_up": np.ascontiguousarray(inp["rwkv_w_up"], np.float32),
        "rwkv_a_up": np.ascontiguousarray(inp["rwkv_a_up"], np.float32),
        "rwkv_g_up": np.ascontiguousarray(inp["rwkv_g_up"], np.float32),
        "pp": pp, "rb": rb, "lbd": lbd, "t2": t2.reshape(L_ALL, 128, 14 * 256), "lvl": build_level_masks(),
    }
    x = np.ascontiguousarray(inp["x"], np.float32)
    in_maps = [dict(shared, x=x[b]) for b in range(8)]
    res = run_bass_kernel_spmd(nc, in_maps, core_ids=list(range(8)))
    return np.stack([np.asarray(r["out"], np.float32) for r in res.results], axis=0)
```

```python
import numpy as np
from contextlib import ExitStack
import concourse.bass as bass
import concourse.mybir as mybir
from concourse.bass_utils import run_bass_kernel_spmd

F32 = mybir.dt.float32
BF16 = mybir.dt.bfloat16
AF = mybir.ActivationFunctionType
ALU = mybir.AluOpType
AX = mybir.AxisListType
DSZ = {F32: 4, BF16: 2}

L_ALL = 4
D = 1024
S = 2048
NT = 16
IN_COLS = 3464
RW0, SSD0, LRU0, NA0 = 0, 1152, 2184, 2696
CELL = 32
NSLOT = 8
ARENA_F32 = 53000
DECAY = 0.6065306597126334
NEG = -30000.0

PP_SPEC = [("n1", 8), ("n2", 8), ("mu0", 9), ("mu1", 9), ("w0", 4), ("a0", 4), ("kk", 2), ("ka", 2),
           ("rk", 2), ("scw", 24), ("scb", 6), ("lcw", 8), ("lcb", 2), ("lab", 4), ("lxb", 4), ("lam", 4)]
PP_OFF = {}
_o = 0
for _n, _k in PP_SPEC:
    PP_OFF[_n] = (_o, _k)
    _o += _k
PP_L = _o
PP_TOT = PP_L * L_ALL + 8
RB_SPEC = [("gnw", 256), ("gnb", 256), ("dtb", 8), ("alog", 8), ("dsk", 4), ("snw", 256)]
RB_OFF = {}
_o = 0
for _n, _k in RB_SPEC:
    RB_OFF[_n] = (_o, _k)
    _o += _k
RB_L = _o


def _colvec(a):
    a = np.asarray(a, np.float32)
    n = a.shape[-1] // 128
    b = a.reshape(a.shape[:-1] + (n, 128))
    return np.moveaxis(b, -1, 0)


def build_host_params(inp):
    pp = np.zeros((128, PP_TOT), np.float32)
    rb = np.zeros((L_ALL, 128, RB_L), np.float32)
    for l in range(L_ALL):
        def put(name, arr):
            o, k = PP_OFF[name]
            pp[:, l * PP_L + o: l * PP_L + o + k] = arr.reshape(128, k)
        put("n1", _colvec(inp["norm1_w"][l]))
        put("n2", _colvec(inp["norm2_w"][l]))
        put("mu0", _colvec(inp["rwkv_shift_mu"][l, 0]))
        put("mu1", _colvec(inp["rwkv_shift_mu"][l, 1]))
        put("w0", _colvec(inp["rwkv_w0"][l]))
        put("a0", _colvec(inp["rwkv_a0"][l]))
        put("kk", _colvec(inp["rwkv_k_k"][l]))
        put("ka", _colvec(inp["rwkv_k_a"][l]))
        put("rk", _colvec(inp["rwkv_r_k"][l].reshape(256)))
        put("scw", _colvec(inp["ssd_conv_w"][l]))
        put("scb", _colvec(inp["ssd_conv_b"][l]))
        put("lcw", _colvec(inp["lru_conv_w"][l]))
        put("lcb", _colvec(inp["lru_conv_b"][l]))
        put("lab", _colvec(inp["lru_gate_a_b"][l]))
        put("lxb", _colvec(inp["lru_gate_x_b"][l]))
        put("lam", _colvec(inp["lru_lambda"][l]))

        def putr(name, vec):
            o, k = RB_OFF[name]
            rb[l, :, o:o + k] = np.broadcast_to(np.asarray(vec, np.float32).reshape(1, k), (128, k))
        putr("gnw", inp["rwkv_gn_w"][l])
        putr("gnb", inp["rwkv_gn_b"][l])
        putr("dtb", inp["ssd_dt_bias"][l].reshape(8))
        putr("alog", inp["ssd_a_log"][l].reshape(8))
        putr("dsk", inp["ssd_d"][l])
        putr("snw", inp["ssd_norm_w"][l])
    pp[:, PP_L * L_ALL:] = _colvec(inp["final_norm_w"])
    lbd = np.zeros((L_ALL, 2, 2, 2, 128, 128), np.float32)
    for wi, nm in enumerate(("lru_gate_a_w", "lru_gate_x_w")):
        w = np.asarray(inp[nm], np.float32)
        for pt in range(2):
            for j in range(2):
                lbd[:, wi, :, pt, j * 64:(j + 1) * 64, j * 64:(j + 1) * 64] = w[:, :, 2 * pt + j]
    rbias = np.asarray(inp["na_rel_bias"], np.float32)
    qc = np.arange(64)
    kc = np.arange(64)
    ws = np.clip(qc - 8, 0, 48)
    inwin = (kc[:, None] >= ws[None, :]) & (kc[:, None] < ws[None, :] + 16)
    ci = np.clip(kc[:, None] - qc[None, :] + 15, 0, 30)
    t2 = np.full((L_ALL, 128, 14, 4, 64), NEG, np.float32)
    for kh in range(2):
        for dr in range(14):
            g = rbias[:, :, dr + kh, :][:, :, ci]
            g = np.where(inwin[None, None], g, np.float32(NEG))
            t2[:, kh * 64:(kh + 1) * 64, dr, :, :] = np.transpose(g, (0, 2, 1, 3))
    return pp, rb, lbd, t2


def build_level_masks():
    t = np.arange(128)
    out = np.zeros((128, 14, 128), np.float32)
    for k in range(7):
        ll = (((t[:, None] >> (k + 1)) == (t[None, :] >> (k + 1))) & (((t[:, None] >> k) & 1) == 1)
              & (((t[None, :] >> k) & 1) == 0))
        out[:, k, :] = ll
        out[:, 7 + k, :] = ll.T
    return out.reshape(128, 14 * 128)


class Sched:
    ISSUERS = ["pe", "act", "dve", "pool", "sp"]

    def __init__(self, nc, arena_name, psum_name):
        self.nc = nc
        self.semnames = ["pe", "act", "dve", "pool"]
        for q in ("sp", "act", "pool"):
            for i in range(NSLOT):
                self.semnames.append(f"{q}q{i}")
        self.sidx = {n: i for i, n in enumerate(self.semnames)}
        ns = len(self.semnames)
        self.ns = ns
        self.spaces = {arena_name: 0, psum_name: 1}
        ncell = [ARENA_F32 * 4 // CELL + 2, 16384 // CELL + 2]
        self.W = [np.zeros((4, c, ns), np.int64) for c in ncell]
        self.R = [np.zeros((4, c, ns), np.int64) for c in ncell]
        self.cnt = {e: 0 for e in ["pe", "act", "dve", "pool"]}
        self.dman = {"sp": 0, "act": 0, "pool": 0}
        self.seen = {e: np.zeros(ns, np.int64) for e in self.ISSUERS}
        self.q = {e: [] for e in self.ISSUERS}
        self.nops = 0
        self.limit = None
        self.trace = []

    def region(self, ap):
        sp = self.spaces.get(ap.tensor.name)
        if sp is None:
            return None
        a = ap.ap
        ds = DSZ[ap.dtype]
        rowlen = a[0][0]
        pcount = a[0][1]
        p0 = ap.offset // rowlen
        c0 = ap.offset % rowlen
        lo = hi = c0
        for step, cnt in a[1:]:
            ext = step * (cnt - 1)
            if ext < 0:
                lo += ext
            else:
                hi += ext
        blo, bhi = lo * ds, (hi + 1) * ds
        if sp == 1:
            blo = blo // 2048 * 2048
            bhi = (bhi + 2047) // 2048 * 2048
        return (sp, p0 // 32, (p0 + pcount + 31) // 32, blo // CELL, (bhi + CELL - 1) // CELL)

    def _need(self, r, w):
        need = np.zeros(self.ns, np.int64)
        rr = [x for x in (self.region(a) for a in r) if x is not None]
        wr = [x for x in (self.region(a) for a in w) if x is not None]
        for (sp, p0, p1, c0, c1) in rr:
            need = np.maximum(need, self.W[sp][p0:p1, c0:c1].max(axis=(0, 1)))
        for (sp, p0, p1, c0, c1) in wr:
            need = np.maximum(need, self.W[sp][p0:p1, c0:c1].max(axis=(0, 1)))
            need = np.maximum(need, self.R[sp][p0:p1, c0:c1].max(axis=(0, 1)))
        return need, rr, wr

    def _mark(self, rr, wr, si, val):
        for (sp, p0, p1, c0, c1) in wr:
            self.W[sp][p0:p1, c0:c1, si] = val
        for (sp, p0, p1, c0, c1) in rr:
            self.R[sp][p0:p1, c0:c1, si] = val

    def op(self, eng, fn, r=(), w=()):
        if self.limit is not None and self.nops >= self.limit:
            return
        need, rr, wr = self._need(r, w)
        if eng == "pe":
            need[self.sidx["pe"]] = 0
        seen = self.seen[eng]
        waits = [(i, int(need[i])) for i in np.nonzero(need > seen)[0]]
        np.maximum(seen, need, out=seen)
        self.cnt[eng] += 1
        if self.limit is not None:
            import sys as _s
            f = _s._getframe(1)
            self.trace.append((self.nops, eng, f.f_code.co_name, [(a.tensor.name, a.offset, a.ap) for a in list(w) + list(r)]))
        self.q[eng].append((waits, fn, self.sidx[eng], 1))
        self._mark(rr, wr, self.sidx[eng], self.cnt[eng])
        self.nops += 1

    def dma(self, queue, out, in_):
        if self.limit is not None and self.nops >= self.limit:
            return
        need, rr, wr = self._need([in_], [out])
        n = self.dman[queue]
        self.dman[queue] = n + 1
        slot, rnd = n % NSLOT, n // NSLOT
        si = self.sidx[f"{queue}q{slot}"]
        need[si] = max(need[si], 16 * rnd)
        seen = self.seen[queue]
        waits = [(i, int(need[i])) for i in np.nonzero(need > seen)[0]]
        np.maximum(seen, need, out=seen)
        fn = (lambda e, out=out, in_=in_: e.dma_start(out=out, in_=in_))
        self.q[queue].append((waits, fn, si, 16))
        self._mark(rr, wr, si, 16 * (rnd + 1))
        self.nops += 1

    def finish(self):
        final = np.zeros(self.ns, np.int64)
        for e in ["pe", "act", "dve", "pool"]:
            final[self.sidx[e]] = self.cnt[e]
        for qn in ("sp", "act", "pool"):
            n = self.dman[qn]
            for s in range(NSLOT):
                k = (n - s + NSLOT - 1) // NSLOT if n > s else 0
                final[self.sidx[f"{qn}q{s}"]] = 16 * k
        self.final = final

    def emit(self, stack):
        nc = self.nc
        sems = [stack.enter_context(nc.semaphore(f"s_{n}")) for n in self.semnames]
        block = stack.enter_context(nc.Block())

        def replay(name, e):
            for waits, fn, si, inc in self.q[name]:
                for (s, v) in waits:
                    e.wait_ge(sems[s], v)
                fn(e).then_inc(sems[si], inc)
            if name == "sp":
                for s in range(self.ns):
                    if self.final[s] > 0:
                        e.wait_ge(sems[s], int(self.final[s]))

        @block.tensor
        def _(e):
            replay("pe", e)

        @block.scalar
        def _(e):
            replay("act", e)

        @block.vector
        def _(e):
            replay("dve", e)

        @block.gpsimd
        def _(e):
            replay("pool", e)

        @block.sync
        def _(e):
            replay("sp", e)


class Arena:
    def __init__(self, base_ap_f32, ncols_f32, start=0):
        self.base = base_ap_f32
        self.lo = start * 4
        self.top = self.lo
        self.limit = (start + ncols_f32) * 4
        self.peak = self.top

    def alloc(self, cols, dtype, parts=128):
        nbytes = cols * DSZ[dtype]
        nbytes = (nbytes + CELL - 1) // CELL * CELL
        off = self.top
        self.top += nbytes
        self.peak = max(self.peak, self.top)
        assert self.top <= self.limit, f"arena overflow {self.top} > {self.limit}"
        ap = self.base[0:parts, off // 4:(off + nbytes) // 4]
        if dtype != F32:
            ap = ap.bitcast(dtype)
        return ap[:, 0:cols]

    def mark(self):
        return self.top

    def release(self, m):
        self.top = m


def build_program(n_layers=L_ALL, dbg=None):
    nc = bass.Bass("TRN2", target_bir_lowering=False)
    dt = nc.dram_tensor
    x_d = dt("x", [S, D], F32, kind="ExternalInput").ap()
    w_in_d = dt("w_in", [L_ALL, D, IN_COLS], F32, kind="ExternalInput").ap()
    w_out_d = dt("w_out", [L_ALL, D, D], F32, kind="ExternalInput").ap()
    w1_d = dt("w_mlp1", [L_ALL, D, 4 * D], F32, kind="ExternalInput").ap()
    w2_d = dt("w_mlp2", [L_ALL, 4 * D, D], F32, kind="ExternalInput").ap()
    wup_d = dt("rwkv_w_up", [L_ALL, 2, 64, 256], F32, kind="ExternalInput").ap()
    aup_d = dt("rwkv_a_up", [L_ALL, 2, 64, 256], F32, kind="ExternalInput").ap()
    gup_d = dt("rwkv_g_up", [L_ALL, 128, 256], F32, kind="ExternalInput").ap()
    pp_d = dt("pp", [128, PP_TOT], F32, kind="ExternalInput").ap()
    rb_d = dt("rb", [L_ALL, 128, RB_L], F32, kind="ExternalInput").ap()
    lbd_d = dt("lbd", [L_ALL, 2, 2, 2, 128, 128], F32, kind="ExternalInput").ap()
    t2_d = dt("t2", [L_ALL, 128, 14 * 256], F32, kind="ExternalInput").ap()
    lvl_d = dt("lvl", [128, 14 * 128], F32, kind="ExternalInput").ap()
    out_d = dt("out", [S, D], F32, kind="ExternalOutput").ap()
    dbg_d = None
    if dbg:
        dbg_d = dt("dbg", [D, S], F32, kind="ExternalOutput").ap()

    stack = ExitStack()
    arena_t = stack.enter_context(nc.sbuf_tensor("arena", [128, ARENA_F32], F32))
    ps_t = stack.enter_context(nc.psum_tensor("ps", [128, 4096], F32))
    arena_ap = arena_t[:, :]
    ps_ap = ps_t[:, :]
    Sc = Sched(nc, "arena", "ps")
    if dbg and dbg.get("limit"):
        Sc.limit = dbg["limit"]
    A = Arena(arena_ap, ARENA_F32)

    def bank(b):
        return ps_ap[:, b * 512:(b + 1) * 512]

    rot = [0]

    def gbank():
        b = rot[0] % 8
        rot[0] += 1
        return bank(b)

    def mm(out, lhsT, rhs, start=True, stop=True, skip=False):
        Sc.op("pe", lambda e: e.matmul(out, lhsT=lhsT, rhs=rhs, start=start, stop=stop, skip_group_check=skip),
              r=[lhsT, rhs], w=[out])

    def tr(out, in_, ident):
        Sc.op("pe", lambda e: e.transpose(out, in_, ident), r=[in_, ident], w=[out])

    def act(out, in_, func, scale=None, bias=None):
        kw = {}
        rd = [in_]
        if scale is not None:
            kw["scale"] = scale
            if not isinstance(scale, (int, float)):
                rd.append(scale)
        if bias is not None:
            kw["bias"] = bias
            if not isinstance(bias, (int, float)):
                rd.append(bias)
        Sc.op("act", lambda e: e.activation(out=out, in_=in_, func=func, **kw), r=rd, w=[out])

    def tt(out, in0, in1, op, eng="dve"):
        Sc.op(eng, lambda e: e.tensor_tensor(out=out, in0=in0, in1=in1, op=op), r=[in0, in1], w=[out])

    def ts(out, in0, s1, op0, s2=None, op1=None, eng="dve"):
        rd = [in0] + [s for s in (s1, s2) if s is not None and not isinstance(s, (int, float))]
        if op1 is None:
            Sc.op(eng, lambda e: e.tensor_scalar(out=out, in0=in0, scalar1=s1, scalar2=None, op0=op0), r=rd, w=[out])
        else:
            Sc.op(eng, lambda e: e.tensor_scalar(out=out, in0=in0, scalar1=s1, scalar2=s2, op0=op0, op1=op1),
                  r=rd, w=[out])

    def stt(out, in0, scalar, in1, op0, op1):
        rd = [in0, in1] + ([] if isinstance(scalar, (int, float)) else [scalar])
        Sc.op("dve", lambda e: e.scalar_tensor_tensor(out=out, in0=in0, scalar=scalar, in1=in1, op0=op0, op1=op1),
              r=rd, w=[out])

    def cp(out, in_, eng="dve"):
        if eng == "act":
            Sc.op("act", lambda e: e.activation(out=out, in_=in_, func=AF.Copy), r=[in_], w=[out])
        else:
            Sc.op(eng, lambda e: e.tensor_copy(out=out, in_=in_), r=[in_], w=[out])

    def memset(out, val, eng="pool"):
        Sc.op(eng, lambda e: e.memset(out, val), r=[], w=[out])

    def recip(out, in_):
        Sc.op("dve", lambda e: e.reciprocal(out=out, in_=in_), r=[in_], w=[out])

    def scan(out, d0, d1, init=0.0):
        Sc.op("dve", lambda e: e.tensor_tensor_scan(out=out, data0=d0, data1=d1, initial=init, op0=ALU.mult,
                                                    op1=ALU.add), r=[d0, d1], w=[out])

    def reduce_sum(out, in_):
        Sc.op("dve", lambda e: e.tensor_reduce(out=out, in_=in_, axis=AX.X, op=ALU.add), r=[in_], w=[out])

    def affsel(out, in_, pattern, cmp, fill, base, cm):
        Sc.op("pool", lambda e: e.affine_select(out=out, in_=in_, pattern=pattern, compare_op=cmp, fill=fill,
                                                base=base, channel_multiplier=cm), r=[in_], w=[out])

    def dma(q, out, in_):
        Sc.dma(q, out, in_)

    hF = A.alloc(8 * S, F32).rearrange("p (f t) -> p f t", f=8)
    xnT = A.alloc(8 * S, BF16).rearrange("p (f t) -> p f t", f=8)
    xn_region_start_f32 = (A.top - 8 * S * 2) // 4
    pp = A.alloc(PP_TOT, F32)
    ppd = A.alloc(L_ALL * 32, F32)
    rbt = A.alloc(RB_L, F32)
    identb = A.alloc(128, BF16)
    identf = A.alloc(128, F32)
    onesb = A.alloc(128, BF16)
    onesf = A.alloc(128, F32)
    blkb = A.alloc(128, BF16)
    headsel = A.alloc(2, BF16)
    mU = A.alloc(128, F32)
    mL = A.alloc(128, F32)
    mUI = A.alloc(128, F32)
    mLI = A.alloc(128, F32)
    nmU = A.alloc(128, F32)
    nmL = A.alloc(128, F32)
    nmUI = A.alloc(128, F32)
    nmLI = A.alloc(128, F32)
    negU4 = A.alloc(512, BF16)
    negL4 = A.alloc(512, BF16)
    lvl = A.alloc(14 * 128, BF16).rearrange("p (k c) -> p k c", k=14)
    ident4 = A.alloc(512, BF16)
    wb_region_start_f32 = A.top // 4
    wbufs = [A.alloc(8 * 512, BF16).rearrange("p (k c) -> p k c", k=8) for _ in range(2)]
    wb_i = [0]

    def next_wbuf():
        b = wbufs[wb_i[0] % 2]
        wb_i[0] += 1
        return b

    memset(identf, 0.0)
    affsel(identf, identf, [[-1, 128]], ALU.not_equal, 1.0, 0, 1)
    cp(identb, identf, "pool")
    memset(onesf, 1.0)
    memset(onesb, 1.0)
    for h in range(4):
        cp(ident4[:, h * 128:(h + 1) * 128], identf, "pool")
    dma("pool", lvl, lvl_d.rearrange("p (k c) -> p k c", k=14))
    affsel(mU, onesf, [[1, 128]], ALU.is_gt, 0.0, 0, -1)
    affsel(mUI, onesf, [[1, 128]], ALU.is_ge, 0.0, 0, -1)
    affsel(mL, onesf, [[-1, 128]], ALU.is_gt, 0.0, 0, 1)
    affsel(mLI, onesf, [[-1, 128]], ALU.is_ge, 0.0, 0, 1)
    for src, dst in ((mU, nmU), (mL, nmL), (mUI, nmUI), (mLI, nmLI)):
        ts(dst, src, -1.0, ALU.mult, eng="pool")
    for h in range(4):
        ts(negU4[:, h * 128:(h + 1) * 128], mU, NEG, ALU.mult, eng="pool")
        ts(negL4[:, h * 128:(h + 1) * 128], mL, NEG, ALU.mult, eng="pool")
    memset(blkb, 0.0)
    memset(blkb[0:64, 0:64], 1.0)
    memset(blkb[64:128, 64:128], 1.0)
    memset(headsel, 0.0)
    memset(headsel[0:64, 0:1], 1.0)
    memset(headsel[64:128, 1:2], 1.0)
    dma("sp", pp, pp_d)
    for l in range(n_layers):
        b0 = l * PP_L
        d0 = l * 32
        o_mu0, o_mu1 = PP_OFF["mu0"][0] + b0, PP_OFF["mu1"][0] + b0
        tt(ppd[:, d0:d0 + 9], pp[:, o_mu0:o_mu0 + 9], pp[:, o_mu1:o_mu1 + 9], ALU.add)
        ts(ppd[:, d0:d0 + 9], ppd[:, d0:d0 + 9], -1.0, ALU.mult, 1.0, ALU.add)
        o_ka = PP_OFF["ka"][0] + b0
        ts(ppd[:, d0 + 9:d0 + 11], pp[:, o_ka:o_ka + 2], -1.0, ALU.mult, 1.0, ALU.add)
        o_lam = PP_OFF["lam"][0] + b0
        act(ppd[:, d0 + 11:d0 + 15], pp[:, o_lam:o_lam + 4], AF.Exp, scale=-1.0)
        act(ppd[:, d0 + 11:d0 + 15], ppd[:, d0 + 11:d0 + 15], AF.Ln, bias=1.0)
        ts(ppd[:, d0 + 15:d0 + 19], ppd[:, d0 + 11:d0 + 15], -16.0, ALU.mult)
        ts(ppd[:, d0 + 11:d0 + 15], ppd[:, d0 + 11:d0 + 15], -8.0, ALU.mult)
        o_w0, o_a0 = PP_OFF["w0"][0] + b0, PP_OFF["a0"][0] + b0
        ts(ppd[:, d0 + 19:d0 + 23], pp[:, o_w0:o_w0 + 4], -1.0, ALU.mult)
        ts(ppd[:, d0 + 23:d0 + 27], pp[:, o_a0:o_a0 + 4], -1.0, ALU.mult)

    def P_(l, name, j=0, n=1):
        o, k = PP_OFF[name]
        return pp[:, l * PP_L + o + j: l * PP_L + o + j + n]

    m0 = A.mark()
    xst = [A.alloc(D, F32) for _ in range(2)]
    for tt_i in range(NT):
        xs_ = xst[tt_i % 2]
        dma("sp", xs_, x_d[tt_i * 128:(tt_i + 1) * 128, :])
        for half in range(2):
            pb = gbank()
            for j in range(4):
                f = half * 4 + j
                tr(pb[:, j * 128:(j + 1) * 128], xs_[:, f * 128:(f + 1) * 128], identf)
            cp(hF[:, half * 4:half * 4 + 4, tt_i * 128:(tt_i + 1) * 128],
               pb.rearrange("p (j t) -> p j t", j=4), "act" if half else "dve")
    A.release(m0)

    def rmsnorm_to_xnT(nw_of_f):
        m = A.mark()
        sq = A.alloc(8 * 512, BF16).rearrange("p (f t) -> p f t", f=8)
        rstd = A.alloc(512, F32)
        for tc in range(4):
            tsl = slice(tc * 512, (tc + 1) * 512)
            for f in range(8):
                act(sq[:, f, :], hF[:, f, tsl], AF.Square)
            pb = gbank()
            for f in range(8):
                mm(pb, onesb, sq[:, f, :], start=(f == 0), stop=(f == 7))
            act(rstd, pb, AF.Sqrt, scale=1.0 / D, bias=1e-5)
            recip(rstd, rstd)
            for f in range(8):
                stt(xnT[:, f, tsl], hF[:, f, tsl], nw_of_f(f), rstd, ALU.mult, ALU.mult)
        A.release(m)

    def load_w_cols(src2d, c0, ncols, q="pool"):
        wb = next_wbuf()
        dma(q, wb[:, :, 0:ncols], src2d.rearrange("(k p) c -> p k c", p=128)[:, :, c0:c0 + ncols])
        return wb

    def proj_fm(wb, cw, ncw, evac):
        for tc in range(4):
            pb = gbank()
            for k in range(8):
                mm(pb[0:ncw, :], wb[:, k, cw:cw + ncw], xnT[:, k, tc * 512:(tc + 1) * 512], start=(k == 0),
                   stop=(k == 7))
            evac(tc, pb[0:ncw, :])

    def proj_tm(wb, cw, ncw, tile_i, tok0, out_psum):
        for k in range(8):
            mm(out_psum, xnT[:, k, tok0:tok0 + 128], wb[:, k, cw:cw + ncw], start=(k == 0), stop=(k == 7))

    def outproj_acc(l, g, ymT):
        m = A.mark()
        wo = A.alloc(2 * 1024, BF16).rearrange("p (k c) -> p k c", k=2)
        dma("pool", wo, w_out_d[l, g * 256:(g + 1) * 256, :].rearrange("(k p) c -> p k c", p=128))
        for ot in range(8):
            for tc in range(4):
                tsl = slice(tc * 512, (tc + 1) * 512)
                pb = gbank()
                for k in range(2):
                    mm(pb, wo[:, k, ot * 128:(ot + 1) * 128], ymT[:, k, tsl], start=(k == 0), stop=(k == 1))
                tt(hF[:, ot, tsl], pb, hF[:, ot, tsl], ALU.add)
        A.release(m)

    def dump_dbg(g, ymT):
        if dbg_d is None or dbg.get('nodump'):
            return
        m = A.mark()
        st = A.alloc(S, F32)
        for k in range(2):
            cp(st, ymT[:, k, :], "pool")
            dma("sp", dbg_d[g * 256 + k * 128: g * 256 + (k + 1) * 128, :], st)
        A.release(m)

    def conv4(stage, wcol, bias_ap, out, func=None, out2=None):
        m = A.mark()
        acc = A.alloc(S, F32)
        ts(acc, stage[:, 0:S], wcol(0), ALU.mult)
        for k in range(1, 4):
            stt(acc, stage[:, k:k + S], wcol(k), acc, ALU.mult, ALU.add)
        if func is None:
            ts(out, acc, bias_ap, ALU.add)
            if out2 is not None:
                cp(out2, out, "pool")
        else:
            act(out, acc, func, bias=bias_ap)
        A.release(m)

    def ssd_mixer(l):
        m_ssd = A.mark()
        xs_tok = A.alloc(NT * 256, BF16).rearrange("p (i c) -> p i c", i=NT)
        B_tok = A.alloc(NT * 256, BF16).rearrange("p (i c) -> p i c", i=NT)
        BT = A.alloc(2 * S, BF16).rearrange("p (k t) -> p k t", k=2)
        CT = A.alloc(2 * S, BF16).rearrange("p (k t) -> p k t", k=2)
        o_alog, o_dtb = RB_OFF["alog"][0], RB_OFF["dtb"][0]
        m1 = A.mark()
        stage = A.alloc(S + 3, F32)
        xsT = A.alloc(2 * S, BF16).rearrange("p (k t) -> p k t", k=2)
        memset(stage[:, 0:2], 0.0)
        memset(stage[:, S + 2:S + 3], 0.0)
        for half in range(2):
            wb = load_w_cols(w_in_d[l], SSD0 + 256 + half * 384, 384)
            for j in range(3):
                c = half * 3 + j
                proj_fm(wb, j * 128, 128, lambda tc, pb: cp(stage[:, 2 + tc * 512: 2 + (tc + 1) * 512], pb, "act"))
                dest = (xsT[:, c, :] if c < 2 else (BT[:, c - 2, :] if c < 4 else CT[:, c - 4, :]))
                conv4(stage, lambda k, c=c: P_(l, "scw", k * 6 + c), P_(l, "scb", c), dest, func=AF.Silu)
        for i in range(NT):
            pbb = gbank().bitcast(BF16)
            for k in range(2):
                tr(pbb[:, k * 128:(k + 1) * 128], xsT[:, k, i * 128:(i + 1) * 128], identb)
                tr(pbb[:, 256 + k * 128:256 + (k + 1) * 128], BT[:, k, i * 128:(i + 1) * 128], identb)
            cp(xs_tok[:, i, :], pbb[:, 0:256], "act")
            cp(B_tok[:, i, :], pbb[:, 256:512], "act")
        A.release(m1)
        z_tok = A.alloc(NT * 256, BF16).rearrange("p (i c) -> p i c", i=NT)
        y_tok = A.alloc(NT * 256, F32).rearrange("p (i c) -> p i c", i=NT)
        dt_tok = A.alloc(NT * 8, F32).rearrange("p (i c) -> p i c", i=NT)
        adt = A.alloc(NT * 8, F32).rearrange("p (i c) -> p i c", i=NT)
        Abc = A.alloc(8, F32)
        act(Abc, rbt[:, o_alog:o_alog + 8], AF.Exp)
        ts(Abc, Abc, -1.0, ALU.mult)
        wz = load_w_cols(w_in_d[l], SSD0, 256)
        for i in range(NT):
            pb = gbank()
            proj_tm(wz, 0, 256, i, i * 128, pb[:, 0:256])
            act(z_tok[:, i, :], pb[:, 0:256], AF.Silu)
        wdt = load_w_cols(w_in_d[l], SSD0 + 1024, 8)
        for i in range(NT):
            pb = gbank()
            proj_tm(wdt, 0, 8, i, i * 128, pb[:, 0:8])
            tt(dt_tok[:, i, :], pb[:, 0:8], rbt[:, o_dtb:o_dtb + 8], ALU.add)
        dtf = dt_tok.rearrange("p i c -> p (i c)")
        act(dtf, dtf, AF.Exp)
        act(dtf, dtf, AF.Ln, bias=1.0)
        tt(adt, dt_tok, Abc.unsqueeze(1).to_broadcast([128, NT, 8]), ALU.mult)
        m2 = A.mark()
        Hs = [A.alloc(256, F32) for _ in range(2)]
        Hb = [A.alloc(256, BF16) for _ in range(2)]
        for d in range(2):
            memset(Hs[d], 0.0)
            memset(Hb[d], 0.0)
        class _T:
            pass
        tmps = []
        for d in range(2):
            T_ = _T()
            T_.rhs_seg = A.alloc(512, F32)
            T_.dec = A.alloc(512, BF16)
            T_.scT = A.alloc(512, BF16)
            T_.xdt = A.alloc(256, BF16)
            T_.xde = A.alloc(256, BF16)
            T_.ecs = A.alloc(8, F32)
            T_.dte = A.alloc(8, F32)
            T_.etot = A.alloc(8, F32)
            tmps.append(T_)

        def ssd_step(d, c, first):
            T_ = tmps[d]
            rhs_seg, dec, scT, xdt, xde, ecs, dte, etot = (T_.rhs_seg, T_.dec, T_.scT, T_.xdt, T_.xde, T_.ecs,
                                                           T_.dte, T_.etot)
            csl = slice(c * 128, (c + 1) * 128)
            a4 = adt[:, c, d * 4:(d + 1) * 4]
            msk = mUI if d == 0 else mLI
            pS = gbank()
            mm(pS[:, 0:4], msk, a4)
            mm(pS[:, 8:12], onesf, a4)
            act(ecs[:, 0:4], pS[:, 0:4], AF.Exp)
            cp(dte[:, 4:8], pS[:, 0:4], "dve")
            tt(dte[:, 0:4], pS[:, 8:12], dte[:, 4:8], ALU.subtract)
            act(dte[:, 0:4], dte[:, 0:4], AF.Exp)
            act(etot[:, 0:4], pS[:, 8:12], AF.Exp)
            tt(rhs_seg.rearrange("p (h l) -> p h l", h=4),
               msk.unsqueeze(1).to_broadcast([128, 4, 128]),
               a4.unsqueeze(2).to_broadcast([128, 4, 128]), ALU.mult)
            dt4 = dt_tok[:, c, d * 4:(d + 1) * 4]
            tt(xdt.rearrange("p (h q) -> p h q", h=4), xs_tok[:, c, :].rearrange("p (h q) -> p h q", h=4),
               dt4.unsqueeze(2).to_broadcast([128, 4, 64]), ALU.mult)
            yield
            pD = gbank()
            mm(pD, (mL if d == 0 else mU), rhs_seg, start=True, stop=False)
            mm(pD, identb, (negL4 if d == 0 else negU4), start=False, stop=True)
            act(dec, pD, AF.Exp)
            tt(xde.rearrange("p (h q) -> p h q", h=4), xdt.rearrange("p (h q) -> p h q", h=4),
               dte[:, 0:4].unsqueeze(2).to_broadcast([128, 4, 64]), ALU.mult)
            yield
            pG = gbank()
            for g in range(2):
                mm(pG[:, g * 128:(g + 1) * 128], BT[:, g, csl], CT[:, g, csl])
            tt(scT.rearrange("p (g r l) -> p g r l", g=2, r=2),
               dec.rearrange("p (g r l) -> p g r l", g=2, r=2),
               pG[:, 0:256].rearrange("p (g l) -> p g l", g=2).unsqueeze(2).to_broadcast([128, 2, 2, 128]),
               ALU.mult)
            yield
            pY = gbank()
            for h in range(4):
                mm(pY[:, h * 64:(h + 1) * 64], scT[:, h * 128:(h + 1) * 128], xdt[:, h * 64:(h + 1) * 64])
                mm(pY[:, 256 + h * 64:256 + (h + 1) * 64], CT[:, h // 2, csl], Hb[d][:, h * 64:(h + 1) * 64])
            pH = gbank()
            for h in range(4):
                mm(pH[:, h * 64:(h + 1) * 64], B_tok[:, c, (h // 2) * 128:(h // 2 + 1) * 128],
                   xde[:, h * 64:(h + 1) * 64])
            if first:
                cp(y_tok[:, c, :], pY[:, 0:256], "act")
            else:
                tt(y_tok[:, c, :], pY[:, 0:256], y_tok[:, c, :], ALU.add)
            for h in range(4):
                stt(y_tok[:, c, h * 64:(h + 1) * 64], pY[:, 256 + h * 64:256 + (h + 1) * 64], ecs[:, h:h + 1],
                    y_tok[:, c, h * 64:(h + 1) * 64], ALU.mult, ALU.add)
            for h in range(4):
                stt(Hs[d][:, h * 64:(h + 1) * 64], Hs[d][:, h * 64:(h + 1) * 64], etot[:, h:h + 1],
                    pH[:, h * 64:(h + 1) * 64], ALU.mult, ALU.add)
            cp(Hb[d], Hs[d], "act")
            yield

        for st_i in range(NT):
            first = st_i < NT // 2
            gens = [ssd_step(0, st_i, first), ssd_step(1, NT - 1 - st_i, first)]
            alive = True
            while alive:
                alive = False
                for g_ in gens:
                    try:
                        next(g_)
                        alive = True
                    except StopIteration:
                        pass
        A.release(m2)
        ymT = A.alloc(2 * S, BF16).rearrange("p (k t) -> p k t", k=2)
        m3 = A.mark()
        o_dsk, o_snw = RB_OFF["dsk"][0], RB_OFF["snw"][0]
        NF = 4
        ftmps = [(A.alloc(256, F32), A.alloc(256, BF16), A.alloc(2, F32)) for _ in range(NF)]

        def ssd_fin(i, bufs):
            tmp, tmpb, ss = bufs
            tt(tmp.rearrange("p (h q) -> p h q", h=4), xs_tok[:, i, :].rearrange("p (h q) -> p h q", h=4),
               rbt[:, o_dsk:o_dsk + 4].unsqueeze(2).to_broadcast([128, 4, 64]), ALU.mult)
            tt(tmp, tmp, y_tok[:, i, :], ALU.add)
            tt(tmp, tmp, z_tok[:, i, :], ALU.mult)
            yield
            tt(y_tok[:, i, :], tmp, tmp, ALU.mult)
            reduce_sum(ss[:, 0:1], y_tok[:, i, :])
            yield
            act(ss[:, 1:2], ss[:, 0:1], AF.Ln, scale=1.0 / 256, bias=1e-5)
            act(ss[:, 1:2], ss[:, 1:2], AF.Exp, scale=-0.5)
            yield
            stt(tmpb, tmp, ss[:, 1:2], rbt[:, o_snw:o_snw + 256], ALU.mult, ALU.mult)
            yield
            pbb = gbank().bitcast(BF16)
            for k in range(2):
                tr(pbb[:, k * 128:(k + 1) * 128], tmpb[:, k * 128:(k + 1) * 128], identb)
            cp(ymT[:, :, i * 128:(i + 1) * 128], pbb[:, 0:256].rearrange("p (k t) -> p k t", k=2), "act")
            yield

        for i0_ in range(0, NT, NF):
            gens = [ssd_fin(i0_ + j, ftmps[j]) for j in range(NF)]
            alive = True
            while alive:
                alive = False
                for g_ in gens:
                    try:
                        next(g_)
                        alive = True
                    except StopIteration:
                        pass
        A.release(m3)
        dump_dbg(1, ymT)
        outproj_acc(l, 1, ymT)
        A.release(m_ssd)

    def lru_mixer(l):
        m_l = A.mark()
        ymT = A.alloc(2 * S, BF16).rearrange("p (k t) -> p k t", k=2)
        gate = A.alloc(2 * S, BF16).rearrange("p (k t) -> p k t", k=2)
        xf = A.alloc(2 * S, F32).rearrange("p (k t) -> p k t", k=2)
        xfb = A.alloc(2 * S, BF16).rearrange("p (k t) -> p k t", k=2)
        m_st = A.mark()
        stage = A.alloc(S + 3, F32)
        memset(stage[:, 0:2], 0.0)
        memset(stage[:, S + 2:S + 3], 0.0)
        wb = load_w_cols(w_in_d[l], LRU0, 512)
        for c in range(2):
            proj_fm(wb, c * 128, 128,
                    lambda tc, pb, c=c: act(gate[:, c, tc * 512:(tc + 1) * 512], pb, AF.Gelu_apprx_tanh))
        for c in range(2):
            proj_fm(wb, 256 + c * 128, 128,
                    lambda tc, pb: cp(stage[:, 2 + tc * 512: 2 + (tc + 1) * 512], pb, "act"))
            conv4(stage, lambda k, c=c: P_(l, "lcw", k * 2 + c), P_(l, "lcb", c), xf[:, c, :], out2=xfb[:, c, :])
        A.release(m_st)
        m_tmp = A.mark()
        wg = A.alloc(8 * 128, BF16).rearrange("p (a c) -> p a c", a=8)
        dma("pool", wg, lbd_d[l].rearrange("w d t p c -> p (w d t) c"))
        rec = A.alloc(S, F32)
        inp = A.alloc(S, F32)
        aa = A.alloc(S, F32)
        hsum = A.alloc(S, F32)
        for pt in range(2):
            for d in range(2):
                for which, dst, bname in ((0, rec, "lab"), (1, inp, "lxb")):
                    for tc in range(4):
                        tsl = slice(tc * 512, (tc + 1) * 512)
                        pb = gbank()
                        mm(pb, wg[:, which * 4 + d * 2 + pt, :], xfb[:, pt, tsl])
                        act(dst[:, tsl], pb, AF.Sigmoid, bias=P_(l, bname, d * 2 + pt))
                lc = ppd[:, l * 32 + 11 + d * 2 + pt: l * 32 + 12 + d * 2 + pt]
                lc2 = ppd[:, l * 32 + 15 + d * 2 + pt: l * 32 + 16 + d * 2 + pt]
                act(aa, rec, AF.Exp, scale=lc)
                act(rec, rec, AF.Exp, scale=lc2)
                act(rec, rec, AF.Sqrt, scale=-1.0, bias=1.0)
                tt(inp, inp, rec, ALU.mult)
                tt(inp, inp, xf[:, pt, :], ALU.mult)
                if d == 0:
                    scan(hsum, aa, inp)
                else:
                    scan(rec[:, ::-1], aa[:, ::-1], inp[:, ::-1])
                    tt(hsum, hsum, rec, ALU.add)
                    tt(ymT[:, pt, :], hsum, gate[:, pt, :], ALU.mult)
        A.release(m_tmp)
        dump_dbg(2, ymT)
        outproj_acc(l, 2, ymT)
        A.release(m_l)

    def na_mixer(l):
        m_n = A.mark()
        ymT = A.alloc(2 * S, BF16).rearrange("p (k t) -> p k t", k=2)
        m_na_tmp = A.mark()
        qm = [A.alloc(S, BF16) for _ in range(4)]
        for h in range(4):
            z0 = 64 - (h % 2) * 64
            memset(qm[h][z0:z0 + 64, :], 0.0)
        kT = A.alloc(2 * S, BF16).rearrange("p (k t) -> p k t", k=2)
        va = [A.alloc(NT * 260, BF16).rearrange("p (i h q) -> p i h q", i=NT, h=4) for _ in range(2)]
        t2 = A.alloc(14 * 256, BF16).rearrange("p (r x) -> p r x", r=14)
        dma("pool", t2, t2_d[l].rearrange("p (r x) -> p r x", r=14))
        wb = load_w_cols(w_in_d[l], NA0, 512)
        def q_evac(tc, pb, c):
            tsl = slice(tc * 512, (tc + 1) * 512)
            act(qm[2 * c][0:64, tsl], pb[0:64, :], AF.Copy, scale=0.125)
            act(qm[2 * c + 1][64:128, tsl], pb[64:128, :], AF.Copy, scale=0.125)
        for c in range(2):
            proj_fm(wb, c * 128, 128, lambda tc, pb, c=c: q_evac(tc, pb, c))
        for c in range(2):
            proj_fm(wb, 256 + c * 128, 128, lambda tc, pb, c=c: cp(kT[:, c, tc * 512:(tc + 1) * 512], pb, "act"))
        wv = load_w_cols(w_in_d[l], NA0 + 512, 256)
        for al in range(2):
            memset(va[al][:, :, :, 64:65], 1.0)
            for i in range(NT - al):
                pb = gbank()
                proj_tm(wv, 0, 256, i, i * 128 + al * 64, pb[:, 0:256])
                cp(va[al][:, i, :, 0:64], pb[:, 0:256].rearrange("p (h q) -> p h q", h=4), "act" if i % 2 else "dve")
        NSLOT_NA = 3
        slots = []
        for _ in range(NSLOT_NA):
            slots.append((A.alloc(1024, F32), A.alloc(1024, BF16), A.alloc(4, F32), A.alloc(256, BF16, parts=64)))

        def na_row(r, slot):
            ein, pT, rc, ytok = slot
            kr0 = min(max(r - 4, 0), 24)
            dr0 = kr0 - r + 7
            al = kr0 % 2
            ti0 = kr0 // 2
            q0 = r * 64
            pS = [gbank(), gbank()]
            for kt in range(4):
                k0 = kr0 * 64 + kt * 128
                for h in range(4):
                    col = (kt * 4 + h) * 64
                    mm(pS[col // 512][:, col % 512: col % 512 + 64], kT[:, h // 2, k0:k0 + 128],
                       qm[h][:, q0:q0 + 64])
            for half in range(2):
                tt(ein[:, half * 512:(half + 1) * 512].rearrange("p (a x) -> p a x", a=2),
                   pS[half].rearrange("p (a x) -> p a x", a=2),
                   t2[:, dr0 + 4 * half: dr0 + 4 * half + 3: 2, :], ALU.add)
            yield
            act(pT, ein, AF.Exp)
            yield
            pO = gbank()
            for h in range(4):
                for kt in range(4):
                    col = (kt * 4 + h) * 64
                    mm(pO[0:64, h * 65:(h + 1) * 65], pT[:, col:col + 64], va[al][:, ti0 + kt, h, :],
                       start=(kt == 0), stop=(kt == 3))
            pO3 = pO[0:64, 0:260].rearrange("p (h q) -> p h q", h=4)
            recip(rc[0:64, :], pO3[:, :, 64])
            tt(ytok.rearrange("p (h q) -> p h q", h=4), pO3[:, :, 0:64],
               rc[0:64, :].unsqueeze(2).to_broadcast([64, 4, 64]), ALU.mult)
            yield
            pbb = gbank().bitcast(BF16)
            for k in range(2):
                tr(pbb[:, k * 64:(k + 1) * 64], ytok[:, k * 128:(k + 1) * 128], identb[0:64, 0:64])
            cp(ymT[:, :, q0:q0 + 64], pbb[:, 0:128].rearrange("p (k t) -> p k t", k=2), "act")
            yield

        for r0 in range(0, 32, NSLOT_NA):
            gens = [na_row(r, slots[i]) for i, r in enumerate(range(r0, min(32, r0 + NSLOT_NA)))]
            alive = True
            while alive:
                alive = False
                for g_ in gens:
                    try:
                        next(g_)
                        alive = True
                    except StopIteration:
                        pass
        A.release(m_na_tmp)
        dump_dbg(3, ymT)
        outproj_acc(l, 3, ymT)
        A.release(m_n)

    def rwkv_mixer(l):
        m_r = A.mark()
        rT = A.alloc(2 * S, BF16).rearrange("p (k t) -> p k t", k=2)
        kT = A.alloc(2 * S, BF16).rearrange("p (k t) -> p k t", k=2)
        wdT = A.alloc(S, BF16)
        adT = A.alloc(S, BF16)
        gdT = A.alloc(S, BF16)
        v_tok = A.alloc(NT * 256, BF16).rearrange("p (i c) -> p i c", i=NT)
        wup = A.alloc(256, BF16)
        aup = A.alloc(256, BF16)
        gup = A.alloc(256, BF16)
        dma("pool", wup, wup_d[l].rearrange("d r c -> (d r) c"))
        dma("pool", aup, aup_d[l].rearrange("d r c -> (d r) c"))
        dma("pool", gup, gup_d[l])
        m1 = A.mark()
        stage = A.alloc(S + 2, F32)
        t1 = A.alloc(S, F32)
        vT = A.alloc(2 * S, BF16).rearrange("p (k t) -> p k t", k=2)
        memset(stage[:, 0:1], 0.0)
        memset(stage[:, S + 1:S + 2], 0.0)
        dests = [rT[:, 0, :], rT[:, 1, :], kT[:, 0, :], kT[:, 1, :], vT[:, 0, :], vT[:, 1, :], None, None, gdT]
        for grp in range(3):
            wb = load_w_cols(w_in_d[l], RW0 + grp * 384, 384)
            for j in range(3):
                c = grp * 3 + j
                proj_fm(wb, j * 128, 128, lambda tc, pb: cp(stage[:, 1 + tc * 512: 1 + (tc + 1) * 512], pb, "act"))
                c0 = ppd[:, l * 32 + c: l * 32 + c + 1]
                ts(t1, stage[:, 1:S + 1], c0, ALU.mult)
                stt(t1, stage[:, 0:S], P_(l, "mu0", c), t1, ALU.mult, ALU.add)
                if c == 6:
                    stt(t1, stage[:, 2:S + 2], P_(l, "mu1", c), t1, ALU.mult, ALU.add)
                    act(wdT, t1, AF.Tanh)
                elif c == 7:
                    stt(t1, stage[:, 2:S + 2], P_(l, "mu1", c), t1, ALU.mult, ALU.add)
                    cp(adT, t1, "act")
                elif c == 8:
                    stt(t1, stage[:, 2:S + 2], P_(l, "mu1", c), t1, ALU.mult, ALU.add)
                    act(dests[c], t1, AF.Sigmoid)
                else:
                    stt(dests[c], stage[:, 2:S + 2], P_(l, "mu1", c), t1, ALU.mult, ALU.add)
        for i in range(NT):
            pbb = gbank().bitcast(BF16)
            for k in range(2):
                tr(pbb[:, k * 128:(k + 1) * 128], vT[:, k, i * 128:(i + 1) * 128], identb)
            cp(v_tok[:, i, :], pbb[:, 0:256], "act")
        A.release(m1)
        ymT = A.alloc(2 * S, BF16).rearrange("p (k t) -> p k t", k=2)
        y_tok = A.alloc(NT * 256, BF16).rearrange("p (i c) -> p i c", i=NT)
        bonus = A.alloc(NT * 4, F32).rearrange("p (i c) -> p i c", i=NT)

        XA_ = Arena(arena_ap, 8 * S // 2, start=xn_region_start_f32)

        XB_ = Arena(arena_ap, 2 * 8 * 512 // 2, start=wb_region_start_f32)

        class _XA:
            @staticmethod
            def alloc(cols, dtype):
                for ar in (XA_, XB_):
                    if ar.top + (cols * DSZ[dtype] + CELL - 1) // CELL * CELL <= ar.limit:
                        return ar.alloc(cols, dtype)
                return A.alloc(cols, dtype)
        XA = _XA

        class DirBuf:
            pass
        G_SETS = 3
        STs, STbs = [], []
        for d in range(2):
            st_ = XA.alloc(2 * 64, F32).rearrange("p (k i) -> p k i", k=2)
            stb_ = XA.alloc(2 * 64, BF16).rearrange("p (k i) -> p k i", k=2)
            memset(st_, 0.0)
            memset(stb_, 0.0)
            STs.append(st_)
            STbs.append(stb_)
        Xbs = [XA.alloc(256, BF16) for _ in range(2)]
        Ubs = [XA.alloc(256, BF16) for _ in range(2)]
        dbs = []
        for g_ in range(G_SETS):
            b = DirBuf()
            b.sa = XA.alloc(256, F32)
            b.sig = b.sa[:, 0:128]
            b.al = b.sa[:, 128:256]
            b.ein = [XA.alloc(129, F32) for _ in range(2)]
            for nm in ("eni", "eh", "kd", "bb", "kks", "kk"):
                setattr(b, nm, XA.alloc(128, BF16))
            b.sq = b.eh
            b.Bh = [XA.alloc(128, BF16) for _ in range(2)]
            b.Kh = [XA.alloc(128, BF16) for _ in range(2)]
            b.rkd = [XA.alloc(128, BF16) for _ in range(2)]
            b.Rt = [XA.alloc(128, BF16) for _ in range(2)]
            b.Qk = [XA.alloc(128, BF16) for _ in range(2)]
            b.Btm = [[XA.alloc(128, BF16) for _ in range(2)] for _ in range(2)]
            b.Qkm = [[XA.alloc(128, BF16) for _ in range(2)] for _ in range(2)]
            b.Ktm = [[XA.alloc(128, BF16) for _ in range(2)] for _ in range(2)]
            b.Rtm = [[XA.alloc(128, BF16) for _ in range(2)] for _ in range(2)]
            for grp in (b.Btm, b.Qkm, b.Ktm, b.Rtm):
                for hp_ in range(2):
                    for h2 in range(2):
                        z0 = 64 - h2 * 64
                        memset(grp[hp_][h2][z0:z0 + 64, :], 0.0)
            b.wdm = [XA.alloc(128, BF16) for _ in range(2)]
            b.adm = [XA.alloc(128, BF16) for _ in range(2)]
            for d_ in range(2):
                z0 = 64 - d_ * 64
                memset(b.wdm[d_][z0:z0 + 64, :], 0.0)
                memset(b.adm[d_][z0:z0 + 64, :], 0.0)
            b.BhT = XA.alloc(256, BF16)
            b.KhT = XA.alloc(256, BF16)
            b.Mf = XA.alloc(512, BF16)
            b.T = XA.alloc(512, BF16)
            b.Z = XA.alloc(512, BF16)
            b.W = XA.alloc(512, BF16)
            b.Akk = XA.alloc(512, BF16)
            b.Arb = XA.alloc(512, BF16)
            b.Ark = XA.alloc(512, BF16)
            dbs.append(b)
        gtmp = XA.alloc(256, F32)
        gt2 = XA.alloc(256, F32)
        gst = XA.alloc(16, F32)
        gout = gt2.bitcast(BF16)[:, 0:256]

        def finalize_chunk(c):
            y4 = gtmp.rearrange("p (h q) -> p h q", h=4)
            cp(gtmp, y_tok[:, c, :], "pool")
            reduce_sum(gst[:, 0:4], y4)
            tt(gt2, gtmp, gtmp, ALU.mult)
            reduce_sum(gst[:, 4:8], gt2.rearrange("p (h q) -> p h q", h=4))
            ts(gst[:, 0:4], gst[:, 0:4], 1.0 / 64, ALU.mult)
            tt(gst[:, 8:12], gst[:, 0:4], gst[:, 0:4], ALU.mult)
            stt(gst[:, 4:8], gst[:, 4:8], 1.0 / 64, gst[:, 8:12], ALU.mult, ALU.subtract)
            act(gst[:, 4:8], gst[:, 4:8], AF.Ln, bias=64e-5)
            act(gst[:, 4:8], gst[:, 4:8], AF.Exp, scale=-0.5)
            tt(y4, y4, gst[:, 0:4].unsqueeze(2).to_broadcast([128, 4, 64]), ALU.subtract)
            tt(y4, y4, gst[:, 4:8].unsqueeze(2).to_broadcast([128, 4, 64]), ALU.mult)
            o_w, o_b = RB_OFF["gnw"][0], RB_OFF["gnb"][0]
            tt(gtmp, gtmp, rbt[:, o_w:o_w + 256], ALU.mult)
            tt(gtmp, gtmp, rbt[:, o_b:o_b + 256], ALU.add)
            tt(gt2.rearrange("p (h q) -> p h q", h=4), v_tok[:, c, :].rearrange("p (h q) -> p h q", h=4),
               bonus[:, c, :].unsqueeze(2).to_broadcast([128, 4, 64]), ALU.mult)
            tt(gtmp, gtmp, gt2, ALU.add)
            pg = gbank()
            mm(pg[:, 0:256], gdT[:, c * 128:(c + 1) * 128], gup)
            tt(gout, gtmp, pg[:, 0:256], ALU.mult)
            pbb = gbank().bitcast(BF16)
            for k in range(2):
                tr(pbb[:, k * 128:(k + 1) * 128], gout[:, k * 128:(k + 1) * 128], identb)
            cp(ymT[:, :, c * 128:(c + 1) * 128], pbb[:, 0:256].rearrange("p (k t) -> p k t", k=2), "act")

        seen_y = [False] * NT
        seen_bonus = [False] * NT

        def step(d, c, b):
            csl = slice(c * 128, (c + 1) * 128)
            fwd = (d == 0)
            ST, STb, Xb, Ub = STs[d], STbs[d], Xbs[d], Ubs[d]
            m_strict_neg = nmU if fwd else nmL
            m_strict = mU if fwd else mL
            m_incl = mUI if fwd else mLI
            m_incl_neg = nmUI if fwd else nmLI

            def heads(fn):
                for h in range(4):
                    fn(h, h // 2, slice((h % 2) * 64, (h % 2) * 64 + 64), slice(h * 128, (h + 1) * 128))
            dsl = slice(d * 64, d * 64 + 64)
            cp(b.wdm[d][dsl, :], wdT[dsl, csl], "dve")
            cp(b.adm[d][dsl, :], adT[dsl, csl], "dve")
            for hp in range(2):
                pz = gbank()
                mm(pz[:, 0:128], wup[:, hp * 128:(hp + 1) * 128], b.wdm[d])
                mm(pz[:, 128:256], aup[:, hp * 128:(hp + 1) * 128], b.adm[d])
                nw0 = ppd[:, l * 32 + 19 + d * 2 + hp: l * 32 + 20 + d * 2 + hp]
                na0 = ppd[:, l * 32 + 23 + d * 2 + hp: l * 32 + 24 + d * 2 + hp]
                act(b.sig, pz[:, 0:128], AF.Exp, scale=-1.0, bias=nw0)
                act(b.al, pz[:, 128:256], AF.Exp, scale=-1.0, bias=na0)
                act(b.sa, b.sa, AF.Ln, bias=1.0)
                act(b.sa, b.sa, AF.Exp, scale=-1.0)
                if fwd:
                    cs = b.ein[hp][:, 1:129]
                    scan(cs, onesf, b.sig)
                    pad = b.ein[hp][:, 0:1]
                    eex = b.ein[hp][:, 0:128]
                    etot = b.ein[hp][:, 128:129]
                else:
                    cs = b.ein[hp][:, 0:128]
                    scan(cs[:, ::-1], onesf, b.sig[:, ::-1])
                    pad = b.ein[hp][:, 128:129]
                    eex = b.ein[hp][:, 1:129]
                    etot = b.ein[hp][:, 0:1]
                ts(b.kks, kT[:, hp, csl], P_(l, "kk", hp), ALU.mult)
                tt(b.sq, b.kks, b.kks, ALU.mult)
                pn = gbank()
                mm(pn[:, 0:128], blkb, b.sq)
                act(b.eni, cs, AF.Exp, scale=DECAY)
                act(cs, cs, AF.Exp, scale=-DECAY)
                act(b.kk, pn[:, 0:128], AF.Ln, bias=1e-12)
                act(b.kk, b.kk, AF.Exp, scale=-0.5)
                yield
                memset(pad, 1.0, "pool")
                ts(b.eh, b.eni, etot, ALU.mult)
                tt(b.kk, b.kk, b.kks, ALU.mult)
                omka = ppd[:, l * 32 + 9 + hp: l * 32 + 10 + hp]
                ts(b.kd, b.al, P_(l, "ka", hp), ALU.mult, omka, ALU.add)
                tt(b.kd, b.kd, kT[:, hp, csl], ALU.mult)
                tt(b.bb, b.al, b.kk, ALU.mult)
                tt(b.Rt[hp], rT[:, hp, csl], cs, ALU.mult)
                tt(b.Qk[hp], b.kk, eex, ALU.mult)
                for h2 in range(2):
                    pr = slice(h2 * 64, h2 * 64 + 64)
                    tt(b.Btm[hp][h2][pr, :], b.bb[pr, :], b.eni[pr, :], ALU.mult)
                    tt(b.Ktm[hp][h2][pr, :], b.kd[pr, :], b.eni[pr, :], ALU.mult)
                    cp(b.Qkm[hp][h2][pr, :], b.Qk[hp][pr, :], "pool")
                    cp(b.Rtm[hp][h2][pr, :], b.Rt[hp][pr, :], "pool")
                stt(b.Bh[hp], b.bb, -1.0, b.eh, ALU.mult, ALU.mult)
                tt(b.Kh[hp], b.kd, b.eh, ALU.mult)
                stt(b.rkd[hp], rT[:, hp, csl], P_(l, "rk", hp), b.kd, ALU.mult, ALU.mult)
                yield
            pbn = gbank()
            pbb = gbank().bitcast(BF16)
            for hp in range(2):
                mm(pbn[:, hp * 2:hp * 2 + 2], b.rkd[hp], headsel)
                tr(pbb[:, hp * 128:(hp + 1) * 128], b.Bh[hp], identb)
                tr(pbb[:, 256 + hp * 128:256 + (hp + 1) * 128], b.Kh[hp], identb)
            if not seen_bonus[c]:
                seen_bonus[c] = True
                cp(bonus[:, c, :], pbn[:, 0:4], "dve")
            else:
                tt(bonus[:, c, :], pbn[:, 0:4], bonus[:, c, :], ALU.add)
            cp(b.BhT, pbb[:, 0:256], "act")
            cp(b.KhT, pbb[:, 256:512], "act")
            for lhs_, rhs_, dst_, msk_ in ((b.Btm, b.Qk, b.Mf, m_strict_neg), (b.Ktm, b.Qk, b.Akk, m_strict),
                                           (b.Btm, b.Rt, b.Arb, m_incl_neg), (b.Ktm, b.Rt, b.Ark, m_incl)):
                pa = gbank()
                heads(lambda h, hp, jr, hs: mm(pa[:, hs], lhs_[hp][h % 2], rhs_[hp]))
                tt(dst_.rearrange("p (h t) -> p h t", h=4), pa.rearrange("p (h t) -> p h t", h=4),
                   msk_.unsqueeze(1).to_broadcast([128, 4, 128]), ALU.mult)
            yield
            for k in range(7):
                Tc = ident4 if k == 0 else b.T
                Zc = ident4 if k == 0 else b.Z
                pa = gbank()
                heads(lambda h, hp, jr, hs: mm(pa[:, hs], b.Mf[:, hs], Tc[:, hs]))
                mk = lvl[:, (k if fwd else 7 + k), :]
                tt(b.W.rearrange("p (h t) -> p h t", h=4), pa.rearrange("p (h t) -> p h t", h=4),
                   mk.unsqueeze(1).to_broadcast([128, 4, 128]), ALU.mult)
                yield
                if k < 6:
                    pt = gbank()
                    mm(pt, identb, Tc, start=True, stop=False, skip=True)
                    heads(lambda h, hp, jr, hs: mm(pt[:, hs], Zc[:, hs], b.W[:, hs], start=False, stop=False,
                                                   skip=True))
                pz = gbank()
                mm(pz, identb, Zc, start=True, stop=False, skip=True)
                heads(lambda h, hp, jr, hs: mm(pz[:, hs], b.W[:, hs], Zc[:, hs], start=False, stop=False, skip=True))
                if k < 6:
                    cp(b.T, pt, "act")
                cp(b.Z, pz, "act")
                yield
            yield "CHAIN"
            px = gbank()

            def xinit(h, hp, jr, hs):
                xs_ = slice(h * 64, (h + 1) * 64)
                mm(px[:, xs_], b.Qkm[hp][h % 2], STb[:, hp, :], start=True, stop=False)
                mm(px[:, xs_], b.Akk[:, hs], v_tok[:, c, xs_], start=False, stop=True)
            heads(xinit)
            cp(Xb, px[:, 0:256], "act")
            pu = gbank()
            heads(lambda h, hp, jr, hs: mm(pu[:, h * 64:(h + 1) * 64], b.Z[:, hs], Xb[:, h * 64:(h + 1) * 64]))
            cp(Ub, pu[:, 0:256], "act")
            py = gbank()

            def yfn(h, hp, jr, hs):
                xs_ = slice(h * 64, (h + 1) * 64)
                mm(py[:, xs_], b.Rtm[hp][h % 2], STb[:, hp, :], start=True, stop=False)
                mm(py[:, xs_], b.Arb[:, hs], Ub[:, xs_], start=False, stop=False)
                mm(py[:, xs_], b.Ark[:, hs], v_tok[:, c, xs_], start=False, stop=True)
            heads(yfn)
            if not seen_y[c]:
                seen_y[c] = True
                cp(y_tok[:, c, :], py[:, 0:256], "act")
            else:
                tt(y_tok[:, c, :], py[:, 0:256], y_tok[:, c, :], ALU.add)
            ph = gbank()
            for h in range(4):
                hp = h // 2
                xs_ = slice(h * 64, (h + 1) * 64)
                mm(ph[:, xs_], b.BhT[:, hp * 128:(hp + 1) * 128], Ub[:, xs_], start=True, stop=False)
                mm(ph[:, xs_], b.KhT[:, hp * 128:(hp + 1) * 128], v_tok[:, c, xs_], start=False, stop=True)
            for h in range(4):
                hp = h // 2
                jr = slice((h % 2) * 64, (h % 2) * 64 + 64)
                etot = (b.ein[hp][jr, 128:129] if fwd else b.ein[hp][jr, 0:1])
                stt(ST[jr, hp, :], ST[jr, hp, :], etot, ph[jr, h * 64:(h + 1) * 64], ALU.mult, ALU.add)
            cp(STb, ST, "act")
            yield

        pending = []
        for st_i in range(NT):
            pending.append((0, st_i, st_i))
            pending.append((1, NT - 1 - st_i, st_i))
        live = []
        free_sets = list(range(G_SETS))
        done = [0, 0]
        ndone_chunk = [0] * NT
        while pending or live:
            while pending and free_sets:
                d_, c_, idx_ = pending.pop(0)
                si = free_sets.pop(0)
                live.append({"g": step(d_, c_, dbs[si]), "d": d_, "c": c_, "idx": idx_, "set": si, "wait": False})
            for it in list(live):
                if it["wait"] and done[it["d"]] < it["idx"]:
                    continue
                try:
                    r_ = next(it["g"])
                    if r_ == "CHAIN":
                        it["wait"] = True
                except StopIteration:
                    live.remove(it)
                    free_sets.append(it["set"])
                    done[it["d"]] += 1
                    ndone_chunk[it["c"]] += 1
                    if ndone_chunk[it["c"]] == 2:
                        finalize_chunk(it["c"])
        dump_dbg(0, ymT)
        outproj_acc(l, 0, ymT)
        A.release(m_r)

    def mlp(l):
        m_m = A.mark()
        mT = A.alloc(8 * S, BF16).rearrange("p (k t) -> p k t", k=8)
        w2b = [A.alloc(8 * 512, BF16).rearrange("p (k c) -> p k c", k=8) for _ in range(2)]
        rls = [A.alloc(512, BF16) for _ in range(2)]
        rl_i = [0]
        for hg in range(4):
            for half in range(2):
                wb = load_w_cols(w1_d[l], hg * 1024 + half * 512, 512)
                for j in range(4):
                    hpt = half * 4 + j
                    for tc in range(4):
                        tsl = slice(tc * 512, (tc + 1) * 512)
                        pb = gbank()
                        for k in range(8):
                            mm(pb, wb[:, k, j * 128:(j + 1) * 128], xnT[:, k, tsl], start=(k == 0), stop=(k == 7))
                        rl = rls[rl_i[0] % 2]
                        rl_i[0] += 1
                        act(rl, pb, AF.Relu)
                        tt(mT[:, hpt, tsl], rl, rl, ALU.mult)
            for half in range(2):
                dma("pool", w2b[half],
                    w2_d[l, hg * 1024:(hg + 1) * 1024, half * 512:(half + 1) * 512].rearrange("(k p) c -> p k c", p=128))
            for ot in range(8):
                for tc in range(4):
                    tsl = slice(tc * 512, (tc + 1) * 512)
                    pb = gbank()
                    for k in range(8):
                        mm(pb, w2b[ot // 4][:, k, (ot % 4) * 128:(ot % 4 + 1) * 128], mT[:, k, tsl], start=(k == 0),
                           stop=(k == 7))
                    tt(hF[:, ot, tsl], pb, hF[:, ot, tsl], ALU.add)
        A.release(m_m)

    mixers = (dbg or {}).get("mixers", "srna") if dbg else "srna"
    for l in range(n_layers):
        dma("sp", rbt, rb_d[l])
        rmsnorm_to_xnT(lambda f, l=l: P_(l, "n1", f))
        if "s" in mixers:
            ssd_mixer(l)
        if "r" in mixers:
            lru_mixer(l)
        if "n" in mixers:
            na_mixer(l)
        if "a" in mixers:
            rwkv_mixer(l)
        if not (dbg and dbg.get("no_mlp")):
            rmsnorm_to_xnT(lambda f, l=l: P_(l, "n2", f))
            mlp(l)

    m0 = A.mark()
    sq = A.alloc(8 * 512, BF16).rearrange("p (f t) -> p f t", f=8)
    rstd = A.alloc(512, F32)
    yn = A.alloc(8 * 512, F32).rearrange("p (f t) -> p f t", f=8)
    ost = [A.alloc(D, F32) for _ in range(2)]
    fo = PP_L * L_ALL
    for tc in range(4):
        tsl = slice(tc * 512, (tc + 1) * 512)
        for f in range(8):
            act(sq[:, f, :], hF[:, f, tsl], AF.Square)
        pb = gbank()
        for f in range(8):
            mm(pb, onesb, sq[:, f, :], start=(f == 0), stop=(f == 7))
        act(rstd, pb, AF.Sqrt, scale=1.0 / D, bias=1e-5)
        recip(rstd, rstd)
        for f in range(8):
            stt(yn[:, f, :], hF[:, f, tsl], pp[:, fo + f:fo + f + 1], rstd, ALU.mult, ALU.mult)
        for ti in range(4):
            o = ost[ti % 2]
            for half in range(2):
                pb2 = gbank()
                for j in range(4):
                    f = half * 4 + j
                    tr(pb2[:, j * 128:(j + 1) * 128], yn[:, f, ti * 128:(ti + 1) * 128], identf)
                cp(o[:, half * 512:(half + 1) * 512], pb2, "act" if half else "dve")
            tok0 = tc * 512 + ti * 128
            dma("sp", out_d[tok0:tok0 + 128, :], o)
    A.release(m0)

    Sc.finish()
    Sc.emit(stack)
    stack.close()
    return nc, Sc, A


_CACHE = {}


def kernel(**inputs):
    inp = {k: np.asarray(v) for k, v in inputs.items()}
    pp, rb, lbd, t2 = build_host_params(inp)
    if "nc" not in _CACHE:
        _CACHE["nc"] = build_program(L_ALL)[0]
    nc = _CACHE["nc"]
    shared = {
        "w_in": np.ascontiguousarray(inp["w_in"], np.float32),
        "w_out": np.ascontiguousarray(inp["w_out"], np.float32),
        "w_mlp1": np.ascontiguousarray(inp["w_mlp1"], np.float32),
        "w_mlp2": np.ascontiguousarray(inp["w_mlp2"], np.float32),
        "rwkv_w_up": np.ascontiguousarray(inp["rwkv_w_up"], np.float32),
        "rwkv_a_up": np.ascontiguousarray(inp["rwkv_a_up"], np.float32),
        "rwkv_g_up": np.ascontiguousarray(inp["rwkv_g_up"], np.float32),
        "pp": pp, "rb": rb, "lbd": lbd, "t2": t2.reshape(L_ALL, 128, 14 * 256), "lvl": build_level_masks(),
    }
    x = np.ascontiguousarray(inp["x"], np.float32)
    in_maps = [dict(shared, x=x[b]) for b in range(8)]
    res = run_bass_kernel_spmd(nc, in_maps, core_ids=list(range(8)))
    return np.stack([np.asarray(r["out"], np.float32) for r in res.results], axis=0)
```

```python
import numpy as np
from contextlib import ExitStack
import concourse.bass as bass
import concourse.mybir as mybir
from concourse.bass_utils import run_bass_kernel_spmd

F32 = mybir.dt.float32
BF16 = mybir.dt.bfloat16
AF = mybir.ActivationFunctionType
ALU = mybir.AluOpType
AX = mybir.AxisListType
DSZ = {F32: 4, BF16: 2}

L_ALL = 4
D = 1024
S = 2048
NT = 16
IN_COLS = 3464
RW0, SSD0, LRU0, NA0 = 0, 1152, 2184, 2696
CELL = 32
NSLOT = 8
ARENA_F32 = 53000
DECAY = 0.6065306597126334
NEG = -30000.0

PP_SPEC = [("n1", 8), ("n2", 8), ("mu0", 9), ("mu1", 9), ("w0", 4), ("a0", 4), ("kk", 2), ("ka", 2),
           ("rk", 2), ("scw", 24), ("scb", 6), ("lcw", 8), ("lcb", 2), ("lab", 4), ("lxb", 4), ("lam", 4)]
PP_OFF = {}
_o = 0
for _n, _k in PP_SPEC:
    PP_OFF[_n] = (_o, _k)
    _o += _k
PP_L = _o
PP_TOT = PP_L * L_ALL + 8
RB_SPEC = [("gnw", 256), ("gnb", 256), ("dtb", 8), ("alog", 8), ("dsk", 4), ("snw", 256)]
RB_OFF = {}
_o = 0
for _n, _k in RB_SPEC:
    RB_OFF[_n] = (_o, _k)
    _o += _k
RB_L = _o


def _colvec(a):
    a = np.asarray(a, np.float32)
    n = a.shape[-1] // 128
    b = a.reshape(a.shape[:-1] + (n, 128))
    return np.moveaxis(b, -1, 0)


def build_host_params(inp):
    pp = np.zeros((128, PP_TOT), np.float32)
    rb = np.zeros((L_ALL, 128, RB_L), np.float32)
    for l in range(L_ALL):
        def put(name, arr):
            o, k = PP_OFF[name]
            pp[:, l * PP_L + o: l * PP_L + o + k] = arr.reshape(128, k)
        put("n1", _colvec(inp["norm1_w"][l]))
        put("n2", _colvec(inp["norm2_w"][l]))
        put("mu0", _colvec(inp["rwkv_shift_mu"][l, 0]))
        put("mu1", _colvec(inp["rwkv_shift_mu"][l, 1]))
        put("w0", _colvec(inp["rwkv_w0"][l]))
        put("a0", _colvec(inp["rwkv_a0"][l]))
        put("kk", _colvec(inp["rwkv_k_k"][l]))
        put("ka", _colvec(inp["rwkv_k_a"][l]))
        put("rk", _colvec(inp["rwkv_r_k"][l].reshape(256)))
        put("scw", _colvec(inp["ssd_conv_w"][l]))
        put("scb", _colvec(inp["ssd_conv_b"][l]))
        put("lcw", _colvec(inp["lru_conv_w"][l]))
        put("lcb", _colvec(inp["lru_conv_b"][l]))
        put("lab", _colvec(inp["lru_gate_a_b"][l]))
        put("lxb", _colvec(inp["lru_gate_x_b"][l]))
        put("lam", _colvec(inp["lru_lambda"][l]))

        def putr(name, vec):
            o, k = RB_OFF[name]
            rb[l, :, o:o + k] = np.broadcast_to(np.asarray(vec, np.float32).reshape(1, k), (128, k))
        putr("gnw", inp["rwkv_gn_w"][l])
        putr("gnb", inp["rwkv_gn_b"][l])
        putr("dtb", inp["ssd_dt_bias"][l].reshape(8))
        putr("alog", inp["ssd_a_log"][l].reshape(8))
        putr("dsk", inp["ssd_d"][l])
        putr("snw", inp["ssd_norm_w"][l])
    pp[:, PP_L * L_ALL:] = _colvec(inp["final_norm_w"])
    lbd = np.zeros((L_ALL, 2, 2, 2, 128, 128), np.float32)
    for wi, nm in enumerate(("lru_gate_a_w", "lru_gate_x_w")):
        w = np.asarray(inp[nm], np.float32)
        for pt in range(2):
            for j in range(2):
                lbd[:, wi, :, pt, j * 64:(j + 1) * 64, j * 64:(j + 1) * 64] = w[:, :, 2 * pt + j]
    rbias = np.asarray(inp["na_rel_bias"], np.float32)
    qc = np.arange(64)
    kc = np.arange(64)
    ws = np.clip(qc - 8, 0, 48)
    inwin = (kc[:, None] >= ws[None, :]) & (kc[:, None] < ws[None, :] + 16)
    ci = np.clip(kc[:, None] - qc[None, :] + 15, 0, 30)
    t2 = np.full((L_ALL, 128, 14, 4, 64), NEG, np.float32)
    for kh in range(2):
        for dr in range(14):
            g = rbias[:, :, dr + kh, :][:, :, ci]
            g = np.where(inwin[None, None], g, np.float32(NEG))
            t2[:, kh * 64:(kh + 1) * 64, dr, :, :] = np.transpose(g, (0, 2, 1, 3))
    return pp, rb, lbd, t2


def build_level_masks():
    t = np.arange(128)
    out = np.zeros((128, 14, 128), np.float32)
    for k in range(7):
        ll = (((t[:, None] >> (k + 1)) == (t[None, :] >> (k + 1))) & (((t[:, None] >> k) & 1) == 1)
              & (((t[None, :] >> k) & 1) == 0))
        out[:, k, :] = ll
        out[:, 7 + k, :] = ll.T
    return out.reshape(128, 14 * 128)


class Sched:
    ISSUERS = ["pe", "act", "dve", "pool", "sp"]

    def __init__(self, nc, arena_name, psum_name):
        self.nc = nc
        self.semnames = ["pe", "act", "dve", "pool"]
        for q in ("sp", "act", "pool"):
            for i in range(NSLOT):
                self.semnames.append(f"{q}q{i}")
        self.sidx = {n: i for i, n in enumerate(self.semnames)}
        ns = len(self.semnames)
        self.ns = ns
        self.spaces = {arena_name: 0, psum_name: 1}
        ncell = [ARENA_F32 * 4 // CELL + 2, 16384 // CELL + 2]
        self.W = [np.zeros((4, c, ns), np.int64) for c in ncell]
        self.R = [np.zeros((4, c, ns), np.int64) for c in ncell]
        self.cnt = {e: 0 for e in ["pe", "act", "dve", "pool"]}
        self.dman = {"sp": 0, "act": 0, "pool": 0}
        self.seen = {e: np.zeros(ns, np.int64) for e in self.ISSUERS}
        self.q = {e: [] for e in self.ISSUERS}
        self.nops = 0
        self.limit = None
        self.trace = []

    def region(self, ap):
        sp = self.spaces.get(ap.tensor.name)
        if sp is None:
            return None
        a = ap.ap
        ds = DSZ[ap.dtype]
        rowlen = a[0][0]
        pcount = a[0][1]
        p0 = ap.offset // rowlen
        c0 = ap.offset % rowlen
        lo = hi = c0
        for step, cnt in a[1:]:
            ext = step * (cnt - 1)
            if ext < 0:
                lo += ext
            else:
                hi += ext
        blo, bhi = lo * ds, (hi + 1) * ds
        if sp == 1:
            blo = blo // 2048 * 2048
            bhi = (bhi + 2047) // 2048 * 2048
        return (sp, p0 // 32, (p0 + pcount + 31) // 32, blo // CELL, (bhi + CELL - 1) // CELL)

    def _need(self, r, w):
        need = np.zeros(self.ns, np.int64)
        rr = [x for x in (self.region(a) for a in r) if x is not None]
        wr = [x for x in (self.region(a) for a in w) if x is not None]
        for (sp, p0, p1, c0, c1) in rr:
            need = np.maximum(need, self.W[sp][p0:p1, c0:c1].max(axis=(0, 1)))
        for (sp, p0, p1, c0, c1) in wr:
            need = np.maximum(need, self.W[sp][p0:p1, c0:c1].max(axis=(0, 1)))
            need = np.maximum(need, self.R[sp][p0:p1, c0:c1].max(axis=(0, 1)))
        return need, rr, wr

    def _mark(self, rr, wr, si, val):
        for (sp, p0, p1, c0, c1) in wr:
            self.W[sp][p0:p1, c0:c1, si] = val
        for (sp, p0, p1, c0, c1) in rr:
            self.R[sp][p0:p1, c0:c1, si] = val

    def op(self, eng, fn, r=(), w=()):
        if self.limit is not None and self.nops >= self.limit:
            return
        need, rr, wr = self._need(r, w)
        if eng == "pe":
            need[self.sidx["pe"]] = 0
        seen = self.seen[eng]
        waits = [(i, int(need[i])) for i in np.nonzero(need > seen)[0]]
        np.maximum(seen, need, out=seen)
        self.cnt[eng] += 1
        if self.limit is not None:
            import sys as _s
            f = _s._getframe(1)
            self.trace.append((self.nops, eng, f.f_code.co_name, [(a.tensor.name, a.offset, a.ap) for a in list(w) + list(r)]))
        self.q[eng].append((waits, fn, self.sidx[eng], 1))
        self._mark(rr, wr, self.sidx[eng], self.cnt[eng])
        self.nops += 1

    def dma(self, queue, out, in_):
        if self.limit is not None and self.nops >= self.limit:
            return
        need, rr, wr = self._need([in_], [out])
        n = self.dman[queue]
        self.dman[queue] = n + 1
        slot, rnd = n % NSLOT, n // NSLOT
        si = self.sidx[f"{queue}q{slot}"]
        need[si] = max(need[si], 16 * rnd)
        seen = self.seen[queue]
        waits = [(i, int(need[i])) for i in np.nonzero(need > seen)[0]]
        np.maximum(seen, need, out=seen)
        fn = (lambda e, out=out, in_=in_: e.dma_start(out=out, in_=in_))
        self.q[queue].append((waits, fn, si, 16))
        self._mark(rr, wr, si, 16 * (rnd + 1))
        self.nops += 1

    def finish(self):
        final = np.zeros(self.ns, np.int64)
        for e in ["pe", "act", "dve", "pool"]:
            final[self.sidx[e]] = self.cnt[e]
        for qn in ("sp", "act", "pool"):
            n = self.dman[qn]
            for s in range(NSLOT):
                k = (n - s + NSLOT - 1) // NSLOT if n > s else 0
                final[self.sidx[f"{qn}q{s}"]] = 16 * k
        self.final = final

    def emit(self, stack):
        nc = self.nc
        sems = [stack.enter_context(nc.semaphore(f"s_{n}")) for n in self.semnames]
        block = stack.enter_context(nc.Block())

        def replay(name, e):
            for waits, fn, si, inc in self.q[name]:
                for (s, v) in waits:
                    e.wait_ge(sems[s], v)
                fn(e).then_inc(sems[si], inc)
            if name == "sp":
                for s in range(self.ns):
                    if self.final[s] > 0:
                        e.wait_ge(sems[s], int(self.final[s]))

        @block.tensor
        def _(e):
            replay("pe", e)

        @block.scalar
        def _(e):
            replay("act", e)

        @block.vector
        def _(e):
            replay("dve", e)

        @block.gpsimd
        def _(e):
            replay("pool", e)

        @block.sync
        def _(e):
            replay("sp", e)


class Arena:
    def __init__(self, base_ap_f32, ncols_f32, start=0):
        self.base = base_ap_f32
        self.lo = start * 4
        self.top = self.lo
        self.limit = (start + ncols_f32) * 4
        self.peak = self.top

    def alloc(self, cols, dtype, parts=128):
        nbytes = cols * DSZ[dtype]
        nbytes = (nbytes + CELL - 1) // CELL * CELL
        off = self.top
        self.top += nbytes
        self.peak = max(self.peak, self.top)
        assert self.top <= self.limit, f"arena overflow {self.top} > {self.limit}"
        ap = self.base[0:parts, off // 4:(off + nbytes) // 4]
        if dtype != F32:
            ap = ap.bitcast(dtype)
        return ap[:, 0:cols]

    def mark(self):
        return self.top

    def release(self, m):
        self.top = m


def build_program(n_layers=L_ALL, dbg=None):
    nc = bass.Bass("TRN2", target_bir_lowering=False)
    dt = nc.dram_tensor
    x_d = dt("x", [S, D], F32, kind="ExternalInput").ap()
    w_in_d = dt("w_in", [L_ALL, D, IN_COLS], F32, kind="ExternalInput").ap()
    w_out_d = dt("w_out", [L_ALL, D, D], F32, kind="ExternalInput").ap()
    w1_d = dt("w_mlp1", [L_ALL, D, 4 * D], F32, kind="ExternalInput").ap()
    w2_d = dt("w_mlp2", [L_ALL, 4 * D, D], F32, kind="ExternalInput").ap()
    wup_d = dt("rwkv_w_up", [L_ALL, 2, 64, 256], F32, kind="ExternalInput").ap()
    aup_d = dt("rwkv_a_up", [L_ALL, 2, 64, 256], F32, kind="ExternalInput").ap()
    gup_d = dt("rwkv_g_up", [L_ALL, 128, 256], F32, kind="ExternalInput").ap()
    pp_d = dt("pp", [128, PP_TOT], F32, kind="ExternalInput").ap()
    rb_d = dt("rb", [L_ALL, 128, RB_L], F32, kind="ExternalInput").ap()
    lbd_d = dt("lbd", [L_ALL, 2, 2, 2, 128, 128], F32, kind="ExternalInput").ap()
    t2_d = dt("t2", [L_ALL, 128, 14 * 256], F32, kind="ExternalInput").ap()
    lvl_d = dt("lvl", [128, 14 * 128], F32, kind="ExternalInput").ap()
    out_d = dt("out", [S, D], F32, kind="ExternalOutput").ap()
    dbg_d = None
    if dbg:
        dbg_d = dt("dbg", [D, S], F32, kind="ExternalOutput").ap()

    stack = ExitStack()
    arena_t = stack.enter_context(nc.sbuf_tensor("arena", [128, ARENA_F32], F32))
    ps_t = stack.enter_context(nc.psum_tensor("ps", [128, 4096], F32))
    arena_ap = arena_t[:, :]
    ps_ap = ps_t[:, :]
    Sc = Sched(nc, "arena", "ps")
    if dbg and dbg.get("limit"):
        Sc.limit = dbg["limit"]
    A = Arena(arena_ap, ARENA_F32)

    def bank(b):
        return ps_ap[:, b * 512:(b + 1) * 512]

    rot = [0]

    def gbank():
        b = rot[0] % 8
        rot[0] += 1
        return bank(b)

    def mm(out, lhsT, rhs, start=True, stop=True, skip=False):
        Sc.op("pe", lambda e: e.matmul(out, lhsT=lhsT, rhs=rhs, start=start, stop=stop, skip_group_check=skip),
              r=[lhsT, rhs], w=[out])

    def tr(out, in_, ident):
        Sc.op("pe", lambda e: e.transpose(out, in_, ident), r=[in_, ident], w=[out])

    def act(out, in_, func, scale=None, bias=None):
        kw = {}
        rd = [in_]
        if scale is not None:
            kw["scale"] = scale
            if not isinstance(scale, (int, float)):
                rd.append(scale)
        if bias is not None:
            kw["bias"] = bias
            if not isinstance(bias, (int, float)):
                rd.append(bias)
        Sc.op("act", lambda e: e.activation(out=out, in_=in_, func=func, **kw), r=rd, w=[out])

    def tt(out, in0, in1, op, eng="dve"):
        Sc.op(eng, lambda e: e.tensor_tensor(out=out, in0=in0, in1=in1, op=op), r=[in0, in1], w=[out])

    def ts(out, in0, s1, op0, s2=None, op1=None, eng="dve"):
        rd = [in0] + [s for s in (s1, s2) if s is not None and not isinstance(s, (int, float))]
        if op1 is None:
            Sc.op(eng, lambda e: e.tensor_scalar(out=out, in0=in0, scalar1=s1, scalar2=None, op0=op0), r=rd, w=[out])
        else:
            Sc.op(eng, lambda e: e.tensor_scalar(out=out, in0=in0, scalar1=s1, scalar2=s2, op0=op0, op1=op1),
                  r=rd, w=[out])

    def stt(out, in0, scalar, in1, op0, op1):
        rd = [in0, in1] + ([] if isinstance(scalar, (int, float)) else [scalar])
        Sc.op("dve", lambda e: e.scalar_tensor_tensor(out=out, in0=in0, scalar=scalar, in1=in1, op0=op0, op1=op1),
              r=rd, w=[out])

    def cp(out, in_, eng="dve"):
        if eng == "act":
            Sc.op("act", lambda e: e.activation(out=out, in_=in_, func=AF.Copy), r=[in_], w=[out])
        else:
            Sc.op(eng, lambda e: e.tensor_copy(out=out, in_=in_), r=[in_], w=[out])

    def memset(out, val, eng="pool"):
        Sc.op(eng, lambda e: e.memset(out, val), r=[], w=[out])

    def recip(out, in_):
        Sc.op("dve", lambda e: e.reciprocal(out=out, in_=in_), r=[in_], w=[out])

    def scan(out, d0, d1, init=0.0):
        Sc.op("dve", lambda e: e.tensor_tensor_scan(out=out, data0=d0, data1=d1, initial=init, op0=ALU.mult,
                                                    op1=ALU.add), r=[d0, d1], w=[out])

    def reduce_sum(out, in_):
        Sc.op("dve", lambda e: e.tensor_reduce(out=out, in_=in_, axis=AX.X, op=ALU.add), r=[in_], w=[out])

    def affsel(out, in_, pattern, cmp, fill, base, cm):
        Sc.op("pool", lambda e: e.affine_select(out=out, in_=in_, pattern=pattern, compare_op=cmp, fill=fill,
                                                base=base, channel_multiplier=cm), r=[in_], w=[out])

    def dma(q, out, in_):
        Sc.dma(q, out, in_)

    hF = A.alloc(8 * S, F32).rearrange("p (f t) -> p f t", f=8)
    xnT = A.alloc(8 * S, BF16).rearrange("p (f t) -> p f t", f=8)
    xn_region_start_f32 = (A.top - 8 * S * 2) // 4
    pp = A.alloc(PP_TOT, F32)
    ppd = A.alloc(L_ALL * 32, F32)
    rbt = A.alloc(RB_L, F32)
    identb = A.alloc(128, BF16)
    identf = A.alloc(128, F32)
    onesb = A.alloc(128, BF16)
    onesf = A.alloc(128, F32)
    blkb = A.alloc(128, BF16)
    headsel = A.alloc(2, BF16)
    mU = A.alloc(128, F32)
    mL = A.alloc(128, F32)
    mUI = A.alloc(128, F32)
    mLI = A.alloc(128, F32)
    nmU = A.alloc(128, F32)
    nmL = A.alloc(128, F32)
    nmUI = A.alloc(128, F32)
    nmLI = A.alloc(128, F32)
    negU4 = A.alloc(512, BF16)
    negL4 = A.alloc(512, BF16)
    lvl = A.alloc(14 * 128, BF16).rearrange("p (k c) -> p k c", k=14)
    ident4 = A.alloc(512, BF16)
    wb_region_start_f32 = A.top // 4
    wbufs = [A.alloc(8 * 512, BF16).rearrange("p (k c) -> p k c", k=8) for _ in range(2)]
    wb_i = [0]

    def next_wbuf():
        b = wbufs[wb_i[0] % 2]
        wb_i[0] += 1
        return b

    memset(identf, 0.0)
    affsel(identf, identf, [[-1, 128]], ALU.not_equal, 1.0, 0, 1)
    cp(identb, identf, "pool")
    memset(onesf, 1.0)
    memset(onesb, 1.0)
    for h in range(4):
        cp(ident4[:, h * 128:(h + 1) * 128], identf, "pool")
    dma("pool", lvl, lvl_d.rearrange("p (k c) -> p k c", k=14))
    affsel(mU, onesf, [[1, 128]], ALU.is_gt, 0.0, 0, -1)
    affsel(mUI, onesf, [[1, 128]], ALU.is_ge, 0.0, 0, -1)
    affsel(mL, onesf, [[-1, 128]], ALU.is_gt, 0.0, 0, 1)
    affsel(mLI, onesf, [[-1, 128]], ALU.is_ge, 0.0, 0, 1)
    for src, dst in ((mU, nmU), (mL, nmL), (mUI, nmUI), (mLI, nmLI)):
        ts(dst, src, -1.0, ALU.mult, eng="pool")
    for h in range(4):
        ts(negU4[:, h * 128:(h + 1) * 128], mU, NEG, ALU.mult, eng="pool")
        ts(negL4[:, h * 128:(h + 1) * 128], mL, NEG, ALU.mult, eng="pool")
    memset(blkb, 0.0)
    memset(blkb[0:64, 0:64], 1.0)
    memset(blkb[64:128, 64:128], 1.0)
    memset(headsel, 0.0)
    memset(headsel[0:64, 0:1], 1.0)
    memset(headsel[64:128, 1:2], 1.0)
    dma("sp", pp, pp_d)
    for l in range(n_layers):
        b0 = l * PP_L
        d0 = l * 32
        o_mu0, o_mu1 = PP_OFF["mu0"][0] + b0, PP_OFF["mu1"][0] + b0
        tt(ppd[:, d0:d0 + 9], pp[:, o_mu0:o_mu0 + 9], pp[:, o_mu1:o_mu1 + 9], ALU.add)
        ts(ppd[:, d0:d0 + 9], ppd[:, d0:d0 + 9], -1.0, ALU.mult, 1.0, ALU.add)
        o_ka = PP_OFF["ka"][0] + b0
        ts(ppd[:, d0 + 9:d0 + 11], pp[:, o_ka:o_ka + 2], -1.0, ALU.mult, 1.0, ALU.add)
        o_lam = PP_OFF["lam"][0] + b0
        act(ppd[:, d0 + 11:d0 + 15], pp[:, o_lam:o_lam + 4], AF.Exp, scale=-1.0)
        act(ppd[:, d0 + 11:d0 + 15], ppd[:, d0 + 11:d0 + 15], AF.Ln, bias=1.0)
        ts(ppd[:, d0 + 15:d0 + 19], ppd[:, d0 + 11:d0 + 15], -16.0, ALU.mult)
        ts(ppd[:, d0 + 11:d0 + 15], ppd[:, d0 + 11:d0 + 15], -8.0, ALU.mult)
        o_w0, o_a0 = PP_OFF["w0"][0] + b0, PP_OFF["a0"][0] + b0
        ts(ppd[:, d0 + 19:d0 + 23], pp[:, o_w0:o_w0 + 4], -1.0, ALU.mult)
        ts(ppd[:, d0 + 23:d0 + 27], pp[:, o_a0:o_a0 + 4], -1.0, ALU.mult)

    def P_(l, name, j=0, n=1):
        o, k = PP_OFF[name]
        return pp[:, l * PP_L + o + j: l * PP_L + o + j + n]

    m0 = A.mark()
    xst = [A.alloc(D, F32) for _ in range(2)]
    for tt_i in range(NT):
        xs_ = xst[tt_i % 2]
        dma("sp", xs_, x_d[tt_i * 128:(tt_i + 1) * 128, :])
        for half in range(2):
            pb = gbank()
            for j in range(4):
                f = half * 4 + j
                tr(pb[:, j * 128:(j + 1) * 128], xs_[:, f * 128:(f + 1) * 128], identf)
            cp(hF[:, half * 4:half * 4 + 4, tt_i * 128:(tt_i + 1) * 128],
               pb.rearrange("p (j t) -> p j t", j=4), "act" if half else "dve")
    A.release(m0)

    def rmsnorm_to_xnT(nw_of_f):
        m = A.mark()
        sq = A.alloc(8 * 512, BF16).rearrange("p (f t) -> p f t", f=8)
        rstd = A.alloc(512, F32)
        for tc in range(4):
            tsl = slice(tc * 512, (tc + 1) * 512)
            for f in range(8):
                act(sq[:, f, :], hF[:, f, tsl], AF.Square)
            pb = gbank()
            for f in range(8):
                mm(pb, onesb, sq[:, f, :], start=(f == 0), stop=(f == 7))
            act(rstd, pb, AF.Sqrt, scale=1.0 / D, bias=1e-5)
            recip(rstd, rstd)
            for f in range(8):
                stt(xnT[:, f, tsl], hF[:, f, tsl], nw_of_f(f), rstd, ALU.mult, ALU.mult)
        A.release(m)

    def load_w_cols(src2d, c0, ncols, q="pool"):
        wb = next_wbuf()
        dma(q, wb[:, :, 0:ncols], src2d.rearrange("(k p) c -> p k c", p=128)[:, :, c0:c0 + ncols])
        return wb

    def proj_fm(wb, cw, ncw, evac):
        for tc in range(4):
            pb = gbank()
            for k in range(8):
                mm(pb[0:ncw, :], wb[:, k, cw:cw + ncw], xnT[:, k, tc * 512:(tc + 1) * 512], start=(k == 0),
                   stop=(k == 7))
            evac(tc, pb[0:ncw, :])

    def proj_tm(wb, cw, ncw, tile_i, tok0, out_psum):
        for k in range(8):
            mm(out_psum, xnT[:, k, tok0:tok0 + 128], wb[:, k, cw:cw + ncw], start=(k == 0), stop=(k == 7))

    def outproj_acc(l, g, ymT):
        m = A.mark()
        wo = A.alloc(2 * 1024, BF16).rearrange("p (k c) -> p k c", k=2)
        dma("pool", wo, w_out_d[l, g * 256:(g + 1) * 256, :].rearrange("(k p) c -> p k c", p=128))
        for ot in range(8):
            for tc in range(4):
                tsl = slice(tc * 512, (tc + 1) * 512)
                pb = gbank()
                for k in range(2):
                    mm(pb, wo[:, k, ot * 128:(ot + 1) * 128], ymT[:, k, tsl], start=(k == 0), stop=(k == 1))
                tt(hF[:, ot, tsl], pb, hF[:, ot, tsl], ALU.add)
        A.release(m)

    def dump_dbg(g, ymT):
        if dbg_d is None or dbg.get('nodump'):
            return
        m = A.mark()
        st = A.alloc(S, F32)
        for k in range(2):
            cp(st, ymT[:, k, :], "pool")
            dma("sp", dbg_d[g * 256 + k * 128: g * 256 + (k + 1) * 128, :], st)
        A.release(m)

    def conv4(stage, wcol, bias_ap, out, func=None, out2=None):
        m = A.mark()
        acc = A.alloc(S, F32)
        ts(acc, stage[:, 0:S], wcol(0), ALU.mult)
        for k in range(1, 4):
            stt(acc, stage[:, k:k + S], wcol(k), acc, ALU.mult, ALU.add)
        if func is None:
            ts(out, acc, bias_ap, ALU.add)
            if out2 is not None:
                cp(out2, out, "pool")
        else:
            act(out, acc, func, bias=bias_ap)
        A.release(m)

    def ssd_mixer(l):
        m_ssd = A.mark()
        xs_tok = A.alloc(NT * 256, BF16).rearrange("p (i c) -> p i c", i=NT)
        B_tok = A.alloc(NT * 256, BF16).rearrange("p (i c) -> p i c", i=NT)
        BT = A.alloc(2 * S, BF16).rearrange("p (k t) -> p k t", k=2)
        CT = A.alloc(2 * S, BF16).rearrange("p (k t) -> p k t", k=2)
        o_alog, o_dtb = RB_OFF["alog"][0], RB_OFF["dtb"][0]
        m1 = A.mark()
        stage = A.alloc(S + 3, F32)
        xsT = A.alloc(2 * S, BF16).rearrange("p (k t) -> p k t", k=2)
        memset(stage[:, 0:2], 0.0)
        memset(stage[:, S + 2:S + 3], 0.0)
        for half in range(2):
            wb = load_w_cols(w_in_d[l], SSD0 + 256 + half * 384, 384)
            for j in range(3):
                c = half * 3 + j
                proj_fm(wb, j * 128, 128, lambda tc, pb: cp(stage[:, 2 + tc * 512: 2 + (tc + 1) * 512], pb, "act"))
                dest = (xsT[:, c, :] if c < 2 else (BT[:, c - 2, :] if c < 4 else CT[:, c - 4, :]))
                conv4(stage, lambda k, c=c: P_(l, "scw", k * 6 + c), P_(l, "scb", c), dest, func=AF.Silu)
        for i in range(NT):
            pbb = gbank().bitcast(BF16)
            for k in range(2):
                tr(pbb[:, k * 128:(k + 1) * 128], xsT[:, k, i * 128:(i + 1) * 128], identb)
                tr(pbb[:, 256 + k * 128:256 + (k + 1) * 128], BT[:, k, i * 128:(i + 1) * 128], identb)
            cp(xs_tok[:, i, :], pbb[:, 0:256], "act")
            cp(B_tok[:, i, :], pbb[:, 256:512], "act")
        A.release(m1)
        z_tok = A.alloc(NT * 256, BF16).rearrange("p (i c) -> p i c", i=NT)
        y_tok = A.alloc(NT * 256, F32).rearrange("p (i c) -> p i c", i=NT)
        dt_tok = A.alloc(NT * 8, F32).rearrange("p (i c) -> p i c", i=NT)
        adt = A.alloc(NT * 8, F32).rearrange("p (i c) -> p i c", i=NT)
        Abc = A.alloc(8, F32)
        act(Abc, rbt[:, o_alog:o_alog + 8], AF.Exp)
        ts(Abc, Abc, -1.0, ALU.mult)
        wz = load_w_cols(w_in_d[l], SSD0, 256)
        for i in range(NT):
            pb = gbank()
            proj_tm(wz, 0, 256, i, i * 128, pb[:, 0:256])
            act(z_tok[:, i, :], pb[:, 0:256], AF.Silu)
        wdt = load_w_cols(w_in_d[l], SSD0 + 1024, 8)
        for i in range(NT):
            pb = gbank()
            proj_tm(wdt, 0, 8, i, i * 128, pb[:, 0:8])
            tt(dt_tok[:, i, :], pb[:, 0:8], rbt[:, o_dtb:o_dtb + 8], ALU.add)
        dtf = dt_tok.rearrange("p i c -> p (i c)")
        act(dtf, dtf, AF.Exp)
        act(dtf, dtf, AF.Ln, bias=1.0)
        tt(adt, dt_tok, Abc.unsqueeze(1).to_broadcast([128, NT, 8]), ALU.mult)
        m2 = A.mark()
        Hs = [A.alloc(256, F32) for _ in range(2)]
        Hb = [A.alloc(256, BF16) for _ in range(2)]
        for d in range(2):
            memset(Hs[d], 0.0)
            memset(Hb[d], 0.0)
        class _T:
            pass
        tmps = []
        for d in range(2):
            T_ = _T()
            T_.rhs_seg = A.alloc(512, F32)
            T_.dec = A.alloc(512, BF16)
            T_.scT = A.alloc(512, BF16)
            T_.xdt = A.alloc(256, BF16)
            T_.xde = A.alloc(256, BF16)
            T_.ecs = A.alloc(8, F32)
            T_.dte = A.alloc(8, F32)
            T_.etot = A.alloc(8, F32)
            tmps.append(T_)

        def ssd_step(d, c, first):
            T_ = tmps[d]
            rhs_seg, dec, scT, xdt, xde, ecs, dte, etot = (T_.rhs_seg, T_.dec, T_.scT, T_.xdt, T_.xde, T_.ecs,
                                                           T_.dte, T_.etot)
            csl = slice(c * 128, (c + 1) * 128)
            a4 = adt[:, c, d * 4:(d + 1) * 4]
            msk = mUI if d == 0 else mLI
            pS = gbank()
            mm(pS[:, 0:4], msk, a4)
            mm(pS[:, 8:12], onesf, a4)
            act(ecs[:, 0:4], pS[:, 0:4], AF.Exp)
            cp(dte[:, 4:8], pS[:, 0:4], "dve")
            tt(dte[:, 0:4], pS[:, 8:12], dte[:, 4:8], ALU.subtract)
            act(dte[:, 0:4], dte[:, 0:4], AF.Exp)
            act(etot[:, 0:4], pS[:, 8:12], AF.Exp)
            tt(rhs_seg.rearrange("p (h l) -> p h l", h=4),
               msk.unsqueeze(1).to_broadcast([128, 4, 128]),
               a4.unsqueeze(2).to_broadcast([128, 4, 128]), ALU.mult)
            dt4 = dt_tok[:, c, d * 4:(d + 1) * 4]
            tt(xdt.rearrange("p (h q) -> p h q", h=4), xs_tok[:, c, :].rearrange("p (h q) -> p h q", h=4),
               dt4.unsqueeze(2).to_broadcast([128, 4, 64]), ALU.mult)
            yield
            pD = gbank()
            mm(pD, (mL if d == 0 else mU), rhs_seg, start=True, stop=False)
            mm(pD, identb, (negL4 if d == 0 else negU4), start=False, stop=True)
            act(dec, pD, AF.Exp)
            tt(xde.rearrange("p (h q) -> p h q", h=4), xdt.rearrange("p (h q) -> p h q", h=4),
               dte[:, 0:4].unsqueeze(2).to_broadcast([128, 4, 64]), ALU.mult)
            yield
            pG = gbank()
            for g in range(2):
                mm(pG[:, g * 128:(g + 1) * 128], BT[:, g, csl], CT[:, g, csl])
            tt(scT.rearrange("p (g r l) -> p g r l", g=2, r=2),
               dec.rearrange("p (g r l) -> p g r l", g=2, r=2),
               pG[:, 0:256].rearrange("p (g l) -> p g l", g=2).unsqueeze(2).to_broadcast([128, 2, 2, 128]),
               ALU.mult)
            yield
            pY = gbank()
            for h in range(4):
                mm(pY[:, h * 64:(h + 1) * 64], scT[:, h * 128:(h + 1) * 128], xdt[:, h * 64:(h + 1) * 64])
                mm(pY[:, 256 + h * 64:256 + (h + 1) * 64], CT[:, h // 2, csl], Hb[d][:, h * 64:(h + 1) * 64])
            pH = gbank()
            for h in range(4):
                mm(pH[:, h * 64:(h + 1) * 64], B_tok[:, c, (h // 2) * 128:(h // 2 + 1) * 128],
                   xde[:, h * 64:(h + 1) * 64])
            if first:
                cp(y_tok[:, c, :], pY[:, 0:256], "act")
            else:
                tt(y_tok[:, c, :], pY[:, 0:256], y_tok[:, c, :], ALU.add)
            for h in range(4):
                stt(y_tok[:, c, h * 64:(h + 1) * 64], pY[:, 256 + h * 64:256 + (h + 1) * 64], ecs[:, h:h + 1],
                    y_tok[:, c, h * 64:(h + 1) * 64], ALU.mult, ALU.add)
            for h in range(4):
                stt(Hs[d][:, h * 64:(h + 1) * 64], Hs[d][:, h * 64:(h + 1) * 64], etot[:, h:h + 1],
                    pH[:, h * 64:(h + 1) * 64], ALU.mult, ALU.add)
            cp(Hb[d], Hs[d], "act")
            yield

        for st_i in range(NT):
            first = st_i < NT // 2
            gens = [ssd_step(0, st_i, first), ssd_step(1, NT - 1 - st_i, first)]
            alive = True
            while alive:
                alive = False
                for g_ in gens:
                    try:
                        next(g_)
                        alive = True
                    except StopIteration:
                        pass
        A.release(m2)
        ymT = A.alloc(2 * S, BF16).rearrange("p (k t) -> p k t", k=2)
        m3 = A.mark()
        o_dsk, o_snw = RB_OFF["dsk"][0], RB_OFF["snw"][0]
        NF = 8
        ftmps = [(A.alloc(256, F32), A.alloc(256, BF16), A.alloc(2, F32)) for _ in range(NF)]

        def ssd_fin(i, bufs):
            tmp, tmpb, ss = bufs
            tt(tmp.rearrange("p (h q) -> p h q", h=4), xs_tok[:, i, :].rearrange("p (h q) -> p h q", h=4),
               rbt[:, o_dsk:o_dsk + 4].unsqueeze(2).to_broadcast([128, 4, 64]), ALU.mult)
            tt(tmp, tmp, y_tok[:, i, :], ALU.add)
            tt(tmp, tmp, z_tok[:, i, :], ALU.mult)
            yield
            tt(y_tok[:, i, :], tmp, tmp, ALU.mult)
            reduce_sum(ss[:, 0:1], y_tok[:, i, :])
            yield
            act(ss[:, 1:2], ss[:, 0:1], AF.Ln, scale=1.0 / 256, bias=1e-5)
            act(ss[:, 1:2], ss[:, 1:2], AF.Exp, scale=-0.5)
            yield
            stt(tmpb, tmp, ss[:, 1:2], rbt[:, o_snw:o_snw + 256], ALU.mult, ALU.mult)
            yield
            pbb = gbank().bitcast(BF16)
            for k in range(2):
                tr(pbb[:, k * 128:(k + 1) * 128], tmpb[:, k * 128:(k + 1) * 128], identb)
            cp(ymT[:, :, i * 128:(i + 1) * 128], pbb[:, 0:256].rearrange("p (k t) -> p k t", k=2), "act")
            yield

        for i0_ in range(0, NT, NF):
            gens = [ssd_fin(i0_ + j, ftmps[j]) for j in range(NF)]
            alive = True
            while alive:
                alive = False
                for g_ in gens:
                    try:
                        next(g_)
                        alive = True
                    except StopIteration:
                        pass
        A.release(m3)
        dump_dbg(1, ymT)
        outproj_acc(l, 1, ymT)
        A.release(m_ssd)

    def lru_mixer(l):
        m_l = A.mark()
        ymT = A.alloc(2 * S, BF16).rearrange("p (k t) -> p k t", k=2)
        gate = A.alloc(2 * S, BF16).rearrange("p (k t) -> p k t", k=2)
        xf = A.alloc(2 * S, F32).rearrange("p (k t) -> p k t", k=2)
        xfb = A.alloc(2 * S, BF16).rearrange("p (k t) -> p k t", k=2)
        m_st = A.mark()
        stage = A.alloc(S + 3, F32)
        memset(stage[:, 0:2], 0.0)
        memset(stage[:, S + 2:S + 3], 0.0)
        wb = load_w_cols(w_in_d[l], LRU0, 512)
        for c in range(2):
            proj_fm(wb, c * 128, 128,
                    lambda tc, pb, c=c: act(gate[:, c, tc * 512:(tc + 1) * 512], pb, AF.Gelu_apprx_tanh))
        for c in range(2):
            proj_fm(wb, 256 + c * 128, 128,
                    lambda tc, pb: cp(stage[:, 2 + tc * 512: 2 + (tc + 1) * 512], pb, "act"))
            conv4(stage, lambda k, c=c: P_(l, "lcw", k * 2 + c), P_(l, "lcb", c), xf[:, c, :], out2=xfb[:, c, :])
        A.release(m_st)
        m_tmp = A.mark()
        wg = A.alloc(8 * 128, BF16).rearrange("p (a c) -> p a c", a=8)
        dma("pool", wg, lbd_d[l].rearrange("w d t p c -> p (w d t) c"))
        rec = A.alloc(S, F32)
        inp = A.alloc(S, F32)
        aa = A.alloc(S, F32)
        hsum = A.alloc(S, F32)
        for pt in range(2):
            for d in range(2):
                for which, dst, bname in ((0, rec, "lab"), (1, inp, "lxb")):
                    for tc in range(4):
                        tsl = slice(tc * 512, (tc + 1) * 512)
                        pb = gbank()
                        mm(pb, wg[:, which * 4 + d * 2 + pt, :], xfb[:, pt, tsl])
                        act(dst[:, tsl], pb, AF.Sigmoid, bias=P_(l, bname, d * 2 + pt))
                lc = ppd[:, l * 32 + 11 + d * 2 + pt: l * 32 + 12 + d * 2 + pt]
                lc2 = ppd[:, l * 32 + 15 + d * 2 + pt: l * 32 + 16 + d * 2 + pt]
                act(aa, rec, AF.Exp, scale=lc)
                act(rec, rec, AF.Exp, scale=lc2)
                act(rec, rec, AF.Sqrt, scale=-1.0, bias=1.0)
                tt(inp, inp, rec, ALU.mult)
                tt(inp, inp, xf[:, pt, :], ALU.mult)
                if d == 0:
                    scan(hsum, aa, inp)
                else:
                    scan(rec[:, ::-1], aa[:, ::-1], inp[:, ::-1])
                    tt(hsum, hsum, rec, ALU.add)
                    tt(ymT[:, pt, :], hsum, gate[:, pt, :], ALU.mult)
        A.release(m_tmp)
        dump_dbg(2, ymT)
        outproj_acc(l, 2, ymT)
        A.release(m_l)

    def na_mixer(l):
        m_n = A.mark()
        ymT = A.alloc(2 * S, BF16).rearrange("p (k t) -> p k t", k=2)
        m_na_tmp = A.mark()
        qm = [A.alloc(S, BF16) for _ in range(4)]
        for h in range(4):
            z0 = 64 - (h % 2) * 64
            memset(qm[h][z0:z0 + 64, :], 0.0)
        kT = A.alloc(2 * S, BF16).rearrange("p (k t) -> p k t", k=2)
        va = [A.alloc(NT * 260, BF16).rearrange("p (i h q) -> p i h q", i=NT, h=4) for _ in range(2)]
        t2 = A.alloc(14 * 256, BF16).rearrange("p (r x) -> p r x", r=14)
        dma("pool", t2, t2_d[l].rearrange("p (r x) -> p r x", r=14))
        wb = load_w_cols(w_in_d[l], NA0, 512)
        def q_evac(tc, pb, c):
            tsl = slice(tc * 512, (tc + 1) * 512)
            act(qm[2 * c][0:64, tsl], pb[0:64, :], AF.Copy, scale=0.125)
            act(qm[2 * c + 1][64:128, tsl], pb[64:128, :], AF.Copy, scale=0.125)
        for c in range(2):
            proj_fm(wb, c * 128, 128, lambda tc, pb, c=c: q_evac(tc, pb, c))
        for c in range(2):
            proj_fm(wb, 256 + c * 128, 128, lambda tc, pb, c=c: cp(kT[:, c, tc * 512:(tc + 1) * 512], pb, "act"))
        wv = load_w_cols(w_in_d[l], NA0 + 512, 256)
        for al in range(2):
            memset(va[al][:, :, :, 64:65], 1.0)
            for i in range(NT - al):
                pb = gbank()
                proj_tm(wv, 0, 256, i, i * 128 + al * 64, pb[:, 0:256])
                cp(va[al][:, i, :, 0:64], pb[:, 0:256].rearrange("p (h q) -> p h q", h=4), "act" if i % 2 else "dve")
        NSLOT_NA = 3
        slots = []
        for _ in range(NSLOT_NA):
            slots.append((A.alloc(1024, F32), A.alloc(1024, BF16), A.alloc(4, F32), A.alloc(256, BF16, parts=64)))

        def na_row(r, slot):
            ein, pT, rc, ytok = slot
            kr0 = min(max(r - 4, 0), 24)
            dr0 = kr0 - r + 7
            al = kr0 % 2
            ti0 = kr0 // 2
            q0 = r * 64
            pS = [gbank(), gbank()]
            for kt in range(4):
                k0 = kr0 * 64 + kt * 128
                for h in range(4):
                    col = (kt * 4 + h) * 64
                    mm(pS[col // 512][:, col % 512: col % 512 + 64], kT[:, h // 2, k0:k0 + 128],
                       qm[h][:, q0:q0 + 64])
            for half in range(2):
                tt(ein[:, half * 512:(half + 1) * 512].rearrange("p (a x) -> p a x", a=2),
                   pS[half].rearrange("p (a x) -> p a x", a=2),
                   t2[:, dr0 + 4 * half: dr0 + 4 * half + 3: 2, :], ALU.add)
            yield
            act(pT, ein, AF.Exp)
            yield
            pO = gbank()
            for h in range(4):
                for kt in range(4):
                    col = (kt * 4 + h) * 64
                    mm(pO[0:64, h * 65:(h + 1) * 65], pT[:, col:col + 64], va[al][:, ti0 + kt, h, :],
                       start=(kt == 0), stop=(kt == 3))
            pO3 = pO[0:64, 0:260].rearrange("p (h q) -> p h q", h=4)
            recip(rc[0:64, :], pO3[:, :, 64])
            tt(ytok.rearrange("p (h q) -> p h q", h=4), pO3[:, :, 0:64],
               rc[0:64, :].unsqueeze(2).to_broadcast([64, 4, 64]), ALU.mult)
            yield
            pbb = gbank().bitcast(BF16)
            for k in range(2):
                tr(pbb[:, k * 64:(k + 1) * 64], ytok[:, k * 128:(k + 1) * 128], identb[0:64, 0:64])
            cp(ymT[:, :, q0:q0 + 64], pbb[:, 0:128].rearrange("p (k t) -> p k t", k=2), "act")
            yield

        for r0 in range(0, 32, NSLOT_NA):
            gens = [na_row(r, slots[i]) for i, r in enumerate(range(r0, min(32, r0 + NSLOT_NA)))]
            alive = True
            while alive:
                alive = False
                for g_ in gens:
                    try:
                        next(g_)
                        alive = True
                    except StopIteration:
                        pass
        A.release(m_na_tmp)
        dump_dbg(3, ymT)
        outproj_acc(l, 3, ymT)
        A.release(m_n)

    def rwkv_mixer(l):
        m_r = A.mark()
        rT = A.alloc(2 * S, BF16).rearrange("p (k t) -> p k t", k=2)
        kT = A.alloc(2 * S, BF16).rearrange("p (k t) -> p k t", k=2)
        wdT = A.alloc(S, BF16)
        adT = A.alloc(S, BF16)
        gdT = A.alloc(S, BF16)
        v_tok = A.alloc(NT * 256, BF16).rearrange("p (i c) -> p i c", i=NT)
        wup = A.alloc(256, BF16)
        aup = A.alloc(256, BF16)
        gup = A.alloc(256, BF16)
        dma("pool", wup, wup_d[l].rearrange("d r c -> (d r) c"))
        dma("pool", aup, aup_d[l].rearrange("d r c -> (d r) c"))
        dma("pool", gup, gup_d[l])
        m1 = A.mark()
        stage = A.alloc(S + 2, F32)
        t1 = A.alloc(S, F32)
        vT = A.alloc(2 * S, BF16).rearrange("p (k t) -> p k t", k=2)
        memset(stage[:, 0:1], 0.0)
        memset(stage[:, S + 1:S + 2], 0.0)
        dests = [rT[:, 0, :], rT[:, 1, :], kT[:, 0, :], kT[:, 1, :], vT[:, 0, :], vT[:, 1, :], None, None, gdT]
        for grp in range(3):
            wb = load_w_cols(w_in_d[l], RW0 + grp * 384, 384)
            for j in range(3):
                c = grp * 3 + j
                proj_fm(wb, j * 128, 128, lambda tc, pb: cp(stage[:, 1 + tc * 512: 1 + (tc + 1) * 512], pb, "act"))
                c0 = ppd[:, l * 32 + c: l * 32 + c + 1]
                ts(t1, stage[:, 1:S + 1], c0, ALU.mult)
                stt(t1, stage[:, 0:S], P_(l, "mu0", c), t1, ALU.mult, ALU.add)
                if c == 6:
                    stt(t1, stage[:, 2:S + 2], P_(l, "mu1", c), t1, ALU.mult, ALU.add)
                    act(wdT, t1, AF.Tanh)
                elif c == 7:
                    stt(t1, stage[:, 2:S + 2], P_(l, "mu1", c), t1, ALU.mult, ALU.add)
                    cp(adT, t1, "act")
                elif c == 8:
                    stt(t1, stage[:, 2:S + 2], P_(l, "mu1", c), t1, ALU.mult, ALU.add)
                    act(dests[c], t1, AF.Sigmoid)
                else:
                    stt(dests[c], stage[:, 2:S + 2], P_(l, "mu1", c), t1, ALU.mult, ALU.add)
        for i in range(NT):
            pbb = gbank().bitcast(BF16)
            for k in range(2):
                tr(pbb[:, k * 128:(k + 1) * 128], vT[:, k, i * 128:(i + 1) * 128], identb)
            cp(v_tok[:, i, :], pbb[:, 0:256], "act")
        A.release(m1)
        ymT = A.alloc(2 * S, BF16).rearrange("p (k t) -> p k t", k=2)
        y_tok = A.alloc(NT * 256, BF16).rearrange("p (i c) -> p i c", i=NT)
        bonus = A.alloc(NT * 4, F32).rearrange("p (i c) -> p i c", i=NT)

        XA_ = Arena(arena_ap, 8 * S // 2, start=xn_region_start_f32)

        XB_ = Arena(arena_ap, 2 * 8 * 512 // 2, start=wb_region_start_f32)

        class _XA:
            @staticmethod
            def alloc(cols, dtype):
                for ar in (XA_, XB_):
                    if ar.top + (cols * DSZ[dtype] + CELL - 1) // CELL * CELL <= ar.limit:
                        return ar.alloc(cols, dtype)
                return A.alloc(cols, dtype)
        XA = _XA

        class DirBuf:
            pass
        G_SETS = 3
        STs, STbs = [], []
        for d in range(2):
            st_ = XA.alloc(2 * 64, F32).rearrange("p (k i) -> p k i", k=2)
            stb_ = XA.alloc(2 * 64, BF16).rearrange("p (k i) -> p k i", k=2)
            memset(st_, 0.0)
            memset(stb_, 0.0)
            STs.append(st_)
            STbs.append(stb_)
        Xbs = [XA.alloc(256, BF16) for _ in range(2)]
        Ubs = [XA.alloc(256, BF16) for _ in range(2)]
        dbs = []
        for g_ in range(G_SETS):
            b = DirBuf()
            b.sa = XA.alloc(256, F32)
            b.sig = b.sa[:, 0:128]
            b.al = b.sa[:, 128:256]
            b.ein = [XA.alloc(129, F32) for _ in range(2)]
            for nm in ("eni", "eh", "kd", "bb", "kks", "kk"):
                setattr(b, nm, XA.alloc(128, BF16))
            b.sq = b.eh
            b.Bh = [XA.alloc(128, BF16) for _ in range(2)]
            b.Kh = [XA.alloc(128, BF16) for _ in range(2)]
            b.rkd = [XA.alloc(128, BF16) for _ in range(2)]
            b.Rt = [XA.alloc(128, BF16) for _ in range(2)]
            b.Qk = [XA.alloc(128, BF16) for _ in range(2)]
            b.Btm = [[XA.alloc(128, BF16) for _ in range(2)] for _ in range(2)]
            b.Qkm = [[XA.alloc(128, BF16) for _ in range(2)] for _ in range(2)]
            b.Ktm = [[XA.alloc(128, BF16) for _ in range(2)] for _ in range(2)]
            b.Rtm = [[XA.alloc(128, BF16) for _ in range(2)] for _ in range(2)]
            for grp in (b.Btm, b.Qkm, b.Ktm, b.Rtm):
                for hp_ in range(2):
                    for h2 in range(2):
                        z0 = 64 - h2 * 64
                        memset(grp[hp_][h2][z0:z0 + 64, :], 0.0)
            b.wdm = [XA.alloc(128, BF16) for _ in range(2)]
            b.adm = [XA.alloc(128, BF16) for _ in range(2)]
            for d_ in range(2):
                z0 = 64 - d_ * 64
                memset(b.wdm[d_][z0:z0 + 64, :], 0.0)
                memset(b.adm[d_][z0:z0 + 64, :], 0.0)
            b.BhT = XA.alloc(256, BF16)
            b.KhT = XA.alloc(256, BF16)
            b.Mf = XA.alloc(512, BF16)
            b.T = XA.alloc(512, BF16)
            b.Z = XA.alloc(512, BF16)
            b.W = XA.alloc(512, BF16)
            b.Akk = XA.alloc(512, BF16)
            b.Arb = XA.alloc(512, BF16)
            b.Ark = XA.alloc(512, BF16)
            dbs.append(b)
        gtmp = XA.alloc(256, F32)
        gt2 = XA.alloc(256, F32)
        gst = XA.alloc(16, F32)
        gout = gt2.bitcast(BF16)[:, 0:256]

        def finalize_chunk(c):
            y4 = gtmp.rearrange("p (h q) -> p h q", h=4)
            cp(gtmp, y_tok[:, c, :], "pool")
            reduce_sum(gst[:, 0:4], y4)
            tt(gt2, gtmp, gtmp, ALU.mult)
            reduce_sum(gst[:, 4:8], gt2.rearrange("p (h q) -> p h q", h=4))
            ts(gst[:, 0:4], gst[:, 0:4], 1.0 / 64, ALU.mult)
            tt(gst[:, 8:12], gst[:, 0:4], gst[:, 0:4], ALU.mult)
            stt(gst[:, 4:8], gst[:, 4:8], 1.0 / 64, gst[:, 8:12], ALU.mult, ALU.subtract)
            act(gst[:, 4:8], gst[:, 4:8], AF.Ln, bias=64e-5)
            act(gst[:, 4:8], gst[:, 4:8], AF.Exp, scale=-0.5)
            tt(y4, y4, gst[:, 0:4].unsqueeze(2).to_broadcast([128, 4, 64]), ALU.subtract)
            tt(y4, y4, gst[:, 4:8].unsqueeze(2).to_broadcast([128, 4, 64]), ALU.mult)
            o_w, o_b = RB_OFF["gnw"][0], RB_OFF["gnb"][0]
            tt(gtmp, gtmp, rbt[:, o_w:o_w + 256], ALU.mult)
            tt(gtmp, gtmp, rbt[:, o_b:o_b + 256], ALU.add)
            tt(gt2.rearrange("p (h q) -> p h q", h=4), v_tok[:, c, :].rearrange("p (h q) -> p h q", h=4),
               bonus[:, c, :].unsqueeze(2).to_broadcast([128, 4, 64]), ALU.mult)
            tt(gtmp, gtmp, gt2, ALU.add)
            pg = gbank()
            mm(pg[:, 0:256], gdT[:, c * 128:(c + 1) * 128], gup)
            tt(gout, gtmp, pg[:, 0:256], ALU.mult)
            pbb = gbank().bitcast(BF16)
            for k in range(2):
                tr(pbb[:, k * 128:(k + 1) * 128], gout[:, k * 128:(k + 1) * 128], identb)
            cp(ymT[:, :, c * 128:(c + 1) * 128], pbb[:, 0:256].rearrange("p (k t) -> p k t", k=2), "act")

        seen_y = [False] * NT
        seen_bonus = [False] * NT

        def step(d, c, b):
            csl = slice(c * 128, (c + 1) * 128)
            fwd = (d == 0)
            ST, STb, Xb, Ub = STs[d], STbs[d], Xbs[d], Ubs[d]
            m_strict_neg = nmU if fwd else nmL
            m_strict = mU if fwd else mL
            m_incl = mUI if fwd else mLI
            m_incl_neg = nmUI if fwd else nmLI

            def heads(fn):
                for h in range(4):
                    fn(h, h // 2, slice((h % 2) * 64, (h % 2) * 64 + 64), slice(h * 128, (h + 1) * 128))
            dsl = slice(d * 64, d * 64 + 64)
            cp(b.wdm[d][dsl, :], wdT[dsl, csl], "dve")
            cp(b.adm[d][dsl, :], adT[dsl, csl], "dve")
            for hp in range(2):
                pz = gbank()
                mm(pz[:, 0:128], wup[:, hp * 128:(hp + 1) * 128], b.wdm[d])
                mm(pz[:, 128:256], aup[:, hp * 128:(hp + 1) * 128], b.adm[d])
                nw0 = ppd[:, l * 32 + 19 + d * 2 + hp: l * 32 + 20 + d * 2 + hp]
                na0 = ppd[:, l * 32 + 23 + d * 2 + hp: l * 32 + 24 + d * 2 + hp]
                act(b.sig, pz[:, 0:128], AF.Exp, scale=-1.0, bias=nw0)
                act(b.al, pz[:, 128:256], AF.Exp, scale=-1.0, bias=na0)
                act(b.sa, b.sa, AF.Ln, bias=1.0)
                act(b.sa, b.sa, AF.Exp, scale=-1.0)
                if fwd:
                    cs = b.ein[hp][:, 1:129]
                    scan(cs, onesf, b.sig)
                    pad = b.ein[hp][:, 0:1]
                    eex = b.ein[hp][:, 0:128]
                    etot = b.ein[hp][:, 128:129]
                else:
                    cs = b.ein[hp][:, 0:128]
                    scan(cs[:, ::-1], onesf, b.sig[:, ::-1])
                    pad = b.ein[hp][:, 128:129]
                    eex = b.ein[hp][:, 1:129]
                    etot = b.ein[hp][:, 0:1]
                ts(b.kks, kT[:, hp, csl], P_(l, "kk", hp), ALU.mult)
                tt(b.sq, b.kks, b.kks, ALU.mult)
                pn = gbank()
                mm(pn[:, 0:128], blkb, b.sq)
                act(b.eni, cs, AF.Exp, scale=DECAY)
                act(cs, cs, AF.Exp, scale=-DECAY)
                act(b.kk, pn[:, 0:128], AF.Ln, bias=1e-12)
                act(b.kk, b.kk, AF.Exp, scale=-0.5)
                yield
                memset(pad, 1.0, "pool")
                ts(b.eh, b.eni, etot, ALU.mult)
                tt(b.kk, b.kk, b.kks, ALU.mult)
                omka = ppd[:, l * 32 + 9 + hp: l * 32 + 10 + hp]
                ts(b.kd, b.al, P_(l, "ka", hp), ALU.mult, omka, ALU.add)
                tt(b.kd, b.kd, kT[:, hp, csl], ALU.mult)
                tt(b.bb, b.al, b.kk, ALU.mult)
                tt(b.Rt[hp], rT[:, hp, csl], cs, ALU.mult)
                tt(b.Qk[hp], b.kk, eex, ALU.mult)
                for h2 in range(2):
                    pr = slice(h2 * 64, h2 * 64 + 64)
                    tt(b.Btm[hp][h2][pr, :], b.bb[pr, :], b.eni[pr, :], ALU.mult)
                    tt(b.Ktm[hp][h2][pr, :], b.kd[pr, :], b.eni[pr, :], ALU.mult)
                    cp(b.Qkm[hp][h2][pr, :], b.Qk[hp][pr, :], "pool")
                    cp(b.Rtm[hp][h2][pr, :], b.Rt[hp][pr, :], "pool")
                stt(b.Bh[hp], b.bb, -1.0, b.eh, ALU.mult, ALU.mult)
                tt(b.Kh[hp], b.kd, b.eh, ALU.mult)
                stt(b.rkd[hp], rT[:, hp, csl], P_(l, "rk", hp), b.kd, ALU.mult, ALU.mult)
                yield
            pbn = gbank()
            pbb = gbank().bitcast(BF16)
            for hp in range(2):
                mm(pbn[:, hp * 2:hp * 2 + 2], b.rkd[hp], headsel)
                tr(pbb[:, hp * 128:(hp + 1) * 128], b.Bh[hp], identb)
                tr(pbb[:, 256 + hp * 128:256 + (hp + 1) * 128], b.Kh[hp], identb)
            if not seen_bonus[c]:
                seen_bonus[c] = True
                cp(bonus[:, c, :], pbn[:, 0:4], "dve")
            else:
                tt(bonus[:, c, :], pbn[:, 0:4], bonus[:, c, :], ALU.add)
            cp(b.BhT, pbb[:, 0:256], "act")
            cp(b.KhT, pbb[:, 256:512], "act")
            for lhs_, rhs_, dst_, msk_ in ((b.Btm, b.Qk, b.Mf, m_strict_neg), (b.Ktm, b.Qk, b.Akk, m_strict),
                                           (b.Btm, b.Rt, b.Arb, m_incl_neg), (b.Ktm, b.Rt, b.Ark, m_incl)):
                pa = gbank()
                heads(lambda h, hp, jr, hs: mm(pa[:, hs], lhs_[hp][h % 2], rhs_[hp]))
                tt(dst_.rearrange("p (h t) -> p h t", h=4), pa.rearrange("p (h t) -> p h t", h=4),
                   msk_.unsqueeze(1).to_broadcast([128, 4, 128]), ALU.mult)
            yield
            for k in range(7):
                Tc = ident4 if k == 0 else b.T
                Zc = ident4 if k == 0 else b.Z
                pa = gbank()
                heads(lambda h, hp, jr, hs: mm(pa[:, hs], b.Mf[:, hs], Tc[:, hs]))
                mk = lvl[:, (k if fwd else 7 + k), :]
                tt(b.W.rearrange("p (h t) -> p h t", h=4), pa.rearrange("p (h t) -> p h t", h=4),
                   mk.unsqueeze(1).to_broadcast([128, 4, 128]), ALU.mult)
                yield
                if k < 6:
                    pt = gbank()
                    mm(pt, identb, Tc, start=True, stop=False, skip=True)
                    heads(lambda h, hp, jr, hs: mm(pt[:, hs], Zc[:, hs], b.W[:, hs], start=False, stop=False,
                                                   skip=True))
                pz = gbank()
                mm(pz, identb, Zc, start=True, stop=False, skip=True)
                heads(lambda h, hp, jr, hs: mm(pz[:, hs], b.W[:, hs], Zc[:, hs], start=False, stop=False, skip=True))
                if k < 6:
                    cp(b.T, pt, "act")
                cp(b.Z, pz, "act")
                yield
            yield "CHAIN"
            px = gbank()

            def xinit(h, hp, jr, hs):
                xs_ = slice(h * 64, (h + 1) * 64)
                mm(px[:, xs_], b.Qkm[hp][h % 2], STb[:, hp, :], start=True, stop=False)
                mm(px[:, xs_], b.Akk[:, hs], v_tok[:, c, xs_], start=False, stop=True)
            heads(xinit)
            cp(Xb, px[:, 0:256], "act")
            pu = gbank()
            heads(lambda h, hp, jr, hs: mm(pu[:, h * 64:(h + 1) * 64], b.Z[:, hs], Xb[:, h * 64:(h + 1) * 64]))
            cp(Ub, pu[:, 0:256], "act")
            py = gbank()

            def yfn(h, hp, jr, hs):
                xs_ = slice(h * 64, (h + 1) * 64)
                mm(py[:, xs_], b.Rtm[hp][h % 2], STb[:, hp, :], start=True, stop=False)
                mm(py[:, xs_], b.Arb[:, hs], Ub[:, xs_], start=False, stop=False)
                mm(py[:, xs_], b.Ark[:, hs], v_tok[:, c, xs_], start=False, stop=True)
            heads(yfn)
            if not seen_y[c]:
                seen_y[c] = True
                cp(y_tok[:, c, :], py[:, 0:256], "act")
            else:
                tt(y_tok[:, c, :], py[:, 0:256], y_tok[:, c, :], ALU.add)
            ph = gbank()
            for h in range(4):
                hp = h // 2
                xs_ = slice(h * 64, (h + 1) * 64)
                mm(ph[:, xs_], b.BhT[:, hp * 128:(hp + 1) * 128], Ub[:, xs_], start=True, stop=False)
                mm(ph[:, xs_], b.KhT[:, hp * 128:(hp + 1) * 128], v_tok[:, c, xs_], start=False, stop=True)
            for h in range(4):
                hp = h // 2
                jr = slice((h % 2) * 64, (h % 2) * 64 + 64)
                etot = (b.ein[hp][jr, 128:129] if fwd else b.ein[hp][jr, 0:1])
                stt(ST[jr, hp, :], ST[jr, hp, :], etot, ph[jr, h * 64:(h + 1) * 64], ALU.mult, ALU.add)
            cp(STb, ST, "act")
            yield

        pending = []
        for st_i in range(NT):
            pending.append((0, st_i, st_i))
            pending.append((1, NT - 1 - st_i, st_i))
        live = []
        free_sets = list(range(G_SETS))
        done = [0, 0]
        ndone_chunk = [0] * NT
        while pending or live:
            while pending and free_sets:
                d_, c_, idx_ = pending.pop(0)
                si = free_sets.pop(0)
                live.append({"g": step(d_, c_, dbs[si]), "d": d_, "c": c_, "idx": idx_, "set": si, "wait": False})
            for it in list(live):
                if it["wait"] and done[it["d"]] < it["idx"]:
                    continue
                try:
                    r_ = next(it["g"])
                    if r_ == "CHAIN":
                        it["wait"] = True
                except StopIteration:
                    live.remove(it)
                    free_sets.append(it["set"])
                    done[it["d"]] += 1
                    ndone_chunk[it["c"]] += 1
                    if ndone_chunk[it["c"]] == 2:
                        finalize_chunk(it["c"])
        dump_dbg(0, ymT)
        outproj_acc(l, 0, ymT)
        A.release(m_r)

    def mlp(l):
        m_m = A.mark()
        mT = A.alloc(8 * S, BF16).rearrange("p (k t) -> p k t", k=8)
        w2b = [A.alloc(8 * 512, BF16).rearrange("p (k c) -> p k c", k=8) for _ in range(2)]
        rls = [A.alloc(512, BF16) for _ in range(2)]
        rl_i = [0]
        for hg in range(4):
            for half in range(2):
                wb = load_w_cols(w1_d[l], hg * 1024 + half * 512, 512)
                for j in range(4):
                    hpt = half * 4 + j
                    for tc in range(4):
                        tsl = slice(tc * 512, (tc + 1) * 512)
                        pb = gbank()
                        for k in range(8):
                            mm(pb, wb[:, k, j * 128:(j + 1) * 128], xnT[:, k, tsl], start=(k == 0), stop=(k == 7))
                        rl = rls[rl_i[0] % 2]
                        rl_i[0] += 1
                        act(rl, pb, AF.Relu)
                        tt(mT[:, hpt, tsl], rl, rl, ALU.mult)
            for half in range(2):
                dma("pool", w2b[half],
                    w2_d[l, hg * 1024:(hg + 1) * 1024, half * 512:(half + 1) * 512].rearrange("(k p) c -> p k c", p=128))
            for ot in range(8):
                for tc in range(4):
                    tsl = slice(tc * 512, (tc + 1) * 512)
                    pb = gbank()
                    for k in range(8):
                        mm(pb, w2b[ot // 4][:, k, (ot % 4) * 128:(ot % 4 + 1) * 128], mT[:, k, tsl], start=(k == 0),
                           stop=(k == 7))
                    tt(hF[:, ot, tsl], pb, hF[:, ot, tsl], ALU.add)
        A.release(m_m)

    mixers = (dbg or {}).get("mixers", "srna") if dbg else "srna"
    for l in range(n_layers):
        dma("sp", rbt, rb_d[l])
        rmsnorm_to_xnT(lambda f, l=l: P_(l, "n1", f))
        if "s" in mixers:
            ssd_mixer(l)
        if "r" in mixers:
            lru_mixer(l)
        if "n" in mixers:
            na_mixer(l)
        if "a" in mixers:
            rwkv_mixer(l)
        if not (dbg and dbg.get("no_mlp")):
            rmsnorm_to_xnT(lambda f, l=l: P_(l, "n2", f))
            mlp(l)

    m0 = A.mark()
    sq = A.alloc(8 * 512, BF16).rearrange("p (f t) -> p f t", f=8)
    rstd = A.alloc(512, F32)
    yn = A.alloc(8 * 512, F32).rearrange("p (f t) -> p f t", f=8)
    ost = [A.alloc(D, F32) for _ in range(2)]
    fo = PP_L * L_ALL
    for tc in range(4):
        tsl = slice(tc * 512, (tc + 1) * 512)
        for f in range(8):
            act(sq[:, f, :], hF[:, f, tsl], AF.Square)
        pb = gbank()
        for f in range(8):
            mm(pb, onesb, sq[:, f, :], start=(f == 0), stop=(f == 7))
        act(rstd, pb, AF.Sqrt, scale=1.0 / D, bias=1e-5)
        recip(rstd, rstd)
        for f in range(8):
            stt(yn[:, f, :], hF[:, f, tsl], pp[:, fo + f:fo + f + 1], rstd, ALU.mult, ALU.mult)
        for ti in range(4):
            o = ost[ti % 2]
            for half in range(2):
                pb2 = gbank()
                for j in range(4):
                    f = half * 4 + j
                    tr(pb2[:, j * 128:(j + 1) * 128], yn[:, f, ti * 128:(ti + 1) * 128], identf)
                cp(o[:, half * 512:(half + 1) * 512], pb2, "act" if half else "dve")
            tok0 = tc * 512 + ti * 128
            dma("sp", out_d[tok0:tok0 + 128, :], o)
    A.release(m0)

    Sc.finish()
    Sc.emit(stack)
    stack.close()
    return nc, Sc, A


_CACHE = {}


def kernel(**inputs):
    inp = {k: np.asarray(v) for k, v in inputs.items()}
    pp, rb, lbd, t2 = build_host_params(inp)
    if "nc" not in _CACHE:
        _CACHE["nc"] = build_program(L_ALL)[0]
    nc = _CACHE["nc"]
    shared = {
        "w_in": np.ascontiguousarray(inp["w_in"], np.float32),
        "w_out": np.ascontiguousarray(inp["w_out"], np.float32),
        "w_mlp1": np.ascontiguousarray(inp["w_mlp1"], np.float32),
        "w_mlp2": np.ascontiguousarray(inp["w_mlp2"], np.float32),
        "rwkv_w_up": np.ascontiguousarray(inp["rwkv_w_up"], np.float32),
        "rwkv_a_up": np.ascontiguousarray(inp["rwkv_a_up"], np.float32),
        "rwkv_g_up": np.ascontiguousarray(inp["rwkv_g_up"], np.float32),
        "pp": pp, "rb": rb, "lbd": lbd, "t2": t2.reshape(L_ALL, 128, 14 * 256), "lvl": build_level_masks(),
    }
    x = np.ascontiguousarray(inp["x"], np.float32)
    in_maps = [dict(shared, x=x[b]) for b in range(8)]
    res = run_bass_kernel_spmd(nc, in_maps, core_ids=list(range(8)))
    return np.stack([np.asarray(r["out"], np.float32) for r in res.results], axis=0)
```
